# Optimizing a Trainium2 kernel written in Bass

```python
import math
import jax, jax.numpy as jnp
from jax import lax
import numpy as np

D_MODEL = 1024
BATCH = 32
SEQ = 2048
DEPTH = 2
DEC_BATCH = 8
DEC_SEQ = 32
PAST_LEN = 2048

CHUNK = 64
Q_BLOCK = 128
D_MIX = D_MODEL
A_HEADS = 8
A_HALF = 32
A_VDIM = 2 * A_HALF
B_HEADS = 4
B_DIM = 64
B_LEFT_CHUNKS = 8
B_REL_CLIP = 128
C_HEADS = 4
C_NOPE = 64
C_ROPE = 32
C_VDIM = 64
C_Q_LORA = 256
C_KV_LORA = 128
ROPE_THETA = 10000.0
T5_BUCKETS = 32
T5_MAX_DIST = 128
MEM_LEN = 256
M_HEADS = 4
M_DIM = 64
D_FF = 2816
CONV_W = 3
EPS = 1e-6
NEG = -1e30
N_MIX_PRE, N_MIX_POST, N_X_PRE, N_X_POST, N_F_PRE, N_F_POST, N_MEM = 0, 1, 2, 3, 4, 5, 6
N_NORMS = 7
IN_SIZES = (A_HEADS * 2 * A_HALF, A_HEADS * 2 * A_HALF, A_HEADS * A_VDIM,
            B_HEADS * B_DIM, B_HEADS * B_DIM, B_HEADS * B_DIM,
            C_Q_LORA, C_KV_LORA, C_ROPE)
IN_OFFSETS = tuple(int(v) for v in np.cumsum(IN_SIZES)[:-1])
D_IN = sum(IN_SIZES)

kernel_name = 'hybrid_chunk_stream_encoder_step'


def rms_norm(x, g):
    xf = x.astype(jnp.float32)
    y = xf * lax.rsqrt(jnp.mean(xf * xf, axis=-1, keepdims=True) + EPS)
    return (y * g.astype(jnp.float32)).astype(x.dtype)


def chunk_causal(qpos, kpos):
    return (kpos[None, :] // CHUNK) <= (qpos[:, None] // CHUNK)


def t5_bucket(rel):
    half = T5_BUCKETS // 2
    exact = half // 2
    ret = jnp.where(rel > 0, half, 0)
    n = jnp.abs(rel)
    nf = jnp.maximum(n, 1).astype(jnp.float32)
    large = exact + (jnp.log(nf / exact) / math.log(T5_MAX_DIST / exact) * (half - exact)).astype(jnp.int32)
    large = jnp.minimum(large, half - 1)
    return ret + jnp.where(n < exact, n, large)


def rope(x, pos):
    half = C_ROPE // 2
    inv = ROPE_THETA ** (-jnp.arange(half, dtype=jnp.float32) / half)
    ang = pos.astype(jnp.float32)[:, None] * inv[None, :]
    shape = (1, pos.shape[0]) + (1,) * (x.ndim - 3) + (half,)
    cos = jnp.cos(ang).reshape(shape)
    sin = jnp.sin(ang).reshape(shape)
    xf = x.astype(jnp.float32)
    x1, x2 = xf[..., :half], xf[..., half:]
    return jnp.concatenate([x1 * cos - x2 * sin, x2 * cos + x1 * sin], axis=-1).astype(x.dtype)


def split_in(h, w_in):
    B, T = h.shape[:2]
    aq, ak, av, bq, bk, bv, cq, ckv, cpe = jnp.split(h @ w_in, IN_OFFSETS, axis=-1)
    return dict(
        a_q=aq.reshape(B, T, A_HEADS, 2, A_HALF), a_k=ak.reshape(B, T, A_HEADS, 2, A_HALF),
        a_v=av.reshape(B, T, A_HEADS, A_VDIM),
        b_q=bq.reshape(B, T, B_HEADS, B_DIM), b_k=bk.reshape(B, T, B_HEADS, B_DIM),
        b_v=bv.reshape(B, T, B_HEADS, B_DIM),
        c_q=cq, c_kv=ckv, c_pe=cpe)


def blocked_queries(fn, qs, qpos):
    B, S = qs[0].shape[:2]
    n = S // Q_BLOCK
    blocks = tuple(jnp.moveaxis(q.reshape((B, n, Q_BLOCK) + q.shape[2:]), 1, 0) for q in qs)
    out = lax.map(lambda args: fn(args[0], *args[1:]), (qpos.reshape(n, Q_BLOCK),) + blocks)
    return jnp.moveaxis(out, 0, 1).reshape((B, S) + out.shape[3:])


def diff_lambda_value(lam, lam_init):
    lf = lam.astype(jnp.float32)
    return jnp.exp(jnp.sum(lf[0] * lf[1])) - jnp.exp(jnp.sum(lf[2] * lf[3])) + lam_init


def diff_attention(q, k, v, qpos, kpos, lam, t5_table):
    s = jnp.einsum('bqhcd,bkhcd->cbhqk', q, k).astype(jnp.float32) * (A_HALF ** -0.5)
    bias = jnp.transpose(t5_table[t5_bucket(kpos[None, :] - qpos[:, None])], (2, 0, 1)).astype(jnp.float32)
    s = jnp.where(chunk_causal(qpos, kpos), s + bias, NEG)
    p = jax.nn.softmax(s, axis=-1)
    attn = p[0] - lam * p[1]
    return jnp.einsum('bhqk,bkhd->bqhd', attn.astype(v.dtype), v)


def diff_finish(o, subln, lam_init):
    B, T = o.shape[:2]
    return (rms_norm(o, subln) * (1.0 - lam_init)).reshape(B, T, A_HEADS * A_VDIM)


def band_attention(q, k, v, qpos, kpos, rel_table):
    s = jnp.einsum('bcqhd,bckhd->bchqk', q, k).astype(jnp.float32) * (B_DIM ** -0.5)
    rel = jnp.clip(kpos[:, None, :] - qpos[:, :, None], -B_REL_CLIP, B_REL_CLIP) + B_REL_CLIP
    bias = jnp.transpose(rel_table[:, rel], (1, 0, 2, 3)).astype(jnp.float32)
    qc = qpos[:, :, None] // CHUNK
    kc = kpos[:, None, :] // CHUNK
    mask = (kpos[:, None, :] >= 0) & (kc <= qc) & (kc >= qc - B_LEFT_CHUNKS)
    s = jnp.where(mask[None, :, None], s + bias, NEG)
    p = jax.nn.softmax(s, axis=-1)
    return jnp.einsum('bchqk,bckhd->bcqhd', p.astype(v.dtype), v)


def band_prompt(q, k, v, rel_table):
    B, S = q.shape[:2]
    nc = S // CHUNK
    band = (B_LEFT_CHUNKS + 1) * CHUNK
    idx = jnp.arange(nc)[:, None] + jnp.arange(B_LEFT_CHUNKS + 1)[None, :]
    pad = ((0, 0), (B_LEFT_CHUNKS, 0), (0, 0), (0, 0), (0, 0))

    def chunks(t):
        return t.reshape(B, nc, CHUNK, B_HEADS, B_DIM)

    def gather_band(t):
        return jnp.pad(chunks(t), pad)[:, idx].reshape(B, nc, band, B_HEADS, B_DIM)

    qpos = jnp.arange(S, dtype=jnp.int32).reshape(nc, CHUNK)
    kpos = (jnp.arange(nc, dtype=jnp.int32)[:, None] - B_LEFT_CHUNKS) * CHUNK + jnp.arange(band, dtype=jnp.int32)[None, :]
    o = band_attention(chunks(q), gather_band(k), gather_band(v), qpos, kpos, rel_table)
    return o.reshape(B, S, B_HEADS * B_DIM)


def mla_queries(cq, pos, g_q, w_uq):
    B, T = cq.shape[:2]
    q = (rms_norm(cq, g_q) @ w_uq).reshape(B, T, C_HEADS, C_NOPE + C_ROPE)
    return q[..., :C_NOPE], rope(q[..., C_NOPE:], pos)


def mla_expand(latent, w_ukv):
    B, K = latent.shape[:2]
    kv = (latent @ w_ukv).reshape(B, K, C_HEADS, C_NOPE + C_VDIM)
    return kv[..., :C_NOPE], kv[..., C_NOPE:]


def mla_attention(q_nope, q_pe, k_nope, k_pe, v, qpos, kpos):
    s = (jnp.einsum('bqhd,bkhd->bhqk', q_nope, k_nope)
         + jnp.einsum('bqhr,bkr->bhqk', q_pe, k_pe)).astype(jnp.float32) * ((C_NOPE + C_ROPE) ** -0.5)
    s = jnp.where(chunk_causal(qpos, kpos), s, NEG)
    p = jax.nn.softmax(s, axis=-1)
    return jnp.einsum('bhqk,bkhd->bqhd', p.astype(v.dtype), v)


def mem_kv(mem, g, w_mk, w_mv):
    B = mem.shape[0]
    m = rms_norm(mem, g)
    return (m @ w_mk).reshape(B, MEM_LEN, M_HEADS, M_DIM), (m @ w_mv).reshape(B, MEM_LEN, M_HEADS, M_DIM)


def cross_attention(h, mk, mv, w_xq, w_xo):
    B, T = h.shape[:2]
    q = (h @ w_xq).reshape(B, T, M_HEADS, M_DIM)
    s = jnp.einsum('bqhd,bkhd->bhqk', q, mk.astype(q.dtype)).astype(jnp.float32) * (M_DIM ** -0.5)
    p = jax.nn.softmax(s, axis=-1)
    o = jnp.einsum('bhqk,bkhd->bqhd', p.astype(q.dtype), mv.astype(q.dtype)).reshape(B, T, M_HEADS * M_DIM)
    return o @ w_xo


def conv_ffn(h, past, w_gate, w_up, conv_w, conv_b, w_down):
    T = h.shape[1]
    g = h @ w_gate
    ext = jnp.concatenate([past.astype(g.dtype), g], axis=1)
    c = conv_b
    for j in range(CONV_W):
        c = c + ext[:, j:j + T] * conv_w[j]
    y = (jax.nn.silu(c) * (h @ w_up)) @ w_down
    return y, ext[:, ext.shape[1] - (CONV_W - 1):]


def tail_sublayers(x, mk, mv, conv_past, p):
    g = p['norms']
    x = x + rms_norm(cross_attention(rms_norm(x, g[N_X_PRE]), mk, mv, p['w_xq'], p['w_xo']), g[N_X_POST])
    f, conv_state = conv_ffn(rms_norm(x, g[N_F_PRE]), conv_past, p['w_gate'], p['w_up'],
                             p['conv_w'], p['conv_b'], p['w_down'])
    return x + rms_norm(f, g[N_F_POST]), conv_state


def layer_prompt(x, mem, p, t5_table, lam_init):
    B, S, _ = x.shape
    g = p['norms']
    pos = jnp.arange(S, dtype=jnp.int32)
    z = split_in(rms_norm(x, g[N_MIX_PRE]), p['w_in'])
    lam = diff_lambda_value(p['lam'], lam_init)
    a_k, a_v = z['a_k'], z['a_v']
    a = blocked_queries(lambda qp, q: diff_attention(q, a_k, a_v, qp, pos, lam, t5_table), (z['a_q'],), pos)
    a = diff_finish(a, p['subln'], lam_init)
    b = band_prompt(z['b_q'], z['b_k'], z['b_v'], p['rel'])
    qn, qe = mla_queries(z['c_q'], pos, p['g_q'], p['w_uq'])
    lat = rms_norm(z['c_kv'], p['g_kv'])
    kpe = rope(z['c_pe'], pos)
    kn, cv = mla_expand(lat, p['w_ukv'])
    c = blocked_queries(lambda qp, q1, q2: mla_attention(q1, q2, kn, kpe, cv, qp, pos), (qn, qe), pos)
    c = c.reshape(B, S, C_HEADS * C_VDIM)
    x = x + rms_norm(jnp.concatenate([a, b, c], axis=-1) @ p['w_out'], g[N_MIX_POST])
    mk, mv = mem_kv(mem, g[N_MEM], p['w_mk'], p['w_mv'])
    x, conv_state = tail_sublayers(x, mk, mv, jnp.zeros((B, CONV_W - 1, D_FF), x.dtype), p)
    nb = min(B_LEFT_CHUNKS * CHUNK, S)
    state = (a_k.reshape(B, S, A_HEADS, 2 * A_HALF), a_v, z['b_k'][:, S - nb:], z['b_v'][:, S - nb:],
             lat, kpe, mk, mv, conv_state)
    return x, state


def layer_sample(x, cache, p, t5_table, lam_init):
    ak_c, av_c, bk_c, bv_c, lat_c, kpe_c, mk, mv, conv_c = cache
    B, T, _ = x.shape
    P = ak_c.shape[1]
    nb = bk_c.shape[1]
    g = p['norms']
    pos = P + jnp.arange(T, dtype=jnp.int32)
    kpos = jnp.arange(P + T, dtype=jnp.int32)
    z = split_in(rms_norm(x, g[N_MIX_PRE]), p['w_in'])
    dt = z['a_q'].dtype
    lam = diff_lambda_value(p['lam'], lam_init)
    ak = jnp.concatenate([ak_c.reshape(B, P, A_HEADS, 2, A_HALF).astype(dt), z['a_k']], axis=1)
    av = jnp.concatenate([av_c.astype(dt), z['a_v']], axis=1)
    a = diff_finish(diff_attention(z['a_q'], ak, av, pos, kpos, lam, t5_table), p['subln'], lam_init)
    bk = jnp.concatenate([bk_c.astype(dt), z['b_k']], axis=1)
    bv = jnp.concatenate([bv_c.astype(dt), z['b_v']], axis=1)
    bkpos = (P - nb) + jnp.arange(nb + T, dtype=jnp.int32)
    b = band_attention(z['b_q'][:, None], bk[:, None], bv[:, None], pos[None], bkpos[None], p['rel'])[:, 0]
    b = b.reshape(B, T, B_HEADS * B_DIM)
    qn, qe = mla_queries(z['c_q'], pos, p['g_q'], p['w_uq'])
    lat_new = rms_norm(z['c_kv'], p['g_kv'])
    kpe_new = rope(z['c_pe'], pos)
    lat = jnp.concatenate([lat_c.astype(dt), lat_new], axis=1)
    kpe = jnp.concatenate([kpe_c.astype(dt), kpe_new], axis=1)
    kn, cv = mla_expand(lat, p['w_ukv'])
    c = mla_attention(qn, qe, kn, kpe, cv, pos, kpos).reshape(B, T, C_HEADS * C_VDIM)
    x = x + rms_norm(jnp.concatenate([a, b, c], axis=-1) @ p['w_out'], g[N_MIX_POST])
    x, conv_state = tail_sublayers(x, mk, mv, conv_c, p)
    state = (z['a_k'].reshape(B, T, A_HEADS, 2 * A_HALF), z['a_v'], z['b_k'], z['b_v'],
             lat_new, kpe_new, conv_state)
    return x, state


def setup_inputs(seed: int = 0) -> dict:
    key = jax.random.key(seed)
    ks = iter(jax.random.split(key, 40))

    def nrm(shape, scale=1.0):
        return jax.random.normal(next(ks), shape, jnp.float32) * scale

    def gain(shape):
        return 1.0 + nrm(shape, 0.05)

    nb = min(B_LEFT_CHUNKS * CHUNK, PAST_LEN)
    return {
        'x_prompt': nrm((BATCH, SEQ, D_MODEL)),
        'x_sample': nrm((DEC_BATCH, DEC_SEQ, D_MODEL)),
        'cache_a_k': nrm((DEPTH, DEC_BATCH, PAST_LEN, A_HEADS, 2 * A_HALF)),
        'cache_a_v': nrm((DEPTH, DEC_BATCH, PAST_LEN, A_HEADS, A_VDIM)),
        'cache_b_k': nrm((DEPTH, DEC_BATCH, nb, B_HEADS, B_DIM)),
        'cache_b_v': nrm((DEPTH, DEC_BATCH, nb, B_HEADS, B_DIM)),
        'cache_c_latent': nrm((DEPTH, DEC_BATCH, PAST_LEN, C_KV_LORA)),
        'cache_c_rope_k': nrm((DEPTH, DEC_BATCH, PAST_LEN, C_ROPE)),
        'cache_mem_k': nrm((DEPTH, DEC_BATCH, MEM_LEN, M_HEADS, M_DIM)),
        'cache_mem_v': nrm((DEPTH, DEC_BATCH, MEM_LEN, M_HEADS, M_DIM)),
        'state_ffn_conv': nrm((DEPTH, DEC_BATCH, CONV_W - 1, D_FF)),
        'mem_prompt': nrm((BATCH, MEM_LEN, D_MODEL)),
        'w_in': nrm((DEPTH, D_MODEL, D_IN), D_MODEL ** -0.5),
        'w_out': nrm((DEPTH, D_MIX, D_MODEL), D_MIX ** -0.5),
        'norms': gain((DEPTH, N_NORMS, D_MODEL)),
        'diff_lambda': nrm((DEPTH, 4, A_HALF), 0.1),
        'diff_subln': gain((DEPTH, A_VDIM)),
        't5_bias': nrm((T5_BUCKETS, A_HEADS), 0.5),
        'band_rel_bias': nrm((DEPTH, B_HEADS, 2 * B_REL_CLIP + 1), 0.5),
        'mla_q_norm': gain((DEPTH, C_Q_LORA)),
        'mla_w_uq': nrm((DEPTH, C_Q_LORA, C_HEADS * (C_NOPE + C_ROPE)), C_Q_LORA ** -0.5),
        'mla_kv_norm': gain((DEPTH, C_KV_LORA)),
        'mla_w_ukv': nrm((DEPTH, C_KV_LORA, C_HEADS * (C_NOPE + C_VDIM)), C_KV_LORA ** -0.5),
        'w_xq': nrm((DEPTH, D_MODEL, M_HEADS * M_DIM), D_MODEL ** -0.5),
        'w_mk': nrm((DEPTH, D_MODEL, M_HEADS * M_DIM), D_MODEL ** -0.5),
        'w_mv': nrm((DEPTH, D_MODEL, M_HEADS * M_DIM), D_MODEL ** -0.5),
        'w_xo': nrm((DEPTH, M_HEADS * M_DIM, D_MODEL), (M_HEADS * M_DIM) ** -0.5),
        'w_gate': nrm((DEPTH, D_MODEL, D_FF), D_MODEL ** -0.5),
        'w_up': nrm((DEPTH, D_MODEL, D_FF), D_MODEL ** -0.5),
        'conv_w': nrm((DEPTH, CONV_W, D_FF), CONV_W ** -0.5),
        'conv_b': nrm((DEPTH, D_FF), 0.01),
        'w_down': nrm((DEPTH, D_FF, D_MODEL), D_FF ** -0.5),
    }


def reference(x_prompt, x_sample, cache_a_k, cache_a_v, cache_b_k, cache_b_v, cache_c_latent,
              cache_c_rope_k, cache_mem_k, cache_mem_v, state_ffn_conv, mem_prompt,
              w_in, w_out, norms, diff_lambda, diff_subln, t5_bias, band_rel_bias,
              mla_q_norm, mla_w_uq, mla_kv_norm, mla_w_ukv, w_xq, w_mk, w_mv, w_xo,
              w_gate, w_up, conv_w, conv_b, w_down):
    xp, xs = x_prompt, x_sample
    sp_all, ss_all = [], []
    for l in range(DEPTH):
        p = dict(w_in=w_in[l], w_out=w_out[l], norms=norms[l], lam=diff_lambda[l], subln=diff_subln[l],
                 rel=band_rel_bias[l], g_q=mla_q_norm[l], w_uq=mla_w_uq[l], g_kv=mla_kv_norm[l],
                 w_ukv=mla_w_ukv[l], w_xq=w_xq[l], w_mk=w_mk[l], w_mv=w_mv[l], w_xo=w_xo[l],
                 w_gate=w_gate[l], w_up=w_up[l], conv_w=conv_w[l], conv_b=conv_b[l], w_down=w_down[l])
        lam_init = 0.8 - 0.6 * math.exp(-0.3 * l)
        xp, sp = layer_prompt(xp, mem_prompt, p, t5_bias, lam_init)
        cache_l = (cache_a_k[l], cache_a_v[l], cache_b_k[l], cache_b_v[l], cache_c_latent[l],
                   cache_c_rope_k[l], cache_mem_k[l], cache_mem_v[l], state_ffn_conv[l])
        xs, ss = layer_sample(xs, cache_l, p, t5_bias, lam_init)
        sp_all.append(sp)
        ss_all.append(ss)

    def stk(states, i):
        return jnp.stack([s[i] for s in states])

    return (xp, xs,
            stk(sp_all, 0), stk(sp_all, 1), stk(sp_all, 2), stk(sp_all, 3), stk(sp_all, 4),
            stk(sp_all, 5), stk(sp_all, 6), stk(sp_all, 7), stk(sp_all, 8),
            stk(ss_all, 0), stk(ss_all, 1), stk(ss_all, 2), stk(ss_all, 3), stk(ss_all, 4),
            stk(ss_all, 5), stk(ss_all, 6))
```

```python
import contextlib
import math
import os
DBG = os.environ.get('KDBG', '')
import numpy as np
import concourse.bass as bass
import concourse.mybir as mybir
from concourse.bass_utils import run_bass_kernel_spmd

F32 = mybir.dt.float32
BF16 = mybir.dt.bfloat16
ALU = mybir.AluOpType
AF = mybir.ActivationFunctionType
AX = mybir.AxisListType

D = 1024
DIN = 2720
DFF = 2816
NFC = 22
MEM = 256
EPS = 1e-6
NDS = 32
SAME_ENGINE_SYNC = False


class StopBuild(Exception):
    pass


CKSTOP = os.environ.get('KCK', '')


STOPPED = [False]


def ck(tag):
    if CKSTOP and tag == CKSTOP:
        STOPPED[0] = True


class Buf:
    __slots__ = ("w", "r", "excl", "strict")

    def __init__(self, excl=False, strict=False):
        self.w = None
        self.r = {}
        self.excl = excl
        self.strict = strict


class Prog:
    def __init__(self, nc, stack):
        self.nc = nc
        self.eh = {"pe": nc.tensor, "act": nc.scalar, "dve": nc.vector, "pool": nc.gpsimd, "sp": nc.sync}
        self.sems = {}
        for e in self.eh:
            self.sems[e] = stack.enter_context(nc.semaphore("s_" + e))
        self.cnt = {e: 0 for e in self.eh}
        self.seen = {e: {} for e in self.eh}
        for i in range(NDS):
            self.sems[("d", i)] = stack.enter_context(nc.semaphore("s_d%d" % i))
        self.dcnt = [0] * NDS
        self.dnext = {"sp": 0, "pool": 0, "act": 0}
        self.drange = {"sp": (0, NDS - 4), "pool": (NDS - 4, NDS), "act": (0, NDS - 4)}
        self.nins = 0

    def _deps(self, eng, reads, writes, extra=()):
        needs = {}
        strict_own = 0
        for b in reads:
            t = b.w
            if t is not None and needs.get(t[0], 0) < t[1]:
                needs[t[0]] = t[1]
            if t is not None and t[0] == eng and t[1] > strict_own and eng != "pe":
                strict_own = t[1]
            if b.excl:
                for k, v in b.r.items():
                    if needs.get(k, 0) < v:
                        needs[k] = v
        for b in writes:
            t = b.w
            if t is not None and needs.get(t[0], 0) < t[1]:
                needs[t[0]] = t[1]
            for k, v in b.r.items():
                if needs.get(k, 0) < v:
                    needs[k] = v
        for t in extra:
            if needs.get(t[0], 0) < t[1]:
                needs[t[0]] = t[1]
        seen = self.seen[eng]
        for k, v in needs.items():
            if k == eng and (eng == "pe" or not SAME_ENGINE_SYNC) and eng != "pool":
                if strict_own and seen.get(k, 0) < strict_own:
                    seen[k] = strict_own
                    self.eh[eng].wait_ge(self.sems[k], strict_own)
                continue
            if seen.get(k, 0) >= v:
                continue
            seen[k] = v
            self.eh[eng].wait_ge(self.sems[k], v)

    def op(self, eng, fn, reads=(), writes=(), inc=True):
        if STOPPED[0]:
            return
        self._deps(eng, reads, writes)
        self.nins += 1
        ins = fn(self.eh[eng])
        if inc:
            self.cnt[eng] += 1
            v = self.cnt[eng]
            ins.then_inc(self.sems[eng], 1)
        else:
            v = self.cnt[eng] + 1
        for b in reads:
            b.r[eng] = v
        for b in writes:
            b.w = (eng, v)
            b.r = {}

    def dma(self, eng, out, in_, reads=(), writes=(), **kw):
        if STOPPED[0]:
            return
        lo, hi = self.drange[eng]
        i = lo + self.dnext[eng]
        self.dnext[eng] = (self.dnext[eng] + 1) % (hi - lo)
        key = ("d", i)
        extra = ((key, self.dcnt[i]),) if self.dcnt[i] else ()
        self._deps(eng, reads, writes, extra)
        self.dcnt[i] += 16
        v = self.dcnt[i]
        self.nins += 1
        self.eh[eng].dma_start(out=out, in_=in_, **kw).then_inc(self.sems[key], 16)
        for b in reads:
            b.r[key] = v
        for b in writes:
            b.w = (key, v)
            b.r = {}

    def barrier(self, force=False):
        if STOPPED[0] and not force:
            return
        keys = [e for e in self.eh if self.cnt[e]] + [("d", i) for i in range(NDS) if self.dcnt[i]]
        for e in self.eh:
            seen = self.seen[e]
            for k in keys:
                v = self.cnt[k] if not isinstance(k, tuple) else self.dcnt[k[1]]
                if k == e and e != "pool":
                    continue
                if seen.get(k, 0) >= v:
                    continue
                seen[k] = v
                self.eh[e].wait_ge(self.sems[k], v)

    def finish(self):
        self.barrier(force=True)


def t5_bucket_np(rel):
    half, exact = 16, 8
    ret = np.where(rel > 0, half, 0)
    n = np.abs(rel)
    nf = np.maximum(n, 1).astype(np.float32)
    large = exact + (np.log(nf / np.float32(exact)) / np.float32(math.log(128 / exact)) * np.float32(half - exact)).astype(np.int32)
    large = np.minimum(large, half - 1)
    return ret + np.where(n < exact, n, large)


def host_consts(SEQ, PAST, TS):
    m = np.arange(1151)
    bk = t5_bucket_np(511 - m)
    oh_t5 = (bk[None, :] == np.arange(32)[:, None]).astype(np.float32)
    m2 = np.arange(1151)
    j = np.clip(511 - m2, -128, 128) + 128
    ohb = (j[None, :] == np.arange(384)[:, None]).astype(np.float32)

    def rope_tab(pos):
        half = 16
        inv = np.float32(10000.0) ** (-np.arange(half, dtype=np.float32) / np.float32(half))
        ang = pos.astype(np.float32)[:, None] * inv[None, :]
        c = np.cos(ang).astype(np.float32)
        s = np.sin(ang).astype(np.float32)
        c4 = np.repeat(c[:, None, :], 4, axis=1)
        s4 = np.repeat(s[:, None, :], 4, axis=1)
        return np.ascontiguousarray(np.stack([c4, s4], axis=1).reshape(len(pos), 128))

    return {"oh_t5": oh_t5, "oh_b": ohb, "rope_p": rope_tab(np.arange(SEQ)), "rope_s": rope_tab(PAST + np.arange(TS))}


def vis_info(mode, q0, nq, k0, nk):
    qch = [(c, min(c + 64, nq)) for c in range(0, nq, 64)]
    kch = [(p, min(p + 64, nk)) for p in range(0, nk, 64)]
    V = {}
    for (p0, p1) in kch:
        kc = (k0 + p0) // 64
        for (c0, c1) in qch:
            qc = (q0 + c0) // 64
            if mode == "causal":
                ok = kc <= qc
            elif mode == "band":
                ok = (kc <= qc) and (kc >= qc - 8)
            else:
                ok = True
            V[(p0, c0)] = ok
    viscols = [(c0, c1) for (c0, c1) in qch if any(V[(p0, c0)] for (p0, _) in kch)]
    if not viscols:
        return None
    cs = (min(c0 for c0, _ in viscols) // 128) * 128
    ce = min(nq, ((max(c1 for _, c1 in viscols) + 127) // 128) * 128)
    masks = []
    for (p0, p1) in kch:
        run = None
        for (c0, c1) in qch:
            if c0 < cs or c0 >= ce:
                continue
            if not V[(p0, c0)]:
                if run is not None and run[1] == c0:
                    run[1] = c1
                else:
                    if run is not None:
                        masks.append((p0, p1, run[0], run[1]))
                    run = [c0, c1]
        if run is not None:
            masks.append((p0, p1, run[0], run[1]))
    return cs, ce, masks


def build(NP, SEQ, PAST, TS=32, DEPTH=2, stop=0):
    nc = bass.Bass("TRN2", target_bir_lowering=False)
    I, O = {}, {}

    def inp(n, shape):
        I[n] = nc.dram_tensor(n, list(shape), F32, kind="ExternalInput")

    def outp(n, shape):
        O[n] = nc.dram_tensor(n, list(shape), F32, kind="ExternalOutput")

    NBP = min(512, SEQ)
    NBS = min(512, PAST)
    inp("x_prompt", (NP, SEQ, D)); inp("x_sample", (1, TS, D))
    inp("cache_a_k", (DEPTH, PAST, 512)); inp("cache_a_v", (DEPTH, PAST, 512))
    inp("cache_b_k", (DEPTH, NBS, 256)); inp("cache_b_v", (DEPTH, NBS, 256))
    inp("cache_c_latent", (DEPTH, PAST, 128)); inp("cache_c_rope_k", (DEPTH, PAST, 32))
    inp("cache_mem_k", (DEPTH, MEM, 256)); inp("cache_mem_v", (DEPTH, MEM, 256))
    inp("state_ffn_conv", (DEPTH, 2, DFF)); inp("mem_prompt", (NP, MEM, D))
    inp("w_in", (DEPTH, D, DIN)); inp("w_out", (DEPTH, D, D)); inp("norms", (DEPTH, 7, D))
    inp("diff_lambda", (DEPTH, 128)); inp("diff_subln", (DEPTH, 64)); inp("t5_bias", (32, 8))
    inp("band_rel_bias", (DEPTH, 4, 257)); inp("mla_q_norm", (DEPTH, 256)); inp("mla_w_uq", (DEPTH, 256, 384))
    inp("mla_kv_norm", (DEPTH, 128)); inp("mla_w_ukv", (DEPTH, 128, 512))
    inp("w_xq", (DEPTH, D, 256)); inp("w_mk", (DEPTH, D, 256)); inp("w_mv", (DEPTH, D, 256)); inp("w_xo", (DEPTH, 256, D))
    inp("w_gate", (DEPTH, D, DFF)); inp("w_up", (DEPTH, D, DFF)); inp("conv_w", (DEPTH, 3, DFF)); inp("conv_b", (DEPTH, DFF))
    inp("w_down", (DEPTH, DFF, D))
    inp("oh_t5", (32, 1151)); inp("oh_b", (384, 1151)); inp("rope_p", (SEQ, 128)); inp("rope_s", (TS, 128))
    outp("y_prompt", (NP, SEQ, D)); outp("y_sample", (1, TS, D))
    outp("p_a_k", (DEPTH, NP, SEQ, 512)); outp("p_a_v", (DEPTH, NP, SEQ, 512))
    outp("p_b_k", (DEPTH, NP, NBP, 256)); outp("p_b_v", (DEPTH, NP, NBP, 256))
    outp("p_lat", (DEPTH, NP, SEQ, 128)); outp("p_kpe", (DEPTH, NP, SEQ, 32))
    outp("p_mk", (DEPTH, NP, MEM, 256)); outp("p_mv", (DEPTH, NP, MEM, 256)); outp("p_conv", (DEPTH, NP, 2, DFF))
    outp("s_a_k", (DEPTH, 1, TS, 512)); outp("s_a_v", (DEPTH, 1, TS, 512))
    outp("s_b_k", (DEPTH, 1, TS, 256)); outp("s_b_v", (DEPTH, 1, TS, 256))
    outp("s_lat", (DEPTH, 1, TS, 128)); outp("s_kpe", (DEPTH, 1, TS, 32)); outp("s_conv", (DEPTH, 1, 2, DFF))
    IA = {k: v.ap() for k, v in I.items()}
    OA = {k: v.ap() for k, v in O.items()}
    WB = {}
    for n in ("w_in", "w_out", "mla_w_uq", "mla_w_ukv", "w_xq", "w_mk", "w_mv", "w_xo", "w_down"):
        WB[n] = nc.dram_tensor("wb_" + n, list(I[n].shape), BF16, kind="Internal")
    WBA = {k: v.ap() for k, v in WB.items()}
    WGUh = nc.dram_tensor("wb_wgu", [DEPTH, NFC, 128, 2048], BF16, kind="Internal")
    WGU = WGUh.ap()
    Escr = nc.dram_tensor("Escr", [8 + DEPTH * 4, 128, 1024], BF16, kind="Internal")
    EscrA = Escr.ap()
    scrA = nc.dram_tensor("scrA", [8, 1151], F32, kind="Internal")
    scrB = nc.dram_tensor("scrB", [DEPTH * 4, 1151], F32, kind="Internal")

    uid = [0]
    G = contextlib.ExitStack()
    with G:
        P = Prog(nc, G)

        def sb(stack, shape, dt, name="t"):
            uid[0] += 1
            return stack.enter_context(nc.sbuf_tensor("%s_%d" % (name, uid[0]), list(shape), dt))

        dbl = [G.enter_context(nc.psum_tensor("dbl%d" % i, [128, 1024], F32)) for i in range(3)]
        bdbl = [Buf(True) for _ in range(3)]
        ptr = [G.enter_context(nc.psum_tensor("ptr%d" % i, [128, 1024], BF16)) for i in range(2)]
        bptr = [Buf(True) for _ in range(2)]
        rot = {"d": 0, "t": 0}

        def next_dbl():
            i = rot["d"]; rot["d"] = 1 - i
            return dbl[i], bdbl[i]

        def next_ptr():
            i = rot["t"]; rot["t"] = 1 - i
            return ptr[i], bptr[i]

        pe = lambda fn, r=(), w=(), inc=True: P.op("pe", fn, r, w, inc)
        act = lambda fn, r=(), w=(): P.op("act", fn, r, w)
        dve = lambda fn, r=(), w=(): P.op("dve", fn, r, w)

        zc = sb(G, [128, 1], F32, "zc"); ec = sb(G, [128, 1], F32, "ec"); b_zc = Buf()
        dve(lambda e: e.memset(zc[:], 0.0), (), [b_zc])
        dve(lambda e: e.memset(ec[:], EPS), (), [b_zc])
        P.barrier()
        nc.const_aps.register(F32, 0.0, zc[:, 0:1])
        nc.const_aps.register(F32, EPS, ec[:, 0:1])
        ident = sb(G, [128, 128], BF16, "ident"); b_ident = Buf()
        Jm = sb(G, [128, 128], F32, "J"); b_J = Buf()
        P.op("pool", lambda e: e.memset(ident[:], 0.0), (), [b_ident])
        P.op("pool", lambda e: e.affine_select(out=ident[:], in_=ident[:], pattern=[[-1, 128]], compare_op=ALU.not_equal, fill=1.0, base=0, channel_multiplier=1), [b_ident], [b_ident])
        P.op("pool", lambda e: e.memset(Jm[:], 0.0), (), [b_J])
        P.op("pool", lambda e: e.affine_select(out=Jm[:], in_=Jm[:], pattern=[[1, 128]], compare_op=ALU.not_equal, fill=1.0, base=-127, channel_multiplier=1), [b_J], [b_J])
        bW = Buf()
        for n in ([] if 'W' in DBG else WB):
            src = IA[n].rearrange("l a b -> (l a) b")
            dst = WBA[n].rearrange("l a b -> (l a) b")
            rows = src.shape[0]
            step = 256
            for r in range(0, rows, step):
                P.dma("pool", dst[r:min(r + step, rows)], src[r:min(r + step, rows)], (), [Buf()])
        for l_ in range(DEPTH):
            for j_, wn in enumerate(("w_gate", "w_up")):
                wsrc_ = IA[wn][l_].rearrange("(c p) n -> p c n", p=128)
                for f_ in range(NFC):
                    P.dma("pool", WGU[l_, f_][:, j_ * 1024:(j_ + 1) * 1024].rearrange("p (c j) -> p c j", c=8), wsrc_[:, :, f_ * 128:(f_ + 1) * 128], (), [Buf()])
        b_Escr = Buf()
        with contextlib.ExitStack() as S:
            EA = sb(S, [128, 8, 1024], BF16, "EA"); b_EA = Buf()
            EB = sb(S, [128, DEPTH * 4, 1024], BF16, "EB"); b_EB = Buf()
            t5 = sb(S, [32, 8], F32); b_t5 = Buf()
            oh = sb(S, [32, 1151], F32); b_oh = Buf()
            c15 = sb(S, [8, 1], F32); b_c15 = Buf(strict=True)
            wA = sb(S, [8, 1151], F32); b_wA = Buf()
            P.dma("sp", t5[:], IA["t5_bias"], (), [b_t5])
            P.dma("sp", oh[:], IA["oh_t5"], (), [b_oh])
            P.dma("sp", c15[:], bass.AP(I["t5_bias"], 15 * 8, [[1, 8], [1, 1]]), (), [b_c15])
            dve(lambda e: e.tensor_scalar(out=c15[:], in0=c15[:], scalar1=-1.0, scalar2=None, op0=ALU.mult), [b_c15], [b_c15])
            for (c0, c1) in ((0, 512), (512, 1024), (1024, 1151)):
                pd, bpd = next_dbl()
                pe(lambda e, c0=c0, c1=c1, pd=pd: e.matmul(pd[0:8, 0:c1 - c0], lhsT=t5[:, :], rhs=oh[:, c0:c1], start=True, stop=True), [b_t5, b_oh], [bpd])
                act(lambda e, c0=c0, c1=c1, pd=pd: e.activation(out=wA[:, c0:c1], in_=pd[0:8, 0:c1 - c0], func=AF.Exp, bias=c15[:, 0:1], scale=1.0), [bpd, b_c15], [b_wA])
            b_scrA = Buf()
            P.dma("sp", scrA.ap(), wA[:], [b_wA], [b_scrA])
            Gp = sb(S, [128, 1024], F32); b_Gp = Buf()
            for h in range(8):
                P.dma("sp", Gp[:], bass.AP(scrA, h * 1151, [[1, 128], [1, 1024]]), [b_scrA], [b_Gp])
                pd, bpd = next_dbl()
                for hf in range(2):
                    pe(lambda e, hf=hf, pd=pd: e.matmul(pd[:, hf * 512:(hf + 1) * 512], lhsT=Jm[:], rhs=Gp[:, hf * 512:(hf + 1) * 512], start=True, stop=True), [b_J, b_Gp], [bpd])
                act(lambda e, h=h, pd=pd: e.copy(out=EA[:, h, :], in_=pd[:, :]), [bpd], [b_EA])
            relT = sb(S, [128, 3, DEPTH * 4], F32); b_relT = Buf()
            ohb = sb(S, [128, 3, 1151], F32); b_ohb = Buf()
            c0b = sb(S, [DEPTH * 4, 1], F32); b_c0b = Buf(strict=True)
            wB = sb(S, [DEPTH * 4, 1151], F32); b_wB = Buf()
            dve(lambda e: e.memset(relT[:], 0.0), (), [b_relT])
            for c in range(3):
                nj = 128 if c < 2 else 1
                P.dma("sp", relT[0:nj, c, :], bass.AP(I["band_rel_bias"], c * 128, [[1, nj], [257, DEPTH * 4]]), (), [b_relT], allow_slow_non_contiguous=True)
            P.dma("sp", ohb[:], IA["oh_b"].rearrange("(c p) m -> p c m", p=128), (), [b_ohb])
            P.dma("sp", c0b[:], bass.AP(I["band_rel_bias"], 0, [[257, DEPTH * 4], [1, 1]]), (), [b_c0b], allow_slow_non_contiguous=True)
            dve(lambda e: e.tensor_scalar(out=c0b[:], in0=c0b[:], scalar1=-1.0, scalar2=None, op0=ALU.mult), [b_c0b], [b_c0b])
            for (c0_, c1_) in ((0, 512), (512, 1024), (1024, 1151)):
                pd, bpd = next_dbl()
                for c in range(3):
                    pe(lambda e, c=c, pd=pd, c0_=c0_, c1_=c1_: e.matmul(pd[0:DEPTH * 4, 0:c1_ - c0_], lhsT=relT[:, c, :], rhs=ohb[:, c, c0_:c1_], start=(c == 0), stop=(c == 2)), [b_relT, b_ohb], [bpd], inc=(c == 2))
                act(lambda e, pd=pd, c0_=c0_, c1_=c1_: e.activation(out=wB[:, c0_:c1_], in_=pd[0:DEPTH * 4, 0:c1_ - c0_], func=AF.Exp, bias=c0b[:, 0:1], scale=1.0), [bpd, b_c0b], [b_wB])
            b_scrB = Buf()
            P.dma("sp", scrB.ap(), wB[:], [b_wB], [b_scrB])
            for lh in range(DEPTH * 4):
                P.dma("sp", Gp[:, :], bass.AP(scrB, lh * 1151, [[1, 128], [1, 1024]]), [b_scrB], [b_Gp])
                pd, bpd = next_dbl()
                for hf in range(2):
                    pe(lambda e, hf=hf, pd=pd: e.matmul(pd[:, hf * 512:(hf + 1) * 512], lhsT=Jm[:], rhs=Gp[:, hf * 512:(hf + 1) * 512], start=True, stop=True), [b_J, b_Gp], [bpd])
                act(lambda e, lh=lh, pd=pd: e.copy(out=EB[:, lh, :], in_=pd[:, :]), [bpd], [b_EB])
            P.dma("sp", EscrA[0:8].rearrange("h p u -> p h u"), EA[:, :, :], [b_EA], [b_Escr])
            P.dma("sp", EscrA[8:8 + DEPTH * 4].rearrange("h p u -> p h u"), EB[:, :, :], [b_EB], [Buf()])
            P.barrier()

        neglam = sb(G, [128, DEPTH], F32, "neglam"); b_lam = Buf(strict=True)
        subln = sb(G, [128, DEPTH, 64], F32, "subln"); b_subln = Buf()
        LAM_INIT = [0.8 - 0.6 * math.exp(-0.3 * l) for l in range(DEPTH)]
        with contextlib.ExitStack() as S:
            lt = sb(S, [128, 128], F32); b_lt = Buf()
            junk = sb(S, [128, 32], F32); b_junk = Buf()
            s12 = sb(S, [128, 2], F32); b_s12 = Buf(strict=True)
            for l in range(DEPTH):
                P.dma("sp", lt[:], bass.AP(I["diff_lambda"], l * 128, [[0, 128], [1, 128]]), (), [b_lt])
                P.dma("sp", subln[:, l, :], bass.AP(I["diff_subln"], l * 64, [[0, 128], [1, 64]]), (), [b_subln])
                for i in range(2):
                    dve(lambda e, i=i: e.scalar_tensor_tensor(out=junk[:], in0=lt[:, 64 * i:64 * i + 32], scalar=1.0, in1=lt[:, 64 * i + 32:64 * i + 64], op0=ALU.mult, op1=ALU.mult, accum_out=s12[:, i:i + 1]), [b_lt], [b_junk, b_s12])
                act(lambda e: e.activation(out=s12[:], in_=s12[:], func=AF.Exp, bias=zc[:, 0:1]), [b_s12], [b_s12])
                dve(lambda e, l=l: e.tensor_tensor(out=neglam[:, l:l + 1], in0=s12[:, 1:2], in1=s12[:, 0:1], op=ALU.subtract), [b_s12], [b_lam])
                dve(lambda e, l=l: e.tensor_scalar(out=neglam[:, l:l + 1], in0=neglam[:, l:l + 1], scalar1=-LAM_INIT[l], scalar2=None, op0=ALU.add), [b_lam], [b_lam])
                dve(lambda e, l=l: e.tensor_scalar(out=subln[:, l, :], in0=subln[:, l, :], scalar1=1.0 - LAM_INIT[l], scalar2=None, op0=ALU.mult), [b_subln], [b_subln])
            P.barrier()

        if stop == 1:
            P.finish()
            return nc

        def bcast_row(dst, src_handle, off, n, bufs):
            P.dma("sp", dst, bass.AP(src_handle, off, [[0, 128], [1, n]]), (), bufs)

        def rstd_of(ss, n, Dd, bss):
            act(lambda e: e.activation(out=ss, in_=ss, func=AF.Ln, bias=ec[0:n, 0:1], scale=1.0 / Dd), [bss], [bss])
            act(lambda e: e.activation(out=ss, in_=ss, func=AF.Exp, bias=zc[0:n, 0:1], scale=-0.5), [bss], [bss])

        def run_layer(kind, s, l):
            prm = kind == "p"
            T = SEQ if prm else TS
            tiles = [(t0, min(128, T - t0)) for t0 in range(0, T, 128)]
            NT = len(tiles)
            past = 0 if prm else PAST
            NKB_past = past // 128
            NK = past + T
            kblocks = [(j * 128, j * 128, 128) for j in range(NKB_past)] + [(past + t0, past + t0, n) for (t0, n) in tiles]
            NKB = len(kblocks)
            qblocks = []
            for q0 in range(0, T, 512):
                nq = min(512, T - q0)
                qblocks.append((q0, past + q0, nq))
            xsrc = (IA["x_prompt"][s] if prm else IA["x_sample"][0]) if l == 0 else (OA["y_prompt"][s] if prm else OA["y_sample"][0])
            ydst = OA["y_prompt"][s] if prm else OA["y_sample"][0]
            ybufs = YB[(kind, s)]
            pfx = "p_" if prm else "s_"
            so = s if prm else 0
            rope_h = I["rope_p"] if prm else I["rope_s"]
            wl = {k: v[l] for k, v in WBA.items()}
            scale_of = {"A": 32 ** -0.5, "B": 64 ** -0.5, "C": 96 ** -0.5, "X": 64 ** -0.5}

            L = contextlib.ExitStack()
            with L:
                b_atok = Buf()
                NPT = 4
                ptiles = [sb(L, [128, 2, 512], BF16, "PT") for _ in range(NPT)]
                bpt_ = [Buf() for _ in range(NPT)]
                prot = [0]
                rec = sb(L, [128, 2, 4], F32, "rec"); b_rec = Buf(strict=True)
                otmp = sb(L, [128, 4, 64], F32, "otmp"); b_otmp = Buf()
                o0t = sb(L, [128, 4, 64], F32, "o0t"); b_o0t = Buf()
                sq = sb(L, [128, 4, 64], F32, "sq"); b_sq = Buf()
                ssA = sb(L, [128, 4], F32, "ssA"); b_ssA = Buf(strict=True)

                accs = [sb(L, [128, 2, 260], F32, "accs") for _ in range(2)]
                baccs = [Buf(), Buf()]
                arot = [0]
                pending = []

                def flush_pending(step=None):
                    while pending and (step is None or pending[0][0] <= step):
                        pending.pop(0)[1]()

                def attend(grp, lanes, qb, mode, kbl, rbufs, epilogue):
                    attend_many([(grp, lanes, qb, mode, kbl, rbufs, epilogue)])

                def attend_many(calls):
                    acc, bacc = dbl[2], bdbl[2]
                    G_ = []
                    for ci, (grp, lanes, qb, mode, kbl, rbufs, epilogue) in enumerate(calls):
                        (qc0, qpos, nq) = qb
                        st_ = []
                        for kbi, (kc0, kpos, nk) in kbl:
                            vi = vis_info(mode, qpos, nq, kpos, nk)
                            if vi is not None:
                                st_.append((kbi, kc0, kpos, nk, vi[0], vi[2], vi[1]))
                        for si, stp in enumerate(st_):
                            G_.append((ci, si, len(st_), stp))
                    qk = {}

                    def emit_qk(g):
                        ci, si, ns, (kbi, kc0, kpos, nk, cs, masks, ce) = G_[g]
                        (grp, lanes, qb, mode, kbl, rbufs, epilogue) = calls[ci]
                        (qc0, qpos, nq) = qb
                        pd, bpd = next_dbl()
                        for li, ln in enumerate(lanes):
                            pe(lambda e, ln=ln, li=li, pd=pd: e.matmul(pd[0:nk, li * 512 + cs:li * 512 + ce], lhsT=ln["kt"](kc0, nk), rhs=ln["qt"](qc0 + cs, qc0 + ce),
                                                                      start=True, stop=True, **({"tile_position": ln["tp"]} if ln.get("tp") else {})),
                               rbufs, [bpd], inc=(li == 1))
                        qk[g] = (pd, bpd)

                    emit_qk(0)
                    if len(G_) > 1:
                        emit_qk(1)
                    for g, (ci, si, ns, (kbi, kc0, kpos, nk, cs, masks, ce)) in enumerate(G_):
                        (grp, lanes, qb, mode, kbl, rbufs, epilogue) = calls[ci]
                        (qc0, qpos, nq) = qb
                        scale = scale_of[grp]
                        nsub = (nq + 127) // 128
                        if si == 0:
                            accz = acc[:, :].rearrange("p (a b) -> p a b", a=2)
                            dve(lambda e, accz=accz, nsub=nsub: e.memset(accz[:, :, 0:nsub * 65], 0.0), (), [bacc])
                        pd, bpd = qk.pop(g)
                        pi = prot[0]; prot[0] = (pi + 1) % NPT
                        PT, bPT = ptiles[pi], bpt_[pi]
                        pdv = pd[:, :].rearrange("p (a b) -> p a b", a=2)
                        act(lambda e, PT=PT, pdv=pdv: e.activation(out=PT[0:nk, :, cs:ce], in_=pdv[0:nk, :, cs:ce], func=AF.Exp, bias=zc[0:nk, 0:1], scale=scale), [bpd], [bPT])
                        if g + 2 < len(G_):
                            emit_qk(g + 2)
                        relmax = (kpos + nk - 1) - (qpos + cs)
                        for li, ln in enumerate(lanes):
                            if ln.get("E") is not None and relmax > ln["Ethr"]:
                                off = ln["c0"] - (kpos - qpos)
                                Et = ln["E"]
                                dve(lambda e, li=li, Et=Et, off=off, PT=PT: e.tensor_tensor(out=PT[0:nk, li, cs:ce], in0=PT[0:nk, li, cs:ce], in1=Et[0:nk, off + cs:off + ce], op=ALU.mult), [bPT, ln["Eb"]], [bPT])
                        for (p0, p1, c0, c1) in masks:
                            dve(lambda e, PT=PT, p0=p0, p1=p1, c0=c0, c1=c1: e.memset(PT[p0:p1, :, c0:c1], 0.0), (), [bPT])
                        for li, ln in enumerate(lanes):
                            for t in range(nsub):
                                a0, a1 = t * 128, min(t * 128 + 128, nq)
                                if a1 <= cs or a0 >= ce:
                                    continue
                                last = (li == 1 and a1 >= ce)
                                pe(lambda e, li=li, t=t, a0=a0, a1=a1, ln=ln, PT=PT: e.matmul(acc[0:a1 - a0, li * 512 + t * 65:li * 512 + t * 65 + 65], lhsT=PT[0:nk, li, a0:a1], rhs=ln["v"](kbi, nk),
                                                                                            start=False, stop=False, skip_group_check=True),
                                   [bPT] + rbufs, [bacc], inc=last)
                        flush_pending(si)
                        if si == ns - 1:
                            flush_pending()
                            k = arot[0]; arot[0] = 1 - k
                            A_, bA_ = accs[k], baccs[k]
                            nn = min(128, nq)
                            accv = acc[:, :].rearrange("p (a b) -> p a b", a=2)
                            dve(lambda e, A_=A_, nn=nn, nsub=nsub, accv=accv: e.tensor_copy(out=A_[0:nn, :, 0:nsub * 65], in_=accv[0:nn, :, 0:nsub * 65]), [bacc], [bA_])
                            epilogue(A_, bA_, qb, nsub)

                def epi_plain(col_of_lane, dst=None, bdst=None):
                    def f(A_, bA_, qb, nsub):
                        pending.append((1, lambda: g(A_, bA_, qb, nsub)))

                    def g(A_, bA_, qb, nsub):
                        (qc0, qpos, nq) = qb
                        nn = min(128, nq)
                        recv = A_[0:nn, :, 0:nsub * 65].rearrange("p a (t c) -> p a t c", c=65)
                        dve(lambda e: e.reciprocal(out=rec[0:nn, :, 0:nsub], in_=recv[:, :, :, 64]), [bA_], [b_rec])
                        for li in range(2):
                            for t in range(nsub):
                                a0, a1 = t * 128, min(t * 128 + 128, nq)
                                ti = (qc0 + a0) // 128
                                col = col_of_lane[li]
                                if dst is None:
                                    o_ap, o_b = a_tok[0:a1 - a0, ti, col:col + 64], b_atok
                                else:
                                    o_ap, o_b = dst[0:a1 - a0, t, col:col + 64], bdst
                                dve(lambda e, li=li, t=t, a0=a0, a1=a1, o_ap=o_ap: e.tensor_scalar(out=o_ap, in0=A_[0:a1 - a0, li, t * 65:t * 65 + 64],
                                                                                                 scalar1=rec[0:a1 - a0, li, t:t + 1], scalar2=None, op0=ALU.mult), [bA_, b_rec], [o_b])
                    return f

                def epi_diff(h):
                    def f(A_, bA_, qb, nsub):
                        (qc0, qpos, nq) = qb
                        nn = min(128, nq)
                        recv = A_[0:nn, :, 0:nsub * 65].rearrange("p a (t c) -> p a t c", c=65)

                        def s1():
                            dve(lambda e: e.reciprocal(out=rec[0:nn, :, 0:nsub], in_=recv[:, :, :, 64]), [bA_], [b_rec])
                            dve(lambda e: e.tensor_scalar(out=rec[0:nn, 1, 0:nsub], in0=rec[0:nn, 1, 0:nsub], scalar1=neglam[0:nn, l:l + 1], scalar2=None, op0=ALU.mult), [b_rec, b_lam], [b_rec])
                            for t in range(nsub):
                                n_ = min(128, nq - t * 128)
                                dve(lambda e, t=t, n_=n_: e.tensor_scalar(out=o0t[0:n_, t, :], in0=A_[0:n_, 0, t * 65:t * 65 + 64], scalar1=rec[0:n_, 0, t:t + 1], scalar2=None, op0=ALU.mult), [bA_, b_rec], [b_o0t])
                            for t in range(nsub):
                                n_ = min(128, nq - t * 128)
                                dve(lambda e, t=t, n_=n_: e.scalar_tensor_tensor(out=otmp[0:n_, t, :], in0=A_[0:n_, 1, t * 65:t * 65 + 64], scalar=rec[0:n_, 1, t:t + 1], in1=o0t[0:n_, t, :], op0=ALU.mult, op1=ALU.add), [bA_, b_rec, b_o0t], [b_otmp])
                            dve(lambda e: e.tensor_tensor(out=sq[0:nn, 0:nsub, :], in0=otmp[0:nn, 0:nsub, :], in1=otmp[0:nn, 0:nsub, :], op=ALU.mult), [b_otmp], [b_sq])
                            dve(lambda e: e.tensor_reduce(out=ssA[0:nn, 0:nsub], in_=sq[0:nn, 0:nsub, :], axis=AX.X, op=ALU.add), [b_sq], [b_ssA])

                        def s2():
                            rstd_of(ssA[0:nn, 0:nsub], nn, 64, b_ssA)

                        def s3():
                            for t in range(nsub):
                                n_ = min(128, nq - t * 128)
                                ti = (qc0 + t * 128) // 128
                                dve(lambda e, t=t, n_=n_, ti=ti: e.scalar_tensor_tensor(out=a_tok[0:n_, ti, h * 64:h * 64 + 64], in0=otmp[0:n_, t, :], scalar=ssA[0:n_, t:t + 1], in1=subln[0:n_, l, :], op0=ALU.mult, op1=ALU.mult),
                                    [b_otmp, b_ssA, b_subln], [b_atok])
                        pending.append((1, s1))
                        pending.append((5, s2))
                        pending.append((6, s3))
                    return f

                with contextlib.ExitStack() as S1:
                    a_tok = sb(S1, [128, NT, D], BF16, "atok")
                    EA = sb(S1, [128, 8, 1024], BF16, "EAt"); b_EA = Buf()
                    EB = sb(S1, [128, 4, 1024], BF16, "EBt"); b_EB = Buf()
                    P.dma("sp", EA[:, :, :], EscrA[0:8].rearrange("h p u -> p h u"), (), [b_EA])
                    P.dma("sp", EB[:, :, :], EscrA[8 + 4 * l:12 + 4 * l].rearrange("h p u -> p h u"), (), [b_EB])
                    hT = sb(S1, [128, 8, T], BF16, "hT"); b_hT = Buf()
                    gbc = sb(S1, [128, D], F32, "gbc"); b_gbc = Buf()
                    bcast_row(gbc[:], I["norms"], (l * 7 + 0) * D, D, [b_gbc])
                    xt = [sb(S1, [128, D], F32, "xt") for _ in range(2)]; bxt = [Buf(), Buf()]
                    hb = [sb(S1, [128, D], BF16, "hb") for _ in range(2)]; bhb = [Buf(), Buf()]
                    junkb = sb(S1, [128, D], BF16, "junkb"); b_junkb = Buf()
                    ss1 = [sb(S1, [128, 1], F32, "ss1") for _ in range(2)]; bss1 = [Buf(strict=True), Buf(strict=True)]
                    def h_stages(ti, t0, n):
                        k = ti % 2

                        def g0():
                            P.dma("sp", xt[k][0:n, :], xsrc[t0:t0 + n, :], [ybufs[ti]], [bxt[k]])
                            dve(lambda e: e.scalar_tensor_tensor(out=junkb[0:n, :], in0=xt[k][0:n, :], scalar=1.0, in1=xt[k][0:n, :], op0=ALU.mult, op1=ALU.mult, accum_out=ss1[k][0:n, :]), [bxt[k]], [b_junkb, bss1[k]])

                        def g1():
                            rstd_of(ss1[k][0:n, :], n, D, bss1[k])

                        def g2():
                            dve(lambda e: e.scalar_tensor_tensor(out=hb[k][0:n, :], in0=xt[k][0:n, :], scalar=ss1[k][0:n, 0:1], in1=gbc[0:n, :], op0=ALU.mult, op1=ALU.mult), [bxt[k], bss1[k], b_gbc], [bhb[k]])

                        def g3():
                            pt, bpt = next_ptr()
                            for c in range(8):
                                pe(lambda e, c=c: e.transpose(out=pt[:, c * 128:c * 128 + n], in_=hb[k][0:n, c * 128:(c + 1) * 128], identity=ident[0:n, 0:n]), [bhb[k], b_ident], [bpt], inc=(c == 7))
                            ptv = pt[:, :].rearrange("p (c t) -> p c t", c=8)
                            act(lambda e: e.copy(out=hT[:, :, t0:t0 + n], in_=ptv[:, :, 0:n]), [bpt], [b_hT])
                        return [g0, g1, g2, g3]

                    for ti0 in range(0, NT, 2):
                        grp_ = [h_stages(ti, *tiles[ti]) for ti in range(ti0, min(NT, ti0 + 2))]
                        for si in range(4):
                            for stg in grp_:
                                stg[si]()

                    ck("hT")

                    def proj_fm(dst_fn, wt, bw, wcols, nchunks, rows=128):
                        for q0 in range(0, T, 512):
                            nq = min(512, T - q0)
                            for j0 in range(0, nchunks, 2):
                                pd, bpd = next_dbl()
                                nj = min(2, nchunks - j0)
                                for jj in range(nj):
                                    for c in range(8):
                                        pe(lambda e, jj=jj, c=c, pd=pd, j0=j0: e.matmul(pd[0:rows, jj * 512:jj * 512 + nq], lhsT=wt[:, c, wcols[j0 + jj]:wcols[j0 + jj] + rows], rhs=hT[:, c, q0:q0 + nq], start=(c == 0), stop=(c == 7)),
                                           [bw, b_hT], [bpd], inc=(c == 7 and jj == nj - 1))
                                for jj in range(nj):
                                    dst, bd = dst_fn(j0 + jj, q0, q0 + nq)
                                    act(lambda e, jj=jj, pd=pd, dst=dst: e.copy(out=dst, in_=pd[0:rows, jj * 512:jj * 512 + nq]), [bpd], [bd])

                    for ah in range(2):
                        with contextlib.ExitStack() as SA:
                            wA_ = sb(SA, [128, 8, 768], BF16, "wA"); b_wA_ = Buf()
                            wsrc = wl["w_in"].rearrange("(c p) n -> p c n", p=128)
                            for i3 in range(3):
                                P.dma("sp", wA_[:, :, i3 * 256:(i3 + 1) * 256], wsrc[:, :, i3 * 512 + ah * 256:i3 * 512 + ah * 256 + 256], [bW], [b_wA_])
                            ck("Aw")
                            QT = sb(SA, [128, 2, T], BF16, "QTA"); b_QT = Buf()
                            KT = sb(SA, [128, 2, NK], BF16, "KTA"); b_KT = Buf()
                            VA = sb(SA, [128, NKB, 4, 65], BF16, "VA"); b_VA = Buf()
                            dve(lambda e: e.memset(VA[:, :, :, :].rearrange("p a b c -> p (a b c)"), 1.0), (), [b_VA])
                            kv32 = [sb(SA, [128, 2, 256], F32, "kv32") for _ in range(2)]; bkv32 = [Buf(), Buf()]
                            if not prm:
                                ckb = [sb(SA, [128, 256], BF16, "ckb") for _ in range(2)]; bckb = [Buf(), Buf()]
                                for j in range(NKB_past):
                                    k = j % 2
                                    P.dma("pool", ckb[k][:, :], IA["cache_a_k"][l, j * 128:(j + 1) * 128, ah * 256:(ah + 1) * 256], (), [bckb[k]])
                                    P.dma("pool", VA[:, j, :, 0:64], IA["cache_a_v"][l, j * 128:(j + 1) * 128, ah * 256:(ah + 1) * 256].rearrange("p (h d) -> p h d", h=4), (), [b_VA])
                                    pt, bpt = next_ptr()
                                    for c in range(2):
                                        pe(lambda e, c=c, k=k, pt=pt: e.transpose(out=pt[:, c * 128:(c + 1) * 128], in_=ckb[k][:, c * 128:(c + 1) * 128], identity=ident[:, :]), [bckb[k], b_ident], [bpt], inc=(c == 1))
                                    ptv = pt[:, 0:256].rearrange("p (c t) -> p c t", c=2)
                                    act(lambda e, j=j, ptv=ptv: e.copy(out=KT[:, :, j * 128:(j + 1) * 128], in_=ptv), [bpt], [b_KT])
                            proj_fm(lambda j, c0, c1: (QT[:, j, c0:c1], b_QT), wA_, b_wA_, [0, 128], 2)
                            ck("Aq")
                            proj_fm(lambda j, c0, c1: (KT[:, j, past + c0:past + c1], b_KT), wA_, b_wA_, [256, 384], 2)
                            ck("Ak")
                            for ti, (t0, n) in enumerate(tiles):
                                pd, bpd = next_dbl()
                                for jj in range(2):
                                    for c in range(8):
                                        pe(lambda e, jj=jj, c=c, pd=pd, t0=t0, n=n: e.matmul(pd[0:n, jj * 512:jj * 512 + 256], lhsT=hT[:, c, t0:t0 + n], rhs=wA_[:, c, 256 + jj * 256:512 + jj * 256], start=(c == 0), stop=(c == 7)),
                                           [b_wA_, b_hT], [bpd], inc=(c == 7 and jj == 1))
                                k = ti % 2
                                pdv = pd[:, :].rearrange("p (a b) -> p a b", a=2)
                                if '1' not in DBG:
                                    act(lambda e, k=k, n=n, pdv=pdv: e.copy(out=kv32[k][0:n, :, :], in_=pdv[0:n, :, 0:256]), [bpd], [bkv32[k]])
                                if '2' not in DBG:
                                    dve(lambda e, ti=ti, n=n, pd=pd: e.tensor_copy(out=VA[0:n, NKB_past + ti, :, 0:64], in_=pd[0:n, 512:768].rearrange("p (h d) -> p h d", h=4)), [bpd], [b_VA])
                                if 'D' in DBG and ti == 0 and l == 0 and ah == 0:
                                    P.dma("pool", ydst[384:512, 0:256], kv32[k][:, 0, :], [bkv32[k]], [])
                                    P.dma("pool", ydst[512:640, 0:768], wA_[:, 0, :], [b_wA_], [])
                                if 'O' not in DBG:
                                    P.dma("pool", OA[pfx + "a_k"][l, so, t0:t0 + n, ah * 256:(ah + 1) * 256], kv32[k][0:n, 0, :], [bkv32[k]], [])
                                    P.dma("pool", OA[pfx + "a_v"][l, so, t0:t0 + n, ah * 256:(ah + 1) * 256], kv32[k][0:n, 1, :], [bkv32[k]], [])
                            ck("Aproj")
                            callsA = []
                            for hh in range(4):
                                h = ah * 4 + hh
                                c, r0 = hh // 2, (hh % 2) * 64
                                lanes = []
                                for half in range(2):
                                    rr = r0 + 32 * half
                                    lanes.append(dict(kt=lambda c0, nk, rr=rr, c=c: KT[rr:rr + 32, c, c0:c0 + nk], qt=lambda c0, c1, rr=rr, c=c: QT[rr:rr + 32, c, c0:c1],
                                                      v=lambda kbi, nk, hh=hh: VA[0:nk, kbi, hh, :], E=EA[:, h, :], Eb=b_EA, Ethr=-91, c0=384, tp=(rr, 0)))
                                for qb in qblocks:
                                    callsA.append(("A", lanes, qb, "causal", list(enumerate(kblocks)), [b_QT, b_KT, b_VA], epi_diff(h)))
                            attend_many(callsA)
                            flush_pending()
                            P.barrier()
                            ck("Ahalf")

                    with contextlib.ExitStack() as SB:
                        wB_ = sb(SB, [128, 8, 768], BF16, "wB"); b_wB_ = Buf()
                        wsrc = wl["w_in"].rearrange("(c p) n -> p c n", p=128)
                        P.dma("sp", wB_[:, :, :], wsrc[:, :, 1536:2304], [bW], [b_wB_])
                        pastB = 0 if prm else NBS
                        NKb = pastB + T
                        kbB = [(j * 128, past - pastB + j * 128, 128) for j in range(pastB // 128)] + [(pastB + t0, past + t0, n) for (t0, n) in tiles]
                        QT = sb(SB, [128, 2, T], BF16, "QTB"); b_QT = Buf()
                        KT = sb(SB, [128, 2, NKb], BF16, "KTB"); b_KT = Buf()
                        VB = sb(SB, [128, len(kbB), 4, 65], BF16, "VB"); b_VB = Buf()
                        dve(lambda e: e.memset(VB[:, :, :, :].rearrange("p a b c -> p (a b c)"), 1.0), (), [b_VB])
                        kv32 = [sb(SB, [128, 512], F32, "kv32b") for _ in range(2)]; bkv32 = [Buf(), Buf()]
                        if not prm:
                            ckb = [sb(SB, [128, 256], BF16, "ckbb") for _ in range(2)]; bckb = [Buf(), Buf()]
                            for j in range(pastB // 128):
                                k = j % 2
                                P.dma("pool", ckb[k][:, :], IA["cache_b_k"][l, j * 128:(j + 1) * 128, :], (), [bckb[k]])
                                P.dma("pool", VB[:, j, :, 0:64], IA["cache_b_v"][l, j * 128:(j + 1) * 128, :].rearrange("p (h d) -> p h d", h=4), (), [b_VB])
                                pt, bpt = next_ptr()
                                for c in range(2):
                                    pe(lambda e, c=c, k=k, pt=pt: e.transpose(out=pt[:, c * 128:(c + 1) * 128], in_=ckb[k][:, c * 128:(c + 1) * 128], identity=ident[:, :]), [bckb[k], b_ident], [bpt], inc=(c == 1))
                                ptv = pt[:, 0:256].rearrange("p (c t) -> p c t", c=2)
                                act(lambda e, j=j, ptv=ptv: e.copy(out=KT[:, :, j * 128:(j + 1) * 128], in_=ptv), [bpt], [b_KT])
                        proj_fm(lambda j, c0, c1: (QT[:, j, c0:c1], b_QT), wB_, b_wB_, [0, 128], 2)
                        proj_fm(lambda j, c0, c1: (KT[:, j, pastB + c0:pastB + c1], b_KT), wB_, b_wB_, [256, 384], 2)
                        for ti, (t0, n) in enumerate(tiles):
                            pd, bpd = next_dbl()
                            for c in range(8):
                                pe(lambda e, c=c, pd=pd, t0=t0, n=n: e.matmul(pd[0:n, 0:512], lhsT=hT[:, c, t0:t0 + n], rhs=wB_[:, c, 256:768], start=(c == 0), stop=(c == 7)), [b_wB_, b_hT], [bpd], inc=(c == 7))
                            k = ti % 2
                            act(lambda e, k=k, n=n, pd=pd: e.copy(out=kv32[k][0:n, :], in_=pd[0:n, 0:512]), [bpd], [bkv32[k]])
                            dve(lambda e, ti=ti, n=n, pd=pd: e.tensor_copy(out=VB[0:n, pastB // 128 + ti, :, 0:64], in_=pd[0:n, 256:512].rearrange("p (h d) -> p h d", h=4)), [bpd], [b_VB])
                            if prm:
                                if t0 >= SEQ - NBP:
                                    r0_ = t0 - (SEQ - NBP)
                                    P.dma("pool", OA["p_b_k"][l, so, r0_:r0_ + n, :], kv32[k][0:n, 0:256], [bkv32[k]], [])
                                    P.dma("pool", OA["p_b_v"][l, so, r0_:r0_ + n, :], kv32[k][0:n, 256:512], [bkv32[k]], [])
                            else:
                                P.dma("pool", OA["s_b_k"][l, 0, t0:t0 + n, :], kv32[k][0:n, 0:256], [bkv32[k]], [])
                                P.dma("pool", OA["s_b_v"][l, 0, t0:t0 + n, :], kv32[k][0:n, 256:512], [bkv32[k]], [])
                        callsB = []
                        for hp in range(2):
                            lanes = []
                            for li in range(2):
                                h = hp * 2 + li
                                r0 = li * 64
                                lanes.append(dict(kt=lambda c0, nk, r0=r0, hp=hp: KT[r0:r0 + 64, hp, c0:c0 + nk], qt=lambda c0, c1, r0=r0, hp=hp: QT[r0:r0 + 64, hp, c0:c1],
                                                  v=lambda kbi, nk, h=h: VB[0:nk, kbi, h, :], E=EB[:, h, :], Eb=b_EB, Ethr=-128, c0=384, tp=None))
                            for qb in qblocks:
                                callsB.append(("B", lanes, qb, "band", list(enumerate(kbB)), [b_QT, b_KT, b_VB], epi_plain([512 + (hp * 2) * 64, 512 + (hp * 2 + 1) * 64])))
                        attend_many(callsB)
                        flush_pending()
                        P.barrier()

                    ck("B")
                    with contextlib.ExitStack() as SC:
                        wC_ = sb(SC, [128, 8, 416], BF16, "wC"); b_wC_ = Buf()
                        wsrc = wl["w_in"].rearrange("(c p) n -> p c n", p=128)
                        P.dma("sp", wC_[:, :, :], wsrc[:, :, 2304:2720], [bW], [b_wC_])
                        wuq = sb(SC, [128, 2, 384], BF16, "wuq"); b_wuq = Buf()
                        P.dma("sp", wuq[:, :, :], wl["mla_w_uq"].rearrange("(c p) n -> p c n", p=128), [bW], [b_wuq])
                        wukv = sb(SC, [128, 512], BF16, "wukv"); b_wukv = Buf()
                        P.dma("sp", wukv[:, :], wl["mla_w_ukv"], [bW], [b_wukv])
                        gq = sb(SC, [128, 256], F32, "gq"); b_gq = Buf()
                        gkv = sb(SC, [128, 128], F32, "gkv"); b_gkv = Buf()
                        bcast_row(gq[:], I["mla_q_norm"], l * 256, 256, [b_gq])
                        bcast_row(gkv[:], I["mla_kv_norm"], l * 128, 128, [b_gkv])
                        cqnT = sb(SC, [128, 2, T], BF16, "cqnT"); b_cqnT = Buf()
                        latT = sb(SC, [128, NK], BF16, "latT"); b_latT = Buf()
                        kT = sb(SC, [96, 4, NK], BF16, "kTC"); b_kT = Buf()
                        qT = sb(SC, [96, 4, T], BF16, "qTC"); b_qT = Buf()
                        VC = sb(SC, [128, NKB, 4, 65], BF16, "VC"); b_VC = Buf()
                        dve(lambda e: e.memset(VC[:, :, :, :].rearrange("p a b c -> p (a b c)"), 1.0), (), [b_VC])
                        rp = sb(SC, [128, NT, 128], F32, "rope"); b_rp = Buf()
                        for ti, (t0, n) in enumerate(tiles):
                            P.dma("sp", rp[0:n, ti, :], rope_h.ap()[t0:t0 + n, :], (), [b_rp])
                        c32 = [sb(SC, [128, 416], F32, "c32") for _ in range(2)]; bc32 = [Buf(), Buf()]
                        ssc = [sb(SC, [128, 2], F32, "ssc") for _ in range(2)]; bssc = [Buf(strict=True), Buf(strict=True)]
                        junkc = sb(SC, [128, 384], BF16, "junkc"); b_junkc = Buf()
                        cqn = [sb(SC, [128, 256], BF16, "cqn") for _ in range(2)]; bcqn = [Buf(), Buf()]
                        lat32 = [sb(SC, [128, 128], F32, "lat32") for _ in range(2)]; blat32 = [Buf(), Buf()]
                        latb = [sb(SC, [128, 160], BF16, "latb") for _ in range(2)]; blatb = [Buf(), Buf()]
                        kpe32 = [sb(SC, [128, 32], F32, "kpe32") for _ in range(2)]; bkpe32 = [Buf(), Buf()]
                        rt = sb(SC, [128, 4, 4, 16], F32, "rt"); b_rt = Buf()
                        q32 = [sb(SC, [128, 4, 96], F32, "q32") for _ in range(2)]; bq32 = [Buf(), Buf()]
                        qb16 = [sb(SC, [128, 4, 96], BF16, "qb16") for _ in range(2)]; bqb16 = [Buf(), Buf()]

                        def rope_apply(dst1, dst2, x1, x2, cs_, sn_, shape_n, rbufs_, wbufs_):
                            nh = shape_n
                            dve(lambda e: e.tensor_tensor(out=rt[0:nh[0], 0, 0:nh[1], :], in0=x1, in1=cs_, op=ALU.mult), rbufs_, [b_rt])
                            dve(lambda e: e.tensor_tensor(out=rt[0:nh[0], 1, 0:nh[1], :], in0=x2, in1=sn_, op=ALU.mult), rbufs_, [b_rt])
                            dve(lambda e: e.tensor_tensor(out=rt[0:nh[0], 2, 0:nh[1], :], in0=x2, in1=cs_, op=ALU.mult), rbufs_, [b_rt])
                            dve(lambda e: e.tensor_tensor(out=rt[0:nh[0], 3, 0:nh[1], :], in0=x1, in1=sn_, op=ALU.mult), rbufs_, [b_rt])
                            dve(lambda e: e.tensor_tensor(out=dst1, in0=rt[0:nh[0], 0, 0:nh[1], :], in1=rt[0:nh[0], 1, 0:nh[1], :], op=ALU.subtract), [b_rt], wbufs_)
                            dve(lambda e: e.tensor_tensor(out=dst2, in0=rt[0:nh[0], 2, 0:nh[1], :], in1=rt[0:nh[0], 3, 0:nh[1], :], op=ALU.add), [b_rt], wbufs_)

                        if not prm:
                            for j in range(NKB_past):
                                k = j % 2
                                P.dma("pool", latb[k][:, 0:128], IA["cache_c_latent"][l, j * 128:(j + 1) * 128, :], (), [blatb[k]])
                                P.dma("pool", latb[k][:, 128:160], IA["cache_c_rope_k"][l, j * 128:(j + 1) * 128, :], (), [blatb[k]])
                                pt, bpt = next_ptr()
                                pe(lambda e, k=k, pt=pt: e.transpose(out=pt[:, 0:128], in_=latb[k][:, 0:128], identity=ident[:, :]), [blatb[k], b_ident], [bpt], inc=False)
                                pe(lambda e, k=k, pt=pt: e.transpose(out=pt[0:32, 128:256], in_=latb[k][:, 128:160], identity=ident[:, :]), [blatb[k], b_ident], [bpt])
                                act(lambda e, j=j, pt=pt: e.copy(out=latT[:, j * 128:(j + 1) * 128], in_=pt[:, 0:128]), [bpt], [b_latT])
                                for h in range(4):
                                    dve(lambda e, j=j, h=h, pt=pt: e.tensor_copy(out=kT[64:96, h, j * 128:(j + 1) * 128], in_=pt[0:32, 128:256]), [bpt], [b_kT])
                        rt2 = [rt, sb(SC, [128, 4, 4, 16], F32, "rtb")]; b_rt2 = [b_rt, Buf()]
                        junkc2 = [junkc, junkc]; b_junkc2 = [b_junkc, b_junkc]

                        def rope2(k, dst1, dst2, x1, x2, cs_, sn_, nh, rbufs_, wbufs_):
                            R_, bR = rt2[k], b_rt2[k]
                            dve(lambda e: e.tensor_tensor(out=R_[0:nh[0], 0, 0:nh[1], :], in0=x1, in1=cs_, op=ALU.mult), rbufs_, [bR])
                            dve(lambda e: e.tensor_tensor(out=R_[0:nh[0], 1, 0:nh[1], :], in0=x2, in1=sn_, op=ALU.mult), rbufs_, [bR])
                            dve(lambda e: e.tensor_tensor(out=R_[0:nh[0], 2, 0:nh[1], :], in0=x2, in1=cs_, op=ALU.mult), rbufs_, [bR])
                            dve(lambda e: e.tensor_tensor(out=R_[0:nh[0], 3, 0:nh[1], :], in0=x1, in1=sn_, op=ALU.mult), rbufs_, [bR])
                            dve(lambda e: e.tensor_tensor(out=dst1, in0=R_[0:nh[0], 0, 0:nh[1], :], in1=R_[0:nh[0], 1, 0:nh[1], :], op=ALU.subtract), [bR], wbufs_)
                            dve(lambda e: e.tensor_tensor(out=dst2, in0=R_[0:nh[0], 2, 0:nh[1], :], in1=R_[0:nh[0], 3, 0:nh[1], :], op=ALU.add), [bR], wbufs_)

                        def c_stages(ti, t0, n):
                            k = ti % 2
                            st = {}

                            def g0():
                                pd, bpd = next_dbl()
                                for c in range(8):
                                    pe(lambda e, c=c: e.matmul(pd[0:n, 0:416], lhsT=hT[:, c, t0:t0 + n], rhs=wC_[:, c, :], start=(c == 0), stop=(c == 7)), [b_wC_, b_hT], [bpd], inc=(c == 7))
                                act(lambda e: e.copy(out=c32[k][0:n, :], in_=pd[0:n, 0:416]), [bpd], [bc32[k]])

                            def g1():
                                dve(lambda e: e.scalar_tensor_tensor(out=junkc2[k][0:n, 0:256], in0=c32[k][0:n, 0:256], scalar=1.0 / 256, in1=c32[k][0:n, 0:256], op0=ALU.mult, op1=ALU.mult, accum_out=ssc[k][0:n, 0:1]), [bc32[k]], [b_junkc2[k], bssc[k]])
                                dve(lambda e: e.scalar_tensor_tensor(out=junkc2[k][0:n, 0:128], in0=c32[k][0:n, 256:384], scalar=1.0 / 128, in1=c32[k][0:n, 256:384], op0=ALU.mult, op1=ALU.mult, accum_out=ssc[k][0:n, 1:2]), [bc32[k]], [b_junkc2[k], bssc[k]])
                                rstd_of(ssc[k][0:n, 0:2], n, 1, bssc[k])
                                rope2(k, kpe32[k][0:n, 0:16].rearrange("p (a d) -> p a d", a=1), kpe32[k][0:n, 16:32].rearrange("p (a d) -> p a d", a=1),
                                      c32[k][0:n, 384:400].rearrange("p (a d) -> p a d", a=1), c32[k][0:n, 400:416].rearrange("p (a d) -> p a d", a=1),
                                      rp[0:n, ti, 0:16].rearrange("p (a d) -> p a d", a=1), rp[0:n, ti, 64:80].rearrange("p (a d) -> p a d", a=1), (n, 1), [bc32[k], b_rp], [bkpe32[k]])
                                P.dma("pool", OA[pfx + "kpe"][l, so, t0:t0 + n, :], kpe32[k][0:n, :], [bkpe32[k]], [])
                                dve(lambda e: e.tensor_copy(out=latb[k][0:n, 128:160], in_=kpe32[k][0:n, :]), [bkpe32[k]], [blatb[k]])

                            def g2():
                                dve(lambda e: e.scalar_tensor_tensor(out=cqn[k][0:n, :], in0=c32[k][0:n, 0:256], scalar=ssc[k][0:n, 0:1], in1=gq[0:n, :], op0=ALU.mult, op1=ALU.mult), [bc32[k], bssc[k], b_gq], [bcqn[k]])
                                dve(lambda e: e.scalar_tensor_tensor(out=lat32[k][0:n, :], in0=c32[k][0:n, 256:384], scalar=ssc[k][0:n, 1:2], in1=gkv[0:n, :], op0=ALU.mult, op1=ALU.mult), [bc32[k], bssc[k], b_gkv], [blat32[k]])
                                P.dma("pool", OA[pfx + "lat"][l, so, t0:t0 + n, :], lat32[k][0:n, :], [blat32[k]], [])
                                dve(lambda e: e.tensor_copy(out=latb[k][0:n, 0:128], in_=lat32[k][0:n, :]), [blat32[k]], [blatb[k]])

                            def g3():
                                pt, bpt = next_ptr()
                                for c in range(2):
                                    pe(lambda e, c=c: e.transpose(out=pt[:, c * 128:c * 128 + n], in_=cqn[k][0:n, c * 128:(c + 1) * 128], identity=ident[0:n, 0:n]), [bcqn[k], b_ident], [bpt], inc=False)
                                pe(lambda e: e.transpose(out=pt[:, 256:256 + n], in_=latb[k][0:n, 0:128], identity=ident[0:n, 0:n]), [blatb[k], b_ident], [bpt], inc=False)
                                pe(lambda e: e.transpose(out=pt[0:32, 384:384 + n], in_=latb[k][0:n, 128:160], identity=ident[0:n, 0:n]), [blatb[k], b_ident], [bpt])
                                ptv = pt[:, 0:256].rearrange("p (c t) -> p c t", c=2)
                                act(lambda e: e.copy(out=cqnT[:, :, t0:t0 + n], in_=ptv[:, :, 0:n]), [bpt], [b_cqnT])
                                act(lambda e: e.copy(out=latT[:, past + t0:past + t0 + n], in_=pt[:, 256:256 + n]), [bpt], [b_latT])
                                for h in range(4):
                                    dve(lambda e, h=h: e.tensor_copy(out=kT[64:96, h, past + t0:past + t0 + n], in_=pt[0:32, 384:384 + n]), [bpt], [b_kT])

                            def g4():
                                pd, bpd = next_dbl()
                                for c in range(2):
                                    pe(lambda e, c=c: e.matmul(pd[0:n, 0:384], lhsT=cqnT[:, c, t0:t0 + n], rhs=wuq[:, c, :], start=(c == 0), stop=(c == 1)), [b_cqnT, b_wuq], [bpd], inc=(c == 1))
                                act(lambda e: e.copy(out=q32[k][0:n, :, :], in_=pd[0:n, 0:384].rearrange("p (h d) -> p h d", h=4)), [bpd], [bq32[k]])

                            def g5():
                                dve(lambda e: e.tensor_copy(out=qb16[k][0:n, :, 0:64], in_=q32[k][0:n, :, 0:64]), [bq32[k]], [bqb16[k]])
                                rope2(k, qb16[k][0:n, :, 64:80], qb16[k][0:n, :, 80:96], q32[k][0:n, :, 64:80], q32[k][0:n, :, 80:96],
                                      rp[0:n, ti, 0:64].rearrange("p (h d) -> p h d", h=4), rp[0:n, ti, 64:128].rearrange("p (h d) -> p h d", h=4), (n, 4), [bq32[k], b_rp], [bqb16[k]])

                            def g6():
                                pt, bpt = next_ptr()
                                for h in range(4):
                                    pe(lambda e, h=h: e.transpose(out=pt[0:96, h * 128:h * 128 + n], in_=qb16[k][0:n, h, :], identity=ident[0:n, 0:n]), [bqb16[k], b_ident], [bpt], inc=(h == 3))
                                ptv = pt[:, 0:512].rearrange("p (c t) -> p c t", c=4)
                                act(lambda e: e.copy(out=qT[:, :, t0:t0 + n], in_=ptv[0:96, :, 0:n]), [bpt], [b_qT])
                            return [g0, g1, g2, g3, g4, g5, g6]

                        for ti0 in range(0, NT, 2):
                            grp_ = [c_stages(ti, *tiles[ti]) for ti in range(ti0, min(NT, ti0 + 2))]
                            for si in range(7):
                                for stg in grp_:
                                    stg[si]()
                        for c0 in range(0, NK, 512):
                            nn_ = min(512, NK - c0)
                            for hp in range(2):
                                pd, bpd = next_dbl()
                                for jj in range(2):
                                    h = hp * 2 + jj
                                    pe(lambda e, jj=jj, h=h, pd=pd, c0=c0, nn_=nn_: e.matmul(pd[0:64, jj * 512:jj * 512 + nn_], lhsT=wukv[:, h * 128:h * 128 + 64], rhs=latT[:, c0:c0 + nn_], start=True, stop=True), [b_wukv, b_latT], [bpd], inc=(jj == 1))
                                pdv = pd[:, :].rearrange("p (a b) -> p a b", a=2)
                                act(lambda e, hp=hp, pdv=pdv, c0=c0, nn_=nn_: e.copy(out=kT[0:64, hp * 2:hp * 2 + 2, c0:c0 + nn_], in_=pdv[0:64, :, 0:nn_]), [bpd], [b_kT])
                        wv_ = wukv[:, :].rearrange("p (h d) -> p h d", h=4)
                        for kbi, (kc0, kpos, nk) in enumerate(kblocks):
                            pd, bpd = next_dbl()
                            pe(lambda e, pd=pd, kc0=kc0, nk=nk: e.matmul(pd[0:nk, 0:256].rearrange("p (h d) -> p h d", h=4), lhsT=latT[:, kc0:kc0 + nk], rhs=wv_[:, :, 64:128], start=True, stop=True), [b_wukv, b_latT], [bpd])
                            act(lambda e, kbi=kbi, nk=nk, pd=pd: e.copy(out=VC[0:nk, kbi, :, 0:64], in_=pd[0:nk, 0:256].rearrange("p (h d) -> p h d", h=4)), [bpd], [b_VC])
                        if 'Q' in DBG and l == 0:
                            for hq in range(4):
                                P.dma("pool", ydst[0:96, hq * 128:(hq + 1) * 128], qT[:, hq, 0:128], [b_qT], [])
                                P.dma("pool", ydst[128:224, hq * 128:(hq + 1) * 128], kT[:, hq, 0:128], [b_kT], [])
                            P.dma("pool", ydst[256:384, 0:260], VC[:, 0, :, :].rearrange("p a b -> p (a b)"), [b_VC], [])
                            P.dma("pool", ydst[384:512, 0:128], latT[:, 0:128], [b_latT], [])
                        callsC = []
                        for hp in range(2):
                            lanes = []
                            for li in range(2):
                                h = hp * 2 + li
                                lanes.append(dict(kt=lambda c0, nk, h=h: kT[0:96, h, c0:c0 + nk], qt=lambda c0, c1, h=h: qT[0:96, h, c0:c1], v=lambda kbi, nk, h=h: VC[0:nk, kbi, h, :], E=None, tp=None))
                            for qb in qblocks:
                                callsC.append(("C", lanes, qb, "causal", list(enumerate(kblocks)), [b_qT, b_kT, b_VC], epi_plain([768 + (hp * 2) * 64, 768 + (hp * 2 + 1) * 64])))
                        attend_many(callsC)
                        flush_pending()
                        P.barrier()
                    with contextlib.ExitStack() as SO:
                        wout = sb(SO, [128, 8, D], BF16, "wout"); b_wout = Buf()
                        P.dma("sp", wout[:, :, :], wl["w_out"].rearrange("(c p) n -> p c n", p=128), [bW], [b_wout])
                        gmp = sb(SO, [128, D], F32, "gmp"); b_gmp = Buf()
                        bcast_row(gmp[:], I["norms"], (l * 7 + 1) * D, D, [b_gmp])
                        xo_ = [sb(SO, [128, D], F32, "xo") for _ in range(2)]; bxo = [Buf(), Buf()]
                        yo_ = [sb(SO, [128, D], F32, "yo") for _ in range(2)]; byo = [Buf(), Buf()]
                        aT = [sb(SO, [128, 8, 128], BF16, "aT") for _ in range(2)]; baT = [Buf(), Buf()]
                        junko = sb(SO, [128, D], BF16, "junko"); b_junko = Buf()
                        sso = sb(SO, [128, 2], F32, "sso"); b_sso = Buf(strict=True)
                        b_sso2 = [Buf(strict=True), Buf(strict=True)]

                        def wo_xload(ti):
                            (t0, n) = tiles[ti]
                            k = ti % 2
                            P.dma("sp", xo_[k][0:n, :], xsrc[t0:t0 + n, :], [ybufs[ti]], [bxo[k]])

                        def wo_prep(ti):
                            (t0, n) = tiles[ti]
                            k = ti % 2
                            pt, bpt = next_ptr()
                            for c in range(8):
                                pe(lambda e, c=c: e.transpose(out=pt[:, c * 128:c * 128 + n], in_=a_tok[0:n, ti, c * 128:(c + 1) * 128], identity=ident[0:n, 0:n]), [b_atok, b_ident], [bpt], inc=(c == 7))
                            ptv = pt[:, :].rearrange("p (c t) -> p c t", c=8)
                            act(lambda e: e.copy(out=aT[k][:, :, 0:n], in_=ptv[:, :, 0:n]), [bpt], [baT[k]])

                        for ti in range(min(2, NT)):
                            wo_xload(ti)
                            wo_prep(ti)
                        for ti, (t0, n) in enumerate(tiles):
                            k = ti % 2
                            pd, bpd = next_dbl()
                            for hf in range(2):
                                for c in range(8):
                                    pe(lambda e, hf=hf, c=c, n=n, k=k, pd=pd: e.matmul(pd[0:n, hf * 512:(hf + 1) * 512], lhsT=aT[k][:, c, 0:n], rhs=wout[:, c, hf * 512:(hf + 1) * 512], start=(c == 0), stop=(c == 7)), [baT[k], b_wout], [bpd], inc=(c == 7 and hf == 1))
                            act(lambda e, k=k, n=n, pd=pd: e.copy(out=yo_[k][0:n, :], in_=pd[0:n, :]), [bpd], [byo[k]])
                            if ti + 2 < NT:
                                wo_prep(ti + 2)
                            dve(lambda e, k=k, n=n: e.scalar_tensor_tensor(out=junko[0:n, :], in0=yo_[k][0:n, :], scalar=1.0, in1=yo_[k][0:n, :], op0=ALU.mult, op1=ALU.mult, accum_out=sso[0:n, k:k + 1]), [byo[k]], [b_junko, b_sso2[k]])
                            rstd_of(sso[0:n, k:k + 1], n, D, b_sso2[k])
                            dve(lambda e, k=k, n=n: e.scalar_tensor_tensor(out=yo_[k][0:n, :], in0=yo_[k][0:n, :], scalar=sso[0:n, k:k + 1], in1=gmp[0:n, :], op0=ALU.mult, op1=ALU.mult), [byo[k], b_sso2[k], b_gmp], [byo[k]])
                            P.op("pool", lambda e, k=k, n=n: e.tensor_tensor(out=xo_[k][0:n, :], in0=xo_[k][0:n, :], in1=yo_[k][0:n, :], op=ALU.add), [byo[k], bxo[k]], [bxo[k]])
                            P.dma("pool", ydst[t0:t0 + n, :], xo_[k][0:n, :], [bxo[k]], [ybufs[ti]])
                            if ti + 2 < NT:
                                wo_xload(ti + 2)
                        P.barrier()
                    P.barrier()

                ck("C")
                with contextlib.ExitStack() as S2:
                    wxq = sb(S2, [128, 8, 256], BF16, "wxq"); b_wxq = Buf()
                    P.dma("sp", wxq[:, :, :], wl["w_xq"].rearrange("(c p) n -> p c n", p=128), [bW], [b_wxq])
                    wxo = sb(S2, [128, 2, D], BF16, "wxo"); b_wxo = Buf()
                    P.dma("sp", wxo[:, :, :], wl["w_xo"].rearrange("(c p) n -> p c n", p=128), [bW], [b_wxo])
                    gb = sb(S2, [128, 4, D], F32, "gb"); b_gb = Buf()
                    for gi, ni in enumerate((2, 3, 4, 5)):
                        bcast_row(gb[:, gi, :], I["norms"], (l * 7 + ni) * D, D, [b_gb])
                    cw = sb(S2, [128, NFC, 4], F32, "cw"); b_cw = Buf()
                    for j in range(3):
                        P.dma("sp", cw[:, :, j], bass.AP(I["conv_w"], (l * 3 + j) * DFF, [[1, 128], [128, NFC]]), (), [b_cw], allow_slow_non_contiguous=True)
                    P.dma("sp", cw[:, :, 3], bass.AP(I["conv_b"], l * DFF, [[1, 128], [128, NFC]]), (), [b_cw], allow_slow_non_contiguous=True)
                    mkT = sb(S2, [128, 2, MEM], BF16, "mkT"); b_mkT = Buf()
                    MV = sb(S2, [128, 2, 4, 65], BF16, "MV"); b_MV = Buf()
                    dve(lambda e: e.memset(MV[:, :, :, :].rearrange("p a b c -> p (a b c)"), 1.0), (), [b_MV])
                    wd_all = sb(S2, [128, NFC, D], BF16, "wd_all"); b_wd = Buf()
                    wds = wl["w_down"].rearrange("(f p) n -> p f n", p=128)
                    for f0 in range(0, NFC, 6):
                        f1 = min(NFC, f0 + 6)
                        P.dma("sp", wd_all[:, f0:f1, :], wds[:, f0:f1, :], [bW], [b_wd])
                    junkb = sb(S2, [128, D], BF16, "junkb2"); b_junkb = Buf()
                    ss4 = sb(S2, [128, 4], F32, "ss4"); b_ss4 = Buf(strict=True)
                    hb4 = sb(S2, [128, 4, D], BF16, "hb4"); b_hb4 = [Buf() for _ in range(4)]
                    if prm:
                        with contextlib.ExitStack() as SM:
                            wmk = sb(SM, [128, 8, 512], BF16, "wmk"); b_wmk = Buf()
                            P.dma("sp", wmk[:, :, 0:256], wl["w_mk"].rearrange("(c p) n -> p c n", p=128), [bW], [b_wmk])
                            P.dma("sp", wmk[:, :, 256:512], wl["w_mv"].rearrange("(c p) n -> p c n", p=128), [bW], [b_wmk])
                            gm = sb(SM, [128, D], F32, "gm"); b_gm = Buf()
                            bcast_row(gm[:], I["norms"], (l * 7 + 6) * D, D, [b_gm])
                            mT = sb(SM, [128, 8, MEM], BF16, "mT"); b_mT = Buf()
                            m32 = sb(SM, [128, 512], F32, "m32"); b_m32 = Buf()
                            xm = [sb(SM, [128, D], F32, "xm") for _ in range(2)]; bxm = [Buf(), Buf()]
                            for mi in range(2):
                                k = mi % 2
                                P.dma("sp", xm[k][:, :], IA["mem_prompt"][s, mi * 128:(mi + 1) * 128, :], (), [bxm[k]])
                                dve(lambda e, k=k, mi=mi: e.scalar_tensor_tensor(out=junkb[:, :], in0=xm[k][:, :], scalar=1.0, in1=xm[k][:, :], op0=ALU.mult, op1=ALU.mult, accum_out=ss4[:, mi:mi + 1]), [bxm[k]], [b_junkb, b_ss4])
                            rstd_of(ss4[:, 0:2], 128, D, b_ss4)
                            for mi in range(2):
                                k = mi % 2
                                dve(lambda e, k=k, mi=mi: e.scalar_tensor_tensor(out=hb4[:, mi, :], in0=xm[k][:, :], scalar=ss4[:, mi:mi + 1], in1=gm[:, :], op0=ALU.mult, op1=ALU.mult), [bxm[k], b_ss4, b_gm], [b_hb4[mi]])
                                pt, bpt = next_ptr()
                                for c in range(8):
                                    pe(lambda e, c=c, pt=pt, mi=mi: e.transpose(out=pt[:, c * 128:(c + 1) * 128], in_=hb4[:, mi, c * 128:(c + 1) * 128], identity=ident[:, :]), [b_hb4[mi], b_ident], [bpt], inc=(c == 7))
                                ptv = pt[:, :].rearrange("p (c t) -> p c t", c=8)
                                act(lambda e, mi=mi, ptv=ptv: e.copy(out=mT[:, :, mi * 128:(mi + 1) * 128], in_=ptv), [bpt], [b_mT])
                            for mi in range(2):
                                pd, bpd = next_dbl()
                                for c in range(8):
                                    pe(lambda e, c=c, pd=pd, mi=mi: e.matmul(pd[:, 0:512], lhsT=mT[:, c, mi * 128:(mi + 1) * 128], rhs=wmk[:, c, :], start=(c == 0), stop=(c == 7)), [b_mT, b_wmk], [bpd], inc=(c == 7))
                                act(lambda e, pd=pd: e.copy(out=m32[:, :], in_=pd[:, 0:512]), [bpd], [b_m32])
                                dve(lambda e, mi=mi, pd=pd: e.tensor_copy(out=MV[:, mi, :, 0:64], in_=pd[:, 256:512].rearrange("p (h d) -> p h d", h=4)), [bpd], [b_MV])
                                P.dma("pool", OA["p_mk"][l, s, mi * 128:(mi + 1) * 128, :], m32[:, 0:256], [b_m32], [])
                                P.dma("pool", OA["p_mv"][l, s, mi * 128:(mi + 1) * 128, :], m32[:, 256:512], [b_m32], [])
                            pd, bpd = next_dbl()
                            for j in range(2):
                                for c in range(8):
                                    pe(lambda e, j=j, c=c, pd=pd: e.matmul(pd[:, j * 512:j * 512 + MEM], lhsT=wmk[:, c, j * 128:(j + 1) * 128], rhs=mT[:, c, :], start=(c == 0), stop=(c == 7)), [b_mT, b_wmk], [bpd], inc=(c == 7 and j == 1))
                            pdv = pd[:, :].rearrange("p (a b) -> p a b", a=2)
                            act(lambda e, pdv=pdv: e.copy(out=mkT[:, :, :], in_=pdv[:, :, 0:MEM]), [bpd], [b_mkT])
                            P.barrier()
                    else:
                        with contextlib.ExitStack() as SM:
                            ckb = sb(SM, [128, 256], BF16, "ckbm"); bckb = Buf()
                            for mi in range(2):
                                P.dma("pool", ckb[:, :], IA["cache_mem_k"][l, mi * 128:(mi + 1) * 128, :], (), [bckb])
                                P.dma("pool", MV[:, mi, :, 0:64], IA["cache_mem_v"][l, mi * 128:(mi + 1) * 128, :].rearrange("p (h d) -> p h d", h=4), (), [b_MV])
                                pt, bpt = next_ptr()
                                for c in range(2):
                                    pe(lambda e, c=c, pt=pt: e.transpose(out=pt[:, c * 128:(c + 1) * 128], in_=ckb[:, c * 128:(c + 1) * 128], identity=ident[:, :]), [bckb, b_ident], [bpt], inc=(c == 1))
                                ptv = pt[:, 0:256].rearrange("p (c t) -> p c t", c=2)
                                act(lambda e, mi=mi, ptv=ptv: e.copy(out=mkT[:, :, mi * 128:(mi + 1) * 128], in_=ptv), [bpt], [b_mkT])
                            P.barrier()
                    ck("mem")
                    ysb = [sb(S2, [128, D], F32, "ysb") for _ in range(2)]; b_ysb = [Buf(), Buf()]
                    yrot = [0]
                    xa = sb(S2, [128, 4, 256], BF16, "xa"); b_xa = Buf()
                    xaT = sb(S2, [128, 2, 512], BF16, "xaT"); b_xaT = Buf()
                    xblk2 = [sb(S2, [128, 4, D], F32, "xblk") for _ in range(2)]
                    b_xb2 = [[Buf() for _ in range(4)] for _ in range(2)]
                    hT2 = sb(S2, [128, 8, 512], BF16, "hT2"); b_hT2 = Buf()
                    qxT = sb(S2, [128, 2, 512], BF16, "qxT"); b_qxT = Buf()
                    gs = [sb(S2, [128, 514], F32, "gs") for _ in range(2)]; b_gs = [Buf(), Buf()]
                    halo = sb(S2, [128, NFC, 2], F32, "halo"); b_halo = Buf()
                    cc = [sb(S2, [128, 512], F32, "cc") for _ in range(2)]; b_cc = [Buf(), Buf()]
                    sl = [sb(S2, [128, 512], F32, "sl") for _ in range(2)]; b_sl = [Buf(), Buf()]
                    aTf = sb(S2, [128, NFC, 512], BF16, "aTf"); b_aTf = Buf()
                    wg = [sb(S2, [128, 2, 8, 128], BF16, "wg") for _ in range(3)]; bwg = [Buf() for _ in range(3)]
                    ssp = sb(S2, [128, 4], F32, "ssp"); b_ssp = Buf(strict=True)
                    if 'M' in DBG:
                        print("PH2 sbuf remaining", nc.sbuf_bytes_remaining)
                    if prm:
                        dve(lambda e: e.memset(halo[:, :, :], 0.0), (), [b_halo])
                    else:
                        for j in range(2):
                            P.dma("sp", halo[:, :, j], bass.AP(I["state_ffn_conv"], (l * 2 + j) * DFF, [[1, 128], [128, NFC]]), (), [b_halo], allow_slow_non_contiguous=True)

                    def post_residual(pd, bpd, n, t, gidx, xblk, b_xb):
                        k = yrot[0]; yrot[0] = 1 - k
                        Y, bY = ysb[k], b_ysb[k]
                        act(lambda e: e.copy(out=Y[0:n, :], in_=pd[0:n, :]), [bpd], [bY])
                        dve(lambda e: e.scalar_tensor_tensor(out=junkb[0:n, :], in0=Y[0:n, :], scalar=1.0, in1=Y[0:n, :], op0=ALU.mult, op1=ALU.mult, accum_out=ssp[0:n, t:t + 1]), [bY], [b_junkb, b_ssp])
                        rstd_of(ssp[0:n, t:t + 1], n, D, b_ssp)
                        dve(lambda e: e.scalar_tensor_tensor(out=Y[0:n, :], in0=Y[0:n, :], scalar=ssp[0:n, t:t + 1], in1=gb[0:n, gidx, :], op0=ALU.mult, op1=ALU.mult), [bY, b_ssp, b_gb], [bY])
                        P.op("pool", lambda e: e.tensor_tensor(out=xblk[0:n, t, :], in0=xblk[0:n, t, :], in1=Y[0:n, :], op=ALU.add), [bY, b_xb[t]], [b_xb[t]])

                    def pre_norm_dve(subt, gidx, xblk, b_xb):
                        nt = len(subt)
                        nn = subt[0][1]
                        for t, (a0, n) in enumerate(subt):
                            dve(lambda e, t=t, n=n: e.scalar_tensor_tensor(out=junkb[0:n, :], in0=xblk[0:n, t, :], scalar=1.0, in1=xblk[0:n, t, :], op0=ALU.mult, op1=ALU.mult, accum_out=ss4[0:n, t:t + 1]), [b_xb[t]], [b_junkb, b_ss4])
                        rstd_of(ss4[0:nn, 0:nt], nn, D, b_ss4)
                        for t, (a0, n) in enumerate(subt):
                            dve(lambda e, t=t, n=n: e.scalar_tensor_tensor(out=hb4[0:n, t, :], in0=xblk[0:n, t, :], scalar=ss4[0:n, t:t + 1], in1=gb[0:n, gidx, :], op0=ALU.mult, op1=ALU.mult), [b_xb[t], b_ss4, b_gb], [b_hb4[t]])

                    def pre_norm_pe(subt):
                        for t, (a0, n) in enumerate(subt):
                            pt, bpt = next_ptr()
                            for c in range(8):
                                pe(lambda e, c=c, pt=pt, t=t, n=n: e.transpose(out=pt[:, c * 128:c * 128 + n], in_=hb4[0:n, t, c * 128:(c + 1) * 128], identity=ident[0:n, 0:n]), [b_hb4[t], b_ident], [bpt], inc=(c == 7))
                            ptv = pt[:, :].rearrange("p (c t) -> p c t", c=8)
                            act(lambda e, ptv=ptv, a0=a0, n=n: e.copy(out=hT2[:, :, a0:a0 + n], in_=ptv[:, :, 0:n]), [bpt], [b_hT2])

                    def head_dve(bi):
                        (qc0_, qpos_, nq_) = qblocks[bi]
                        subt_ = [(a0, min(128, nq_ - a0)) for a0 in range(0, nq_, 128)]
                        X_, bX_ = xblk2[bi % 2], b_xb2[bi % 2]
                        for t, (a0, n) in enumerate(subt_):
                            ti = (qc0_ + a0) // 128
                            P.dma("sp", X_[0:n, t, :], ydst[qc0_ + a0:qc0_ + a0 + n, :], [ybufs[ti]], [bX_[t]])
                        pre_norm_dve(subt_, 0, X_, bX_)
                        return subt_

                    drot = [0]

                    def next_dbl3():
                        i = drot[0]; drot[0] = (i + 1) % 3
                        return dbl[i], bdbl[i]

                    pre_norm_pe(head_dve(0))
                    for bi, (qc0, qpos, nq) in enumerate(qblocks):
                        subt = [(a0, min(128, nq - a0)) for a0 in range(0, nq, 128)]
                        xblk, b_xb = xblk2[bi % 2], b_xb2[bi % 2]
                        ck("wout")
                        pd, bpd = next_dbl()
                        for j in range(2):
                            for c in range(8):
                                pe(lambda e, j=j, c=c, pd=pd: e.matmul(pd[:, j * 512:j * 512 + nq], lhsT=wxq[:, c, j * 128:(j + 1) * 128], rhs=hT2[:, c, 0:nq], start=(c == 0), stop=(c == 7)), [b_wxq, b_hT2], [bpd], inc=(c == 7 and j == 1))
                        pdv = pd[:, :].rearrange("p (a b) -> p a b", a=2)
                        act(lambda e, pdv=pdv: e.copy(out=qxT[:, :, 0:nq], in_=pdv[:, :, 0:nq]), [bpd], [b_qxT])
                        callsX = []
                        for hp in range(2):
                            lanes = []
                            for li in range(2):
                                h = hp * 2 + li
                                r0 = li * 64
                                lanes.append(dict(kt=lambda c0, nk, r0=r0, hp=hp: mkT[r0:r0 + 64, hp, c0:c0 + nk], qt=lambda c0, c1, r0=r0, hp=hp: qxT[r0:r0 + 64, hp, c0 - qc0:c1 - qc0],
                                                  v=lambda kbi, nk, h=h: MV[0:nk, kbi, h, :], E=None, tp=None))
                            callsX.append(("X", lanes, (qc0, qpos, nq), "all", [(0, (0, 0, 128)), (1, (128, 128, 128))], [b_qxT, b_mkT, b_MV], epi_plain([(hp * 2) * 64, (hp * 2 + 1) * 64], xa, b_xa)))
                        attend_many(callsX)
                        flush_pending()
                        for t, (a0, n) in enumerate(subt):
                            pt, bpt = next_ptr()
                            for c in range(2):
                                pe(lambda e, c=c, pt=pt, t=t, n=n: e.transpose(out=pt[:, c * 128:c * 128 + n], in_=xa[0:n, t, c * 128:(c + 1) * 128], identity=ident[0:n, 0:n]), [b_xa, b_ident], [bpt], inc=(c == 1))
                            ptv = pt[:, 0:256].rearrange("p (c t) -> p c t", c=2)
                            act(lambda e, ptv=ptv, a0=a0, n=n: e.copy(out=xaT[:, :, a0:a0 + n], in_=ptv[:, :, 0:n]), [bpt], [b_xaT])
                        for t, (a0, n) in enumerate(subt):
                            pd, bpd = next_dbl()
                            for hf in range(2):
                                for c in range(2):
                                    pe(lambda e, hf=hf, c=c, a0=a0, n=n, pd=pd: e.matmul(pd[0:n, hf * 512:(hf + 1) * 512], lhsT=xaT[:, c, a0:a0 + n], rhs=wxo[:, c, hf * 512:(hf + 1) * 512], start=(c == 0), stop=(c == 1)), [b_xaT, b_wxo], [bpd], inc=(c == 1 and hf == 1))
                            post_residual(pd, bpd, n, t, 1, xblk, b_xb)
                        pre_norm_dve(subt, 2, xblk, b_xb)
                        pre_norm_pe(subt)
                        ck("xatt")
                        for f in range(NFC):
                            k = f % 3
                            k2 = f % 2
                            P.dma("sp", wg[k][:, :, :, :], WGU[l, f].rearrange("p (a c j) -> p a c j", a=2, c=8), [bW], [bwg[k]])
                            pd, bpd = next_dbl()
                            for j in range(2):
                                for c in range(8):
                                    pe(lambda e, j=j, c=c, pd=pd, k=k: e.matmul(pd[:, j * 512:j * 512 + nq], lhsT=wg[k][:, j, c, :], rhs=hT2[:, c, 0:nq], start=(c == 0), stop=(c == 7)), [bwg[k], b_hT2], [bpd], inc=(c == 7 and j == 1))
                            G_, bG = gs[k2], b_gs[k2]
                            C_, bC = cc[k2], b_cc[k2]
                            S_, bS = sl[k2], b_sl[k2]
                            act(lambda e, f=f, G_=G_: e.copy(out=G_[:, 0:2], in_=halo[:, f, :]), [b_halo], [bG])
                            act(lambda e, pd=pd, G_=G_: e.copy(out=G_[:, 2:2 + nq], in_=pd[:, 0:nq]), [bpd], [bG])
                            act(lambda e, f=f, G_=G_: e.copy(out=halo[:, f, :], in_=G_[:, nq:nq + 2]), [bG], [b_halo])
                            dve(lambda e, f=f, G_=G_, C_=C_: e.tensor_scalar(out=C_[:, 0:nq], in0=G_[:, 2:2 + nq], scalar1=cw[:, f, 2:3], scalar2=cw[:, f, 3:4], op0=ALU.mult, op1=ALU.add), [bG, b_cw], [bC])
                            dve(lambda e, f=f, G_=G_, C_=C_: e.scalar_tensor_tensor(out=C_[:, 0:nq], in0=G_[:, 1:1 + nq], scalar=cw[:, f, 1:2], in1=C_[:, 0:nq], op0=ALU.mult, op1=ALU.add), [bG, b_cw, bC], [bC])
                            dve(lambda e, f=f, G_=G_, C_=C_: e.scalar_tensor_tensor(out=C_[:, 0:nq], in0=G_[:, 0:nq], scalar=cw[:, f, 0:1], in1=C_[:, 0:nq], op0=ALU.mult, op1=ALU.add), [bG, b_cw, bC], [bC])
                            act(lambda e, C_=C_, S_=S_: e.activation(out=S_[:, 0:nq], in_=C_[:, 0:nq], func=AF.Silu, bias=zc[:, 0:1]), [bC], [bS])
                            dve(lambda e, f=f, pd=pd, S_=S_: e.tensor_tensor(out=aTf[:, f, 0:nq], in0=pd[:, 512:512 + nq], in1=S_[:, 0:nq], op=ALU.mult), [bpd, bS], [b_aTf])
                        nxt = None
                        if bi + 1 < len(qblocks):
                            nxt = head_dve(bi + 1)
                        for t, (a0, n) in enumerate(subt):
                            ti = (qc0 + a0) // 128
                            if nxt is not None and t == min(2, len(subt) - 1):
                                pre_norm_pe(nxt)
                            pd, bpd = next_dbl3()
                            for f in range(NFC):
                                for hf in range(2):
                                    pe(lambda e, hf=hf, f=f, a0=a0, n=n, pd=pd: e.matmul(pd[0:n, hf * 512:(hf + 1) * 512], lhsT=aTf[:, f, a0:a0 + n], rhs=wd_all[:, f, hf * 512:(hf + 1) * 512], start=(f == 0), stop=(f == NFC - 1), skip_group_check=True), [b_aTf, b_wd], [bpd], inc=(hf == 1 and f == NFC - 1))
                            post_residual(pd, bpd, n, t, 3, xblk, b_xb)
                            P.dma("pool", ydst[qc0 + a0:qc0 + a0 + n, :], xblk[0:n, t, :], [b_xb[t]], [ybufs[ti]])
                    for j in range(2):
                        P.dma("pool", bass.AP(O[pfx + "conv"], ((l * (NP if prm else 1) + so) * 2 + j) * DFF, [[1, 128], [128, NFC]]), halo[:, :, j], [b_halo], [], allow_slow_non_contiguous=True)
                    P.barrier()

        YB = {}
        for s in range(NP):
            YB[("p", s)] = [Buf() for _ in range((SEQ + 127) // 128)]
        YB[("s", 0)] = [Buf()]
        for l in range(DEPTH):
            pass
        seqs = [("p", s) for s in range(NP)] + [("s", 0)]
        nrun = 0
        for (kind, s) in seqs:
            for l in range(DEPTH):
                if stop >= 10 and nrun >= stop - 9:
                    break
                try:
                    run_layer(kind, s, l)
                except StopBuild:
                    P.finish()
                    return nc
                nrun += 1
        P.finish()
    return nc


NCORES = 8
_cache = {}


def kernel(**inputs):
    SEQ, PAST, TS, NP = 2048, 2048, 32, 4
    key = (NP, SEQ, PAST, TS)
    if key not in _cache:
        _cache[key] = build(NP, SEQ, PAST, TS)
    nc = _cache[key]
    hc = host_consts(SEQ, PAST, TS)
    f = lambda a: np.ascontiguousarray(np.asarray(a, dtype=np.float32))
    in_maps = []
    for i in range(NCORES):
        m = {}
        m["x_prompt"] = f(inputs["x_prompt"][NP * i:NP * (i + 1)])
        m["x_sample"] = f(inputs["x_sample"][i:i + 1])
        m["mem_prompt"] = f(inputs["mem_prompt"][NP * i:NP * (i + 1)])
        for n in ("cache_a_k", "cache_a_v", "cache_b_k", "cache_b_v", "cache_c_latent", "cache_c_rope_k", "cache_mem_k", "cache_mem_v", "state_ffn_conv"):
            a = np.asarray(inputs[n])[:, i]
            m[n] = f(a.reshape(a.shape[0], a.shape[1], -1))
        for n in ("w_in", "w_out", "norms", "diff_subln", "t5_bias", "band_rel_bias", "mla_q_norm", "mla_w_uq", "mla_kv_norm", "mla_w_ukv",
                  "w_xq", "w_mk", "w_mv", "w_xo", "w_gate", "w_up", "conv_w", "conv_b", "w_down"):
            m[n] = f(inputs[n])
        m["diff_lambda"] = f(np.asarray(inputs["diff_lambda"]).reshape(2, 128))
        m.update(hc)
        in_maps.append(m)
    res = run_bass_kernel_spmd(nc, in_maps, core_ids=list(range(NCORES)))
    R = res.results
    cat0 = lambda n: np.concatenate([r[n] for r in R], axis=0)
    cat1 = lambda n: np.concatenate([r[n] for r in R], axis=1)
    y_p = cat0("y_prompt"); y_s = cat0("y_sample")
    B = y_p.shape[0]
    outs = [y_p, y_s,
            cat1("p_a_k").reshape(2, B, SEQ, 8, 64), cat1("p_a_v").reshape(2, B, SEQ, 8, 64),
            cat1("p_b_k").reshape(2, B, 512, 4, 64), cat1("p_b_v").reshape(2, B, 512, 4, 64),
            cat1("p_lat"), cat1("p_kpe"),
            cat1("p_mk").reshape(2, B, MEM, 4, 64), cat1("p_mv").reshape(2, B, MEM, 4, 64), cat1("p_conv"),
            cat1("s_a_k").reshape(2, NCORES, TS, 8, 64), cat1("s_a_v").reshape(2, NCORES, TS, 8, 64),
            cat1("s_b_k").reshape(2, NCORES, TS, 4, 64), cat1("s_b_v").reshape(2, NCORES, TS, 4, 64),
            cat1("s_lat"), cat1("s_kpe"), cat1("s_conv")]
    return tuple(np.ascontiguousarray(o, dtype=np.float32) for o in outs)
```

```python
import contextlib
import math
import os
DBG = os.environ.get('KDBG', '')
import numpy as np
import concourse.bass as bass
import concourse.mybir as mybir
from concourse.bass_utils import run_bass_kernel_spmd

F32 = mybir.dt.float32
BF16 = mybir.dt.bfloat16
ALU = mybir.AluOpType
AF = mybir.ActivationFunctionType
AX = mybir.AxisListType

D = 1024
DIN = 2720
DFF = 2816
NFC = 22
MEM = 256
EPS = 1e-6
NDS = 32
SAME_ENGINE_SYNC = False


class StopBuild(Exception):
    pass


CKSTOP = os.environ.get('KCK', '')


STOPPED = [False]


def ck(tag):
    if CKSTOP and tag == CKSTOP:
        STOPPED[0] = True


class Buf:
    __slots__ = ("w", "r", "excl", "strict")

    def __init__(self, excl=False, strict=False):
        self.w = None
        self.r = {}
        self.excl = excl
        self.strict = strict


class Prog:
    def __init__(self, nc, stack):
        self.nc = nc
        self.eh = {"pe": nc.tensor, "act": nc.scalar, "dve": nc.vector, "pool": nc.gpsimd, "sp": nc.sync}
        self.sems = {}
        for e in self.eh:
            self.sems[e] = stack.enter_context(nc.semaphore("s_" + e))
        self.cnt = {e: 0 for e in self.eh}
        self.seen = {e: {} for e in self.eh}
        for i in range(NDS):
            self.sems[("d", i)] = stack.enter_context(nc.semaphore("s_d%d" % i))
        self.dcnt = [0] * NDS
        self.dnext = {"sp": 0, "pool": 0, "act": 0}
        self.drange = {"sp": (0, NDS - 4), "pool": (NDS - 4, NDS), "act": (0, NDS - 4)}
        self.nins = 0

    def _deps(self, eng, reads, writes, extra=()):
        needs = {}
        strict_own = 0
        for b in reads:
            t = b.w
            if t is not None and needs.get(t[0], 0) < t[1]:
                needs[t[0]] = t[1]
            if t is not None and t[0] == eng and t[1] > strict_own and eng != "pe":
                strict_own = t[1]
            if b.excl:
                for k, v in b.r.items():
                    if needs.get(k, 0) < v:
                        needs[k] = v
        for b in writes:
            t = b.w
            if t is not None and needs.get(t[0], 0) < t[1]:
                needs[t[0]] = t[1]
            for k, v in b.r.items():
                if needs.get(k, 0) < v:
                    needs[k] = v
        for t in extra:
            if needs.get(t[0], 0) < t[1]:
                needs[t[0]] = t[1]
        seen = self.seen[eng]
        for k, v in needs.items():
            if k == eng and (eng == "pe" or not SAME_ENGINE_SYNC) and eng != "pool":
                if strict_own and seen.get(k, 0) < strict_own:
                    seen[k] = strict_own
                    self.eh[eng].wait_ge(self.sems[k], strict_own)
                continue
            if seen.get(k, 0) >= v:
                continue
            seen[k] = v
            self.eh[eng].wait_ge(self.sems[k], v)

    def op(self, eng, fn, reads=(), writes=(), inc=True):
        if STOPPED[0]:
            return
        self._deps(eng, reads, writes)
        self.nins += 1
        ins = fn(self.eh[eng])
        if inc:
            self.cnt[eng] += 1
            v = self.cnt[eng]
            ins.then_inc(self.sems[eng], 1)
        else:
            v = self.cnt[eng] + 1
        for b in reads:
            b.r[eng] = v
        for b in writes:
            b.w = (eng, v)
            b.r = {}

    def dma(self, eng, out, in_, reads=(), writes=(), **kw):
        if STOPPED[0]:
            return
        lo, hi = self.drange[eng]
        i = lo + self.dnext[eng]
        self.dnext[eng] = (self.dnext[eng] + 1) % (hi - lo)
        key = ("d", i)
        extra = ((key, self.dcnt[i]),) if self.dcnt[i] else ()
        self._deps(eng, reads, writes, extra)
        self.dcnt[i] += 16
        v = self.dcnt[i]
        self.nins += 1
        self.eh[eng].dma_start(out=out, in_=in_, **kw).then_inc(self.sems[key], 16)
        for b in reads:
            b.r[key] = v
        for b in writes:
            b.w = (key, v)
            b.r = {}

    def barrier(self, force=False):
        if STOPPED[0] and not force:
            return
        keys = [e for e in self.eh if self.cnt[e]] + [("d", i) for i in range(NDS) if self.dcnt[i]]
        for e in self.eh:
            seen = self.seen[e]
            for k in keys:
                v = self.cnt[k] if not isinstance(k, tuple) else self.dcnt[k[1]]
                if k == e and e != "pool":
                    continue
                if seen.get(k, 0) >= v:
                    continue
                seen[k] = v
                self.eh[e].wait_ge(self.sems[k], v)

    def finish(self):
        self.barrier(force=True)


def t5_bucket_np(rel):
    half, exact = 16, 8
    ret = np.where(rel > 0, half, 0)
    n = np.abs(rel)
    nf = np.maximum(n, 1).astype(np.float32)
    large = exact + (np.log(nf / np.float32(exact)) / np.float32(math.log(128 / exact)) * np.float32(half - exact)).astype(np.int32)
    large = np.minimum(large, half - 1)
    return ret + np.where(n < exact, n, large)


def host_consts(SEQ, PAST, TS):
    m = np.arange(1151)
    bk = t5_bucket_np(511 - m)
    oh_t5 = (bk[None, :] == np.arange(32)[:, None]).astype(np.float32)
    m2 = np.arange(1151)
    j = np.clip(511 - m2, -128, 128) + 128
    ohb = (j[None, :] == np.arange(384)[:, None]).astype(np.float32)

    def rope_tab(pos):
        half = 16
        inv = np.float32(10000.0) ** (-np.arange(half, dtype=np.float32) / np.float32(half))
        ang = pos.astype(np.float32)[:, None] * inv[None, :]
        c = np.cos(ang).astype(np.float32)
        s = np.sin(ang).astype(np.float32)
        c4 = np.repeat(c[:, None, :], 4, axis=1)
        s4 = np.repeat(s[:, None, :], 4, axis=1)
        return np.ascontiguousarray(np.stack([c4, s4], axis=1).reshape(len(pos), 128))

    return {"oh_t5": oh_t5, "oh_b": ohb, "rope_p": rope_tab(np.arange(SEQ)), "rope_s": rope_tab(PAST + np.arange(TS))}


def vis_info(mode, q0, nq, k0, nk):
    qch = [(c, min(c + 64, nq)) for c in range(0, nq, 64)]
    kch = [(p, min(p + 64, nk)) for p in range(0, nk, 64)]
    V = {}
    for (p0, p1) in kch:
        kc = (k0 + p0) // 64
        for (c0, c1) in qch:
            qc = (q0 + c0) // 64
            if mode == "causal":
                ok = kc <= qc
            elif mode == "band":
                ok = (kc <= qc) and (kc >= qc - 8)
            else:
                ok = True
            V[(p0, c0)] = ok
    viscols = [(c0, c1) for (c0, c1) in qch if any(V[(p0, c0)] for (p0, _) in kch)]
    if not viscols:
        return None
    cs = (min(c0 for c0, _ in viscols) // 128) * 128
    ce = min(nq, ((max(c1 for _, c1 in viscols) + 127) // 128) * 128)
    masks = []
    for (p0, p1) in kch:
        run = None
        for (c0, c1) in qch:
            if c0 < cs or c0 >= ce:
                continue
            if not V[(p0, c0)]:
                if run is not None and run[1] == c0:
                    run[1] = c1
                else:
                    if run is not None:
                        masks.append((p0, p1, run[0], run[1]))
                    run = [c0, c1]
        if run is not None:
            masks.append((p0, p1, run[0], run[1]))
    return cs, ce, masks


def build(NP, SEQ, PAST, TS=32, DEPTH=2, stop=0):
    nc = bass.Bass("TRN2", target_bir_lowering=False)
    I, O = {}, {}

    def inp(n, shape):
        I[n] = nc.dram_tensor(n, list(shape), F32, kind="ExternalInput")

    def outp(n, shape):
        O[n] = nc.dram_tensor(n, list(shape), F32, kind="ExternalOutput")

    NBP = min(512, SEQ)
    NBS = min(512, PAST)
    inp("x_prompt", (NP, SEQ, D)); inp("x_sample", (1, TS, D))
    inp("cache_a_k", (DEPTH, PAST, 512)); inp("cache_a_v", (DEPTH, PAST, 512))
    inp("cache_b_k", (DEPTH, NBS, 256)); inp("cache_b_v", (DEPTH, NBS, 256))
    inp("cache_c_latent", (DEPTH, PAST, 128)); inp("cache_c_rope_k", (DEPTH, PAST, 32))
    inp("cache_mem_k", (DEPTH, MEM, 256)); inp("cache_mem_v", (DEPTH, MEM, 256))
    inp("state_ffn_conv", (DEPTH, 2, DFF)); inp("mem_prompt", (NP, MEM, D))
    inp("w_in", (DEPTH, D, DIN)); inp("w_out", (DEPTH, D, D)); inp("norms", (DEPTH, 7, D))
    inp("diff_lambda", (DEPTH, 128)); inp("diff_subln", (DEPTH, 64)); inp("t5_bias", (32, 8))
    inp("band_rel_bias", (DEPTH, 4, 257)); inp("mla_q_norm", (DEPTH, 256)); inp("mla_w_uq", (DEPTH, 256, 384))
    inp("mla_kv_norm", (DEPTH, 128)); inp("mla_w_ukv", (DEPTH, 128, 512))
    inp("w_xq", (DEPTH, D, 256)); inp("w_mk", (DEPTH, D, 256)); inp("w_mv", (DEPTH, D, 256)); inp("w_xo", (DEPTH, 256, D))
    inp("w_gate", (DEPTH, D, DFF)); inp("w_up", (DEPTH, D, DFF)); inp("conv_w", (DEPTH, 3, DFF)); inp("conv_b", (DEPTH, DFF))
    inp("w_down", (DEPTH, DFF, D))
    inp("oh_t5", (32, 1151)); inp("oh_b", (384, 1151)); inp("rope_p", (SEQ, 128)); inp("rope_s", (TS, 128))
    outp("y_prompt", (NP, SEQ, D)); outp("y_sample", (1, TS, D))
    outp("p_a_k", (DEPTH, NP, SEQ, 512)); outp("p_a_v", (DEPTH, NP, SEQ, 512))
    outp("p_b_k", (DEPTH, NP, NBP, 256)); outp("p_b_v", (DEPTH, NP, NBP, 256))
    outp("p_lat", (DEPTH, NP, SEQ, 128)); outp("p_kpe", (DEPTH, NP, SEQ, 32))
    outp("p_mk", (DEPTH, NP, MEM, 256)); outp("p_mv", (DEPTH, NP, MEM, 256)); outp("p_conv", (DEPTH, NP, 2, DFF))
    outp("s_a_k", (DEPTH, 1, TS, 512)); outp("s_a_v", (DEPTH, 1, TS, 512))
    outp("s_b_k", (DEPTH, 1, TS, 256)); outp("s_b_v", (DEPTH, 1, TS, 256))
    outp("s_lat", (DEPTH, 1, TS, 128)); outp("s_kpe", (DEPTH, 1, TS, 32)); outp("s_conv", (DEPTH, 1, 2, DFF))
    IA = {k: v.ap() for k, v in I.items()}
    OA = {k: v.ap() for k, v in O.items()}
    WB = {}
    for n in ("w_in", "w_out", "mla_w_uq", "mla_w_ukv", "w_xq", "w_mk", "w_mv", "w_xo", "w_down"):
        WB[n] = nc.dram_tensor("wb_" + n, list(I[n].shape), BF16, kind="Internal")
    WBA = {k: v.ap() for k, v in WB.items()}
    WGUh = nc.dram_tensor("wb_wgu", [DEPTH, NFC, 128, 2048], BF16, kind="Internal")
    WGU = WGUh.ap()
    Escr = nc.dram_tensor("Escr", [8 + DEPTH * 4, 128, 1024], BF16, kind="Internal")
    EscrA = Escr.ap()
    scrA = nc.dram_tensor("scrA", [8, 1151], F32, kind="Internal")
    scrB = nc.dram_tensor("scrB", [DEPTH * 4, 1151], F32, kind="Internal")

    uid = [0]
    G = contextlib.ExitStack()
    with G:
        P = Prog(nc, G)

        def sb(stack, shape, dt, name="t"):
            uid[0] += 1
            return stack.enter_context(nc.sbuf_tensor("%s_%d" % (name, uid[0]), list(shape), dt))

        dbl = [G.enter_context(nc.psum_tensor("dbl%d" % i, [128, 1024], F32)) for i in range(3)]
        bdbl = [Buf(True) for _ in range(3)]
        ptr = [G.enter_context(nc.psum_tensor("ptr%d" % i, [128, 1024], BF16)) for i in range(2)]
        bptr = [Buf(True) for _ in range(2)]
        rot = {"d": 0, "t": 0}

        def next_dbl():
            i = rot["d"]; rot["d"] = 1 - i
            return dbl[i], bdbl[i]

        def next_ptr():
            i = rot["t"]; rot["t"] = 1 - i
            return ptr[i], bptr[i]

        pe = lambda fn, r=(), w=(), inc=True: P.op("pe", fn, r, w, inc)
        act = lambda fn, r=(), w=(): P.op("act", fn, r, w)
        dve = lambda fn, r=(), w=(): P.op("dve", fn, r, w)

        zc = sb(G, [128, 1], F32, "zc"); ec = sb(G, [128, 1], F32, "ec"); b_zc = Buf()
        dve(lambda e: e.memset(zc[:], 0.0), (), [b_zc])
        dve(lambda e: e.memset(ec[:], EPS), (), [b_zc])
        P.barrier()
        nc.const_aps.register(F32, 0.0, zc[:, 0:1])
        nc.const_aps.register(F32, EPS, ec[:, 0:1])
        ident = sb(G, [128, 128], BF16, "ident"); b_ident = Buf()
        Jm = sb(G, [128, 128], F32, "J"); b_J = Buf()
        P.op("pool", lambda e: e.memset(ident[:], 0.0), (), [b_ident])
        P.op("pool", lambda e: e.affine_select(out=ident[:], in_=ident[:], pattern=[[-1, 128]], compare_op=ALU.not_equal, fill=1.0, base=0, channel_multiplier=1), [b_ident], [b_ident])
        P.op("pool", lambda e: e.memset(Jm[:], 0.0), (), [b_J])
        P.op("pool", lambda e: e.affine_select(out=Jm[:], in_=Jm[:], pattern=[[1, 128]], compare_op=ALU.not_equal, fill=1.0, base=-127, channel_multiplier=1), [b_J], [b_J])
        bW = Buf()
        for n in ([] if 'W' in DBG else WB):
            src = IA[n].rearrange("l a b -> (l a) b")
            dst = WBA[n].rearrange("l a b -> (l a) b")
            rows = src.shape[0]
            step = 256
            for r in range(0, rows, step):
                P.dma("pool", dst[r:min(r + step, rows)], src[r:min(r + step, rows)], (), [Buf()])
        for l_ in range(DEPTH):
            for j_, wn in enumerate(("w_gate", "w_up")):
                wsrc_ = IA[wn][l_].rearrange("(c p) n -> p c n", p=128)
                for f_ in range(NFC):
                    P.dma("pool", WGU[l_, f_][:, j_ * 1024:(j_ + 1) * 1024].rearrange("p (c j) -> p c j", c=8), wsrc_[:, :, f_ * 128:(f_ + 1) * 128], (), [Buf()])
        b_Escr = Buf()
        with contextlib.ExitStack() as S:
            EA = sb(S, [128, 8, 1024], BF16, "EA"); b_EA = Buf()
            EB = sb(S, [128, DEPTH * 4, 1024], BF16, "EB"); b_EB = Buf()
            t5 = sb(S, [32, 8], F32); b_t5 = Buf()
            oh = sb(S, [32, 1151], F32); b_oh = Buf()
            c15 = sb(S, [8, 1], F32); b_c15 = Buf(strict=True)
            wA = sb(S, [8, 1151], F32); b_wA = Buf()
            P.dma("sp", t5[:], IA["t5_bias"], (), [b_t5])
            P.dma("sp", oh[:], IA["oh_t5"], (), [b_oh])
            P.dma("sp", c15[:], bass.AP(I["t5_bias"], 15 * 8, [[1, 8], [1, 1]]), (), [b_c15])
            dve(lambda e: e.tensor_scalar(out=c15[:], in0=c15[:], scalar1=-1.0, scalar2=None, op0=ALU.mult), [b_c15], [b_c15])
            for (c0, c1) in ((0, 512), (512, 1024), (1024, 1151)):
                pd, bpd = next_dbl()
                pe(lambda e, c0=c0, c1=c1, pd=pd: e.matmul(pd[0:8, 0:c1 - c0], lhsT=t5[:, :], rhs=oh[:, c0:c1], start=True, stop=True), [b_t5, b_oh], [bpd])
                act(lambda e, c0=c0, c1=c1, pd=pd: e.activation(out=wA[:, c0:c1], in_=pd[0:8, 0:c1 - c0], func=AF.Exp, bias=c15[:, 0:1], scale=1.0), [bpd, b_c15], [b_wA])
            b_scrA = Buf()
            P.dma("sp", scrA.ap(), wA[:], [b_wA], [b_scrA])
            Gp = sb(S, [128, 1024], F32); b_Gp = Buf()
            for h in range(8):
                P.dma("sp", Gp[:], bass.AP(scrA, h * 1151, [[1, 128], [1, 1024]]), [b_scrA], [b_Gp])
                pd, bpd = next_dbl()
                for hf in range(2):
                    pe(lambda e, hf=hf, pd=pd: e.matmul(pd[:, hf * 512:(hf + 1) * 512], lhsT=Jm[:], rhs=Gp[:, hf * 512:(hf + 1) * 512], start=True, stop=True), [b_J, b_Gp], [bpd])
                act(lambda e, h=h, pd=pd: e.copy(out=EA[:, h, :], in_=pd[:, :]), [bpd], [b_EA])
            relT = sb(S, [128, 3, DEPTH * 4], F32); b_relT = Buf()
            ohb = sb(S, [128, 3, 1151], F32); b_ohb = Buf()
            c0b = sb(S, [DEPTH * 4, 1], F32); b_c0b = Buf(strict=True)
            wB = sb(S, [DEPTH * 4, 1151], F32); b_wB = Buf()
            dve(lambda e: e.memset(relT[:], 0.0), (), [b_relT])
            for c in range(3):
                nj = 128 if c < 2 else 1
                P.dma("sp", relT[0:nj, c, :], bass.AP(I["band_rel_bias"], c * 128, [[1, nj], [257, DEPTH * 4]]), (), [b_relT], allow_slow_non_contiguous=True)
            P.dma("sp", ohb[:], IA["oh_b"].rearrange("(c p) m -> p c m", p=128), (), [b_ohb])
            P.dma("sp", c0b[:], bass.AP(I["band_rel_bias"], 0, [[257, DEPTH * 4], [1, 1]]), (), [b_c0b], allow_slow_non_contiguous=True)
            dve(lambda e: e.tensor_scalar(out=c0b[:], in0=c0b[:], scalar1=-1.0, scalar2=None, op0=ALU.mult), [b_c0b], [b_c0b])
            for (c0_, c1_) in ((0, 512), (512, 1024), (1024, 1151)):
                pd, bpd = next_dbl()
                for c in range(3):
                    pe(lambda e, c=c, pd=pd, c0_=c0_, c1_=c1_: e.matmul(pd[0:DEPTH * 4, 0:c1_ - c0_], lhsT=relT[:, c, :], rhs=ohb[:, c, c0_:c1_], start=(c == 0), stop=(c == 2)), [b_relT, b_ohb], [bpd], inc=(c == 2))
                act(lambda e, pd=pd, c0_=c0_, c1_=c1_: e.activation(out=wB[:, c0_:c1_], in_=pd[0:DEPTH * 4, 0:c1_ - c0_], func=AF.Exp, bias=c0b[:, 0:1], scale=1.0), [bpd, b_c0b], [b_wB])
            b_scrB = Buf()
            P.dma("sp", scrB.ap(), wB[:], [b_wB], [b_scrB])
            for lh in range(DEPTH * 4):
                P.dma("sp", Gp[:, :], bass.AP(scrB, lh * 1151, [[1, 128], [1, 1024]]), [b_scrB], [b_Gp])
                pd, bpd = next_dbl()
                for hf in range(2):
                    pe(lambda e, hf=hf, pd=pd: e.matmul(pd[:, hf * 512:(hf + 1) * 512], lhsT=Jm[:], rhs=Gp[:, hf * 512:(hf + 1) * 512], start=True, stop=True), [b_J, b_Gp], [bpd])
                act(lambda e, lh=lh, pd=pd: e.copy(out=EB[:, lh, :], in_=pd[:, :]), [bpd], [b_EB])
            P.dma("sp", EscrA[0:8].rearrange("h p u -> p h u"), EA[:, :, :], [b_EA], [b_Escr])
            P.dma("sp", EscrA[8:8 + DEPTH * 4].rearrange("h p u -> p h u"), EB[:, :, :], [b_EB], [Buf()])
            P.barrier()

        neglam = sb(G, [128, DEPTH], F32, "neglam"); b_lam = Buf(strict=True)
        subln = sb(G, [128, DEPTH, 64], F32, "subln"); b_subln = Buf()
        LAM_INIT = [0.8 - 0.6 * math.exp(-0.3 * l) for l in range(DEPTH)]
        with contextlib.ExitStack() as S:
            lt = sb(S, [128, 128], F32); b_lt = Buf()
            junk = sb(S, [128, 32], F32); b_junk = Buf()
            s12 = sb(S, [128, 2], F32); b_s12 = Buf(strict=True)
            for l in range(DEPTH):
                P.dma("sp", lt[:], bass.AP(I["diff_lambda"], l * 128, [[0, 128], [1, 128]]), (), [b_lt])
                P.dma("sp", subln[:, l, :], bass.AP(I["diff_subln"], l * 64, [[0, 128], [1, 64]]), (), [b_subln])
                for i in range(2):
                    dve(lambda e, i=i: e.scalar_tensor_tensor(out=junk[:], in0=lt[:, 64 * i:64 * i + 32], scalar=1.0, in1=lt[:, 64 * i + 32:64 * i + 64], op0=ALU.mult, op1=ALU.mult, accum_out=s12[:, i:i + 1]), [b_lt], [b_junk, b_s12])
                act(lambda e: e.activation(out=s12[:], in_=s12[:], func=AF.Exp, bias=zc[:, 0:1]), [b_s12], [b_s12])
                dve(lambda e, l=l: e.tensor_tensor(out=neglam[:, l:l + 1], in0=s12[:, 1:2], in1=s12[:, 0:1], op=ALU.subtract), [b_s12], [b_lam])
                dve(lambda e, l=l: e.tensor_scalar(out=neglam[:, l:l + 1], in0=neglam[:, l:l + 1], scalar1=-LAM_INIT[l], scalar2=None, op0=ALU.add), [b_lam], [b_lam])
                dve(lambda e, l=l: e.tensor_scalar(out=subln[:, l, :], in0=subln[:, l, :], scalar1=1.0 - LAM_INIT[l], scalar2=None, op0=ALU.mult), [b_subln], [b_subln])
            P.barrier()

        if stop == 1:
            P.finish()
            return nc

        def bcast_row(dst, src_handle, off, n, bufs):
            P.dma("sp", dst, bass.AP(src_handle, off, [[0, 128], [1, n]]), (), bufs)

        def rstd_of(ss, n, Dd, bss):
            act(lambda e: e.activation(out=ss, in_=ss, func=AF.Ln, bias=ec[0:n, 0:1], scale=1.0 / Dd), [bss], [bss])
            act(lambda e: e.activation(out=ss, in_=ss, func=AF.Exp, bias=zc[0:n, 0:1], scale=-0.5), [bss], [bss])

        def run_layer(kind, s, l):
            prm = kind == "p"
            T = SEQ if prm else TS
            tiles = [(t0, min(128, T - t0)) for t0 in range(0, T, 128)]
            NT = len(tiles)
            past = 0 if prm else PAST
            NKB_past = past // 128
            NK = past + T
            kblocks = [(j * 128, j * 128, 128) for j in range(NKB_past)] + [(past + t0, past + t0, n) for (t0, n) in tiles]
            NKB = len(kblocks)
            qblocks = []
            for q0 in range(0, T, 512):
                nq = min(512, T - q0)
                qblocks.append((q0, past + q0, nq))
            xsrc = (IA["x_prompt"][s] if prm else IA["x_sample"][0]) if l == 0 else (OA["y_prompt"][s] if prm else OA["y_sample"][0])
            ydst = OA["y_prompt"][s] if prm else OA["y_sample"][0]
            ybufs = YB[(kind, s)]
            pfx = "p_" if prm else "s_"
            so = s if prm else 0
            rope_h = I["rope_p"] if prm else I["rope_s"]
            wl = {k: v[l] for k, v in WBA.items()}
            scale_of = {"A": 32 ** -0.5, "B": 64 ** -0.5, "C": 96 ** -0.5, "X": 64 ** -0.5}

            L = contextlib.ExitStack()
            with L:
                b_atok = Buf()
                NPT = 4
                ptiles = [sb(L, [128, 2, 512], BF16, "PT") for _ in range(NPT)]
                bpt_ = [Buf() for _ in range(NPT)]
                prot = [0]
                rec = sb(L, [128, 2, 4], F32, "rec"); b_rec = Buf(strict=True)
                otmp = sb(L, [128, 4, 64], F32, "otmp"); b_otmp = Buf()
                o0t = sb(L, [128, 4, 64], F32, "o0t"); b_o0t = Buf()
                sq = sb(L, [128, 4, 64], F32, "sq"); b_sq = Buf()
                ssA = sb(L, [128, 4], F32, "ssA"); b_ssA = Buf(strict=True)

                accs = [sb(L, [128, 2, 260], F32, "accs") for _ in range(2)]
                baccs = [Buf(), Buf()]
                arot = [0]
                pending = []

                def flush_pending(step=None):
                    while pending and (step is None or pending[0][0] <= step):
                        pending.pop(0)[1]()

                def attend(grp, lanes, qb, mode, kbl, rbufs, epilogue):
                    attend_many([(grp, lanes, qb, mode, kbl, rbufs, epilogue)])

                def attend_many(calls):
                    acc, bacc = dbl[2], bdbl[2]
                    G_ = []
                    for ci, (grp, lanes, qb, mode, kbl, rbufs, epilogue) in enumerate(calls):
                        (qc0, qpos, nq) = qb
                        st_ = []
                        for kbi, (kc0, kpos, nk) in kbl:
                            vi = vis_info(mode, qpos, nq, kpos, nk)
                            if vi is not None:
                                st_.append((kbi, kc0, kpos, nk, vi[0], vi[2], vi[1]))
                        for si, stp in enumerate(st_):
                            G_.append((ci, si, len(st_), stp))
                    qk = {}

                    def emit_qk(g):
                        ci, si, ns, (kbi, kc0, kpos, nk, cs, masks, ce) = G_[g]
                        (grp, lanes, qb, mode, kbl, rbufs, epilogue) = calls[ci]
                        (qc0, qpos, nq) = qb
                        pd, bpd = next_dbl()
                        for li, ln in enumerate(lanes):
                            pe(lambda e, ln=ln, li=li, pd=pd: e.matmul(pd[0:nk, li * 512 + cs:li * 512 + ce], lhsT=ln["kt"](kc0, nk), rhs=ln["qt"](qc0 + cs, qc0 + ce),
                                                                      start=True, stop=True, **({"tile_position": ln["tp"]} if ln.get("tp") else {})),
                               rbufs, [bpd], inc=(li == 1))
                        qk[g] = (pd, bpd)

                    emit_qk(0)
                    if len(G_) > 1:
                        emit_qk(1)
                    for g, (ci, si, ns, (kbi, kc0, kpos, nk, cs, masks, ce)) in enumerate(G_):
                        (grp, lanes, qb, mode, kbl, rbufs, epilogue) = calls[ci]
                        (qc0, qpos, nq) = qb
                        scale = scale_of[grp]
                        nsub = (nq + 127) // 128
                        if si == 0:
                            accz = acc[:, :].rearrange("p (a b) -> p a b", a=2)
                            dve(lambda e, accz=accz, nsub=nsub: e.memset(accz[:, :, 0:nsub * 65], 0.0), (), [bacc])
                        pd, bpd = qk.pop(g)
                        pi = prot[0]; prot[0] = (pi + 1) % NPT
                        PT, bPT = ptiles[pi], bpt_[pi]
                        pdv = pd[:, :].rearrange("p (a b) -> p a b", a=2)
                        act(lambda e, PT=PT, pdv=pdv: e.activation(out=PT[0:nk, :, cs:ce], in_=pdv[0:nk, :, cs:ce], func=AF.Exp, bias=zc[0:nk, 0:1], scale=scale), [bpd], [bPT])
                        if g + 2 < len(G_):
                            emit_qk(g + 2)
                        relmax = (kpos + nk - 1) - (qpos + cs)
                        for li, ln in enumerate(lanes):
                            if ln.get("E") is not None and relmax > ln["Ethr"]:
                                off = ln["c0"] - (kpos - qpos)
                                Et = ln["E"]
                                dve(lambda e, li=li, Et=Et, off=off, PT=PT: e.tensor_tensor(out=PT[0:nk, li, cs:ce], in0=PT[0:nk, li, cs:ce], in1=Et[0:nk, off + cs:off + ce], op=ALU.mult), [bPT, ln["Eb"]], [bPT])
                        for (p0, p1, c0, c1) in masks:
                            dve(lambda e, PT=PT, p0=p0, p1=p1, c0=c0, c1=c1: e.memset(PT[p0:p1, :, c0:c1], 0.0), (), [bPT])
                        for li, ln in enumerate(lanes):
                            for t in range(nsub):
                                a0, a1 = t * 128, min(t * 128 + 128, nq)
                                if a1 <= cs or a0 >= ce:
                                    continue
                                last = (li == 1 and a1 >= ce)
                                pe(lambda e, li=li, t=t, a0=a0, a1=a1, ln=ln, PT=PT: e.matmul(acc[0:a1 - a0, li * 512 + t * 65:li * 512 + t * 65 + 65], lhsT=PT[0:nk, li, a0:a1], rhs=ln["v"](kbi, nk),
                                                                                            start=False, stop=False, skip_group_check=True),
                                   [bPT] + rbufs, [bacc], inc=last)
                        flush_pending(si)
                        if si == ns - 1:
                            flush_pending()
                            k = arot[0]; arot[0] = 1 - k
                            A_, bA_ = accs[k], baccs[k]
                            nn = min(128, nq)
                            accv = acc[:, :].rearrange("p (a b) -> p a b", a=2)
                            dve(lambda e, A_=A_, nn=nn, nsub=nsub, accv=accv: e.tensor_copy(out=A_[0:nn, :, 0:nsub * 65], in_=accv[0:nn, :, 0:nsub * 65]), [bacc], [bA_])
                            epilogue(A_, bA_, qb, nsub)

                def epi_plain(col_of_lane, dst=None, bdst=None):
                    def f(A_, bA_, qb, nsub):
                        pending.append((1, lambda: g(A_, bA_, qb, nsub)))

                    def g(A_, bA_, qb, nsub):
                        (qc0, qpos, nq) = qb
                        nn = min(128, nq)
                        recv = A_[0:nn, :, 0:nsub * 65].rearrange("p a (t c) -> p a t c", c=65)
                        dve(lambda e: e.reciprocal(out=rec[0:nn, :, 0:nsub], in_=recv[:, :, :, 64]), [bA_], [b_rec])
                        for li in range(2):
                            for t in range(nsub):
                                a0, a1 = t * 128, min(t * 128 + 128, nq)
                                ti = (qc0 + a0) // 128
                                col = col_of_lane[li]
                                if dst is None:
                                    o_ap, o_b = a_tok[0:a1 - a0, ti, col:col + 64], b_atok
                                else:
                                    o_ap, o_b = dst[0:a1 - a0, t, col:col + 64], bdst
                                dve(lambda e, li=li, t=t, a0=a0, a1=a1, o_ap=o_ap: e.tensor_scalar(out=o_ap, in0=A_[0:a1 - a0, li, t * 65:t * 65 + 64],
                                                                                                 scalar1=rec[0:a1 - a0, li, t:t + 1], scalar2=None, op0=ALU.mult), [bA_, b_rec], [o_b])
                    return f

                def epi_diff(h):
                    def f(A_, bA_, qb, nsub):
                        (qc0, qpos, nq) = qb
                        nn = min(128, nq)
                        recv = A_[0:nn, :, 0:nsub * 65].rearrange("p a (t c) -> p a t c", c=65)

                        def s1():
                            dve(lambda e: e.reciprocal(out=rec[0:nn, :, 0:nsub], in_=recv[:, :, :, 64]), [bA_], [b_rec])
                            dve(lambda e: e.tensor_scalar(out=rec[0:nn, 1, 0:nsub], in0=rec[0:nn, 1, 0:nsub], scalar1=neglam[0:nn, l:l + 1], scalar2=None, op0=ALU.mult), [b_rec, b_lam], [b_rec])
                            for t in range(nsub):
                                n_ = min(128, nq - t * 128)
                                dve(lambda e, t=t, n_=n_: e.tensor_scalar(out=o0t[0:n_, t, :], in0=A_[0:n_, 0, t * 65:t * 65 + 64], scalar1=rec[0:n_, 0, t:t + 1], scalar2=None, op0=ALU.mult), [bA_, b_rec], [b_o0t])
                            for t in range(nsub):
                                n_ = min(128, nq - t * 128)
                                dve(lambda e, t=t, n_=n_: e.scalar_tensor_tensor(out=otmp[0:n_, t, :], in0=A_[0:n_, 1, t * 65:t * 65 + 64], scalar=rec[0:n_, 1, t:t + 1], in1=o0t[0:n_, t, :], op0=ALU.mult, op1=ALU.add), [bA_, b_rec, b_o0t], [b_otmp])
                            dve(lambda e: e.tensor_tensor(out=sq[0:nn, 0:nsub, :], in0=otmp[0:nn, 0:nsub, :], in1=otmp[0:nn, 0:nsub, :], op=ALU.mult), [b_otmp], [b_sq])
                            dve(lambda e: e.tensor_reduce(out=ssA[0:nn, 0:nsub], in_=sq[0:nn, 0:nsub, :], axis=AX.X, op=ALU.add), [b_sq], [b_ssA])

                        def s2():
                            rstd_of(ssA[0:nn, 0:nsub], nn, 64, b_ssA)

                        def s3():
                            for t in range(nsub):
                                n_ = min(128, nq - t * 128)
                                ti = (qc0 + t * 128) // 128
                                dve(lambda e, t=t, n_=n_, ti=ti: e.scalar_tensor_tensor(out=a_tok[0:n_, ti, h * 64:h * 64 + 64], in0=otmp[0:n_, t, :], scalar=ssA[0:n_, t:t + 1], in1=subln[0:n_, l, :], op0=ALU.mult, op1=ALU.mult),
                                    [b_otmp, b_ssA, b_subln], [b_atok])
                        pending.append((1, s1))
                        pending.append((5, s2))
                        pending.append((6, s3))
                    return f

                with contextlib.ExitStack() as S1:
                    a_tok = sb(S1, [128, NT, D], BF16, "atok")
                    EA = sb(S1, [128, 8, 1024], BF16, "EAt"); b_EA = Buf()
                    EB = sb(S1, [128, 4, 1024], BF16, "EBt"); b_EB = Buf()
                    hT = sb(S1, [128, 8, T], BF16, "hT"); b_hT = Buf()
                    gbc = sb(S1, [128, D], F32, "gbc"); b_gbc = Buf()
                    bcast_row(gbc[:], I["norms"], (l * 7 + 0) * D, D, [b_gbc])
                    xt = [sb(S1, [128, D], F32, "xt") for _ in range(2)]; bxt = [Buf(), Buf()]
                    hb = [sb(S1, [128, D], BF16, "hb") for _ in range(2)]; bhb = [Buf(), Buf()]
                    junkb = sb(S1, [128, D], BF16, "junkb"); b_junkb = Buf()
                    ss1 = [sb(S1, [128, 1], F32, "ss1") for _ in range(2)]; bss1 = [Buf(strict=True), Buf(strict=True)]
                    def h_stages(ti, t0, n):
                        k = ti % 2

                        def g0():
                            P.dma("sp", xt[k][0:n, :], xsrc[t0:t0 + n, :], [ybufs[ti]], [bxt[k]])
                            dve(lambda e: e.scalar_tensor_tensor(out=junkb[0:n, :], in0=xt[k][0:n, :], scalar=1.0, in1=xt[k][0:n, :], op0=ALU.mult, op1=ALU.mult, accum_out=ss1[k][0:n, :]), [bxt[k]], [b_junkb, bss1[k]])

                        def g1():
                            rstd_of(ss1[k][0:n, :], n, D, bss1[k])

                        def g2():
                            dve(lambda e: e.scalar_tensor_tensor(out=hb[k][0:n, :], in0=xt[k][0:n, :], scalar=ss1[k][0:n, 0:1], in1=gbc[0:n, :], op0=ALU.mult, op1=ALU.mult), [bxt[k], bss1[k], b_gbc], [bhb[k]])

                        def g3():
                            pt, bpt = next_ptr()
                            for c in range(8):
                                pe(lambda e, c=c: e.transpose(out=pt[:, c * 128:c * 128 + n], in_=hb[k][0:n, c * 128:(c + 1) * 128], identity=ident[0:n, 0:n]), [bhb[k], b_ident], [bpt], inc=(c == 7))
                            ptv = pt[:, :].rearrange("p (c t) -> p c t", c=8)
                            act(lambda e: e.copy(out=hT[:, :, t0:t0 + n], in_=ptv[:, :, 0:n]), [bpt], [b_hT])
                        return [g0, g1, g2, g3]

                    for ti0 in range(0, NT, 2):
                        grp_ = [h_stages(ti, *tiles[ti]) for ti in range(ti0, min(NT, ti0 + 2))]
                        for si in range(4):
                            for stg in grp_:
                                stg[si]()

                    ck("hT")
                    P.dma("sp", EA[:, :, :], EscrA[0:8].rearrange("h p u -> p h u"), (), [b_EA])
                    P.dma("sp", EB[:, :, :], EscrA[8 + 4 * l:12 + 4 * l].rearrange("h p u -> p h u"), (), [b_EB])

                    def proj_fm(dst_fn, wt, bw, wcols, nchunks, rows=128):
                        for q0 in range(0, T, 512):
                            nq = min(512, T - q0)
                            for j0 in range(0, nchunks, 2):
                                pd, bpd = next_dbl()
                                nj = min(2, nchunks - j0)
                                for jj in range(nj):
                                    for c in range(8):
                                        pe(lambda e, jj=jj, c=c, pd=pd, j0=j0: e.matmul(pd[0:rows, jj * 512:jj * 512 + nq], lhsT=wt[:, c, wcols[j0 + jj]:wcols[j0 + jj] + rows], rhs=hT[:, c, q0:q0 + nq], start=(c == 0), stop=(c == 7)),
                                           [bw, b_hT], [bpd], inc=(c == 7 and jj == nj - 1))
                                for jj in range(nj):
                                    dst, bd = dst_fn(j0 + jj, q0, q0 + nq)
                                    act(lambda e, jj=jj, pd=pd, dst=dst: e.copy(out=dst, in_=pd[0:rows, jj * 512:jj * 512 + nq]), [bpd], [bd])

                    for ah in range(2):
                        with contextlib.ExitStack() as SA:
                            wA_ = sb(SA, [128, 8, 768], BF16, "wA"); b_wA_ = Buf()
                            wsrc = wl["w_in"].rearrange("(c p) n -> p c n", p=128)
                            for i3 in range(3):
                                P.dma("sp", wA_[:, :, i3 * 256:(i3 + 1) * 256], wsrc[:, :, i3 * 512 + ah * 256:i3 * 512 + ah * 256 + 256], [bW], [b_wA_])
                            ck("Aw")
                            QT = sb(SA, [128, 2, T], BF16, "QTA"); b_QT = Buf()
                            KT = sb(SA, [128, 2, NK], BF16, "KTA"); b_KT = Buf()
                            VA = sb(SA, [128, NKB, 4, 65], BF16, "VA"); b_VA = Buf()
                            dve(lambda e: e.memset(VA[:, :, :, :].rearrange("p a b c -> p (a b c)"), 1.0), (), [b_VA])
                            kv32 = [sb(SA, [128, 2, 256], F32, "kv32") for _ in range(2)]; bkv32 = [Buf(), Buf()]
                            if not prm:
                                ckb = [sb(SA, [128, 256], BF16, "ckb") for _ in range(2)]; bckb = [Buf(), Buf()]
                                for j in range(NKB_past):
                                    k = j % 2
                                    P.dma("pool", ckb[k][:, :], IA["cache_a_k"][l, j * 128:(j + 1) * 128, ah * 256:(ah + 1) * 256], (), [bckb[k]])
                                    P.dma("pool", VA[:, j, :, 0:64], IA["cache_a_v"][l, j * 128:(j + 1) * 128, ah * 256:(ah + 1) * 256].rearrange("p (h d) -> p h d", h=4), (), [b_VA])
                                    pt, bpt = next_ptr()
                                    for c in range(2):
                                        pe(lambda e, c=c, k=k, pt=pt: e.transpose(out=pt[:, c * 128:(c + 1) * 128], in_=ckb[k][:, c * 128:(c + 1) * 128], identity=ident[:, :]), [bckb[k], b_ident], [bpt], inc=(c == 1))
                                    ptv = pt[:, 0:256].rearrange("p (c t) -> p c t", c=2)
                                    act(lambda e, j=j, ptv=ptv: e.copy(out=KT[:, :, j * 128:(j + 1) * 128], in_=ptv), [bpt], [b_KT])
                            proj_fm(lambda j, c0, c1: (QT[:, j, c0:c1], b_QT), wA_, b_wA_, [0, 128], 2)
                            ck("Aq")
                            proj_fm(lambda j, c0, c1: (KT[:, j, past + c0:past + c1], b_KT), wA_, b_wA_, [256, 384], 2)
                            ck("Ak")
                            for ti, (t0, n) in enumerate(tiles):
                                pd, bpd = next_dbl()
                                for jj in range(2):
                                    for c in range(8):
                                        pe(lambda e, jj=jj, c=c, pd=pd, t0=t0, n=n: e.matmul(pd[0:n, jj * 512:jj * 512 + 256], lhsT=hT[:, c, t0:t0 + n], rhs=wA_[:, c, 256 + jj * 256:512 + jj * 256], start=(c == 0), stop=(c == 7)),
                                           [b_wA_, b_hT], [bpd], inc=(c == 7 and jj == 1))
                                k = ti % 2
                                pdv = pd[:, :].rearrange("p (a b) -> p a b", a=2)
                                if '1' not in DBG:
                                    act(lambda e, k=k, n=n, pdv=pdv: e.copy(out=kv32[k][0:n, :, :], in_=pdv[0:n, :, 0:256]), [bpd], [bkv32[k]])
                                if '2' not in DBG:
                                    dve(lambda e, ti=ti, n=n, pd=pd: e.tensor_copy(out=VA[0:n, NKB_past + ti, :, 0:64], in_=pd[0:n, 512:768].rearrange("p (h d) -> p h d", h=4)), [bpd], [b_VA])
                                if 'D' in DBG and ti == 0 and l == 0 and ah == 0:
                                    P.dma("pool", ydst[384:512, 0:256], kv32[k][:, 0, :], [bkv32[k]], [])
                                    P.dma("pool", ydst[512:640, 0:768], wA_[:, 0, :], [b_wA_], [])
                                if 'O' not in DBG:
                                    P.dma("pool", OA[pfx + "a_k"][l, so, t0:t0 + n, ah * 256:(ah + 1) * 256], kv32[k][0:n, 0, :], [bkv32[k]], [])
                                    P.dma("pool", OA[pfx + "a_v"][l, so, t0:t0 + n, ah * 256:(ah + 1) * 256], kv32[k][0:n, 1, :], [bkv32[k]], [])
                            ck("Aproj")
                            callsA = []
                            for hh in range(4):
                                h = ah * 4 + hh
                                c, r0 = hh // 2, (hh % 2) * 64
                                lanes = []
                                for half in range(2):
                                    rr = r0 + 32 * half
                                    lanes.append(dict(kt=lambda c0, nk, rr=rr, c=c: KT[rr:rr + 32, c, c0:c0 + nk], qt=lambda c0, c1, rr=rr, c=c: QT[rr:rr + 32, c, c0:c1],
                                                      v=lambda kbi, nk, hh=hh: VA[0:nk, kbi, hh, :], E=EA[:, h, :], Eb=b_EA, Ethr=-91, c0=384, tp=(rr, 0)))
                                for qb in qblocks:
                                    callsA.append(("A", lanes, qb, "causal", list(enumerate(kblocks)), [b_QT, b_KT, b_VA], epi_diff(h)))
                            attend_many(callsA)
                            flush_pending()
                            P.barrier()
                            ck("Ahalf")

                    with contextlib.ExitStack() as SB:
                        wB_ = sb(SB, [128, 8, 768], BF16, "wB"); b_wB_ = Buf()
                        wsrc = wl["w_in"].rearrange("(c p) n -> p c n", p=128)
                        P.dma("sp", wB_[:, :, :], wsrc[:, :, 1536:2304], [bW], [b_wB_])
                        pastB = 0 if prm else NBS
                        NKb = pastB + T
                        kbB = [(j * 128, past - pastB + j * 128, 128) for j in range(pastB // 128)] + [(pastB + t0, past + t0, n) for (t0, n) in tiles]
                        QT = sb(SB, [128, 2, T], BF16, "QTB"); b_QT = Buf()
                        KT = sb(SB, [128, 2, NKb], BF16, "KTB"); b_KT = Buf()
                        VB = sb(SB, [128, len(kbB), 4, 65], BF16, "VB"); b_VB = Buf()
                        dve(lambda e: e.memset(VB[:, :, :, :].rearrange("p a b c -> p (a b c)"), 1.0), (), [b_VB])
                        kv32 = [sb(SB, [128, 512], F32, "kv32b") for _ in range(2)]; bkv32 = [Buf(), Buf()]
                        if not prm:
                            ckb = [sb(SB, [128, 256], BF16, "ckbb") for _ in range(2)]; bckb = [Buf(), Buf()]
                            for j in range(pastB // 128):
                                k = j % 2
                                P.dma("pool", ckb[k][:, :], IA["cache_b_k"][l, j * 128:(j + 1) * 128, :], (), [bckb[k]])
                                P.dma("pool", VB[:, j, :, 0:64], IA["cache_b_v"][l, j * 128:(j + 1) * 128, :].rearrange("p (h d) -> p h d", h=4), (), [b_VB])
                                pt, bpt = next_ptr()
                                for c in range(2):
                                    pe(lambda e, c=c, k=k, pt=pt: e.transpose(out=pt[:, c * 128:(c + 1) * 128], in_=ckb[k][:, c * 128:(c + 1) * 128], identity=ident[:, :]), [bckb[k], b_ident], [bpt], inc=(c == 1))
                                ptv = pt[:, 0:256].rearrange("p (c t) -> p c t", c=2)
                                act(lambda e, j=j, ptv=ptv: e.copy(out=KT[:, :, j * 128:(j + 1) * 128], in_=ptv), [bpt], [b_KT])
                        proj_fm(lambda j, c0, c1: (QT[:, j, c0:c1], b_QT), wB_, b_wB_, [0, 128], 2)
                        proj_fm(lambda j, c0, c1: (KT[:, j, pastB + c0:pastB + c1], b_KT), wB_, b_wB_, [256, 384], 2)
                        for ti, (t0, n) in enumerate(tiles):
                            pd, bpd = next_dbl()
                            for c in range(8):
                                pe(lambda e, c=c, pd=pd, t0=t0, n=n: e.matmul(pd[0:n, 0:512], lhsT=hT[:, c, t0:t0 + n], rhs=wB_[:, c, 256:768], start=(c == 0), stop=(c == 7)), [b_wB_, b_hT], [bpd], inc=(c == 7))
                            k = ti % 2
                            act(lambda e, k=k, n=n, pd=pd: e.copy(out=kv32[k][0:n, :], in_=pd[0:n, 0:512]), [bpd], [bkv32[k]])
                            dve(lambda e, ti=ti, n=n, pd=pd: e.tensor_copy(out=VB[0:n, pastB // 128 + ti, :, 0:64], in_=pd[0:n, 256:512].rearrange("p (h d) -> p h d", h=4)), [bpd], [b_VB])
                            if prm:
                                if t0 >= SEQ - NBP:
                                    r0_ = t0 - (SEQ - NBP)
                                    P.dma("pool", OA["p_b_k"][l, so, r0_:r0_ + n, :], kv32[k][0:n, 0:256], [bkv32[k]], [])
                                    P.dma("pool", OA["p_b_v"][l, so, r0_:r0_ + n, :], kv32[k][0:n, 256:512], [bkv32[k]], [])
                            else:
                                P.dma("pool", OA["s_b_k"][l, 0, t0:t0 + n, :], kv32[k][0:n, 0:256], [bkv32[k]], [])
                                P.dma("pool", OA["s_b_v"][l, 0, t0:t0 + n, :], kv32[k][0:n, 256:512], [bkv32[k]], [])
                        callsB = []
                        for hp in range(2):
                            lanes = []
                            for li in range(2):
                                h = hp * 2 + li
                                r0 = li * 64
                                lanes.append(dict(kt=lambda c0, nk, r0=r0, hp=hp: KT[r0:r0 + 64, hp, c0:c0 + nk], qt=lambda c0, c1, r0=r0, hp=hp: QT[r0:r0 + 64, hp, c0:c1],
                                                  v=lambda kbi, nk, h=h: VB[0:nk, kbi, h, :], E=EB[:, h, :], Eb=b_EB, Ethr=-128, c0=384, tp=None))
                            for qb in qblocks:
                                callsB.append(("B", lanes, qb, "band", list(enumerate(kbB)), [b_QT, b_KT, b_VB], epi_plain([512 + (hp * 2) * 64, 512 + (hp * 2 + 1) * 64])))
                        attend_many(callsB)
                        flush_pending()
                        P.barrier()

                    ck("B")
                    with contextlib.ExitStack() as SC:
                        wC_ = sb(SC, [128, 8, 416], BF16, "wC"); b_wC_ = Buf()
                        wsrc = wl["w_in"].rearrange("(c p) n -> p c n", p=128)
                        P.dma("sp", wC_[:, :, :], wsrc[:, :, 2304:2720], [bW], [b_wC_])
                        wuq = sb(SC, [128, 2, 384], BF16, "wuq"); b_wuq = Buf()
                        P.dma("sp", wuq[:, :, :], wl["mla_w_uq"].rearrange("(c p) n -> p c n", p=128), [bW], [b_wuq])
                        wukv = sb(SC, [128, 512], BF16, "wukv"); b_wukv = Buf()
                        P.dma("sp", wukv[:, :], wl["mla_w_ukv"], [bW], [b_wukv])
                        gq = sb(SC, [128, 256], F32, "gq"); b_gq = Buf()
                        gkv = sb(SC, [128, 128], F32, "gkv"); b_gkv = Buf()
                        bcast_row(gq[:], I["mla_q_norm"], l * 256, 256, [b_gq])
                        bcast_row(gkv[:], I["mla_kv_norm"], l * 128, 128, [b_gkv])
                        cqnT = sb(SC, [128, 2, T], BF16, "cqnT"); b_cqnT = Buf()
                        latT = sb(SC, [128, NK], BF16, "latT"); b_latT = Buf()
                        kT = sb(SC, [96, 4, NK], BF16, "kTC"); b_kT = Buf()
                        qT = sb(SC, [96, 4, T], BF16, "qTC"); b_qT = Buf()
                        VC = sb(SC, [128, NKB, 4, 65], BF16, "VC"); b_VC = Buf()
                        dve(lambda e: e.memset(VC[:, :, :, :].rearrange("p a b c -> p (a b c)"), 1.0), (), [b_VC])
                        rp = sb(SC, [128, NT, 128], F32, "rope"); b_rp = Buf()
                        for ti, (t0, n) in enumerate(tiles):
                            P.dma("sp", rp[0:n, ti, :], rope_h.ap()[t0:t0 + n, :], (), [b_rp])
                        c32 = [sb(SC, [128, 416], F32, "c32") for _ in range(2)]; bc32 = [Buf(), Buf()]
                        ssc = [sb(SC, [128, 2], F32, "ssc") for _ in range(2)]; bssc = [Buf(strict=True), Buf(strict=True)]
                        junkc = sb(SC, [128, 384], BF16, "junkc"); b_junkc = Buf()
                        cqn = [sb(SC, [128, 256], BF16, "cqn") for _ in range(2)]; bcqn = [Buf(), Buf()]
                        lat32 = [sb(SC, [128, 128], F32, "lat32") for _ in range(2)]; blat32 = [Buf(), Buf()]
                        latb = [sb(SC, [128, 160], BF16, "latb") for _ in range(2)]; blatb = [Buf(), Buf()]
                        kpe32 = [sb(SC, [128, 32], F32, "kpe32") for _ in range(2)]; bkpe32 = [Buf(), Buf()]
                        rt = sb(SC, [128, 4, 4, 16], F32, "rt"); b_rt = Buf()
                        q32 = [sb(SC, [128, 4, 96], F32, "q32") for _ in range(2)]; bq32 = [Buf(), Buf()]
                        qb16 = [sb(SC, [128, 4, 96], BF16, "qb16") for _ in range(2)]; bqb16 = [Buf(), Buf()]

                        def rope_apply(dst1, dst2, x1, x2, cs_, sn_, shape_n, rbufs_, wbufs_):
                            nh = shape_n
                            dve(lambda e: e.tensor_tensor(out=rt[0:nh[0], 0, 0:nh[1], :], in0=x1, in1=cs_, op=ALU.mult), rbufs_, [b_rt])
                            dve(lambda e: e.tensor_tensor(out=rt[0:nh[0], 1, 0:nh[1], :], in0=x2, in1=sn_, op=ALU.mult), rbufs_, [b_rt])
                            dve(lambda e: e.tensor_tensor(out=rt[0:nh[0], 2, 0:nh[1], :], in0=x2, in1=cs_, op=ALU.mult), rbufs_, [b_rt])
                            dve(lambda e: e.tensor_tensor(out=rt[0:nh[0], 3, 0:nh[1], :], in0=x1, in1=sn_, op=ALU.mult), rbufs_, [b_rt])
                            dve(lambda e: e.tensor_tensor(out=dst1, in0=rt[0:nh[0], 0, 0:nh[1], :], in1=rt[0:nh[0], 1, 0:nh[1], :], op=ALU.subtract), [b_rt], wbufs_)
                            dve(lambda e: e.tensor_tensor(out=dst2, in0=rt[0:nh[0], 2, 0:nh[1], :], in1=rt[0:nh[0], 3, 0:nh[1], :], op=ALU.add), [b_rt], wbufs_)

                        if not prm:
                            for j in range(NKB_past):
                                k = j % 2
                                P.dma("pool", latb[k][:, 0:128], IA["cache_c_latent"][l, j * 128:(j + 1) * 128, :], (), [blatb[k]])
                                P.dma("pool", latb[k][:, 128:160], IA["cache_c_rope_k"][l, j * 128:(j + 1) * 128, :], (), [blatb[k]])
                                pt, bpt = next_ptr()
                                pe(lambda e, k=k, pt=pt: e.transpose(out=pt[:, 0:128], in_=latb[k][:, 0:128], identity=ident[:, :]), [blatb[k], b_ident], [bpt], inc=False)
                                pe(lambda e, k=k, pt=pt: e.transpose(out=pt[0:32, 128:256], in_=latb[k][:, 128:160], identity=ident[:, :]), [blatb[k], b_ident], [bpt])
                                act(lambda e, j=j, pt=pt: e.copy(out=latT[:, j * 128:(j + 1) * 128], in_=pt[:, 0:128]), [bpt], [b_latT])
                                for h in range(4):
                                    dve(lambda e, j=j, h=h, pt=pt: e.tensor_copy(out=kT[64:96, h, j * 128:(j + 1) * 128], in_=pt[0:32, 128:256]), [bpt], [b_kT])
                        rt2 = [rt, sb(SC, [128, 4, 4, 16], F32, "rtb")]; b_rt2 = [b_rt, Buf()]
                        junkc2 = [junkc, junkc]; b_junkc2 = [b_junkc, b_junkc]

                        def rope2(k, dst1, dst2, x1, x2, cs_, sn_, nh, rbufs_, wbufs_):
                            R_, bR = rt2[k], b_rt2[k]
                            dve(lambda e: e.tensor_tensor(out=R_[0:nh[0], 0, 0:nh[1], :], in0=x1, in1=cs_, op=ALU.mult), rbufs_, [bR])
                            dve(lambda e: e.tensor_tensor(out=R_[0:nh[0], 1, 0:nh[1], :], in0=x2, in1=sn_, op=ALU.mult), rbufs_, [bR])
                            dve(lambda e: e.tensor_tensor(out=R_[0:nh[0], 2, 0:nh[1], :], in0=x2, in1=cs_, op=ALU.mult), rbufs_, [bR])
                            dve(lambda e: e.tensor_tensor(out=R_[0:nh[0], 3, 0:nh[1], :], in0=x1, in1=sn_, op=ALU.mult), rbufs_, [bR])
                            dve(lambda e: e.tensor_tensor(out=dst1, in0=R_[0:nh[0], 0, 0:nh[1], :], in1=R_[0:nh[0], 1, 0:nh[1], :], op=ALU.subtract), [bR], wbufs_)
                            dve(lambda e: e.tensor_tensor(out=dst2, in0=R_[0:nh[0], 2, 0:nh[1], :], in1=R_[0:nh[0], 3, 0:nh[1], :], op=ALU.add), [bR], wbufs_)

                        def c_stages(ti, t0, n):
                            k = ti % 2
                            st = {}

                            def g0():
                                pd, bpd = next_dbl()
                                for c in range(8):
                                    pe(lambda e, c=c: e.matmul(pd[0:n, 0:416], lhsT=hT[:, c, t0:t0 + n], rhs=wC_[:, c, :], start=(c == 0), stop=(c == 7)), [b_wC_, b_hT], [bpd], inc=(c == 7))
                                act(lambda e: e.copy(out=c32[k][0:n, :], in_=pd[0:n, 0:416]), [bpd], [bc32[k]])

                            def g1():
                                dve(lambda e: e.scalar_tensor_tensor(out=junkc2[k][0:n, 0:256], in0=c32[k][0:n, 0:256], scalar=1.0 / 256, in1=c32[k][0:n, 0:256], op0=ALU.mult, op1=ALU.mult, accum_out=ssc[k][0:n, 0:1]), [bc32[k]], [b_junkc2[k], bssc[k]])
                                dve(lambda e: e.scalar_tensor_tensor(out=junkc2[k][0:n, 0:128], in0=c32[k][0:n, 256:384], scalar=1.0 / 128, in1=c32[k][0:n, 256:384], op0=ALU.mult, op1=ALU.mult, accum_out=ssc[k][0:n, 1:2]), [bc32[k]], [b_junkc2[k], bssc[k]])
                                rstd_of(ssc[k][0:n, 0:2], n, 1, bssc[k])
                                rope2(k, kpe32[k][0:n, 0:16].rearrange("p (a d) -> p a d", a=1), kpe32[k][0:n, 16:32].rearrange("p (a d) -> p a d", a=1),
                                      c32[k][0:n, 384:400].rearrange("p (a d) -> p a d", a=1), c32[k][0:n, 400:416].rearrange("p (a d) -> p a d", a=1),
                                      rp[0:n, ti, 0:16].rearrange("p (a d) -> p a d", a=1), rp[0:n, ti, 64:80].rearrange("p (a d) -> p a d", a=1), (n, 1), [bc32[k], b_rp], [bkpe32[k]])
                                P.dma("pool", OA[pfx + "kpe"][l, so, t0:t0 + n, :], kpe32[k][0:n, :], [bkpe32[k]], [])
                                dve(lambda e: e.tensor_copy(out=latb[k][0:n, 128:160], in_=kpe32[k][0:n, :]), [bkpe32[k]], [blatb[k]])

                            def g2():
                                dve(lambda e: e.scalar_tensor_tensor(out=cqn[k][0:n, :], in0=c32[k][0:n, 0:256], scalar=ssc[k][0:n, 0:1], in1=gq[0:n, :], op0=ALU.mult, op1=ALU.mult), [bc32[k], bssc[k], b_gq], [bcqn[k]])
                                dve(lambda e: e.scalar_tensor_tensor(out=lat32[k][0:n, :], in0=c32[k][0:n, 256:384], scalar=ssc[k][0:n, 1:2], in1=gkv[0:n, :], op0=ALU.mult, op1=ALU.mult), [bc32[k], bssc[k], b_gkv], [blat32[k]])
                                P.dma("pool", OA[pfx + "lat"][l, so, t0:t0 + n, :], lat32[k][0:n, :], [blat32[k]], [])
                                dve(lambda e: e.tensor_copy(out=latb[k][0:n, 0:128], in_=lat32[k][0:n, :]), [blat32[k]], [blatb[k]])

                            def g3():
                                pt, bpt = next_ptr()
                                for c in range(2):
                                    pe(lambda e, c=c: e.transpose(out=pt[:, c * 128:c * 128 + n], in_=cqn[k][0:n, c * 128:(c + 1) * 128], identity=ident[0:n, 0:n]), [bcqn[k], b_ident], [bpt], inc=False)
                                pe(lambda e: e.transpose(out=pt[:, 256:256 + n], in_=latb[k][0:n, 0:128], identity=ident[0:n, 0:n]), [blatb[k], b_ident], [bpt], inc=False)
                                pe(lambda e: e.transpose(out=pt[0:32, 384:384 + n], in_=latb[k][0:n, 128:160], identity=ident[0:n, 0:n]), [blatb[k], b_ident], [bpt])
                                ptv = pt[:, 0:256].rearrange("p (c t) -> p c t", c=2)
                                act(lambda e: e.copy(out=cqnT[:, :, t0:t0 + n], in_=ptv[:, :, 0:n]), [bpt], [b_cqnT])
                                act(lambda e: e.copy(out=latT[:, past + t0:past + t0 + n], in_=pt[:, 256:256 + n]), [bpt], [b_latT])
                                for h in range(4):
                                    dve(lambda e, h=h: e.tensor_copy(out=kT[64:96, h, past + t0:past + t0 + n], in_=pt[0:32, 384:384 + n]), [bpt], [b_kT])

                            def g4():
                                pd, bpd = next_dbl()
                                for c in range(2):
                                    pe(lambda e, c=c: e.matmul(pd[0:n, 0:384], lhsT=cqnT[:, c, t0:t0 + n], rhs=wuq[:, c, :], start=(c == 0), stop=(c == 1)), [b_cqnT, b_wuq], [bpd], inc=(c == 1))
                                act(lambda e: e.copy(out=q32[k][0:n, :, :], in_=pd[0:n, 0:384].rearrange("p (h d) -> p h d", h=4)), [bpd], [bq32[k]])

                            def g5():
                                dve(lambda e: e.tensor_copy(out=qb16[k][0:n, :, 0:64], in_=q32[k][0:n, :, 0:64]), [bq32[k]], [bqb16[k]])
                                rope2(k, qb16[k][0:n, :, 64:80], qb16[k][0:n, :, 80:96], q32[k][0:n, :, 64:80], q32[k][0:n, :, 80:96],
                                      rp[0:n, ti, 0:64].rearrange("p (h d) -> p h d", h=4), rp[0:n, ti, 64:128].rearrange("p (h d) -> p h d", h=4), (n, 4), [bq32[k], b_rp], [bqb16[k]])

                            def g6():
                                pt, bpt = next_ptr()
                                for h in range(4):
                                    pe(lambda e, h=h: e.transpose(out=pt[0:96, h * 128:h * 128 + n], in_=qb16[k][0:n, h, :], identity=ident[0:n, 0:n]), [bqb16[k], b_ident], [bpt], inc=(h == 3))
                                ptv = pt[:, 0:512].rearrange("p (c t) -> p c t", c=4)
                                act(lambda e: e.copy(out=qT[:, :, t0:t0 + n], in_=ptv[0:96, :, 0:n]), [bpt], [b_qT])
                            return [g0, g1, g2, g3, g4, g5, g6]

                        for ti0 in range(0, NT, 2):
                            grp_ = [c_stages(ti, *tiles[ti]) for ti in range(ti0, min(NT, ti0 + 2))]
                            for si in range(7):
                                for stg in grp_:
                                    stg[si]()
                        for c0 in range(0, NK, 512):
                            nn_ = min(512, NK - c0)
                            for hp in range(2):
                                pd, bpd = next_dbl()
                                for jj in range(2):
                                    h = hp * 2 + jj
                                    pe(lambda e, jj=jj, h=h, pd=pd, c0=c0, nn_=nn_: e.matmul(pd[0:64, jj * 512:jj * 512 + nn_], lhsT=wukv[:, h * 128:h * 128 + 64], rhs=latT[:, c0:c0 + nn_], start=True, stop=True), [b_wukv, b_latT], [bpd], inc=(jj == 1))
                                pdv = pd[:, :].rearrange("p (a b) -> p a b", a=2)
                                act(lambda e, hp=hp, pdv=pdv, c0=c0, nn_=nn_: e.copy(out=kT[0:64, hp * 2:hp * 2 + 2, c0:c0 + nn_], in_=pdv[0:64, :, 0:nn_]), [bpd], [b_kT])
                        wv_ = wukv[:, :].rearrange("p (h d) -> p h d", h=4)
                        for kbi, (kc0, kpos, nk) in enumerate(kblocks):
                            pd, bpd = next_dbl()
                            pe(lambda e, pd=pd, kc0=kc0, nk=nk: e.matmul(pd[0:nk, 0:256].rearrange("p (h d) -> p h d", h=4), lhsT=latT[:, kc0:kc0 + nk], rhs=wv_[:, :, 64:128], start=True, stop=True), [b_wukv, b_latT], [bpd])
                            act(lambda e, kbi=kbi, nk=nk, pd=pd: e.copy(out=VC[0:nk, kbi, :, 0:64], in_=pd[0:nk, 0:256].rearrange("p (h d) -> p h d", h=4)), [bpd], [b_VC])
                        if 'Q' in DBG and l == 0:
                            for hq in range(4):
                                P.dma("pool", ydst[0:96, hq * 128:(hq + 1) * 128], qT[:, hq, 0:128], [b_qT], [])
                                P.dma("pool", ydst[128:224, hq * 128:(hq + 1) * 128], kT[:, hq, 0:128], [b_kT], [])
                            P.dma("pool", ydst[256:384, 0:260], VC[:, 0, :, :].rearrange("p a b -> p (a b)"), [b_VC], [])
                            P.dma("pool", ydst[384:512, 0:128], latT[:, 0:128], [b_latT], [])
                        callsC = []
                        for hp in range(2):
                            lanes = []
                            for li in range(2):
                                h = hp * 2 + li
                                lanes.append(dict(kt=lambda c0, nk, h=h: kT[0:96, h, c0:c0 + nk], qt=lambda c0, c1, h=h: qT[0:96, h, c0:c1], v=lambda kbi, nk, h=h: VC[0:nk, kbi, h, :], E=None, tp=None))
                            for qb in qblocks:
                                callsC.append(("C", lanes, qb, "causal", list(enumerate(kblocks)), [b_qT, b_kT, b_VC], epi_plain([768 + (hp * 2) * 64, 768 + (hp * 2 + 1) * 64])))
                        attend_many(callsC)
                        flush_pending()
                        P.barrier()
                    with contextlib.ExitStack() as SO:
                        wout = sb(SO, [128, 8, D], BF16, "wout"); b_wout = Buf()
                        P.dma("sp", wout[:, :, :], wl["w_out"].rearrange("(c p) n -> p c n", p=128), [bW], [b_wout])
                        gmp = sb(SO, [128, D], F32, "gmp"); b_gmp = Buf()
                        bcast_row(gmp[:], I["norms"], (l * 7 + 1) * D, D, [b_gmp])
                        xo_ = [sb(SO, [128, D], F32, "xo") for _ in range(2)]; bxo = [Buf(), Buf()]
                        yo_ = [sb(SO, [128, D], F32, "yo") for _ in range(2)]; byo = [Buf(), Buf()]
                        aT = [sb(SO, [128, 8, 128], BF16, "aT") for _ in range(2)]; baT = [Buf(), Buf()]
                        junko = sb(SO, [128, D], BF16, "junko"); b_junko = Buf()
                        sso = sb(SO, [128, 2], F32, "sso"); b_sso = Buf(strict=True)
                        b_sso2 = [Buf(strict=True), Buf(strict=True)]

                        def wo_xload(ti):
                            (t0, n) = tiles[ti]
                            k = ti % 2
                            P.dma("sp", xo_[k][0:n, :], xsrc[t0:t0 + n, :], [ybufs[ti]], [bxo[k]])

                        def wo_prep(ti):
                            (t0, n) = tiles[ti]
                            k = ti % 2
                            pt, bpt = next_ptr()
                            for c in range(8):
                                pe(lambda e, c=c: e.transpose(out=pt[:, c * 128:c * 128 + n], in_=a_tok[0:n, ti, c * 128:(c + 1) * 128], identity=ident[0:n, 0:n]), [b_atok, b_ident], [bpt], inc=(c == 7))
                            ptv = pt[:, :].rearrange("p (c t) -> p c t", c=8)
                            act(lambda e: e.copy(out=aT[k][:, :, 0:n], in_=ptv[:, :, 0:n]), [bpt], [baT[k]])

                        for ti in range(min(2, NT)):
                            wo_xload(ti)
                            wo_prep(ti)
                        for ti, (t0, n) in enumerate(tiles):
                            k = ti % 2
                            pd, bpd = next_dbl()
                            for hf in range(2):
                                for c in range(8):
                                    pe(lambda e, hf=hf, c=c, n=n, k=k, pd=pd: e.matmul(pd[0:n, hf * 512:(hf + 1) * 512], lhsT=aT[k][:, c, 0:n], rhs=wout[:, c, hf * 512:(hf + 1) * 512], start=(c == 0), stop=(c == 7)), [baT[k], b_wout], [bpd], inc=(c == 7 and hf == 1))
                            act(lambda e, k=k, n=n, pd=pd: e.copy(out=yo_[k][0:n, :], in_=pd[0:n, :]), [bpd], [byo[k]])
                            if ti + 2 < NT:
                                wo_prep(ti + 2)
                            dve(lambda e, k=k, n=n: e.scalar_tensor_tensor(out=junko[0:n, :], in0=yo_[k][0:n, :], scalar=1.0, in1=yo_[k][0:n, :], op0=ALU.mult, op1=ALU.mult, accum_out=sso[0:n, k:k + 1]), [byo[k]], [b_junko, b_sso2[k]])
                            rstd_of(sso[0:n, k:k + 1], n, D, b_sso2[k])
                            dve(lambda e, k=k, n=n: e.scalar_tensor_tensor(out=yo_[k][0:n, :], in0=yo_[k][0:n, :], scalar=sso[0:n, k:k + 1], in1=gmp[0:n, :], op0=ALU.mult, op1=ALU.mult), [byo[k], b_sso2[k], b_gmp], [byo[k]])
                            P.op("pool", lambda e, k=k, n=n: e.tensor_tensor(out=xo_[k][0:n, :], in0=xo_[k][0:n, :], in1=yo_[k][0:n, :], op=ALU.add), [byo[k], bxo[k]], [bxo[k]])
                            P.dma("pool", ydst[t0:t0 + n, :], xo_[k][0:n, :], [bxo[k]], [ybufs[ti]])
                            if ti + 2 < NT:
                                wo_xload(ti + 2)
                        P.barrier()
                    P.barrier()

                ck("C")
                with contextlib.ExitStack() as S2:
                    wxq = sb(S2, [128, 8, 256], BF16, "wxq"); b_wxq = Buf()
                    wxo = sb(S2, [128, 2, D], BF16, "wxo"); b_wxo = Buf()
                    gb = sb(S2, [128, 4, D], F32, "gb"); b_gb = Buf()
                    cw = sb(S2, [128, NFC, 4], F32, "cw"); b_cw = Buf()
                    mkT = sb(S2, [128, 2, MEM], BF16, "mkT"); b_mkT = Buf()
                    MV = sb(S2, [128, 2, 4, 65], BF16, "MV"); b_MV = Buf()
                    dve(lambda e: e.memset(MV[:, :, :, :].rearrange("p a b c -> p (a b c)"), 1.0), (), [b_MV])
                    wd_all = sb(S2, [128, NFC, D], BF16, "wd_all"); b_wd = Buf()
                    wds = wl["w_down"].rearrange("(f p) n -> p f n", p=128)
                    junkb = sb(S2, [128, D], BF16, "junkb2"); b_junkb = Buf()
                    ss4 = sb(S2, [128, 4], F32, "ss4"); b_ss4 = Buf(strict=True)
                    hb4 = sb(S2, [128, 4, D], BF16, "hb4"); b_hb4 = [Buf() for _ in range(4)]
                    if prm:
                        with contextlib.ExitStack() as SM:
                            wmk = sb(SM, [128, 8, 512], BF16, "wmk"); b_wmk = Buf()
                            gm = sb(SM, [128, D], F32, "gm"); b_gm = Buf()
                            mT = sb(SM, [128, 8, MEM], BF16, "mT"); b_mT = Buf()
                            m32 = sb(SM, [128, 512], F32, "m32"); b_m32 = Buf()
                            xm = [sb(SM, [128, D], F32, "xm") for _ in range(2)]; bxm = [Buf(), Buf()]
                            for mi in range(2):
                                P.dma("sp", xm[mi][:, :], IA["mem_prompt"][s, mi * 128:(mi + 1) * 128, :], (), [bxm[mi]])
                            bcast_row(gm[:], I["norms"], (l * 7 + 6) * D, D, [b_gm])
                            P.dma("sp", wmk[:, :, 0:256], wl["w_mk"].rearrange("(c p) n -> p c n", p=128), [bW], [b_wmk])
                            P.dma("sp", wmk[:, :, 256:512], wl["w_mv"].rearrange("(c p) n -> p c n", p=128), [bW], [b_wmk])
                            for mi in range(2):
                                k = mi % 2
                                dve(lambda e, k=k, mi=mi: e.scalar_tensor_tensor(out=junkb[:, :], in0=xm[k][:, :], scalar=1.0, in1=xm[k][:, :], op0=ALU.mult, op1=ALU.mult, accum_out=ss4[:, mi:mi + 1]), [bxm[k]], [b_junkb, b_ss4])
                            rstd_of(ss4[:, 0:2], 128, D, b_ss4)
                            for mi in range(2):
                                k = mi % 2
                                dve(lambda e, k=k, mi=mi: e.scalar_tensor_tensor(out=hb4[:, mi, :], in0=xm[k][:, :], scalar=ss4[:, mi:mi + 1], in1=gm[:, :], op0=ALU.mult, op1=ALU.mult), [bxm[k], b_ss4, b_gm], [b_hb4[mi]])
                                pt, bpt = next_ptr()
                                for c in range(8):
                                    pe(lambda e, c=c, pt=pt, mi=mi: e.transpose(out=pt[:, c * 128:(c + 1) * 128], in_=hb4[:, mi, c * 128:(c + 1) * 128], identity=ident[:, :]), [b_hb4[mi], b_ident], [bpt], inc=(c == 7))
                                ptv = pt[:, :].rearrange("p (c t) -> p c t", c=8)
                                act(lambda e, mi=mi, ptv=ptv: e.copy(out=mT[:, :, mi * 128:(mi + 1) * 128], in_=ptv), [bpt], [b_mT])
                            for mi in range(2):
                                pd, bpd = next_dbl()
                                for c in range(8):
                                    pe(lambda e, c=c, pd=pd, mi=mi: e.matmul(pd[:, 0:512], lhsT=mT[:, c, mi * 128:(mi + 1) * 128], rhs=wmk[:, c, :], start=(c == 0), stop=(c == 7)), [b_mT, b_wmk], [bpd], inc=(c == 7))
                                act(lambda e, pd=pd: e.copy(out=m32[:, :], in_=pd[:, 0:512]), [bpd], [b_m32])
                                dve(lambda e, mi=mi, pd=pd: e.tensor_copy(out=MV[:, mi, :, 0:64], in_=pd[:, 256:512].rearrange("p (h d) -> p h d", h=4)), [bpd], [b_MV])
                                P.dma("pool", OA["p_mk"][l, s, mi * 128:(mi + 1) * 128, :], m32[:, 0:256], [b_m32], [])
                                P.dma("pool", OA["p_mv"][l, s, mi * 128:(mi + 1) * 128, :], m32[:, 256:512], [b_m32], [])
                            pd, bpd = next_dbl()
                            for j in range(2):
                                for c in range(8):
                                    pe(lambda e, j=j, c=c, pd=pd: e.matmul(pd[:, j * 512:j * 512 + MEM], lhsT=wmk[:, c, j * 128:(j + 1) * 128], rhs=mT[:, c, :], start=(c == 0), stop=(c == 7)), [b_mT, b_wmk], [bpd], inc=(c == 7 and j == 1))
                            pdv = pd[:, :].rearrange("p (a b) -> p a b", a=2)
                            act(lambda e, pdv=pdv: e.copy(out=mkT[:, :, :], in_=pdv[:, :, 0:MEM]), [bpd], [b_mkT])
                            P.barrier()
                    else:
                        with contextlib.ExitStack() as SM:
                            ckb = sb(SM, [128, 256], BF16, "ckbm"); bckb = Buf()
                            for mi in range(2):
                                P.dma("pool", ckb[:, :], IA["cache_mem_k"][l, mi * 128:(mi + 1) * 128, :], (), [bckb])
                                P.dma("pool", MV[:, mi, :, 0:64], IA["cache_mem_v"][l, mi * 128:(mi + 1) * 128, :].rearrange("p (h d) -> p h d", h=4), (), [b_MV])
                                pt, bpt = next_ptr()
                                for c in range(2):
                                    pe(lambda e, c=c, pt=pt: e.transpose(out=pt[:, c * 128:(c + 1) * 128], in_=ckb[:, c * 128:(c + 1) * 128], identity=ident[:, :]), [bckb, b_ident], [bpt], inc=(c == 1))
                                ptv = pt[:, 0:256].rearrange("p (c t) -> p c t", c=2)
                                act(lambda e, mi=mi, ptv=ptv: e.copy(out=mkT[:, :, mi * 128:(mi + 1) * 128], in_=ptv), [bpt], [b_mkT])
                            P.barrier()
                    ck("mem")
                    P.dma("sp", wxq[:, :, :], wl["w_xq"].rearrange("(c p) n -> p c n", p=128), [bW], [b_wxq])
                    P.dma("sp", wxo[:, :, :], wl["w_xo"].rearrange("(c p) n -> p c n", p=128), [bW], [b_wxo])
                    for gi, ni in enumerate((2, 3, 4, 5)):
                        bcast_row(gb[:, gi, :], I["norms"], (l * 7 + ni) * D, D, [b_gb])
                    ysb = [sb(S2, [128, D], F32, "ysb") for _ in range(2)]; b_ysb = [Buf(), Buf()]
                    yrot = [0]
                    xa = sb(S2, [128, 4, 256], BF16, "xa"); b_xa = Buf()
                    xaT = sb(S2, [128, 2, 512], BF16, "xaT"); b_xaT = Buf()
                    xblk2 = [sb(S2, [128, 4, D], F32, "xblk") for _ in range(2)]
                    b_xb2 = [[Buf() for _ in range(4)] for _ in range(2)]
                    hT2 = sb(S2, [128, 8, 512], BF16, "hT2"); b_hT2 = Buf()
                    qxT = sb(S2, [128, 2, 512], BF16, "qxT"); b_qxT = Buf()
                    gs = [sb(S2, [128, 514], F32, "gs") for _ in range(2)]; b_gs = [Buf(), Buf()]
                    halo = sb(S2, [128, NFC, 2], F32, "halo"); b_halo = Buf()
                    cc = [sb(S2, [128, 512], F32, "cc") for _ in range(2)]; b_cc = [Buf(), Buf()]
                    sl = [sb(S2, [128, 512], F32, "sl") for _ in range(2)]; b_sl = [Buf(), Buf()]
                    aTf = sb(S2, [128, NFC, 512], BF16, "aTf"); b_aTf = Buf()
                    wg = [sb(S2, [128, 2, 8, 128], BF16, "wg") for _ in range(3)]; bwg = [Buf() for _ in range(3)]
                    ssp = sb(S2, [128, 4], F32, "ssp"); b_ssp = Buf(strict=True)
                    if 'M' in DBG:
                        print("PH2 sbuf remaining", nc.sbuf_bytes_remaining)

                    def post_residual(pd, bpd, n, t, gidx, xblk, b_xb):
                        k = yrot[0]; yrot[0] = 1 - k
                        Y, bY = ysb[k], b_ysb[k]
                        act(lambda e: e.copy(out=Y[0:n, :], in_=pd[0:n, :]), [bpd], [bY])
                        dve(lambda e: e.scalar_tensor_tensor(out=junkb[0:n, :], in0=Y[0:n, :], scalar=1.0, in1=Y[0:n, :], op0=ALU.mult, op1=ALU.mult, accum_out=ssp[0:n, t:t + 1]), [bY], [b_junkb, b_ssp])
                        rstd_of(ssp[0:n, t:t + 1], n, D, b_ssp)
                        dve(lambda e: e.scalar_tensor_tensor(out=Y[0:n, :], in0=Y[0:n, :], scalar=ssp[0:n, t:t + 1], in1=gb[0:n, gidx, :], op0=ALU.mult, op1=ALU.mult), [bY, b_ssp, b_gb], [bY])
                        P.op("pool", lambda e: e.tensor_tensor(out=xblk[0:n, t, :], in0=xblk[0:n, t, :], in1=Y[0:n, :], op=ALU.add), [bY, b_xb[t]], [b_xb[t]])

                    def pre_norm_dve(subt, gidx, xblk, b_xb):
                        nt = len(subt)
                        nn = subt[0][1]
                        for t, (a0, n) in enumerate(subt):
                            dve(lambda e, t=t, n=n: e.scalar_tensor_tensor(out=junkb[0:n, :], in0=xblk[0:n, t, :], scalar=1.0, in1=xblk[0:n, t, :], op0=ALU.mult, op1=ALU.mult, accum_out=ss4[0:n, t:t + 1]), [b_xb[t]], [b_junkb, b_ss4])
                        rstd_of(ss4[0:nn, 0:nt], nn, D, b_ss4)
                        for t, (a0, n) in enumerate(subt):
                            dve(lambda e, t=t, n=n: e.scalar_tensor_tensor(out=hb4[0:n, t, :], in0=xblk[0:n, t, :], scalar=ss4[0:n, t:t + 1], in1=gb[0:n, gidx, :], op0=ALU.mult, op1=ALU.mult), [b_xb[t], b_ss4, b_gb], [b_hb4[t]])

                    def pre_norm_pe(subt):
                        for t, (a0, n) in enumerate(subt):
                            pt, bpt = next_ptr()
                            for c in range(8):
                                pe(lambda e, c=c, pt=pt, t=t, n=n: e.transpose(out=pt[:, c * 128:c * 128 + n], in_=hb4[0:n, t, c * 128:(c + 1) * 128], identity=ident[0:n, 0:n]), [b_hb4[t], b_ident], [bpt], inc=(c == 7))
                            ptv = pt[:, :].rearrange("p (c t) -> p c t", c=8)
                            act(lambda e, ptv=ptv, a0=a0, n=n: e.copy(out=hT2[:, :, a0:a0 + n], in_=ptv[:, :, 0:n]), [bpt], [b_hT2])

                    def head_dve(bi):
                        (qc0_, qpos_, nq_) = qblocks[bi]
                        subt_ = [(a0, min(128, nq_ - a0)) for a0 in range(0, nq_, 128)]
                        X_, bX_ = xblk2[bi % 2], b_xb2[bi % 2]
                        for t, (a0, n) in enumerate(subt_):
                            ti = (qc0_ + a0) // 128
                            P.dma("sp", X_[0:n, t, :], ydst[qc0_ + a0:qc0_ + a0 + n, :], [ybufs[ti]], [bX_[t]])
                        pre_norm_dve(subt_, 0, X_, bX_)
                        return subt_

                    drot = [0]

                    def next_dbl3():
                        i = drot[0]; drot[0] = (i + 1) % 3
                        return dbl[i], bdbl[i]

                    subt0_ = head_dve(0)
                    for j in range(3):
                        P.dma("sp", cw[:, :, j], bass.AP(I["conv_w"], (l * 3 + j) * DFF, [[1, 128], [128, NFC]]), (), [b_cw], allow_slow_non_contiguous=True)
                    P.dma("sp", cw[:, :, 3], bass.AP(I["conv_b"], l * DFF, [[1, 128], [128, NFC]]), (), [b_cw], allow_slow_non_contiguous=True)
                    if prm:
                        dve(lambda e: e.memset(halo[:, :, :], 0.0), (), [b_halo])
                    else:
                        for j in range(2):
                            P.dma("sp", halo[:, :, j], bass.AP(I["state_ffn_conv"], (l * 2 + j) * DFF, [[1, 128], [128, NFC]]), (), [b_halo], allow_slow_non_contiguous=True)
                    for f0 in range(0, NFC, 6):
                        f1 = min(NFC, f0 + 6)
                        P.dma("sp", wd_all[:, f0:f1, :], wds[:, f0:f1, :], [bW], [b_wd])
                    pre_norm_pe(subt0_)
                    for bi, (qc0, qpos, nq) in enumerate(qblocks):
                        subt = [(a0, min(128, nq - a0)) for a0 in range(0, nq, 128)]
                        xblk, b_xb = xblk2[bi % 2], b_xb2[bi % 2]
                        ck("wout")
                        pd, bpd = next_dbl()
                        for j in range(2):
                            for c in range(8):
                                pe(lambda e, j=j, c=c, pd=pd: e.matmul(pd[:, j * 512:j * 512 + nq], lhsT=wxq[:, c, j * 128:(j + 1) * 128], rhs=hT2[:, c, 0:nq], start=(c == 0), stop=(c == 7)), [b_wxq, b_hT2], [bpd], inc=(c == 7 and j == 1))
                        pdv = pd[:, :].rearrange("p (a b) -> p a b", a=2)
                        act(lambda e, pdv=pdv: e.copy(out=qxT[:, :, 0:nq], in_=pdv[:, :, 0:nq]), [bpd], [b_qxT])
                        callsX = []
                        for hp in range(2):
                            lanes = []
                            for li in range(2):
                                h = hp * 2 + li
                                r0 = li * 64
                                lanes.append(dict(kt=lambda c0, nk, r0=r0, hp=hp: mkT[r0:r0 + 64, hp, c0:c0 + nk], qt=lambda c0, c1, r0=r0, hp=hp: qxT[r0:r0 + 64, hp, c0 - qc0:c1 - qc0],
                                                  v=lambda kbi, nk, h=h: MV[0:nk, kbi, h, :], E=None, tp=None))
                            callsX.append(("X", lanes, (qc0, qpos, nq), "all", [(0, (0, 0, 128)), (1, (128, 128, 128))], [b_qxT, b_mkT, b_MV], epi_plain([(hp * 2) * 64, (hp * 2 + 1) * 64], xa, b_xa)))
                        attend_many(callsX)
                        flush_pending()
                        for t, (a0, n) in enumerate(subt):
                            pt, bpt = next_ptr()
                            for c in range(2):
                                pe(lambda e, c=c, pt=pt, t=t, n=n: e.transpose(out=pt[:, c * 128:c * 128 + n], in_=xa[0:n, t, c * 128:(c + 1) * 128], identity=ident[0:n, 0:n]), [b_xa, b_ident], [bpt], inc=(c == 1))
                            ptv = pt[:, 0:256].rearrange("p (c t) -> p c t", c=2)
                            act(lambda e, ptv=ptv, a0=a0, n=n: e.copy(out=xaT[:, :, a0:a0 + n], in_=ptv[:, :, 0:n]), [bpt], [b_xaT])
                        for t, (a0, n) in enumerate(subt):
                            pd, bpd = next_dbl()
                            for hf in range(2):
                                for c in range(2):
                                    pe(lambda e, hf=hf, c=c, a0=a0, n=n, pd=pd: e.matmul(pd[0:n, hf * 512:(hf + 1) * 512], lhsT=xaT[:, c, a0:a0 + n], rhs=wxo[:, c, hf * 512:(hf + 1) * 512], start=(c == 0), stop=(c == 1)), [b_xaT, b_wxo], [bpd], inc=(c == 1 and hf == 1))
                            post_residual(pd, bpd, n, t, 1, xblk, b_xb)
                        pre_norm_dve(subt, 2, xblk, b_xb)
                        pre_norm_pe(subt)
                        ck("xatt")
                        for f in range(NFC):
                            k = f % 3
                            k2 = f % 2
                            P.dma("sp", wg[k][:, :, :, :], WGU[l, f].rearrange("p (a c j) -> p a c j", a=2, c=8), [bW], [bwg[k]])
                            pd, bpd = next_dbl()
                            for j in range(2):
                                for c in range(8):
                                    pe(lambda e, j=j, c=c, pd=pd, k=k: e.matmul(pd[:, j * 512:j * 512 + nq], lhsT=wg[k][:, j, c, :], rhs=hT2[:, c, 0:nq], start=(c == 0), stop=(c == 7)), [bwg[k], b_hT2], [bpd], inc=(c == 7 and j == 1))
                            G_, bG = gs[k2], b_gs[k2]
                            C_, bC = cc[k2], b_cc[k2]
                            S_, bS = sl[k2], b_sl[k2]
                            act(lambda e, f=f, G_=G_: e.copy(out=G_[:, 0:2], in_=halo[:, f, :]), [b_halo], [bG])
                            act(lambda e, pd=pd, G_=G_: e.copy(out=G_[:, 2:2 + nq], in_=pd[:, 0:nq]), [bpd], [bG])
                            act(lambda e, f=f, G_=G_: e.copy(out=halo[:, f, :], in_=G_[:, nq:nq + 2]), [bG], [b_halo])
                            dve(lambda e, f=f, G_=G_, C_=C_: e.tensor_scalar(out=C_[:, 0:nq], in0=G_[:, 2:2 + nq], scalar1=cw[:, f, 2:3], scalar2=cw[:, f, 3:4], op0=ALU.mult, op1=ALU.add), [bG, b_cw], [bC])
                            dve(lambda e, f=f, G_=G_, C_=C_: e.scalar_tensor_tensor(out=C_[:, 0:nq], in0=G_[:, 1:1 + nq], scalar=cw[:, f, 1:2], in1=C_[:, 0:nq], op0=ALU.mult, op1=ALU.add), [bG, b_cw, bC], [bC])
                            dve(lambda e, f=f, G_=G_, C_=C_: e.scalar_tensor_tensor(out=C_[:, 0:nq], in0=G_[:, 0:nq], scalar=cw[:, f, 0:1], in1=C_[:, 0:nq], op0=ALU.mult, op1=ALU.add), [bG, b_cw, bC], [bC])
                            act(lambda e, C_=C_, S_=S_: e.activation(out=S_[:, 0:nq], in_=C_[:, 0:nq], func=AF.Silu, bias=zc[:, 0:1]), [bC], [bS])
                            dve(lambda e, f=f, pd=pd, S_=S_: e.tensor_tensor(out=aTf[:, f, 0:nq], in0=pd[:, 512:512 + nq], in1=S_[:, 0:nq], op=ALU.mult), [bpd, bS], [b_aTf])
                        nxt = None
                        if bi + 1 < len(qblocks):
                            nxt = head_dve(bi + 1)
                        for t, (a0, n) in enumerate(subt):
                            ti = (qc0 + a0) // 128
                            if nxt is not None and t == min(2, len(subt) - 1):
                                pre_norm_pe(nxt)
                            pd, bpd = next_dbl3()
                            for f in range(NFC):
                                for hf in range(2):
                                    pe(lambda e, hf=hf, f=f, a0=a0, n=n, pd=pd: e.matmul(pd[0:n, hf * 512:(hf + 1) * 512], lhsT=aTf[:, f, a0:a0 + n], rhs=wd_all[:, f, hf * 512:(hf + 1) * 512], start=(f == 0), stop=(f == NFC - 1), skip_group_check=True), [b_aTf, b_wd], [bpd], inc=(hf == 1 and f == NFC - 1))
                            post_residual(pd, bpd, n, t, 3, xblk, b_xb)
                            P.dma("pool", ydst[qc0 + a0:qc0 + a0 + n, :], xblk[0:n, t, :], [b_xb[t]], [ybufs[ti]])
                    for j in range(2):
                        P.dma("pool", bass.AP(O[pfx + "conv"], ((l * (NP if prm else 1) + so) * 2 + j) * DFF, [[1, 128], [128, NFC]]), halo[:, :, j], [b_halo], [], allow_slow_non_contiguous=True)
                    P.barrier()

        YB = {}
        for s in range(NP):
            YB[("p", s)] = [Buf() for _ in range((SEQ + 127) // 128)]
        YB[("s", 0)] = [Buf()]
        for l in range(DEPTH):
            pass
        seqs = [("p", s) for s in range(NP)] + [("s", 0)]
        nrun = 0
        for (kind, s) in seqs:
            for l in range(DEPTH):
                if stop >= 10 and nrun >= stop - 9:
                    break
                try:
                    run_layer(kind, s, l)
                except StopBuild:
                    P.finish()
                    return nc
                nrun += 1
        P.finish()
    return nc


NCORES = 8
_cache = {}


def kernel(**inputs):
    SEQ, PAST, TS, NP = 2048, 2048, 32, 4
    key = (NP, SEQ, PAST, TS)
    if key not in _cache:
        _cache[key] = build(NP, SEQ, PAST, TS)
    nc = _cache[key]
    hc = host_consts(SEQ, PAST, TS)
    f = lambda a: np.ascontiguousarray(np.asarray(a, dtype=np.float32))
    in_maps = []
    for i in range(NCORES):
        m = {}
        m["x_prompt"] = f(inputs["x_prompt"][NP * i:NP * (i + 1)])
        m["x_sample"] = f(inputs["x_sample"][i:i + 1])
        m["mem_prompt"] = f(inputs["mem_prompt"][NP * i:NP * (i + 1)])
        for n in ("cache_a_k", "cache_a_v", "cache_b_k", "cache_b_v", "cache_c_latent", "cache_c_rope_k", "cache_mem_k", "cache_mem_v", "state_ffn_conv"):
            a = np.asarray(inputs[n])[:, i]
            m[n] = f(a.reshape(a.shape[0], a.shape[1], -1))
        for n in ("w_in", "w_out", "norms", "diff_subln", "t5_bias", "band_rel_bias", "mla_q_norm", "mla_w_uq", "mla_kv_norm", "mla_w_ukv",
                  "w_xq", "w_mk", "w_mv", "w_xo", "w_gate", "w_up", "conv_w", "conv_b", "w_down"):
            m[n] = f(inputs[n])
        m["diff_lambda"] = f(np.asarray(inputs["diff_lambda"]).reshape(2, 128))
        m.update(hc)
        in_maps.append(m)
    res = run_bass_kernel_spmd(nc, in_maps, core_ids=list(range(NCORES)))
    R = res.results
    cat0 = lambda n: np.concatenate([r[n] for r in R], axis=0)
    cat1 = lambda n: np.concatenate([r[n] for r in R], axis=1)
    y_p = cat0("y_prompt"); y_s = cat0("y_sample")
    B = y_p.shape[0]
    outs = [y_p, y_s,
            cat1("p_a_k").reshape(2, B, SEQ, 8, 64), cat1("p_a_v").reshape(2, B, SEQ, 8, 64),
            cat1("p_b_k").reshape(2, B, 512, 4, 64), cat1("p_b_v").reshape(2, B, 512, 4, 64),
            cat1("p_lat"), cat1("p_kpe"),
            cat1("p_mk").reshape(2, B, MEM, 4, 64), cat1("p_mv").reshape(2, B, MEM, 4, 64), cat1("p_conv"),
            cat1("s_a_k").reshape(2, NCORES, TS, 8, 64), cat1("s_a_v").reshape(2, NCORES, TS, 8, 64),
            cat1("s_b_k").reshape(2, NCORES, TS, 4, 64), cat1("s_b_v").reshape(2, NCORES, TS, 4, 64),
            cat1("s_lat"), cat1("s_kpe"), cat1("s_conv")]
    return tuple(np.ascontiguousarray(o, dtype=np.float32) for o in outs)
```

```python
import contextlib
import math
import os
DBG = os.environ.get('KDBG', '')
import numpy as np
import concourse.bass as bass
import concourse.mybir as mybir
from concourse.bass_utils import run_bass_kernel_spmd

F32 = mybir.dt.float32
BF16 = mybir.dt.bfloat16
ALU = mybir.AluOpType
AF = mybir.ActivationFunctionType
AX = mybir.AxisListType

D = 1024
DIN = 2720
DFF = 2816
NFC = 22
MEM = 256
EPS = 1e-6
NDS = 32
SAME_ENGINE_SYNC = False


class StopBuild(Exception):
    pass


CKSTOP = os.environ.get('KCK', '')


STOPPED = [False]


def ck(tag):
    if CKSTOP and tag == CKSTOP:
        STOPPED[0] = True


class Buf:
    __slots__ = ("w", "r", "excl", "strict")

    def __init__(self, excl=False, strict=False):
        self.w = None
        self.r = {}
        self.excl = excl
        self.strict = strict


class Prog:
    def __init__(self, nc, stack):
        self.nc = nc
        self.eh = {"pe": nc.tensor, "act": nc.scalar, "dve": nc.vector, "pool": nc.gpsimd, "sp": nc.sync}
        self.sems = {}
        for e in self.eh:
            self.sems[e] = stack.enter_context(nc.semaphore("s_" + e))
        self.cnt = {e: 0 for e in self.eh}
        self.seen = {e: {} for e in self.eh}
        for i in range(NDS):
            self.sems[("d", i)] = stack.enter_context(nc.semaphore("s_d%d" % i))
        self.dcnt = [0] * NDS
        self.dnext = {"sp": 0, "pool": 0, "act": 0}
        self.drange = {"sp": (0, NDS - 4), "pool": (NDS - 4, NDS), "act": (0, NDS - 4)}
        self.nins = 0

    def _deps(self, eng, reads, writes, extra=()):
        needs = {}
        strict_own = 0
        for b in reads:
            t = b.w
            if t is not None and needs.get(t[0], 0) < t[1]:
                needs[t[0]] = t[1]
            if t is not None and t[0] == eng and t[1] > strict_own and eng != "pe":
                strict_own = t[1]
            if b.excl:
                for k, v in b.r.items():
                    if needs.get(k, 0) < v:
                        needs[k] = v
        for b in writes:
            t = b.w
            if t is not None and needs.get(t[0], 0) < t[1]:
                needs[t[0]] = t[1]
            for k, v in b.r.items():
                if needs.get(k, 0) < v:
                    needs[k] = v
        for t in extra:
            if needs.get(t[0], 0) < t[1]:
                needs[t[0]] = t[1]
        seen = self.seen[eng]
        for k, v in needs.items():
            if k == eng and (eng == "pe" or not SAME_ENGINE_SYNC) and eng != "pool":
                if strict_own and seen.get(k, 0) < strict_own:
                    seen[k] = strict_own
                    self.eh[eng].wait_ge(self.sems[k], strict_own)
                continue
            if seen.get(k, 0) >= v:
                continue
            seen[k] = v
            self.eh[eng].wait_ge(self.sems[k], v)

    def op(self, eng, fn, reads=(), writes=(), inc=True):
        if STOPPED[0]:
            return
        self._deps(eng, reads, writes)
        self.nins += 1
        ins = fn(self.eh[eng])
        if inc:
            self.cnt[eng] += 1
            v = self.cnt[eng]
            ins.then_inc(self.sems[eng], 1)
        else:
            v = self.cnt[eng] + 1
        for b in reads:
            b.r[eng] = v
        for b in writes:
            b.w = (eng, v)
            b.r = {}

    def dma(self, eng, out, in_, reads=(), writes=(), **kw):
        if STOPPED[0]:
            return
        lo, hi = self.drange[eng]
        i = lo + self.dnext[eng]
        self.dnext[eng] = (self.dnext[eng] + 1) % (hi - lo)
        key = ("d", i)
        extra = ((key, self.dcnt[i]),) if self.dcnt[i] else ()
        self._deps(eng, reads, writes, extra)
        self.dcnt[i] += 16
        v = self.dcnt[i]
        self.nins += 1
        self.eh[eng].dma_start(out=out, in_=in_, **kw).then_inc(self.sems[key], 16)
        for b in reads:
            b.r[key] = v
        for b in writes:
            b.w = (key, v)
            b.r = {}

    def barrier(self, force=False):
        if STOPPED[0] and not force:
            return
        keys = [e for e in self.eh if self.cnt[e]] + [("d", i) for i in range(NDS) if self.dcnt[i]]
        for e in self.eh:
            seen = self.seen[e]
            for k in keys:
                v = self.cnt[k] if not isinstance(k, tuple) else self.dcnt[k[1]]
                if k == e and e != "pool":
                    continue
                if seen.get(k, 0) >= v:
                    continue
                seen[k] = v
                self.eh[e].wait_ge(self.sems[k], v)

    def finish(self):
        self.barrier(force=True)


def t5_bucket_np(rel):
    half, exact = 16, 8
    ret = np.where(rel > 0, half, 0)
    n = np.abs(rel)
    nf = np.maximum(n, 1).astype(np.float32)
    large = exact + (np.log(nf / np.float32(exact)) / np.float32(math.log(128 / exact)) * np.float32(half - exact)).astype(np.int32)
    large = np.minimum(large, half - 1)
    return ret + np.where(n < exact, n, large)


def host_consts(SEQ, PAST, TS):
    m = np.arange(1151)
    bk = t5_bucket_np(511 - m)
    oh_t5 = (bk[None, :] == np.arange(32)[:, None]).astype(np.float32)
    m2 = np.arange(1151)
    j = np.clip(511 - m2, -128, 128) + 128
    ohb = (j[None, :] == np.arange(384)[:, None]).astype(np.float32)

    def rope_tab(pos):
        half = 16
        inv = np.float32(10000.0) ** (-np.arange(half, dtype=np.float32) / np.float32(half))
        ang = pos.astype(np.float32)[:, None] * inv[None, :]
        c = np.cos(ang).astype(np.float32)
        s = np.sin(ang).astype(np.float32)
        c4 = np.repeat(c[:, None, :], 4, axis=1)
        s4 = np.repeat(s[:, None, :], 4, axis=1)
        return np.ascontiguousarray(np.stack([c4, s4], axis=1).reshape(len(pos), 128))

    return {"oh_t5": oh_t5, "oh_b": ohb, "rope_p": rope_tab(np.arange(SEQ)), "rope_s": rope_tab(PAST + np.arange(TS))}


def vis_info(mode, q0, nq, k0, nk):
    qch = [(c, min(c + 64, nq)) for c in range(0, nq, 64)]
    kch = [(p, min(p + 64, nk)) for p in range(0, nk, 64)]
    V = {}
    for (p0, p1) in kch:
        kc = (k0 + p0) // 64
        for (c0, c1) in qch:
            qc = (q0 + c0) // 64
            if mode == "causal":
                ok = kc <= qc
            elif mode == "band":
                ok = (kc <= qc) and (kc >= qc - 8)
            else:
                ok = True
            V[(p0, c0)] = ok
    viscols = [(c0, c1) for (c0, c1) in qch if any(V[(p0, c0)] for (p0, _) in kch)]
    if not viscols:
        return None
    cs = (min(c0 for c0, _ in viscols) // 128) * 128
    ce = min(nq, ((max(c1 for _, c1 in viscols) + 127) // 128) * 128)
    masks = []
    for (p0, p1) in kch:
        run = None
        for (c0, c1) in qch:
            if c0 < cs or c0 >= ce:
                continue
            if not V[(p0, c0)]:
                if run is not None and run[1] == c0:
                    run[1] = c1
                else:
                    if run is not None:
                        masks.append((p0, p1, run[0], run[1]))
                    run = [c0, c1]
        if run is not None:
            masks.append((p0, p1, run[0], run[1]))
    return cs, ce, masks


def build(NP, SEQ, PAST, TS=32, DEPTH=2, stop=0):
    nc = bass.Bass("TRN2", target_bir_lowering=False)
    I, O = {}, {}

    def inp(n, shape):
        I[n] = nc.dram_tensor(n, list(shape), F32, kind="ExternalInput")

    def outp(n, shape):
        O[n] = nc.dram_tensor(n, list(shape), F32, kind="ExternalOutput")

    NBP = min(512, SEQ)
    NBS = min(512, PAST)
    inp("x_prompt", (NP, SEQ, D)); inp("x_sample", (1, TS, D))
    inp("cache_a_k", (DEPTH, PAST, 512)); inp("cache_a_v", (DEPTH, PAST, 512))
    inp("cache_b_k", (DEPTH, NBS, 256)); inp("cache_b_v", (DEPTH, NBS, 256))
    inp("cache_c_latent", (DEPTH, PAST, 128)); inp("cache_c_rope_k", (DEPTH, PAST, 32))
    inp("cache_mem_k", (DEPTH, MEM, 256)); inp("cache_mem_v", (DEPTH, MEM, 256))
    inp("state_ffn_conv", (DEPTH, 2, DFF)); inp("mem_prompt", (NP, MEM, D))
    inp("w_in", (DEPTH, D, DIN)); inp("w_out", (DEPTH, D, D)); inp("norms", (DEPTH, 7, D))
    inp("diff_lambda", (DEPTH, 128)); inp("diff_subln", (DEPTH, 64)); inp("t5_bias", (32, 8))
    inp("band_rel_bias", (DEPTH, 4, 257)); inp("mla_q_norm", (DEPTH, 256)); inp("mla_w_uq", (DEPTH, 256, 384))
    inp("mla_kv_norm", (DEPTH, 128)); inp("mla_w_ukv", (DEPTH, 128, 512))
    inp("w_xq", (DEPTH, D, 256)); inp("w_mk", (DEPTH, D, 256)); inp("w_mv", (DEPTH, D, 256)); inp("w_xo", (DEPTH, 256, D))
    inp("w_gate", (DEPTH, D, DFF)); inp("w_up", (DEPTH, D, DFF)); inp("conv_w", (DEPTH, 3, DFF)); inp("conv_b", (DEPTH, DFF))
    inp("w_down", (DEPTH, DFF, D))
    inp("oh_t5", (32, 1151)); inp("oh_b", (384, 1151)); inp("rope_p", (SEQ, 128)); inp("rope_s", (TS, 128))
    outp("y_prompt", (NP, SEQ, D)); outp("y_sample", (1, TS, D))
    outp("p_a_k", (DEPTH, NP, SEQ, 512)); outp("p_a_v", (DEPTH, NP, SEQ, 512))
    outp("p_b_k", (DEPTH, NP, NBP, 256)); outp("p_b_v", (DEPTH, NP, NBP, 256))
    outp("p_lat", (DEPTH, NP, SEQ, 128)); outp("p_kpe", (DEPTH, NP, SEQ, 32))
    outp("p_mk", (DEPTH, NP, MEM, 256)); outp("p_mv", (DEPTH, NP, MEM, 256)); outp("p_conv", (DEPTH, NP, 2, DFF))
    outp("s_a_k", (DEPTH, 1, TS, 512)); outp("s_a_v", (DEPTH, 1, TS, 512))
    outp("s_b_k", (DEPTH, 1, TS, 256)); outp("s_b_v", (DEPTH, 1, TS, 256))
    outp("s_lat", (DEPTH, 1, TS, 128)); outp("s_kpe", (DEPTH, 1, TS, 32)); outp("s_conv", (DEPTH, 1, 2, DFF))
    IA = {k: v.ap() for k, v in I.items()}
    OA = {k: v.ap() for k, v in O.items()}
    WB = {}
    for n in ("w_in", "w_out", "mla_w_uq", "mla_w_ukv", "w_xq", "w_mk", "w_mv", "w_xo", "w_down"):
        WB[n] = nc.dram_tensor("wb_" + n, list(I[n].shape), BF16, kind="Internal")
    WBA = {k: v.ap() for k, v in WB.items()}
    WGUh = nc.dram_tensor("wb_wgu", [DEPTH, NFC, 128, 2048], BF16, kind="Internal")
    WGU = WGUh.ap()
    Escr = nc.dram_tensor("Escr", [8 + DEPTH * 4, 128, 1024], BF16, kind="Internal")
    EscrA = Escr.ap()
    scrA = nc.dram_tensor("scrA", [8, 1151], F32, kind="Internal")
    scrB = nc.dram_tensor("scrB", [DEPTH * 4, 1151], F32, kind="Internal")

    uid = [0]
    G = contextlib.ExitStack()
    with G:
        P = Prog(nc, G)

        def sb(stack, shape, dt, name="t"):
            uid[0] += 1
            return stack.enter_context(nc.sbuf_tensor("%s_%d" % (name, uid[0]), list(shape), dt))

        dbl = [G.enter_context(nc.psum_tensor("dbl%d" % i, [128, 1024], F32)) for i in range(3)]
        bdbl = [Buf(True) for _ in range(3)]
        ptr = [G.enter_context(nc.psum_tensor("ptr%d" % i, [128, 1024], BF16)) for i in range(2)]
        bptr = [Buf(True) for _ in range(2)]
        rot = {"d": 0, "t": 0}

        def next_dbl():
            i = rot["d"]; rot["d"] = 1 - i
            return dbl[i], bdbl[i]

        def next_ptr():
            i = rot["t"]; rot["t"] = 1 - i
            return ptr[i], bptr[i]

        pe = lambda fn, r=(), w=(), inc=True: P.op("pe", fn, r, w, inc)
        act = lambda fn, r=(), w=(): P.op("act", fn, r, w)
        dve = lambda fn, r=(), w=(): P.op("dve", fn, r, w)

        zc = sb(G, [128, 1], F32, "zc"); ec = sb(G, [128, 1], F32, "ec"); b_zc = Buf()
        dve(lambda e: e.memset(zc[:], 0.0), (), [b_zc])
        dve(lambda e: e.memset(ec[:], EPS), (), [b_zc])
        P.barrier()
        nc.const_aps.register(F32, 0.0, zc[:, 0:1])
        nc.const_aps.register(F32, EPS, ec[:, 0:1])
        ident = sb(G, [128, 128], BF16, "ident"); b_ident = Buf()
        Jm = sb(G, [128, 128], F32, "J"); b_J = Buf()
        P.op("pool", lambda e: e.memset(ident[:], 0.0), (), [b_ident])
        P.op("pool", lambda e: e.affine_select(out=ident[:], in_=ident[:], pattern=[[-1, 128]], compare_op=ALU.not_equal, fill=1.0, base=0, channel_multiplier=1), [b_ident], [b_ident])
        P.op("pool", lambda e: e.memset(Jm[:], 0.0), (), [b_J])
        P.op("pool", lambda e: e.affine_select(out=Jm[:], in_=Jm[:], pattern=[[1, 128]], compare_op=ALU.not_equal, fill=1.0, base=-127, channel_multiplier=1), [b_J], [b_J])
        bW = Buf()
        for n in ([] if 'W' in DBG else WB):
            src = IA[n].rearrange("l a b -> (l a) b")
            dst = WBA[n].rearrange("l a b -> (l a) b")
            rows = src.shape[0]
            step = 256
            for r in range(0, rows, step):
                P.dma("pool", dst[r:min(r + step, rows)], src[r:min(r + step, rows)], (), [Buf()])
        for l_ in range(DEPTH):
            for j_, wn in enumerate(("w_gate", "w_up")):
                wsrc_ = IA[wn][l_].rearrange("(c p) n -> p c n", p=128)
                for f_ in range(NFC):
                    P.dma("pool", WGU[l_, f_][:, j_ * 1024:(j_ + 1) * 1024].rearrange("p (c j) -> p c j", c=8), wsrc_[:, :, f_ * 128:(f_ + 1) * 128], (), [Buf()])
        b_Escr = Buf()
        with contextlib.ExitStack() as S:
            EA = sb(S, [128, 8, 1024], BF16, "EA"); b_EA = Buf()
            EB = sb(S, [128, DEPTH * 4, 1024], BF16, "EB"); b_EB = Buf()
            t5 = sb(S, [32, 8], F32); b_t5 = Buf()
            oh = sb(S, [32, 1151], F32); b_oh = Buf()
            c15 = sb(S, [8, 1], F32); b_c15 = Buf(strict=True)
            wA = sb(S, [8, 1151], F32); b_wA = Buf()
            P.dma("sp", t5[:], IA["t5_bias"], (), [b_t5])
            P.dma("sp", oh[:], IA["oh_t5"], (), [b_oh])
            P.dma("sp", c15[:], bass.AP(I["t5_bias"], 15 * 8, [[1, 8], [1, 1]]), (), [b_c15])
            dve(lambda e: e.tensor_scalar(out=c15[:], in0=c15[:], scalar1=-1.0, scalar2=None, op0=ALU.mult), [b_c15], [b_c15])
            for (c0, c1) in ((0, 512), (512, 1024), (1024, 1151)):
                pd, bpd = next_dbl()
                pe(lambda e, c0=c0, c1=c1, pd=pd: e.matmul(pd[0:8, 0:c1 - c0], lhsT=t5[:, :], rhs=oh[:, c0:c1], start=True, stop=True), [b_t5, b_oh], [bpd])
                act(lambda e, c0=c0, c1=c1, pd=pd: e.activation(out=wA[:, c0:c1], in_=pd[0:8, 0:c1 - c0], func=AF.Exp, bias=c15[:, 0:1], scale=1.0), [bpd, b_c15], [b_wA])
            b_scrA = Buf()
            P.dma("sp", scrA.ap(), wA[:], [b_wA], [b_scrA])
            Gp = sb(S, [128, 1024], F32); b_Gp = Buf()
            for h in range(8):
                P.dma("sp", Gp[:], bass.AP(scrA, h * 1151, [[1, 128], [1, 1024]]), [b_scrA], [b_Gp])
                pd, bpd = next_dbl()
                for hf in range(2):
                    pe(lambda e, hf=hf, pd=pd: e.matmul(pd[:, hf * 512:(hf + 1) * 512], lhsT=Jm[:], rhs=Gp[:, hf * 512:(hf + 1) * 512], start=True, stop=True), [b_J, b_Gp], [bpd])
                act(lambda e, h=h, pd=pd: e.copy(out=EA[:, h, :], in_=pd[:, :]), [bpd], [b_EA])
            relT = sb(S, [128, 3, DEPTH * 4], F32); b_relT = Buf()
            ohb = sb(S, [128, 3, 1151], F32); b_ohb = Buf()
            c0b = sb(S, [DEPTH * 4, 1], F32); b_c0b = Buf(strict=True)
            wB = sb(S, [DEPTH * 4, 1151], F32); b_wB = Buf()
            dve(lambda e: e.memset(relT[:], 0.0), (), [b_relT])
            for c in range(3):
                nj = 128 if c < 2 else 1
                P.dma("sp", relT[0:nj, c, :], bass.AP(I["band_rel_bias"], c * 128, [[1, nj], [257, DEPTH * 4]]), (), [b_relT], allow_slow_non_contiguous=True)
            P.dma("sp", ohb[:], IA["oh_b"].rearrange("(c p) m -> p c m", p=128), (), [b_ohb])
            P.dma("sp", c0b[:], bass.AP(I["band_rel_bias"], 0, [[257, DEPTH * 4], [1, 1]]), (), [b_c0b], allow_slow_non_contiguous=True)
            dve(lambda e: e.tensor_scalar(out=c0b[:], in0=c0b[:], scalar1=-1.0, scalar2=None, op0=ALU.mult), [b_c0b], [b_c0b])
            for (c0_, c1_) in ((0, 512), (512, 1024), (1024, 1151)):
                pd, bpd = next_dbl()
                for c in range(3):
                    pe(lambda e, c=c, pd=pd, c0_=c0_, c1_=c1_: e.matmul(pd[0:DEPTH * 4, 0:c1_ - c0_], lhsT=relT[:, c, :], rhs=ohb[:, c, c0_:c1_], start=(c == 0), stop=(c == 2)), [b_relT, b_ohb], [bpd], inc=(c == 2))
                act(lambda e, pd=pd, c0_=c0_, c1_=c1_: e.activation(out=wB[:, c0_:c1_], in_=pd[0:DEPTH * 4, 0:c1_ - c0_], func=AF.Exp, bias=c0b[:, 0:1], scale=1.0), [bpd, b_c0b], [b_wB])
            b_scrB = Buf()
            P.dma("sp", scrB.ap(), wB[:], [b_wB], [b_scrB])
            for lh in range(DEPTH * 4):
                P.dma("sp", Gp[:, :], bass.AP(scrB, lh * 1151, [[1, 128], [1, 1024]]), [b_scrB], [b_Gp])
                pd, bpd = next_dbl()
                for hf in range(2):
                    pe(lambda e, hf=hf, pd=pd: e.matmul(pd[:, hf * 512:(hf + 1) * 512], lhsT=Jm[:], rhs=Gp[:, hf * 512:(hf + 1) * 512], start=True, stop=True), [b_J, b_Gp], [bpd])
                act(lambda e, lh=lh, pd=pd: e.copy(out=EB[:, lh, :], in_=pd[:, :]), [bpd], [b_EB])
            P.dma("sp", EscrA[0:8].rearrange("h p u -> p h u"), EA[:, :, :], [b_EA], [b_Escr])
            P.dma("sp", EscrA[8:8 + DEPTH * 4].rearrange("h p u -> p h u"), EB[:, :, :], [b_EB], [Buf()])
            P.barrier()

        neglam = sb(G, [128, DEPTH], F32, "neglam"); b_lam = Buf(strict=True)
        subln = sb(G, [128, DEPTH, 64], F32, "subln"); b_subln = Buf()
        LAM_INIT = [0.8 - 0.6 * math.exp(-0.3 * l) for l in range(DEPTH)]
        with contextlib.ExitStack() as S:
            lt = sb(S, [128, 128], F32); b_lt = Buf()
            junk = sb(S, [128, 32], F32); b_junk = Buf()
            s12 = sb(S, [128, 2], F32); b_s12 = Buf(strict=True)
            for l in range(DEPTH):
                P.dma("sp", lt[:], bass.AP(I["diff_lambda"], l * 128, [[0, 128], [1, 128]]), (), [b_lt])
                P.dma("sp", subln[:, l, :], bass.AP(I["diff_subln"], l * 64, [[0, 128], [1, 64]]), (), [b_subln])
                for i in range(2):
                    dve(lambda e, i=i: e.scalar_tensor_tensor(out=junk[:], in0=lt[:, 64 * i:64 * i + 32], scalar=1.0, in1=lt[:, 64 * i + 32:64 * i + 64], op0=ALU.mult, op1=ALU.mult, accum_out=s12[:, i:i + 1]), [b_lt], [b_junk, b_s12])
                act(lambda e: e.activation(out=s12[:], in_=s12[:], func=AF.Exp, bias=zc[:, 0:1]), [b_s12], [b_s12])
                dve(lambda e, l=l: e.tensor_tensor(out=neglam[:, l:l + 1], in0=s12[:, 1:2], in1=s12[:, 0:1], op=ALU.subtract), [b_s12], [b_lam])
                dve(lambda e, l=l: e.tensor_scalar(out=neglam[:, l:l + 1], in0=neglam[:, l:l + 1], scalar1=-LAM_INIT[l], scalar2=None, op0=ALU.add), [b_lam], [b_lam])
                dve(lambda e, l=l: e.tensor_scalar(out=subln[:, l, :], in0=subln[:, l, :], scalar1=1.0 - LAM_INIT[l], scalar2=None, op0=ALU.mult), [b_subln], [b_subln])
            P.barrier()

        if stop == 1:
            P.finish()
            return nc

        def bcast_row(dst, src_handle, off, n, bufs):
            P.dma("sp", dst, bass.AP(src_handle, off, [[0, 128], [1, n]]), (), bufs)

        def rstd_of(ss, n, Dd, bss):
            act(lambda e: e.activation(out=ss, in_=ss, func=AF.Ln, bias=ec[0:n, 0:1], scale=1.0 / Dd), [bss], [bss])
            act(lambda e: e.activation(out=ss, in_=ss, func=AF.Exp, bias=zc[0:n, 0:1], scale=-0.5), [bss], [bss])

        def run_layer(kind, s, l):
            prm = kind == "p"
            T = SEQ if prm else TS
            tiles = [(t0, min(128, T - t0)) for t0 in range(0, T, 128)]
            NT = len(tiles)
            past = 0 if prm else PAST
            NKB_past = past // 128
            NK = past + T
            kblocks = [(j * 128, j * 128, 128) for j in range(NKB_past)] + [(past + t0, past + t0, n) for (t0, n) in tiles]
            NKB = len(kblocks)
            qblocks = []
            for q0 in range(0, T, 512):
                nq = min(512, T - q0)
                qblocks.append((q0, past + q0, nq))
            xsrc = (IA["x_prompt"][s] if prm else IA["x_sample"][0]) if l == 0 else (OA["y_prompt"][s] if prm else OA["y_sample"][0])
            ydst = OA["y_prompt"][s] if prm else OA["y_sample"][0]
            ybufs = YB[(kind, s)]
            pfx = "p_" if prm else "s_"
            so = s if prm else 0
            rope_h = I["rope_p"] if prm else I["rope_s"]
            wl = {k: v[l] for k, v in WBA.items()}
            scale_of = {"A": 32 ** -0.5, "B": 64 ** -0.5, "C": 96 ** -0.5, "X": 64 ** -0.5}

            L = contextlib.ExitStack()
            with L:
                b_atok = Buf()
                NPT = 4
                ptiles = [sb(L, [128, 2, 512], BF16, "PT") for _ in range(NPT)]
                bpt_ = [Buf() for _ in range(NPT)]
                prot = [0]
                rec = sb(L, [128, 2, 4], F32, "rec"); b_rec = Buf(strict=True)
                otmp = sb(L, [128, 4, 64], F32, "otmp"); b_otmp = Buf()
                o0t = sb(L, [128, 4, 64], F32, "o0t"); b_o0t = Buf()
                sq = sb(L, [128, 4, 64], F32, "sq"); b_sq = Buf()
                ssA = sb(L, [128, 4], F32, "ssA"); b_ssA = Buf(strict=True)

                accs = [sb(L, [128, 2, 260], F32, "accs") for _ in range(2)]
                baccs = [Buf(), Buf()]
                arot = [0]
                pending = []

                def flush_pending(step=None):
                    while pending and (step is None or pending[0][0] <= step):
                        pending.pop(0)[1]()

                def attend(grp, lanes, qb, mode, kbl, rbufs, epilogue):
                    attend_many([(grp, lanes, qb, mode, kbl, rbufs, epilogue)])

                def attend_many(calls):
                    acc, bacc = dbl[2], bdbl[2]
                    G_ = []
                    for ci, (grp, lanes, qb, mode, kbl, rbufs, epilogue) in enumerate(calls):
                        (qc0, qpos, nq) = qb
                        st_ = []
                        for kbi, (kc0, kpos, nk) in kbl:
                            vi = vis_info(mode, qpos, nq, kpos, nk)
                            if vi is not None:
                                st_.append((kbi, kc0, kpos, nk, vi[0], vi[2], vi[1]))
                        def heavy(stp):
                            (kbi_, kc0_, kpos_, nk_, cs_, masks_, ce_) = stp
                            relmax_ = (kpos_ + nk_ - 1) - (qpos + cs_)
                            return bool(masks_) or any(ln.get("E") is not None and relmax_ > ln["Ethr"] for ln in lanes)
                        hv = [x for x in st_ if heavy(x)]
                        lt = [x for x in st_ if not heavy(x)]
                        mix = []
                        while hv or lt:
                            if lt:
                                mix.append(lt.pop(0))
                            if hv:
                                mix.append(hv.pop(0))
                        st_ = mix
                        for si, stp in enumerate(st_):
                            G_.append((ci, si, len(st_), stp))
                    qk = {}

                    def emit_qk(g):
                        ci, si, ns, (kbi, kc0, kpos, nk, cs, masks, ce) = G_[g]
                        (grp, lanes, qb, mode, kbl, rbufs, epilogue) = calls[ci]
                        (qc0, qpos, nq) = qb
                        pd, bpd = next_dbl()
                        for li, ln in enumerate(lanes):
                            pe(lambda e, ln=ln, li=li, pd=pd: e.matmul(pd[0:nk, li * 512 + cs:li * 512 + ce], lhsT=ln["kt"](kc0, nk), rhs=ln["qt"](qc0 + cs, qc0 + ce),
                                                                      start=True, stop=True, **({"tile_position": ln["tp"]} if ln.get("tp") else {})),
                               rbufs, [bpd], inc=(li == 1))
                        qk[g] = (pd, bpd)

                    emit_qk(0)
                    if len(G_) > 1:
                        emit_qk(1)
                    for g, (ci, si, ns, (kbi, kc0, kpos, nk, cs, masks, ce)) in enumerate(G_):
                        (grp, lanes, qb, mode, kbl, rbufs, epilogue) = calls[ci]
                        (qc0, qpos, nq) = qb
                        scale = scale_of[grp]
                        nsub = (nq + 127) // 128
                        if si == 0:
                            accz = acc[:, :].rearrange("p (a b) -> p a b", a=2)
                            dve(lambda e, accz=accz, nsub=nsub: e.memset(accz[:, :, 0:nsub * 65], 0.0), (), [bacc])
                        pd, bpd = qk.pop(g)
                        pi = prot[0]; prot[0] = (pi + 1) % NPT
                        PT, bPT = ptiles[pi], bpt_[pi]
                        pdv = pd[:, :].rearrange("p (a b) -> p a b", a=2)
                        act(lambda e, PT=PT, pdv=pdv: e.activation(out=PT[0:nk, :, cs:ce], in_=pdv[0:nk, :, cs:ce], func=AF.Exp, bias=zc[0:nk, 0:1], scale=scale), [bpd], [bPT])
                        if g + 2 < len(G_):
                            emit_qk(g + 2)
                        relmax = (kpos + nk - 1) - (qpos + cs)
                        for li, ln in enumerate(lanes):
                            if ln.get("E") is not None and relmax > ln["Ethr"]:
                                off = ln["c0"] - (kpos - qpos)
                                Et = ln["E"]
                                dve(lambda e, li=li, Et=Et, off=off, PT=PT: e.tensor_tensor(out=PT[0:nk, li, cs:ce], in0=PT[0:nk, li, cs:ce], in1=Et[0:nk, off + cs:off + ce], op=ALU.mult), [bPT, ln["Eb"]], [bPT])
                        for (p0, p1, c0, c1) in masks:
                            dve(lambda e, PT=PT, p0=p0, p1=p1, c0=c0, c1=c1: e.memset(PT[p0:p1, :, c0:c1], 0.0), (), [bPT])
                        for li, ln in enumerate(lanes):
                            for t in range(nsub):
                                a0, a1 = t * 128, min(t * 128 + 128, nq)
                                if a1 <= cs or a0 >= ce:
                                    continue
                                last = (li == 1 and a1 >= ce)
                                pe(lambda e, li=li, t=t, a0=a0, a1=a1, ln=ln, PT=PT: e.matmul(acc[0:a1 - a0, li * 512 + t * 65:li * 512 + t * 65 + 65], lhsT=PT[0:nk, li, a0:a1], rhs=ln["v"](kbi, nk),
                                                                                            start=False, stop=False, skip_group_check=True),
                                   [bPT] + rbufs, [bacc], inc=last)
                        flush_pending(si)
                        if si == ns - 1:
                            flush_pending()
                            k = arot[0]; arot[0] = 1 - k
                            A_, bA_ = accs[k], baccs[k]
                            nn = min(128, nq)
                            accv = acc[:, :].rearrange("p (a b) -> p a b", a=2)
                            dve(lambda e, A_=A_, nn=nn, nsub=nsub, accv=accv: e.tensor_copy(out=A_[0:nn, :, 0:nsub * 65], in_=accv[0:nn, :, 0:nsub * 65]), [bacc], [bA_])
                            epilogue(A_, bA_, qb, nsub)

                def epi_plain(col_of_lane, dst=None, bdst=None):
                    def f(A_, bA_, qb, nsub):
                        pending.append((1, lambda: g(A_, bA_, qb, nsub)))

                    def g(A_, bA_, qb, nsub):
                        (qc0, qpos, nq) = qb
                        nn = min(128, nq)
                        recv = A_[0:nn, :, 0:nsub * 65].rearrange("p a (t c) -> p a t c", c=65)
                        dve(lambda e: e.reciprocal(out=rec[0:nn, :, 0:nsub], in_=recv[:, :, :, 64]), [bA_], [b_rec])
                        for li in range(2):
                            for t in range(nsub):
                                a0, a1 = t * 128, min(t * 128 + 128, nq)
                                ti = (qc0 + a0) // 128
                                col = col_of_lane[li]
                                if dst is None:
                                    o_ap, o_b = a_tok[0:a1 - a0, ti, col:col + 64], b_atok
                                else:
                                    o_ap, o_b = dst[0:a1 - a0, t, col:col + 64], bdst
                                dve(lambda e, li=li, t=t, a0=a0, a1=a1, o_ap=o_ap: e.tensor_scalar(out=o_ap, in0=A_[0:a1 - a0, li, t * 65:t * 65 + 64],
                                                                                                 scalar1=rec[0:a1 - a0, li, t:t + 1], scalar2=None, op0=ALU.mult), [bA_, b_rec], [o_b])
                    return f

                def epi_diff(h):
                    def f(A_, bA_, qb, nsub):
                        (qc0, qpos, nq) = qb
                        nn = min(128, nq)
                        recv = A_[0:nn, :, 0:nsub * 65].rearrange("p a (t c) -> p a t c", c=65)

                        def s1():
                            dve(lambda e: e.reciprocal(out=rec[0:nn, :, 0:nsub], in_=recv[:, :, :, 64]), [bA_], [b_rec])
                            dve(lambda e: e.tensor_scalar(out=rec[0:nn, 1, 0:nsub], in0=rec[0:nn, 1, 0:nsub], scalar1=neglam[0:nn, l:l + 1], scalar2=None, op0=ALU.mult), [b_rec, b_lam], [b_rec])
                            for t in range(nsub):
                                n_ = min(128, nq - t * 128)
                                dve(lambda e, t=t, n_=n_: e.tensor_scalar(out=o0t[0:n_, t, :], in0=A_[0:n_, 0, t * 65:t * 65 + 64], scalar1=rec[0:n_, 0, t:t + 1], scalar2=None, op0=ALU.mult), [bA_, b_rec], [b_o0t])
                            for t in range(nsub):
                                n_ = min(128, nq - t * 128)
                                dve(lambda e, t=t, n_=n_: e.scalar_tensor_tensor(out=otmp[0:n_, t, :], in0=A_[0:n_, 1, t * 65:t * 65 + 64], scalar=rec[0:n_, 1, t:t + 1], in1=o0t[0:n_, t, :], op0=ALU.mult, op1=ALU.add), [bA_, b_rec, b_o0t], [b_otmp])
                            dve(lambda e: e.tensor_tensor(out=sq[0:nn, 0:nsub, :], in0=otmp[0:nn, 0:nsub, :], in1=otmp[0:nn, 0:nsub, :], op=ALU.mult), [b_otmp], [b_sq])
                            dve(lambda e: e.tensor_reduce(out=ssA[0:nn, 0:nsub], in_=sq[0:nn, 0:nsub, :], axis=AX.X, op=ALU.add), [b_sq], [b_ssA])

                        def s2():
                            rstd_of(ssA[0:nn, 0:nsub], nn, 64, b_ssA)

                        def s3():
                            for t in range(nsub):
                                n_ = min(128, nq - t * 128)
                                ti = (qc0 + t * 128) // 128
                                dve(lambda e, t=t, n_=n_, ti=ti: e.scalar_tensor_tensor(out=a_tok[0:n_, ti, h * 64:h * 64 + 64], in0=otmp[0:n_, t, :], scalar=ssA[0:n_, t:t + 1], in1=subln[0:n_, l, :], op0=ALU.mult, op1=ALU.mult),
                                    [b_otmp, b_ssA, b_subln], [b_atok])
                        pending.append((1, s1))
                        pending.append((5, s2))
                        pending.append((6, s3))
                    return f

                with contextlib.ExitStack() as S1:
                    a_tok = sb(S1, [128, NT, D], BF16, "atok")
                    hT = sb(S1, [128, 8, T], BF16, "hT"); b_hT = Buf()
                    gbc = sb(S1, [128, D], F32, "gbc"); b_gbc = Buf()
                    bcast_row(gbc[:], I["norms"], (l * 7 + 0) * D, D, [b_gbc])
                    xt = [sb(S1, [128, D], F32, "xt") for _ in range(2)]; bxt = [Buf(), Buf()]
                    hb = [sb(S1, [128, D], BF16, "hb") for _ in range(2)]; bhb = [Buf(), Buf()]
                    junkb = sb(S1, [128, D], BF16, "junkb"); b_junkb = Buf()
                    ss1 = [sb(S1, [128, 1], F32, "ss1") for _ in range(2)]; bss1 = [Buf(strict=True), Buf(strict=True)]
                    def h_stages(ti, t0, n):
                        k = ti % 2

                        def g0():
                            P.dma("sp", xt[k][0:n, :], xsrc[t0:t0 + n, :], [ybufs[ti]], [bxt[k]])
                            dve(lambda e: e.scalar_tensor_tensor(out=junkb[0:n, :], in0=xt[k][0:n, :], scalar=1.0, in1=xt[k][0:n, :], op0=ALU.mult, op1=ALU.mult, accum_out=ss1[k][0:n, :]), [bxt[k]], [b_junkb, bss1[k]])

                        def g1():
                            rstd_of(ss1[k][0:n, :], n, D, bss1[k])

                        def g2():
                            dve(lambda e: e.scalar_tensor_tensor(out=hb[k][0:n, :], in0=xt[k][0:n, :], scalar=ss1[k][0:n, 0:1], in1=gbc[0:n, :], op0=ALU.mult, op1=ALU.mult), [bxt[k], bss1[k], b_gbc], [bhb[k]])

                        def g3():
                            pt, bpt = next_ptr()
                            for c in range(8):
                                pe(lambda e, c=c: e.transpose(out=pt[:, c * 128:c * 128 + n], in_=hb[k][0:n, c * 128:(c + 1) * 128], identity=ident[0:n, 0:n]), [bhb[k], b_ident], [bpt], inc=(c == 7))
                            ptv = pt[:, :].rearrange("p (c t) -> p c t", c=8)
                            act(lambda e: e.copy(out=hT[:, :, t0:t0 + n], in_=ptv[:, :, 0:n]), [bpt], [b_hT])
                        return [g0, g1, g2, g3]

                    for ti0 in range(0, NT, 2):
                        grp_ = [h_stages(ti, *tiles[ti]) for ti in range(ti0, min(NT, ti0 + 2))]
                        for si in range(4):
                            for stg in grp_:
                                stg[si]()

                    ck("hT")
                    SE = contextlib.ExitStack()
                    EA = sb(SE, [128, 8, 1024], BF16, "EAt"); b_EA = Buf()
                    EB = sb(SE, [128, 4, 1024], BF16, "EBt"); b_EB = Buf()
                    P.dma("sp", EA[:, :, :], EscrA[0:8].rearrange("h p u -> p h u"), (), [b_EA])
                    P.dma("sp", EB[:, :, :], EscrA[8 + 4 * l:12 + 4 * l].rearrange("h p u -> p h u"), (), [b_EB])

                    def proj_fm(dst_fn, wt, bw, wcols, nchunks, rows=128):
                        for q0 in range(0, T, 512):
                            nq = min(512, T - q0)
                            for j0 in range(0, nchunks, 2):
                                pd, bpd = next_dbl()
                                nj = min(2, nchunks - j0)
                                for jj in range(nj):
                                    for c in range(8):
                                        pe(lambda e, jj=jj, c=c, pd=pd, j0=j0: e.matmul(pd[0:rows, jj * 512:jj * 512 + nq], lhsT=wt[:, c, wcols[j0 + jj]:wcols[j0 + jj] + rows], rhs=hT[:, c, q0:q0 + nq], start=(c == 0), stop=(c == 7)),
                                           [bw, b_hT], [bpd], inc=(c == 7 and jj == nj - 1))
                                for jj in range(nj):
                                    dst, bd = dst_fn(j0 + jj, q0, q0 + nq)
                                    act(lambda e, jj=jj, pd=pd, dst=dst: e.copy(out=dst, in_=pd[0:rows, jj * 512:jj * 512 + nq]), [bpd], [bd])

                    for ah in range(2):
                        with contextlib.ExitStack() as SA:
                            wA_ = sb(SA, [128, 8, 768], BF16, "wA"); b_wA_ = Buf()
                            wsrc = wl["w_in"].rearrange("(c p) n -> p c n", p=128)
                            for i3 in range(3):
                                P.dma("sp", wA_[:, :, i3 * 256:(i3 + 1) * 256], wsrc[:, :, i3 * 512 + ah * 256:i3 * 512 + ah * 256 + 256], [bW], [b_wA_])
                            ck("Aw")
                            QT = sb(SA, [128, 2, T], BF16, "QTA"); b_QT = Buf()
                            KT = sb(SA, [128, 2, NK], BF16, "KTA"); b_KT = Buf()
                            VA = sb(SA, [128, NKB, 4, 65], BF16, "VA"); b_VA = Buf()
                            dve(lambda e: e.memset(VA[:, :, :, :].rearrange("p a b c -> p (a b c)"), 1.0), (), [b_VA])
                            kv32 = [sb(SA, [128, 2, 256], F32, "kv32") for _ in range(2)]; bkv32 = [Buf(), Buf()]
                            if not prm:
                                ckb = [sb(SA, [128, 256], BF16, "ckb") for _ in range(2)]; bckb = [Buf(), Buf()]
                                for j in range(NKB_past):
                                    k = j % 2
                                    P.dma("pool", ckb[k][:, :], IA["cache_a_k"][l, j * 128:(j + 1) * 128, ah * 256:(ah + 1) * 256], (), [bckb[k]])
                                    P.dma("pool", VA[:, j, :, 0:64], IA["cache_a_v"][l, j * 128:(j + 1) * 128, ah * 256:(ah + 1) * 256].rearrange("p (h d) -> p h d", h=4), (), [b_VA])
                                    pt, bpt = next_ptr()
                                    for c in range(2):
                                        pe(lambda e, c=c, k=k, pt=pt: e.transpose(out=pt[:, c * 128:(c + 1) * 128], in_=ckb[k][:, c * 128:(c + 1) * 128], identity=ident[:, :]), [bckb[k], b_ident], [bpt], inc=(c == 1))
                                    ptv = pt[:, 0:256].rearrange("p (c t) -> p c t", c=2)
                                    act(lambda e, j=j, ptv=ptv: e.copy(out=KT[:, :, j * 128:(j + 1) * 128], in_=ptv), [bpt], [b_KT])
                            proj_fm(lambda j, c0, c1: (QT[:, j, c0:c1], b_QT), wA_, b_wA_, [0, 128], 2)
                            ck("Aq")
                            proj_fm(lambda j, c0, c1: (KT[:, j, past + c0:past + c1], b_KT), wA_, b_wA_, [256, 384], 2)
                            ck("Ak")
                            for ti, (t0, n) in enumerate(tiles):
                                pd, bpd = next_dbl()
                                for jj in range(2):
                                    for c in range(8):
                                        pe(lambda e, jj=jj, c=c, pd=pd, t0=t0, n=n: e.matmul(pd[0:n, jj * 512:jj * 512 + 256], lhsT=hT[:, c, t0:t0 + n], rhs=wA_[:, c, 256 + jj * 256:512 + jj * 256], start=(c == 0), stop=(c == 7)),
                                           [b_wA_, b_hT], [bpd], inc=(c == 7 and jj == 1))
                                k = ti % 2
                                pdv = pd[:, :].rearrange("p (a b) -> p a b", a=2)
                                if '1' not in DBG:
                                    act(lambda e, k=k, n=n, pdv=pdv: e.copy(out=kv32[k][0:n, :, :], in_=pdv[0:n, :, 0:256]), [bpd], [bkv32[k]])
                                if '2' not in DBG:
                                    dve(lambda e, ti=ti, n=n, pd=pd: e.tensor_copy(out=VA[0:n, NKB_past + ti, :, 0:64], in_=pd[0:n, 512:768].rearrange("p (h d) -> p h d", h=4)), [bpd], [b_VA])
                                if 'D' in DBG and ti == 0 and l == 0 and ah == 0:
                                    P.dma("pool", ydst[384:512, 0:256], kv32[k][:, 0, :], [bkv32[k]], [])
                                    P.dma("pool", ydst[512:640, 0:768], wA_[:, 0, :], [b_wA_], [])
                                if 'O' not in DBG:
                                    P.dma("pool", OA[pfx + "a_k"][l, so, t0:t0 + n, ah * 256:(ah + 1) * 256], kv32[k][0:n, 0, :], [bkv32[k]], [])
                                    P.dma("pool", OA[pfx + "a_v"][l, so, t0:t0 + n, ah * 256:(ah + 1) * 256], kv32[k][0:n, 1, :], [bkv32[k]], [])
                            ck("Aproj")
                            callsA = []
                            for hh in range(4):
                                h = ah * 4 + hh
                                c, r0 = hh // 2, (hh % 2) * 64
                                lanes = []
                                for half in range(2):
                                    rr = r0 + 32 * half
                                    lanes.append(dict(kt=lambda c0, nk, rr=rr, c=c: KT[rr:rr + 32, c, c0:c0 + nk], qt=lambda c0, c1, rr=rr, c=c: QT[rr:rr + 32, c, c0:c1],
                                                      v=lambda kbi, nk, hh=hh: VA[0:nk, kbi, hh, :], E=EA[:, h, :], Eb=b_EA, Ethr=-91, c0=384, tp=(rr, 0)))
                                for qb in qblocks:
                                    callsA.append(("A", lanes, qb, "causal", list(enumerate(kblocks)), [b_QT, b_KT, b_VA], epi_diff(h)))
                            attend_many(callsA)
                            flush_pending()
                            P.barrier()
                            ck("Ahalf")

                    with contextlib.ExitStack() as SB:
                        wB_ = sb(SB, [128, 8, 768], BF16, "wB"); b_wB_ = Buf()
                        wsrc = wl["w_in"].rearrange("(c p) n -> p c n", p=128)
                        P.dma("sp", wB_[:, :, :], wsrc[:, :, 1536:2304], [bW], [b_wB_])
                        pastB = 0 if prm else NBS
                        NKb = pastB + T
                        kbB = [(j * 128, past - pastB + j * 128, 128) for j in range(pastB // 128)] + [(pastB + t0, past + t0, n) for (t0, n) in tiles]
                        QT = sb(SB, [128, 2, T], BF16, "QTB"); b_QT = Buf()
                        KT = sb(SB, [128, 2, NKb], BF16, "KTB"); b_KT = Buf()
                        VB = sb(SB, [128, len(kbB), 4, 65], BF16, "VB"); b_VB = Buf()
                        dve(lambda e: e.memset(VB[:, :, :, :].rearrange("p a b c -> p (a b c)"), 1.0), (), [b_VB])
                        kv32 = [sb(SB, [128, 512], F32, "kv32b") for _ in range(2)]; bkv32 = [Buf(), Buf()]
                        if not prm:
                            ckb = [sb(SB, [128, 256], BF16, "ckbb") for _ in range(2)]; bckb = [Buf(), Buf()]
                            for j in range(pastB // 128):
                                k = j % 2
                                P.dma("pool", ckb[k][:, :], IA["cache_b_k"][l, j * 128:(j + 1) * 128, :], (), [bckb[k]])
                                P.dma("pool", VB[:, j, :, 0:64], IA["cache_b_v"][l, j * 128:(j + 1) * 128, :].rearrange("p (h d) -> p h d", h=4), (), [b_VB])
                                pt, bpt = next_ptr()
                                for c in range(2):
                                    pe(lambda e, c=c, k=k, pt=pt: e.transpose(out=pt[:, c * 128:(c + 1) * 128], in_=ckb[k][:, c * 128:(c + 1) * 128], identity=ident[:, :]), [bckb[k], b_ident], [bpt], inc=(c == 1))
                                ptv = pt[:, 0:256].rearrange("p (c t) -> p c t", c=2)
                                act(lambda e, j=j, ptv=ptv: e.copy(out=KT[:, :, j * 128:(j + 1) * 128], in_=ptv), [bpt], [b_KT])
                        proj_fm(lambda j, c0, c1: (QT[:, j, c0:c1], b_QT), wB_, b_wB_, [0, 128], 2)
                        proj_fm(lambda j, c0, c1: (KT[:, j, pastB + c0:pastB + c1], b_KT), wB_, b_wB_, [256, 384], 2)
                        for ti, (t0, n) in enumerate(tiles):
                            pd, bpd = next_dbl()
                            for c in range(8):
                                pe(lambda e, c=c, pd=pd, t0=t0, n=n: e.matmul(pd[0:n, 0:512], lhsT=hT[:, c, t0:t0 + n], rhs=wB_[:, c, 256:768], start=(c == 0), stop=(c == 7)), [b_wB_, b_hT], [bpd], inc=(c == 7))
                            k = ti % 2
                            act(lambda e, k=k, n=n, pd=pd: e.copy(out=kv32[k][0:n, :], in_=pd[0:n, 0:512]), [bpd], [bkv32[k]])
                            dve(lambda e, ti=ti, n=n, pd=pd: e.tensor_copy(out=VB[0:n, pastB // 128 + ti, :, 0:64], in_=pd[0:n, 256:512].rearrange("p (h d) -> p h d", h=4)), [bpd], [b_VB])
                            if prm:
                                if t0 >= SEQ - NBP:
                                    r0_ = t0 - (SEQ - NBP)
                                    P.dma("pool", OA["p_b_k"][l, so, r0_:r0_ + n, :], kv32[k][0:n, 0:256], [bkv32[k]], [])
                                    P.dma("pool", OA["p_b_v"][l, so, r0_:r0_ + n, :], kv32[k][0:n, 256:512], [bkv32[k]], [])
                            else:
                                P.dma("pool", OA["s_b_k"][l, 0, t0:t0 + n, :], kv32[k][0:n, 0:256], [bkv32[k]], [])
                                P.dma("pool", OA["s_b_v"][l, 0, t0:t0 + n, :], kv32[k][0:n, 256:512], [bkv32[k]], [])
                        callsB = []
                        for hp in range(2):
                            lanes = []
                            for li in range(2):
                                h = hp * 2 + li
                                r0 = li * 64
                                lanes.append(dict(kt=lambda c0, nk, r0=r0, hp=hp: KT[r0:r0 + 64, hp, c0:c0 + nk], qt=lambda c0, c1, r0=r0, hp=hp: QT[r0:r0 + 64, hp, c0:c1],
                                                  v=lambda kbi, nk, h=h: VB[0:nk, kbi, h, :], E=EB[:, h, :], Eb=b_EB, Ethr=-128, c0=384, tp=None))
                            for qb in qblocks:
                                callsB.append(("B", lanes, qb, "band", list(enumerate(kbB)), [b_QT, b_KT, b_VB], epi_plain([512 + (hp * 2) * 64, 512 + (hp * 2 + 1) * 64])))
                        attend_many(callsB)
                        flush_pending()
                        P.barrier()

                    ck("B")
                    SE.close()
                    with contextlib.ExitStack() as SC:
                        wC_ = sb(SC, [128, 8, 416], BF16, "wC"); b_wC_ = Buf()
                        wsrc = wl["w_in"].rearrange("(c p) n -> p c n", p=128)
                        P.dma("sp", wC_[:, :, :], wsrc[:, :, 2304:2720], [bW], [b_wC_])
                        wuq = sb(SC, [128, 2, 384], BF16, "wuq"); b_wuq = Buf()
                        P.dma("sp", wuq[:, :, :], wl["mla_w_uq"].rearrange("(c p) n -> p c n", p=128), [bW], [b_wuq])
                        wukv = sb(SC, [128, 512], BF16, "wukv"); b_wukv = Buf()
                        P.dma("sp", wukv[:, :], wl["mla_w_ukv"], [bW], [b_wukv])
                        gq = sb(SC, [128, 256], F32, "gq"); b_gq = Buf()
                        gkv = sb(SC, [128, 128], F32, "gkv"); b_gkv = Buf()
                        bcast_row(gq[:], I["mla_q_norm"], l * 256, 256, [b_gq])
                        bcast_row(gkv[:], I["mla_kv_norm"], l * 128, 128, [b_gkv])
                        cqnT = sb(SC, [128, 2, T], BF16, "cqnT"); b_cqnT = Buf()
                        latT = sb(SC, [128, NK], BF16, "latT"); b_latT = Buf()
                        kT = sb(SC, [96, 4, NK], BF16, "kTC"); b_kT = Buf()
                        qT = sb(SC, [96, 4, T], BF16, "qTC"); b_qT = Buf()
                        VC = sb(SC, [128, NKB, 4, 65], BF16, "VC"); b_VC = Buf()
                        dve(lambda e: e.memset(VC[:, :, :, :].rearrange("p a b c -> p (a b c)"), 1.0), (), [b_VC])
                        rp = sb(SC, [128, NT, 128], F32, "rope"); b_rp = Buf()
                        for ti, (t0, n) in enumerate(tiles):
                            P.dma("sp", rp[0:n, ti, :], rope_h.ap()[t0:t0 + n, :], (), [b_rp])
                        c32 = [sb(SC, [128, 416], F32, "c32") for _ in range(4)]; bc32 = [Buf() for _ in range(4)]
                        ssc = [sb(SC, [128, 2], F32, "ssc") for _ in range(4)]; bssc = [Buf(strict=True) for _ in range(4)]
                        junkc = sb(SC, [128, 384], BF16, "junkc"); b_junkc = Buf()
                        cqn = [sb(SC, [128, 256], BF16, "cqn") for _ in range(4)]; bcqn = [Buf() for _ in range(4)]
                        lat32 = [sb(SC, [128, 128], F32, "lat32") for _ in range(4)]; blat32 = [Buf() for _ in range(4)]
                        latb = [sb(SC, [128, 160], BF16, "latb") for _ in range(4)]; blatb = [Buf() for _ in range(4)]
                        kpe32 = [sb(SC, [128, 32], F32, "kpe32") for _ in range(4)]; bkpe32 = [Buf() for _ in range(4)]
                        rt = sb(SC, [128, 4, 4, 16], F32, "rt"); b_rt = Buf()
                        q32 = [sb(SC, [128, 4, 96], F32, "q32") for _ in range(4)]; bq32 = [Buf() for _ in range(4)]
                        qb16 = [sb(SC, [128, 4, 96], BF16, "qb16") for _ in range(4)]; bqb16 = [Buf() for _ in range(4)]

                        def rope_apply(dst1, dst2, x1, x2, cs_, sn_, shape_n, rbufs_, wbufs_):
                            nh = shape_n
                            dve(lambda e: e.tensor_tensor(out=rt[0:nh[0], 0, 0:nh[1], :], in0=x1, in1=cs_, op=ALU.mult), rbufs_, [b_rt])
                            dve(lambda e: e.tensor_tensor(out=rt[0:nh[0], 1, 0:nh[1], :], in0=x2, in1=sn_, op=ALU.mult), rbufs_, [b_rt])
                            dve(lambda e: e.tensor_tensor(out=rt[0:nh[0], 2, 0:nh[1], :], in0=x2, in1=cs_, op=ALU.mult), rbufs_, [b_rt])
                            dve(lambda e: e.tensor_tensor(out=rt[0:nh[0], 3, 0:nh[1], :], in0=x1, in1=sn_, op=ALU.mult), rbufs_, [b_rt])
                            dve(lambda e: e.tensor_tensor(out=dst1, in0=rt[0:nh[0], 0, 0:nh[1], :], in1=rt[0:nh[0], 1, 0:nh[1], :], op=ALU.subtract), [b_rt], wbufs_)
                            dve(lambda e: e.tensor_tensor(out=dst2, in0=rt[0:nh[0], 2, 0:nh[1], :], in1=rt[0:nh[0], 3, 0:nh[1], :], op=ALU.add), [b_rt], wbufs_)

                        if not prm:
                            for j in range(NKB_past):
                                k = j % 2
                                P.dma("pool", latb[k][:, 0:128], IA["cache_c_latent"][l, j * 128:(j + 1) * 128, :], (), [blatb[k]])
                                P.dma("pool", latb[k][:, 128:160], IA["cache_c_rope_k"][l, j * 128:(j + 1) * 128, :], (), [blatb[k]])
                                pt, bpt = next_ptr()
                                pe(lambda e, k=k, pt=pt: e.transpose(out=pt[:, 0:128], in_=latb[k][:, 0:128], identity=ident[:, :]), [blatb[k], b_ident], [bpt], inc=False)
                                pe(lambda e, k=k, pt=pt: e.transpose(out=pt[0:32, 128:256], in_=latb[k][:, 128:160], identity=ident[:, :]), [blatb[k], b_ident], [bpt])
                                act(lambda e, j=j, pt=pt: e.copy(out=latT[:, j * 128:(j + 1) * 128], in_=pt[:, 0:128]), [bpt], [b_latT])
                                for h in range(4):
                                    dve(lambda e, j=j, h=h, pt=pt: e.tensor_copy(out=kT[64:96, h, j * 128:(j + 1) * 128], in_=pt[0:32, 128:256]), [bpt], [b_kT])
                        rt2 = [rt] + [sb(SC, [128, 4, 4, 16], F32, "rtb") for _ in range(3)]; b_rt2 = [b_rt, Buf(), Buf(), Buf()]
                        junkc2 = [junkc] * 4; b_junkc2 = [b_junkc] * 4

                        def rope2(k, dst1, dst2, x1, x2, cs_, sn_, nh, rbufs_, wbufs_):
                            R_, bR = rt2[k], b_rt2[k]
                            dve(lambda e: e.tensor_tensor(out=R_[0:nh[0], 0, 0:nh[1], :], in0=x1, in1=cs_, op=ALU.mult), rbufs_, [bR])
                            dve(lambda e: e.tensor_tensor(out=R_[0:nh[0], 1, 0:nh[1], :], in0=x2, in1=sn_, op=ALU.mult), rbufs_, [bR])
                            dve(lambda e: e.tensor_tensor(out=R_[0:nh[0], 2, 0:nh[1], :], in0=x2, in1=cs_, op=ALU.mult), rbufs_, [bR])
                            dve(lambda e: e.tensor_tensor(out=R_[0:nh[0], 3, 0:nh[1], :], in0=x1, in1=sn_, op=ALU.mult), rbufs_, [bR])
                            dve(lambda e: e.tensor_tensor(out=dst1, in0=R_[0:nh[0], 0, 0:nh[1], :], in1=R_[0:nh[0], 1, 0:nh[1], :], op=ALU.subtract), [bR], wbufs_)
                            dve(lambda e: e.tensor_tensor(out=dst2, in0=R_[0:nh[0], 2, 0:nh[1], :], in1=R_[0:nh[0], 3, 0:nh[1], :], op=ALU.add), [bR], wbufs_)

                        def c_stages(ti, t0, n):
                            k = ti % 4
                            st = {}

                            def g0():
                                pd, bpd = next_dbl()
                                for c in range(8):
                                    pe(lambda e, c=c: e.matmul(pd[0:n, 0:416], lhsT=hT[:, c, t0:t0 + n], rhs=wC_[:, c, :], start=(c == 0), stop=(c == 7)), [b_wC_, b_hT], [bpd], inc=(c == 7))
                                act(lambda e: e.copy(out=c32[k][0:n, :], in_=pd[0:n, 0:416]), [bpd], [bc32[k]])

                            def g1():
                                dve(lambda e: e.scalar_tensor_tensor(out=junkc2[k][0:n, 0:256], in0=c32[k][0:n, 0:256], scalar=1.0 / 256, in1=c32[k][0:n, 0:256], op0=ALU.mult, op1=ALU.mult, accum_out=ssc[k][0:n, 0:1]), [bc32[k]], [b_junkc2[k], bssc[k]])
                                dve(lambda e: e.scalar_tensor_tensor(out=junkc2[k][0:n, 0:128], in0=c32[k][0:n, 256:384], scalar=1.0 / 128, in1=c32[k][0:n, 256:384], op0=ALU.mult, op1=ALU.mult, accum_out=ssc[k][0:n, 1:2]), [bc32[k]], [b_junkc2[k], bssc[k]])
                                rstd_of(ssc[k][0:n, 0:2], n, 1, bssc[k])
                                rope2(k, kpe32[k][0:n, 0:16].rearrange("p (a d) -> p a d", a=1), kpe32[k][0:n, 16:32].rearrange("p (a d) -> p a d", a=1),
                                      c32[k][0:n, 384:400].rearrange("p (a d) -> p a d", a=1), c32[k][0:n, 400:416].rearrange("p (a d) -> p a d", a=1),
                                      rp[0:n, ti, 0:16].rearrange("p (a d) -> p a d", a=1), rp[0:n, ti, 64:80].rearrange("p (a d) -> p a d", a=1), (n, 1), [bc32[k], b_rp], [bkpe32[k]])
                                P.dma("pool", OA[pfx + "kpe"][l, so, t0:t0 + n, :], kpe32[k][0:n, :], [bkpe32[k]], [])
                                dve(lambda e: e.tensor_copy(out=latb[k][0:n, 128:160], in_=kpe32[k][0:n, :]), [bkpe32[k]], [blatb[k]])

                            def g2():
                                dve(lambda e: e.scalar_tensor_tensor(out=cqn[k][0:n, :], in0=c32[k][0:n, 0:256], scalar=ssc[k][0:n, 0:1], in1=gq[0:n, :], op0=ALU.mult, op1=ALU.mult), [bc32[k], bssc[k], b_gq], [bcqn[k]])
                                dve(lambda e: e.scalar_tensor_tensor(out=lat32[k][0:n, :], in0=c32[k][0:n, 256:384], scalar=ssc[k][0:n, 1:2], in1=gkv[0:n, :], op0=ALU.mult, op1=ALU.mult), [bc32[k], bssc[k], b_gkv], [blat32[k]])
                                P.dma("pool", OA[pfx + "lat"][l, so, t0:t0 + n, :], lat32[k][0:n, :], [blat32[k]], [])
                                dve(lambda e: e.tensor_copy(out=latb[k][0:n, 0:128], in_=lat32[k][0:n, :]), [blat32[k]], [blatb[k]])

                            def g3():
                                pt, bpt = next_ptr()
                                for c in range(2):
                                    pe(lambda e, c=c: e.transpose(out=pt[:, c * 128:c * 128 + n], in_=cqn[k][0:n, c * 128:(c + 1) * 128], identity=ident[0:n, 0:n]), [bcqn[k], b_ident], [bpt], inc=False)
                                pe(lambda e: e.transpose(out=pt[:, 256:256 + n], in_=latb[k][0:n, 0:128], identity=ident[0:n, 0:n]), [blatb[k], b_ident], [bpt], inc=False)
                                pe(lambda e: e.transpose(out=pt[0:32, 384:384 + n], in_=latb[k][0:n, 128:160], identity=ident[0:n, 0:n]), [blatb[k], b_ident], [bpt])
                                ptv = pt[:, 0:256].rearrange("p (c t) -> p c t", c=2)
                                act(lambda e: e.copy(out=cqnT[:, :, t0:t0 + n], in_=ptv[:, :, 0:n]), [bpt], [b_cqnT])
                                act(lambda e: e.copy(out=latT[:, past + t0:past + t0 + n], in_=pt[:, 256:256 + n]), [bpt], [b_latT])
                                for h in range(4):
                                    dve(lambda e, h=h: e.tensor_copy(out=kT[64:96, h, past + t0:past + t0 + n], in_=pt[0:32, 384:384 + n]), [bpt], [b_kT])

                            def g4():
                                pd, bpd = next_dbl()
                                for c in range(2):
                                    pe(lambda e, c=c: e.matmul(pd[0:n, 0:384], lhsT=cqnT[:, c, t0:t0 + n], rhs=wuq[:, c, :], start=(c == 0), stop=(c == 1)), [b_cqnT, b_wuq], [bpd], inc=(c == 1))
                                act(lambda e: e.copy(out=q32[k][0:n, :, :], in_=pd[0:n, 0:384].rearrange("p (h d) -> p h d", h=4)), [bpd], [bq32[k]])

                            def g5():
                                dve(lambda e: e.tensor_copy(out=qb16[k][0:n, :, 0:64], in_=q32[k][0:n, :, 0:64]), [bq32[k]], [bqb16[k]])
                                rope2(k, qb16[k][0:n, :, 64:80], qb16[k][0:n, :, 80:96], q32[k][0:n, :, 64:80], q32[k][0:n, :, 80:96],
                                      rp[0:n, ti, 0:64].rearrange("p (h d) -> p h d", h=4), rp[0:n, ti, 64:128].rearrange("p (h d) -> p h d", h=4), (n, 4), [bq32[k], b_rp], [bqb16[k]])

                            def g6():
                                pt, bpt = next_ptr()
                                for h in range(4):
                                    pe(lambda e, h=h: e.transpose(out=pt[0:96, h * 128:h * 128 + n], in_=qb16[k][0:n, h, :], identity=ident[0:n, 0:n]), [bqb16[k], b_ident], [bpt], inc=(h == 3))
                                ptv = pt[:, 0:512].rearrange("p (c t) -> p c t", c=4)
                                act(lambda e: e.copy(out=qT[:, :, t0:t0 + n], in_=ptv[0:96, :, 0:n]), [bpt], [b_qT])
                            return [g0, g1, g2, g3, g4, g5, g6]

                        for ti0 in range(0, NT, 4):
                            grp_ = [c_stages(ti, *tiles[ti]) for ti in range(ti0, min(NT, ti0 + 4))]
                            for si in range(7):
                                for stg in grp_:
                                    stg[si]()
                        for c0 in range(0, NK, 512):
                            nn_ = min(512, NK - c0)
                            for hp in range(2):
                                pd, bpd = next_dbl()
                                for jj in range(2):
                                    h = hp * 2 + jj
                                    pe(lambda e, jj=jj, h=h, pd=pd, c0=c0, nn_=nn_: e.matmul(pd[0:64, jj * 512:jj * 512 + nn_], lhsT=wukv[:, h * 128:h * 128 + 64], rhs=latT[:, c0:c0 + nn_], start=True, stop=True), [b_wukv, b_latT], [bpd], inc=(jj == 1))
                                pdv = pd[:, :].rearrange("p (a b) -> p a b", a=2)
                                act(lambda e, hp=hp, pdv=pdv, c0=c0, nn_=nn_: e.copy(out=kT[0:64, hp * 2:hp * 2 + 2, c0:c0 + nn_], in_=pdv[0:64, :, 0:nn_]), [bpd], [b_kT])
                        wv_ = wukv[:, :].rearrange("p (h d) -> p h d", h=4)
                        for kbi, (kc0, kpos, nk) in enumerate(kblocks):
                            pd, bpd = next_dbl()
                            pe(lambda e, pd=pd, kc0=kc0, nk=nk: e.matmul(pd[0:nk, 0:256].rearrange("p (h d) -> p h d", h=4), lhsT=latT[:, kc0:kc0 + nk], rhs=wv_[:, :, 64:128], start=True, stop=True), [b_wukv, b_latT], [bpd])
                            act(lambda e, kbi=kbi, nk=nk, pd=pd: e.copy(out=VC[0:nk, kbi, :, 0:64], in_=pd[0:nk, 0:256].rearrange("p (h d) -> p h d", h=4)), [bpd], [b_VC])
                        if 'Q' in DBG and l == 0:
                            for hq in range(4):
                                P.dma("pool", ydst[0:96, hq * 128:(hq + 1) * 128], qT[:, hq, 0:128], [b_qT], [])
                                P.dma("pool", ydst[128:224, hq * 128:(hq + 1) * 128], kT[:, hq, 0:128], [b_kT], [])
                            P.dma("pool", ydst[256:384, 0:260], VC[:, 0, :, :].rearrange("p a b -> p (a b)"), [b_VC], [])
                            P.dma("pool", ydst[384:512, 0:128], latT[:, 0:128], [b_latT], [])
                        callsC = []
                        for hp in range(2):
                            lanes = []
                            for li in range(2):
                                h = hp * 2 + li
                                lanes.append(dict(kt=lambda c0, nk, h=h: kT[0:96, h, c0:c0 + nk], qt=lambda c0, c1, h=h: qT[0:96, h, c0:c1], v=lambda kbi, nk, h=h: VC[0:nk, kbi, h, :], E=None, tp=None))
                            for qb in qblocks:
                                callsC.append(("C", lanes, qb, "causal", list(enumerate(kblocks)), [b_qT, b_kT, b_VC], epi_plain([768 + (hp * 2) * 64, 768 + (hp * 2 + 1) * 64])))
                        attend_many(callsC)
                        flush_pending()
                        P.barrier()
                    with contextlib.ExitStack() as SO:
                        wout = sb(SO, [128, 8, D], BF16, "wout"); b_wout = Buf()
                        P.dma("sp", wout[:, :, :], wl["w_out"].rearrange("(c p) n -> p c n", p=128), [bW], [b_wout])
                        gmp = sb(SO, [128, D], F32, "gmp"); b_gmp = Buf()
                        bcast_row(gmp[:], I["norms"], (l * 7 + 1) * D, D, [b_gmp])
                        xo_ = [sb(SO, [128, D], F32, "xo") for _ in range(2)]; bxo = [Buf(), Buf()]
                        yo_ = [sb(SO, [128, D], F32, "yo") for _ in range(2)]; byo = [Buf(), Buf()]
                        aT = [sb(SO, [128, 8, 128], BF16, "aT") for _ in range(2)]; baT = [Buf(), Buf()]
                        junko = sb(SO, [128, D], BF16, "junko"); b_junko = Buf()
                        sso = sb(SO, [128, 2], F32, "sso"); b_sso = Buf(strict=True)
                        b_sso2 = [Buf(strict=True), Buf(strict=True)]

                        def wo_xload(ti):
                            (t0, n) = tiles[ti]
                            k = ti % 2
                            P.dma("sp", xo_[k][0:n, :], xsrc[t0:t0 + n, :], [ybufs[ti]], [bxo[k]])

                        def wo_prep(ti):
                            (t0, n) = tiles[ti]
                            k = ti % 2
                            pt, bpt = next_ptr()
                            for c in range(8):
                                pe(lambda e, c=c: e.transpose(out=pt[:, c * 128:c * 128 + n], in_=a_tok[0:n, ti, c * 128:(c + 1) * 128], identity=ident[0:n, 0:n]), [b_atok, b_ident], [bpt], inc=(c == 7))
                            ptv = pt[:, :].rearrange("p (c t) -> p c t", c=8)
                            act(lambda e: e.copy(out=aT[k][:, :, 0:n], in_=ptv[:, :, 0:n]), [bpt], [baT[k]])

                        for ti in range(min(2, NT)):
                            wo_xload(ti)
                            wo_prep(ti)
                        for ti, (t0, n) in enumerate(tiles):
                            k = ti % 2
                            pd, bpd = next_dbl()
                            for hf in range(2):
                                for c in range(8):
                                    pe(lambda e, hf=hf, c=c, n=n, k=k, pd=pd: e.matmul(pd[0:n, hf * 512:(hf + 1) * 512], lhsT=aT[k][:, c, 0:n], rhs=wout[:, c, hf * 512:(hf + 1) * 512], start=(c == 0), stop=(c == 7)), [baT[k], b_wout], [bpd], inc=(c == 7 and hf == 1))
                            act(lambda e, k=k, n=n, pd=pd: e.copy(out=yo_[k][0:n, :], in_=pd[0:n, :]), [bpd], [byo[k]])
                            if ti + 2 < NT:
                                wo_prep(ti + 2)
                            dve(lambda e, k=k, n=n: e.scalar_tensor_tensor(out=junko[0:n, :], in0=yo_[k][0:n, :], scalar=1.0, in1=yo_[k][0:n, :], op0=ALU.mult, op1=ALU.mult, accum_out=sso[0:n, k:k + 1]), [byo[k]], [b_junko, b_sso2[k]])
                            rstd_of(sso[0:n, k:k + 1], n, D, b_sso2[k])
                            dve(lambda e, k=k, n=n: e.scalar_tensor_tensor(out=yo_[k][0:n, :], in0=yo_[k][0:n, :], scalar=sso[0:n, k:k + 1], in1=gmp[0:n, :], op0=ALU.mult, op1=ALU.mult), [byo[k], b_sso2[k], b_gmp], [byo[k]])
                            P.op("pool", lambda e, k=k, n=n: e.tensor_tensor(out=xo_[k][0:n, :], in0=xo_[k][0:n, :], in1=yo_[k][0:n, :], op=ALU.add), [byo[k], bxo[k]], [bxo[k]])
                            P.dma("pool", ydst[t0:t0 + n, :], xo_[k][0:n, :], [bxo[k]], [ybufs[ti]])
                            if ti + 2 < NT:
                                wo_xload(ti + 2)
                        P.barrier()
                    P.barrier()

                ck("C")
                with contextlib.ExitStack() as S2:
                    wxq = sb(S2, [128, 8, 256], BF16, "wxq"); b_wxq = Buf()
                    wxo = sb(S2, [128, 2, D], BF16, "wxo"); b_wxo = Buf()
                    gb = sb(S2, [128, 4, D], F32, "gb"); b_gb = Buf()
                    cw = sb(S2, [128, NFC, 4], F32, "cw"); b_cw = Buf()
                    mkT = sb(S2, [128, 2, MEM], BF16, "mkT"); b_mkT = Buf()
                    MV = sb(S2, [128, 2, 4, 65], BF16, "MV"); b_MV = Buf()
                    dve(lambda e: e.memset(MV[:, :, :, :].rearrange("p a b c -> p (a b c)"), 1.0), (), [b_MV])
                    wd_all = sb(S2, [128, NFC, D], BF16, "wd_all"); b_wd = Buf()
                    wds = wl["w_down"].rearrange("(f p) n -> p f n", p=128)
                    junkb = sb(S2, [128, D], BF16, "junkb2"); b_junkb = Buf()
                    ss4 = sb(S2, [128, 4], F32, "ss4"); b_ss4 = Buf(strict=True)
                    hb4 = sb(S2, [128, 4, D], BF16, "hb4"); b_hb4 = [Buf() for _ in range(4)]
                    if prm:
                        with contextlib.ExitStack() as SM:
                            wmk = sb(SM, [128, 8, 512], BF16, "wmk"); b_wmk = Buf()
                            gm = sb(SM, [128, D], F32, "gm"); b_gm = Buf()
                            mT = sb(SM, [128, 8, MEM], BF16, "mT"); b_mT = Buf()
                            m32 = sb(SM, [128, 512], F32, "m32"); b_m32 = Buf()
                            xm = [sb(SM, [128, D], F32, "xm") for _ in range(2)]; bxm = [Buf(), Buf()]
                            for mi in range(2):
                                P.dma("sp", xm[mi][:, :], IA["mem_prompt"][s, mi * 128:(mi + 1) * 128, :], (), [bxm[mi]])
                            bcast_row(gm[:], I["norms"], (l * 7 + 6) * D, D, [b_gm])
                            P.dma("sp", wmk[:, :, 0:256], wl["w_mk"].rearrange("(c p) n -> p c n", p=128), [bW], [b_wmk])
                            P.dma("sp", wmk[:, :, 256:512], wl["w_mv"].rearrange("(c p) n -> p c n", p=128), [bW], [b_wmk])
                            for mi in range(2):
                                k = mi % 2
                                dve(lambda e, k=k, mi=mi: e.scalar_tensor_tensor(out=junkb[:, :], in0=xm[k][:, :], scalar=1.0, in1=xm[k][:, :], op0=ALU.mult, op1=ALU.mult, accum_out=ss4[:, mi:mi + 1]), [bxm[k]], [b_junkb, b_ss4])
                            rstd_of(ss4[:, 0:2], 128, D, b_ss4)
                            for mi in range(2):
                                k = mi % 2
                                dve(lambda e, k=k, mi=mi: e.scalar_tensor_tensor(out=hb4[:, mi, :], in0=xm[k][:, :], scalar=ss4[:, mi:mi + 1], in1=gm[:, :], op0=ALU.mult, op1=ALU.mult), [bxm[k], b_ss4, b_gm], [b_hb4[mi]])
                                pt, bpt = next_ptr()
                                for c in range(8):
                                    pe(lambda e, c=c, pt=pt, mi=mi: e.transpose(out=pt[:, c * 128:(c + 1) * 128], in_=hb4[:, mi, c * 128:(c + 1) * 128], identity=ident[:, :]), [b_hb4[mi], b_ident], [bpt], inc=(c == 7))
                                ptv = pt[:, :].rearrange("p (c t) -> p c t", c=8)
                                act(lambda e, mi=mi, ptv=ptv: e.copy(out=mT[:, :, mi * 128:(mi + 1) * 128], in_=ptv), [bpt], [b_mT])
                            for mi in range(2):
                                pd, bpd = next_dbl()
                                for c in range(8):
                                    pe(lambda e, c=c, pd=pd, mi=mi: e.matmul(pd[:, 0:512], lhsT=mT[:, c, mi * 128:(mi + 1) * 128], rhs=wmk[:, c, :], start=(c == 0), stop=(c == 7)), [b_mT, b_wmk], [bpd], inc=(c == 7))
                                act(lambda e, pd=pd: e.copy(out=m32[:, :], in_=pd[:, 0:512]), [bpd], [b_m32])
                                dve(lambda e, mi=mi, pd=pd: e.tensor_copy(out=MV[:, mi, :, 0:64], in_=pd[:, 256:512].rearrange("p (h d) -> p h d", h=4)), [bpd], [b_MV])
                                P.dma("pool", OA["p_mk"][l, s, mi * 128:(mi + 1) * 128, :], m32[:, 0:256], [b_m32], [])
                                P.dma("pool", OA["p_mv"][l, s, mi * 128:(mi + 1) * 128, :], m32[:, 256:512], [b_m32], [])
                            pd, bpd = next_dbl()
                            for j in range(2):
                                for c in range(8):
                                    pe(lambda e, j=j, c=c, pd=pd: e.matmul(pd[:, j * 512:j * 512 + MEM], lhsT=wmk[:, c, j * 128:(j + 1) * 128], rhs=mT[:, c, :], start=(c == 0), stop=(c == 7)), [b_mT, b_wmk], [bpd], inc=(c == 7 and j == 1))
                            pdv = pd[:, :].rearrange("p (a b) -> p a b", a=2)
                            act(lambda e, pdv=pdv: e.copy(out=mkT[:, :, :], in_=pdv[:, :, 0:MEM]), [bpd], [b_mkT])
                            P.barrier()
                    else:
                        with contextlib.ExitStack() as SM:
                            ckb = sb(SM, [128, 256], BF16, "ckbm"); bckb = Buf()
                            for mi in range(2):
                                P.dma("pool", ckb[:, :], IA["cache_mem_k"][l, mi * 128:(mi + 1) * 128, :], (), [bckb])
                                P.dma("pool", MV[:, mi, :, 0:64], IA["cache_mem_v"][l, mi * 128:(mi + 1) * 128, :].rearrange("p (h d) -> p h d", h=4), (), [b_MV])
                                pt, bpt = next_ptr()
                                for c in range(2):
                                    pe(lambda e, c=c, pt=pt: e.transpose(out=pt[:, c * 128:(c + 1) * 128], in_=ckb[:, c * 128:(c + 1) * 128], identity=ident[:, :]), [bckb, b_ident], [bpt], inc=(c == 1))
                                ptv = pt[:, 0:256].rearrange("p (c t) -> p c t", c=2)
                                act(lambda e, mi=mi, ptv=ptv: e.copy(out=mkT[:, :, mi * 128:(mi + 1) * 128], in_=ptv), [bpt], [b_mkT])
                            P.barrier()
                    ck("mem")
                    P.dma("sp", wxq[:, :, :], wl["w_xq"].rearrange("(c p) n -> p c n", p=128), [bW], [b_wxq])
                    P.dma("sp", wxo[:, :, :], wl["w_xo"].rearrange("(c p) n -> p c n", p=128), [bW], [b_wxo])
                    for gi, ni in enumerate((2, 3, 4, 5)):
                        bcast_row(gb[:, gi, :], I["norms"], (l * 7 + ni) * D, D, [b_gb])
                    ysb = [sb(S2, [128, D], F32, "ysb") for _ in range(2)]; b_ysb = [Buf(), Buf()]
                    yrot = [0]
                    xa = sb(S2, [128, 4, 256], BF16, "xa"); b_xa = Buf()
                    xaT = sb(S2, [128, 2, 512], BF16, "xaT"); b_xaT = Buf()
                    xblk2 = [sb(S2, [128, 4, D], F32, "xblk") for _ in range(2)]
                    b_xb2 = [[Buf() for _ in range(4)] for _ in range(2)]
                    hT2 = sb(S2, [128, 8, 512], BF16, "hT2"); b_hT2 = Buf()
                    qxT = sb(S2, [128, 2, 512], BF16, "qxT"); b_qxT = Buf()
                    gs = [sb(S2, [128, 514], F32, "gs") for _ in range(2)]; b_gs = [Buf(), Buf()]
                    halo = sb(S2, [128, NFC, 2], F32, "halo"); b_halo = Buf()
                    cc = [sb(S2, [128, 512], F32, "cc") for _ in range(2)]; b_cc = [Buf(), Buf()]
                    sl = [sb(S2, [128, 512], F32, "sl") for _ in range(2)]; b_sl = [Buf(), Buf()]
                    aTf = sb(S2, [128, NFC, 512], BF16, "aTf"); b_aTf = Buf()
                    wg = [sb(S2, [128, 2, 8, 128], BF16, "wg") for _ in range(3)]; bwg = [Buf() for _ in range(3)]
                    ssp = sb(S2, [128, 4], F32, "ssp"); b_ssp = Buf(strict=True)
                    if 'M' in DBG:
                        print("PH2 sbuf remaining", nc.sbuf_bytes_remaining)

                    def post_residual(pd, bpd, n, t, gidx, xblk, b_xb):
                        k = yrot[0]; yrot[0] = 1 - k
                        Y, bY = ysb[k], b_ysb[k]
                        act(lambda e: e.copy(out=Y[0:n, :], in_=pd[0:n, :]), [bpd], [bY])
                        dve(lambda e: e.scalar_tensor_tensor(out=junkb[0:n, :], in0=Y[0:n, :], scalar=1.0, in1=Y[0:n, :], op0=ALU.mult, op1=ALU.mult, accum_out=ssp[0:n, t:t + 1]), [bY], [b_junkb, b_ssp])
                        rstd_of(ssp[0:n, t:t + 1], n, D, b_ssp)
                        dve(lambda e: e.scalar_tensor_tensor(out=Y[0:n, :], in0=Y[0:n, :], scalar=ssp[0:n, t:t + 1], in1=gb[0:n, gidx, :], op0=ALU.mult, op1=ALU.mult), [bY, b_ssp, b_gb], [bY])
                        P.op("pool", lambda e: e.tensor_tensor(out=xblk[0:n, t, :], in0=xblk[0:n, t, :], in1=Y[0:n, :], op=ALU.add), [bY, b_xb[t]], [b_xb[t]])

                    def pre_norm_dve(subt, gidx, xblk, b_xb):
                        nt = len(subt)
                        nn = subt[0][1]
                        for t, (a0, n) in enumerate(subt):
                            dve(lambda e, t=t, n=n: e.scalar_tensor_tensor(out=junkb[0:n, :], in0=xblk[0:n, t, :], scalar=1.0, in1=xblk[0:n, t, :], op0=ALU.mult, op1=ALU.mult, accum_out=ss4[0:n, t:t + 1]), [b_xb[t]], [b_junkb, b_ss4])
                        rstd_of(ss4[0:nn, 0:nt], nn, D, b_ss4)
                        for t, (a0, n) in enumerate(subt):
                            dve(lambda e, t=t, n=n: e.scalar_tensor_tensor(out=hb4[0:n, t, :], in0=xblk[0:n, t, :], scalar=ss4[0:n, t:t + 1], in1=gb[0:n, gidx, :], op0=ALU.mult, op1=ALU.mult), [b_xb[t], b_ss4, b_gb], [b_hb4[t]])

                    def pre_norm_pe(subt):
                        for t, (a0, n) in enumerate(subt):
                            pt, bpt = next_ptr()
                            for c in range(8):
                                pe(lambda e, c=c, pt=pt, t=t, n=n: e.transpose(out=pt[:, c * 128:c * 128 + n], in_=hb4[0:n, t, c * 128:(c + 1) * 128], identity=ident[0:n, 0:n]), [b_hb4[t], b_ident], [bpt], inc=(c == 7))
                            ptv = pt[:, :].rearrange("p (c t) -> p c t", c=8)
                            act(lambda e, ptv=ptv, a0=a0, n=n: e.copy(out=hT2[:, :, a0:a0 + n], in_=ptv[:, :, 0:n]), [bpt], [b_hT2])

                    def head_dve(bi):
                        (qc0_, qpos_, nq_) = qblocks[bi]
                        subt_ = [(a0, min(128, nq_ - a0)) for a0 in range(0, nq_, 128)]
                        X_, bX_ = xblk2[bi % 2], b_xb2[bi % 2]
                        for t, (a0, n) in enumerate(subt_):
                            ti = (qc0_ + a0) // 128
                            P.dma("sp", X_[0:n, t, :], ydst[qc0_ + a0:qc0_ + a0 + n, :], [ybufs[ti]], [bX_[t]])
                        pre_norm_dve(subt_, 0, X_, bX_)
                        return subt_

                    drot = [0]

                    def next_dbl3():
                        i = drot[0]; drot[0] = (i + 1) % 3
                        return dbl[i], bdbl[i]

                    subt0_ = head_dve(0)
                    for j in range(3):
                        P.dma("sp", cw[:, :, j], bass.AP(I["conv_w"], (l * 3 + j) * DFF, [[1, 128], [128, NFC]]), (), [b_cw], allow_slow_non_contiguous=True)
                    P.dma("sp", cw[:, :, 3], bass.AP(I["conv_b"], l * DFF, [[1, 128], [128, NFC]]), (), [b_cw], allow_slow_non_contiguous=True)
                    if prm:
                        dve(lambda e: e.memset(halo[:, :, :], 0.0), (), [b_halo])
                    else:
                        for j in range(2):
                            P.dma("sp", halo[:, :, j], bass.AP(I["state_ffn_conv"], (l * 2 + j) * DFF, [[1, 128], [128, NFC]]), (), [b_halo], allow_slow_non_contiguous=True)
                    for f0 in range(0, NFC, 6):
                        f1 = min(NFC, f0 + 6)
                        P.dma("sp", wd_all[:, f0:f1, :], wds[:, f0:f1, :], [bW], [b_wd])
                    pre_norm_pe(subt0_)
                    for bi, (qc0, qpos, nq) in enumerate(qblocks):
                        subt = [(a0, min(128, nq - a0)) for a0 in range(0, nq, 128)]
                        xblk, b_xb = xblk2[bi % 2], b_xb2[bi % 2]
                        ck("wout")
                        pd, bpd = next_dbl()
                        for j in range(2):
                            for c in range(8):
                                pe(lambda e, j=j, c=c, pd=pd: e.matmul(pd[:, j * 512:j * 512 + nq], lhsT=wxq[:, c, j * 128:(j + 1) * 128], rhs=hT2[:, c, 0:nq], start=(c == 0), stop=(c == 7)), [b_wxq, b_hT2], [bpd], inc=(c == 7 and j == 1))
                        pdv = pd[:, :].rearrange("p (a b) -> p a b", a=2)
                        act(lambda e, pdv=pdv: e.copy(out=qxT[:, :, 0:nq], in_=pdv[:, :, 0:nq]), [bpd], [b_qxT])
                        callsX = []
                        for hp in range(2):
                            lanes = []
                            for li in range(2):
                                h = hp * 2 + li
                                r0 = li * 64
                                lanes.append(dict(kt=lambda c0, nk, r0=r0, hp=hp: mkT[r0:r0 + 64, hp, c0:c0 + nk], qt=lambda c0, c1, r0=r0, hp=hp: qxT[r0:r0 + 64, hp, c0 - qc0:c1 - qc0],
                                                  v=lambda kbi, nk, h=h: MV[0:nk, kbi, h, :], E=None, tp=None))
                            callsX.append(("X", lanes, (qc0, qpos, nq), "all", [(0, (0, 0, 128)), (1, (128, 128, 128))], [b_qxT, b_mkT, b_MV], epi_plain([(hp * 2) * 64, (hp * 2 + 1) * 64], xa, b_xa)))
                        attend_many(callsX)
                        flush_pending()
                        for t, (a0, n) in enumerate(subt):
                            pt, bpt = next_ptr()
                            for c in range(2):
                                pe(lambda e, c=c, pt=pt, t=t, n=n: e.transpose(out=pt[:, c * 128:c * 128 + n], in_=xa[0:n, t, c * 128:(c + 1) * 128], identity=ident[0:n, 0:n]), [b_xa, b_ident], [bpt], inc=(c == 1))
                            ptv = pt[:, 0:256].rearrange("p (c t) -> p c t", c=2)
                            act(lambda e, ptv=ptv, a0=a0, n=n: e.copy(out=xaT[:, :, a0:a0 + n], in_=ptv[:, :, 0:n]), [bpt], [b_xaT])
                        for t, (a0, n) in enumerate(subt):
                            pd, bpd = next_dbl()
                            for hf in range(2):
                                for c in range(2):
                                    pe(lambda e, hf=hf, c=c, a0=a0, n=n, pd=pd: e.matmul(pd[0:n, hf * 512:(hf + 1) * 512], lhsT=xaT[:, c, a0:a0 + n], rhs=wxo[:, c, hf * 512:(hf + 1) * 512], start=(c == 0), stop=(c == 1)), [b_xaT, b_wxo], [bpd], inc=(c == 1 and hf == 1))
                            post_residual(pd, bpd, n, t, 1, xblk, b_xb)
                        pre_norm_dve(subt, 2, xblk, b_xb)
                        pre_norm_pe(subt)
                        ck("xatt")
                        for f in range(NFC):
                            k = f % 3
                            k2 = f % 2
                            P.dma("sp", wg[k][:, :, :, :], WGU[l, f].rearrange("p (a c j) -> p a c j", a=2, c=8), [bW], [bwg[k]])
                            pd, bpd = next_dbl()
                            for j in range(2):
                                for c in range(8):
                                    pe(lambda e, j=j, c=c, pd=pd, k=k: e.matmul(pd[:, j * 512:j * 512 + nq], lhsT=wg[k][:, j, c, :], rhs=hT2[:, c, 0:nq], start=(c == 0), stop=(c == 7)), [bwg[k], b_hT2], [bpd], inc=(c == 7 and j == 1))
                            G_, bG = gs[k2], b_gs[k2]
                            C_, bC = cc[k2], b_cc[k2]
                            S_, bS = sl[k2], b_sl[k2]
                            act(lambda e, f=f, G_=G_: e.copy(out=G_[:, 0:2], in_=halo[:, f, :]), [b_halo], [bG])
                            act(lambda e, pd=pd, G_=G_: e.copy(out=G_[:, 2:2 + nq], in_=pd[:, 0:nq]), [bpd], [bG])
                            act(lambda e, f=f, G_=G_: e.copy(out=halo[:, f, :], in_=G_[:, nq:nq + 2]), [bG], [b_halo])
                            dve(lambda e, f=f, G_=G_, C_=C_: e.tensor_scalar(out=C_[:, 0:nq], in0=G_[:, 2:2 + nq], scalar1=cw[:, f, 2:3], scalar2=cw[:, f, 3:4], op0=ALU.mult, op1=ALU.add), [bG, b_cw], [bC])
                            dve(lambda e, f=f, G_=G_, C_=C_: e.scalar_tensor_tensor(out=C_[:, 0:nq], in0=G_[:, 1:1 + nq], scalar=cw[:, f, 1:2], in1=C_[:, 0:nq], op0=ALU.mult, op1=ALU.add), [bG, b_cw, bC], [bC])
                            dve(lambda e, f=f, G_=G_, C_=C_: e.scalar_tensor_tensor(out=C_[:, 0:nq], in0=G_[:, 0:nq], scalar=cw[:, f, 0:1], in1=C_[:, 0:nq], op0=ALU.mult, op1=ALU.add), [bG, b_cw, bC], [bC])
                            act(lambda e, C_=C_, S_=S_: e.activation(out=S_[:, 0:nq], in_=C_[:, 0:nq], func=AF.Silu, bias=zc[:, 0:1]), [bC], [bS])
                            dve(lambda e, f=f, pd=pd, S_=S_: e.tensor_tensor(out=aTf[:, f, 0:nq], in0=pd[:, 512:512 + nq], in1=S_[:, 0:nq], op=ALU.mult), [bpd, bS], [b_aTf])
                        nxt = None
                        if bi + 1 < len(qblocks):
                            nxt = head_dve(bi + 1)
                        for t, (a0, n) in enumerate(subt):
                            ti = (qc0 + a0) // 128
                            if nxt is not None and t == min(2, len(subt) - 1):
                                pre_norm_pe(nxt)
                            pd, bpd = next_dbl3()
                            for f in range(NFC):
                                for hf in range(2):
                                    pe(lambda e, hf=hf, f=f, a0=a0, n=n, pd=pd: e.matmul(pd[0:n, hf * 512:(hf + 1) * 512], lhsT=aTf[:, f, a0:a0 + n], rhs=wd_all[:, f, hf * 512:(hf + 1) * 512], start=(f == 0), stop=(f == NFC - 1), skip_group_check=True), [b_aTf, b_wd], [bpd], inc=(hf == 1 and f == NFC - 1))
                            post_residual(pd, bpd, n, t, 3, xblk, b_xb)
                            P.dma("pool", ydst[qc0 + a0:qc0 + a0 + n, :], xblk[0:n, t, :], [b_xb[t]], [ybufs[ti]])
                    for j in range(2):
                        P.dma("pool", bass.AP(O[pfx + "conv"], ((l * (NP if prm else 1) + so) * 2 + j) * DFF, [[1, 128], [128, NFC]]), halo[:, :, j], [b_halo], [], allow_slow_non_contiguous=True)
                    P.barrier()

        YB = {}
        for s in range(NP):
            YB[("p", s)] = [Buf() for _ in range((SEQ + 127) // 128)]
        YB[("s", 0)] = [Buf()]
        for l in range(DEPTH):
            pass
        seqs = [("p", s) for s in range(NP)] + [("s", 0)]
        nrun = 0
        for (kind, s) in seqs:
            for l in range(DEPTH):
                if stop >= 10 and nrun >= stop - 9:
                    break
                try:
                    run_layer(kind, s, l)
                except StopBuild:
                    P.finish()
                    return nc
                nrun += 1
        P.finish()
    return nc


NCORES = 8
_cache = {}


def kernel(**inputs):
    SEQ, PAST, TS, NP = 2048, 2048, 32, 4
    key = (NP, SEQ, PAST, TS)
    if key not in _cache:
        _cache[key] = build(NP, SEQ, PAST, TS)
    nc = _cache[key]
    hc = host_consts(SEQ, PAST, TS)
    f = lambda a: np.ascontiguousarray(np.asarray(a, dtype=np.float32))
    in_maps = []
    for i in range(NCORES):
        m = {}
        m["x_prompt"] = f(inputs["x_prompt"][NP * i:NP * (i + 1)])
        m["x_sample"] = f(inputs["x_sample"][i:i + 1])
        m["mem_prompt"] = f(inputs["mem_prompt"][NP * i:NP * (i + 1)])
        for n in ("cache_a_k", "cache_a_v", "cache_b_k", "cache_b_v", "cache_c_latent", "cache_c_rope_k", "cache_mem_k", "cache_mem_v", "state_ffn_conv"):
            a = np.asarray(inputs[n])[:, i]
            m[n] = f(a.reshape(a.shape[0], a.shape[1], -1))
        for n in ("w_in", "w_out", "norms", "diff_subln", "t5_bias", "band_rel_bias", "mla_q_norm", "mla_w_uq", "mla_kv_norm", "mla_w_ukv",
                  "w_xq", "w_mk", "w_mv", "w_xo", "w_gate", "w_up", "conv_w", "conv_b", "w_down"):
            m[n] = f(inputs[n])
        m["diff_lambda"] = f(np.asarray(inputs["diff_lambda"]).reshape(2, 128))
        m.update(hc)
        in_maps.append(m)
    res = run_bass_kernel_spmd(nc, in_maps, core_ids=list(range(NCORES)))
    R = res.results
    cat0 = lambda n: np.concatenate([r[n] for r in R], axis=0)
    cat1 = lambda n: np.concatenate([r[n] for r in R], axis=1)
    y_p = cat0("y_prompt"); y_s = cat0("y_sample")
    B = y_p.shape[0]
    outs = [y_p, y_s,
            cat1("p_a_k").reshape(2, B, SEQ, 8, 64), cat1("p_a_v").reshape(2, B, SEQ, 8, 64),
            cat1("p_b_k").reshape(2, B, 512, 4, 64), cat1("p_b_v").reshape(2, B, 512, 4, 64),
            cat1("p_lat"), cat1("p_kpe"),
            cat1("p_mk").reshape(2, B, MEM, 4, 64), cat1("p_mv").reshape(2, B, MEM, 4, 64), cat1("p_conv"),
            cat1("s_a_k").reshape(2, NCORES, TS, 8, 64), cat1("s_a_v").reshape(2, NCORES, TS, 8, 64),
            cat1("s_b_k").reshape(2, NCORES, TS, 4, 64), cat1("s_b_v").reshape(2, NCORES, TS, 4, 64),
            cat1("s_lat"), cat1("s_kpe"), cat1("s_conv")]
    return tuple(np.ascontiguousarray(o, dtype=np.float32) for o in outs)
```

```python
import contextlib
import math
import os
DBG = os.environ.get('KDBG', '')
import numpy as np
import concourse.bass as bass
import concourse.mybir as mybir
from concourse.bass_utils import run_bass_kernel_spmd

F32 = mybir.dt.float32
BF16 = mybir.dt.bfloat16
ALU = mybir.AluOpType
AF = mybir.ActivationFunctionType
AX = mybir.AxisListType

D = 1024
DIN = 2720
DFF = 2816
NFC = 22
MEM = 256
EPS = 1e-6
NDS = 32
SAME_ENGINE_SYNC = False


class StopBuild(Exception):
    pass


CKSTOP = os.environ.get('KCK', '')


STOPPED = [False]


def ck(tag):
    if CKSTOP and tag == CKSTOP:
        STOPPED[0] = True


class Buf:
    __slots__ = ("w", "r", "excl", "strict")

    def __init__(self, excl=False, strict=False):
        self.w = None
        self.r = {}
        self.excl = excl
        self.strict = strict


class Prog:
    def __init__(self, nc, stack):
        self.nc = nc
        self.eh = {"pe": nc.tensor, "act": nc.scalar, "dve": nc.vector, "pool": nc.gpsimd, "sp": nc.sync}
        self.sems = {}
        for e in self.eh:
            self.sems[e] = stack.enter_context(nc.semaphore("s_" + e))
        self.cnt = {e: 0 for e in self.eh}
        self.seen = {e: {} for e in self.eh}
        for i in range(NDS):
            self.sems[("d", i)] = stack.enter_context(nc.semaphore("s_d%d" % i))
        self.dcnt = [0] * NDS
        self.dnext = {"sp": 0, "pool": 0, "act": 0}
        self.drange = {"sp": (0, NDS - 4), "pool": (NDS - 4, NDS), "act": (0, NDS - 4)}
        self.nins = 0

    def _deps(self, eng, reads, writes, extra=()):
        needs = {}
        strict_own = 0
        for b in reads:
            t = b.w
            if t is not None and needs.get(t[0], 0) < t[1]:
                needs[t[0]] = t[1]
            if t is not None and t[0] == eng and t[1] > strict_own and eng != "pe":
                strict_own = t[1]
            if b.excl:
                for k, v in b.r.items():
                    if needs.get(k, 0) < v:
                        needs[k] = v
        for b in writes:
            t = b.w
            if t is not None and needs.get(t[0], 0) < t[1]:
                needs[t[0]] = t[1]
            for k, v in b.r.items():
                if needs.get(k, 0) < v:
                    needs[k] = v
        for t in extra:
            if needs.get(t[0], 0) < t[1]:
                needs[t[0]] = t[1]
        seen = self.seen[eng]
        for k, v in needs.items():
            if k == eng and (eng == "pe" or not SAME_ENGINE_SYNC) and eng != "pool":
                if strict_own and seen.get(k, 0) < strict_own:
                    seen[k] = strict_own
                    self.eh[eng].wait_ge(self.sems[k], strict_own)
                continue
            if seen.get(k, 0) >= v:
                continue
            seen[k] = v
            self.eh[eng].wait_ge(self.sems[k], v)

    def op(self, eng, fn, reads=(), writes=(), inc=True):
        if STOPPED[0]:
            return
        self._deps(eng, reads, writes)
        self.nins += 1
        ins = fn(self.eh[eng])
        if inc:
            self.cnt[eng] += 1
            v = self.cnt[eng]
            ins.then_inc(self.sems[eng], 1)
        else:
            v = self.cnt[eng] + 1
        for b in reads:
            b.r[eng] = v
        for b in writes:
            b.w = (eng, v)
            b.r = {}

    def dma(self, eng, out, in_, reads=(), writes=(), **kw):
        if STOPPED[0]:
            return
        lo, hi = self.drange[eng]
        i = lo + self.dnext[eng]
        self.dnext[eng] = (self.dnext[eng] + 1) % (hi - lo)
        key = ("d", i)
        extra = ((key, self.dcnt[i]),) if self.dcnt[i] else ()
        self._deps(eng, reads, writes, extra)
        self.dcnt[i] += 16
        v = self.dcnt[i]
        self.nins += 1
        self.eh[eng].dma_start(out=out, in_=in_, **kw).then_inc(self.sems[key], 16)
        for b in reads:
            b.r[key] = v
        for b in writes:
            b.w = (key, v)
            b.r = {}

    def barrier(self, force=False):
        if STOPPED[0] and not force:
            return
        keys = [e for e in self.eh if self.cnt[e]] + [("d", i) for i in range(NDS) if self.dcnt[i]]
        for e in self.eh:
            seen = self.seen[e]
            for k in keys:
                v = self.cnt[k] if not isinstance(k, tuple) else self.dcnt[k[1]]
                if k == e and e != "pool":
                    continue
                if seen.get(k, 0) >= v:
                    continue
                seen[k] = v
                self.eh[e].wait_ge(self.sems[k], v)

    def finish(self):
        self.barrier(force=True)


def t5_bucket_np(rel):
    half, exact = 16, 8
    ret = np.where(rel > 0, half, 0)
    n = np.abs(rel)
    nf = np.maximum(n, 1).astype(np.float32)
    large = exact + (np.log(nf / np.float32(exact)) / np.float32(math.log(128 / exact)) * np.float32(half - exact)).astype(np.int32)
    large = np.minimum(large, half - 1)
    return ret + np.where(n < exact, n, large)


def host_consts(SEQ, PAST, TS):
    m = np.arange(1151)
    bk = t5_bucket_np(511 - m)
    oh_t5 = (bk[None, :] == np.arange(32)[:, None]).astype(np.float32)
    m2 = np.arange(1151)
    j = np.clip(511 - m2, -128, 128) + 128
    ohb = (j[None, :] == np.arange(384)[:, None]).astype(np.float32)

    def rope_tab(pos):
        half = 16
        inv = np.float32(10000.0) ** (-np.arange(half, dtype=np.float32) / np.float32(half))
        ang = pos.astype(np.float32)[:, None] * inv[None, :]
        c = np.cos(ang).astype(np.float32)
        s = np.sin(ang).astype(np.float32)
        c4 = np.repeat(c[:, None, :], 4, axis=1)
        s4 = np.repeat(s[:, None, :], 4, axis=1)
        return np.ascontiguousarray(np.stack([c4, s4], axis=1).reshape(len(pos), 128))

    return {"oh_t5": oh_t5, "oh_b": ohb, "rope_p": rope_tab(np.arange(SEQ)), "rope_s": rope_tab(PAST + np.arange(TS))}


def vis_info(mode, q0, nq, k0, nk):
    qch = [(c, min(c + 64, nq)) for c in range(0, nq, 64)]
    kch = [(p, min(p + 64, nk)) for p in range(0, nk, 64)]
    V = {}
    for (p0, p1) in kch:
        kc = (k0 + p0) // 64
        for (c0, c1) in qch:
            qc = (q0 + c0) // 64
            if mode == "causal":
                ok = kc <= qc
            elif mode == "band":
                ok = (kc <= qc) and (kc >= qc - 8)
            else:
                ok = True
            V[(p0, c0)] = ok
    viscols = [(c0, c1) for (c0, c1) in qch if any(V[(p0, c0)] for (p0, _) in kch)]
    if not viscols:
        return None
    cs = (min(c0 for c0, _ in viscols) // 128) * 128
    ce = min(nq, ((max(c1 for _, c1 in viscols) + 127) // 128) * 128)
    masks = []
    for (p0, p1) in kch:
        run = None
        for (c0, c1) in qch:
            if c0 < cs or c0 >= ce:
                continue
            if not V[(p0, c0)]:
                if run is not None and run[1] == c0:
                    run[1] = c1
                else:
                    if run is not None:
                        masks.append((p0, p1, run[0], run[1]))
                    run = [c0, c1]
        if run is not None:
            masks.append((p0, p1, run[0], run[1]))
    return cs, ce, masks


def build(NP, SEQ, PAST, TS=32, DEPTH=2, stop=0):
    nc = bass.Bass("TRN2", target_bir_lowering=False)
    I, O = {}, {}

    def inp(n, shape):
        I[n] = nc.dram_tensor(n, list(shape), F32, kind="ExternalInput")

    def outp(n, shape):
        O[n] = nc.dram_tensor(n, list(shape), F32, kind="ExternalOutput")

    NBP = min(512, SEQ)
    NBS = min(512, PAST)
    inp("x_prompt", (NP, SEQ, D)); inp("x_sample", (1, TS, D))
    inp("cache_a_k", (DEPTH, PAST, 512)); inp("cache_a_v", (DEPTH, PAST, 512))
    inp("cache_b_k", (DEPTH, NBS, 256)); inp("cache_b_v", (DEPTH, NBS, 256))
    inp("cache_c_latent", (DEPTH, PAST, 128)); inp("cache_c_rope_k", (DEPTH, PAST, 32))
    inp("cache_mem_k", (DEPTH, MEM, 256)); inp("cache_mem_v", (DEPTH, MEM, 256))
    inp("state_ffn_conv", (DEPTH, 2, DFF)); inp("mem_prompt", (NP, MEM, D))
    inp("w_in", (DEPTH, D, DIN)); inp("w_out", (DEPTH, D, D)); inp("norms", (DEPTH, 7, D))
    inp("diff_lambda", (DEPTH, 128)); inp("diff_subln", (DEPTH, 64)); inp("t5_bias", (32, 8))
    inp("band_rel_bias", (DEPTH, 4, 257)); inp("mla_q_norm", (DEPTH, 256)); inp("mla_w_uq", (DEPTH, 256, 384))
    inp("mla_kv_norm", (DEPTH, 128)); inp("mla_w_ukv", (DEPTH, 128, 512))
    inp("w_xq", (DEPTH, D, 256)); inp("w_mk", (DEPTH, D, 256)); inp("w_mv", (DEPTH, D, 256)); inp("w_xo", (DEPTH, 256, D))
    inp("w_gate", (DEPTH, D, DFF)); inp("w_up", (DEPTH, D, DFF)); inp("conv_w", (DEPTH, 3, DFF)); inp("conv_b", (DEPTH, DFF))
    inp("w_down", (DEPTH, DFF, D))
    inp("oh_t5", (32, 1151)); inp("oh_b", (384, 1151)); inp("rope_p", (SEQ, 128)); inp("rope_s", (TS, 128))
    outp("y_prompt", (NP, SEQ, D)); outp("y_sample", (1, TS, D))
    outp("p_a_k", (DEPTH, NP, SEQ, 512)); outp("p_a_v", (DEPTH, NP, SEQ, 512))
    outp("p_b_k", (DEPTH, NP, NBP, 256)); outp("p_b_v", (DEPTH, NP, NBP, 256))
    outp("p_lat", (DEPTH, NP, SEQ, 128)); outp("p_kpe", (DEPTH, NP, SEQ, 32))
    outp("p_mk", (DEPTH, NP, MEM, 256)); outp("p_mv", (DEPTH, NP, MEM, 256)); outp("p_conv", (DEPTH, NP, 2, DFF))
    outp("s_a_k", (DEPTH, 1, TS, 512)); outp("s_a_v", (DEPTH, 1, TS, 512))
    outp("s_b_k", (DEPTH, 1, TS, 256)); outp("s_b_v", (DEPTH, 1, TS, 256))
    outp("s_lat", (DEPTH, 1, TS, 128)); outp("s_kpe", (DEPTH, 1, TS, 32)); outp("s_conv", (DEPTH, 1, 2, DFF))
    IA = {k: v.ap() for k, v in I.items()}
    OA = {k: v.ap() for k, v in O.items()}
    WB = {}
    for n in ("w_in", "w_out", "mla_w_uq", "mla_w_ukv", "w_xq", "w_mk", "w_mv", "w_xo", "w_down"):
        WB[n] = nc.dram_tensor("wb_" + n, list(I[n].shape), BF16, kind="Internal")
    WBA = {k: v.ap() for k, v in WB.items()}
    WGUh = nc.dram_tensor("wb_wgu", [DEPTH, NFC, 128, 2048], BF16, kind="Internal")
    WGU = WGUh.ap()
    Escr = nc.dram_tensor("Escr", [8 + DEPTH * 4, 128, 1024], BF16, kind="Internal")
    EscrA = Escr.ap()
    scrA = nc.dram_tensor("scrA", [8, 1151], F32, kind="Internal")
    scrB = nc.dram_tensor("scrB", [DEPTH * 4, 1151], F32, kind="Internal")

    uid = [0]
    G = contextlib.ExitStack()
    with G:
        P = Prog(nc, G)

        def sb(stack, shape, dt, name="t"):
            uid[0] += 1
            return stack.enter_context(nc.sbuf_tensor("%s_%d" % (name, uid[0]), list(shape), dt))

        dbl = [G.enter_context(nc.psum_tensor("dbl%d" % i, [128, 1024], F32)) for i in range(3)]
        bdbl = [Buf(True) for _ in range(3)]
        ptr = [G.enter_context(nc.psum_tensor("ptr%d" % i, [128, 1024], BF16)) for i in range(2)]
        bptr = [Buf(True) for _ in range(2)]
        rot = {"d": 0, "t": 0}

        def next_dbl():
            i = rot["d"]; rot["d"] = 1 - i
            return dbl[i], bdbl[i]

        def next_ptr():
            i = rot["t"]; rot["t"] = 1 - i
            return ptr[i], bptr[i]

        pe = lambda fn, r=(), w=(), inc=True: P.op("pe", fn, r, w, inc)
        act = lambda fn, r=(), w=(): P.op("act", fn, r, w)
        dve = lambda fn, r=(), w=(): P.op("dve", fn, r, w)

        zc = sb(G, [128, 1], F32, "zc"); ec = sb(G, [128, 1], F32, "ec"); b_zc = Buf()
        dve(lambda e: e.memset(zc[:], 0.0), (), [b_zc])
        dve(lambda e: e.memset(ec[:], EPS), (), [b_zc])
        P.barrier()
        nc.const_aps.register(F32, 0.0, zc[:, 0:1])
        nc.const_aps.register(F32, EPS, ec[:, 0:1])
        ident = sb(G, [128, 128], BF16, "ident"); b_ident = Buf()
        Jm = sb(G, [128, 128], F32, "J"); b_J = Buf()
        P.op("pool", lambda e: e.memset(ident[:], 0.0), (), [b_ident])
        P.op("pool", lambda e: e.affine_select(out=ident[:], in_=ident[:], pattern=[[-1, 128]], compare_op=ALU.not_equal, fill=1.0, base=0, channel_multiplier=1), [b_ident], [b_ident])
        P.op("pool", lambda e: e.memset(Jm[:], 0.0), (), [b_J])
        P.op("pool", lambda e: e.affine_select(out=Jm[:], in_=Jm[:], pattern=[[1, 128]], compare_op=ALU.not_equal, fill=1.0, base=-127, channel_multiplier=1), [b_J], [b_J])
        bW = Buf()
        for n in ([] if 'W' in DBG else WB):
            src = IA[n].rearrange("l a b -> (l a) b")
            dst = WBA[n].rearrange("l a b -> (l a) b")
            rows = src.shape[0]
            step = 256
            for r in range(0, rows, step):
                P.dma("pool", dst[r:min(r + step, rows)], src[r:min(r + step, rows)], (), [Buf()])
        for l_ in range(DEPTH):
            for j_, wn in enumerate(("w_gate", "w_up")):
                wsrc_ = IA[wn][l_].rearrange("(c p) n -> p c n", p=128)
                for f_ in range(NFC):
                    P.dma("pool", WGU[l_, f_][:, j_ * 1024:(j_ + 1) * 1024].rearrange("p (c j) -> p c j", c=8), wsrc_[:, :, f_ * 128:(f_ + 1) * 128], (), [Buf()])
        b_Escr = Buf()
        with contextlib.ExitStack() as S:
            EA = sb(S, [128, 8, 1024], BF16, "EA"); b_EA = Buf()
            EB = sb(S, [128, DEPTH * 4, 1024], BF16, "EB"); b_EB = Buf()
            t5 = sb(S, [32, 8], F32); b_t5 = Buf()
            oh = sb(S, [32, 1151], F32); b_oh = Buf()
            c15 = sb(S, [8, 1], F32); b_c15 = Buf(strict=True)
            wA = sb(S, [8, 1151], F32); b_wA = Buf()
            P.dma("sp", t5[:], IA["t5_bias"], (), [b_t5])
            P.dma("sp", oh[:], IA["oh_t5"], (), [b_oh])
            P.dma("sp", c15[:], bass.AP(I["t5_bias"], 15 * 8, [[1, 8], [1, 1]]), (), [b_c15])
            dve(lambda e: e.tensor_scalar(out=c15[:], in0=c15[:], scalar1=-1.0, scalar2=None, op0=ALU.mult), [b_c15], [b_c15])
            for (c0, c1) in ((0, 512), (512, 1024), (1024, 1151)):
                pd, bpd = next_dbl()
                pe(lambda e, c0=c0, c1=c1, pd=pd: e.matmul(pd[0:8, 0:c1 - c0], lhsT=t5[:, :], rhs=oh[:, c0:c1], start=True, stop=True), [b_t5, b_oh], [bpd])
                act(lambda e, c0=c0, c1=c1, pd=pd: e.activation(out=wA[:, c0:c1], in_=pd[0:8, 0:c1 - c0], func=AF.Exp, bias=c15[:, 0:1], scale=1.0), [bpd, b_c15], [b_wA])
            b_scrA = Buf()
            P.dma("sp", scrA.ap(), wA[:], [b_wA], [b_scrA])
            Gp = sb(S, [128, 1024], F32); b_Gp = Buf()
            for h in range(8):
                P.dma("sp", Gp[:], bass.AP(scrA, h * 1151, [[1, 128], [1, 1024]]), [b_scrA], [b_Gp])
                pd, bpd = next_dbl()
                for hf in range(2):
                    pe(lambda e, hf=hf, pd=pd: e.matmul(pd[:, hf * 512:(hf + 1) * 512], lhsT=Jm[:], rhs=Gp[:, hf * 512:(hf + 1) * 512], start=True, stop=True), [b_J, b_Gp], [bpd])
                act(lambda e, h=h, pd=pd: e.copy(out=EA[:, h, :], in_=pd[:, :]), [bpd], [b_EA])
            relT = sb(S, [128, 3, DEPTH * 4], F32); b_relT = Buf()
            ohb = sb(S, [128, 3, 1151], F32); b_ohb = Buf()
            c0b = sb(S, [DEPTH * 4, 1], F32); b_c0b = Buf(strict=True)
            wB = sb(S, [DEPTH * 4, 1151], F32); b_wB = Buf()
            dve(lambda e: e.memset(relT[:], 0.0), (), [b_relT])
            for c in range(3):
                nj = 128 if c < 2 else 1
                P.dma("sp", relT[0:nj, c, :], bass.AP(I["band_rel_bias"], c * 128, [[1, nj], [257, DEPTH * 4]]), (), [b_relT], allow_slow_non_contiguous=True)
            P.dma("sp", ohb[:], IA["oh_b"].rearrange("(c p) m -> p c m", p=128), (), [b_ohb])
            P.dma("sp", c0b[:], bass.AP(I["band_rel_bias"], 0, [[257, DEPTH * 4], [1, 1]]), (), [b_c0b], allow_slow_non_contiguous=True)
            dve(lambda e: e.tensor_scalar(out=c0b[:], in0=c0b[:], scalar1=-1.0, scalar2=None, op0=ALU.mult), [b_c0b], [b_c0b])
            for (c0_, c1_) in ((0, 512), (512, 1024), (1024, 1151)):
                pd, bpd = next_dbl()
                for c in range(3):
                    pe(lambda e, c=c, pd=pd, c0_=c0_, c1_=c1_: e.matmul(pd[0:DEPTH * 4, 0:c1_ - c0_], lhsT=relT[:, c, :], rhs=ohb[:, c, c0_:c1_], start=(c == 0), stop=(c == 2)), [b_relT, b_ohb], [bpd], inc=(c == 2))
                act(lambda e, pd=pd, c0_=c0_, c1_=c1_: e.activation(out=wB[:, c0_:c1_], in_=pd[0:DEPTH * 4, 0:c1_ - c0_], func=AF.Exp, bias=c0b[:, 0:1], scale=1.0), [bpd, b_c0b], [b_wB])
            b_scrB = Buf()
            P.dma("sp", scrB.ap(), wB[:], [b_wB], [b_scrB])
            for lh in range(DEPTH * 4):
                P.dma("sp", Gp[:, :], bass.AP(scrB, lh * 1151, [[1, 128], [1, 1024]]), [b_scrB], [b_Gp])
                pd, bpd = next_dbl()
                for hf in range(2):
                    pe(lambda e, hf=hf, pd=pd: e.matmul(pd[:, hf * 512:(hf + 1) * 512], lhsT=Jm[:], rhs=Gp[:, hf * 512:(hf + 1) * 512], start=True, stop=True), [b_J, b_Gp], [bpd])
                act(lambda e, lh=lh, pd=pd: e.copy(out=EB[:, lh, :], in_=pd[:, :]), [bpd], [b_EB])
            P.dma("sp", EscrA[0:8].rearrange("h p u -> p h u"), EA[:, :, :], [b_EA], [b_Escr])
            P.dma("sp", EscrA[8:8 + DEPTH * 4].rearrange("h p u -> p h u"), EB[:, :, :], [b_EB], [Buf()])
            P.barrier()

        neglam = sb(G, [128, DEPTH], F32, "neglam"); b_lam = Buf(strict=True)
        subln = sb(G, [128, DEPTH, 64], F32, "subln"); b_subln = Buf()
        LAM_INIT = [0.8 - 0.6 * math.exp(-0.3 * l) for l in range(DEPTH)]
        with contextlib.ExitStack() as S:
            lt = sb(S, [128, 128], F32); b_lt = Buf()
            junk = sb(S, [128, 32], F32); b_junk = Buf()
            s12 = sb(S, [128, 2], F32); b_s12 = Buf(strict=True)
            for l in range(DEPTH):
                P.dma("sp", lt[:], bass.AP(I["diff_lambda"], l * 128, [[0, 128], [1, 128]]), (), [b_lt])
                P.dma("sp", subln[:, l, :], bass.AP(I["diff_subln"], l * 64, [[0, 128], [1, 64]]), (), [b_subln])
                for i in range(2):
                    dve(lambda e, i=i: e.scalar_tensor_tensor(out=junk[:], in0=lt[:, 64 * i:64 * i + 32], scalar=1.0, in1=lt[:, 64 * i + 32:64 * i + 64], op0=ALU.mult, op1=ALU.mult, accum_out=s12[:, i:i + 1]), [b_lt], [b_junk, b_s12])
                act(lambda e: e.activation(out=s12[:], in_=s12[:], func=AF.Exp, bias=zc[:, 0:1]), [b_s12], [b_s12])
                dve(lambda e, l=l: e.tensor_tensor(out=neglam[:, l:l + 1], in0=s12[:, 1:2], in1=s12[:, 0:1], op=ALU.subtract), [b_s12], [b_lam])
                dve(lambda e, l=l: e.tensor_scalar(out=neglam[:, l:l + 1], in0=neglam[:, l:l + 1], scalar1=-LAM_INIT[l], scalar2=None, op0=ALU.add), [b_lam], [b_lam])
                dve(lambda e, l=l: e.tensor_scalar(out=subln[:, l, :], in0=subln[:, l, :], scalar1=1.0 - LAM_INIT[l], scalar2=None, op0=ALU.mult), [b_subln], [b_subln])
            P.barrier()

        if stop == 1:
            P.finish()
            return nc

        def bcast_row(dst, src_handle, off, n, bufs):
            P.dma("sp", dst, bass.AP(src_handle, off, [[0, 128], [1, n]]), (), bufs)

        def rstd_of(ss, n, Dd, bss):
            act(lambda e: e.activation(out=ss, in_=ss, func=AF.Ln, bias=ec[0:n, 0:1], scale=1.0 / Dd), [bss], [bss])
            act(lambda e: e.activation(out=ss, in_=ss, func=AF.Exp, bias=zc[0:n, 0:1], scale=-0.5), [bss], [bss])

        def run_layer(kind, s, l):
            prm = kind == "p"
            T = SEQ if prm else TS
            tiles = [(t0, min(128, T - t0)) for t0 in range(0, T, 128)]
            NT = len(tiles)
            past = 0 if prm else PAST
            NKB_past = past // 128
            NK = past + T
            kblocks = [(j * 128, j * 128, 128) for j in range(NKB_past)] + [(past + t0, past + t0, n) for (t0, n) in tiles]
            NKB = len(kblocks)
            qblocks = []
            for q0 in range(0, T, 512):
                nq = min(512, T - q0)
                qblocks.append((q0, past + q0, nq))
            xsrc = (IA["x_prompt"][s] if prm else IA["x_sample"][0]) if l == 0 else (OA["y_prompt"][s] if prm else OA["y_sample"][0])
            ydst = OA["y_prompt"][s] if prm else OA["y_sample"][0]
            ybufs = YB[(kind, s)]
            pfx = "p_" if prm else "s_"
            so = s if prm else 0
            rope_h = I["rope_p"] if prm else I["rope_s"]
            wl = {k: v[l] for k, v in WBA.items()}
            scale_of = {"A": 32 ** -0.5, "B": 64 ** -0.5, "C": 96 ** -0.5, "X": 64 ** -0.5}

            L = contextlib.ExitStack()
            with L:
                b_atok = Buf()
                NPT = 4
                ptiles = [sb(L, [128, 2, 512], BF16, "PT") for _ in range(NPT)]
                bpt_ = [Buf() for _ in range(NPT)]
                prot = [0]
                rec = sb(L, [128, 2, 4], F32, "rec"); b_rec = Buf(strict=True)
                otmp = sb(L, [128, 4, 64], F32, "otmp"); b_otmp = Buf()
                o0t = sb(L, [128, 4, 64], F32, "o0t"); b_o0t = Buf()
                sq = sb(L, [128, 4, 64], F32, "sq"); b_sq = Buf()
                ssA = sb(L, [128, 4], F32, "ssA"); b_ssA = Buf(strict=True)

                accs = [sb(L, [128, 2, 260], F32, "accs") for _ in range(2)]
                baccs = [Buf(), Buf()]
                arot = [0]
                pending = []

                def bc_last(ap2, n_last):
                    return bass.AP(ap2.tensor, ap2.offset, [list(ap2.ap[0]), list(ap2.ap[1]), [0, n_last]])

                def bc_mid(ap2, k):
                    return bass.AP(ap2.tensor, ap2.offset, [list(ap2.ap[0]), [0, k], list(ap2.ap[1])])

                def flush_pending(step=None):
                    while pending and (step is None or pending[0][0] <= step):
                        pending.pop(0)[1]()

                def attend(grp, lanes, qb, mode, kbl, rbufs, epilogue):
                    attend_many([(grp, lanes, qb, mode, kbl, rbufs, epilogue)])

                def attend_many(calls):
                    acc, bacc = dbl[2], bdbl[2]
                    G_ = []
                    for ci, (grp, lanes, qb, mode, kbl, rbufs, epilogue) in enumerate(calls):
                        (qc0, qpos, nq) = qb
                        st_ = []
                        for kbi, (kc0, kpos, nk) in kbl:
                            vi = vis_info(mode, qpos, nq, kpos, nk)
                            if vi is not None:
                                st_.append((kbi, kc0, kpos, nk, vi[0], vi[2], vi[1]))
                        def heavy(stp):
                            (kbi_, kc0_, kpos_, nk_, cs_, masks_, ce_) = stp
                            relmax_ = (kpos_ + nk_ - 1) - (qpos + cs_)
                            return bool(masks_) or any(ln.get("E") is not None and relmax_ > ln["Ethr"] for ln in lanes)
                        hv = [x for x in st_ if heavy(x)]
                        lt = [x for x in st_ if not heavy(x)]
                        mix = []
                        while hv or lt:
                            if lt:
                                mix.append(lt.pop(0))
                            if hv:
                                mix.append(hv.pop(0))
                        st_ = mix
                        for si, stp in enumerate(st_):
                            G_.append((ci, si, len(st_), stp))
                    qk = {}

                    def emit_qk(g):
                        ci, si, ns, (kbi, kc0, kpos, nk, cs, masks, ce) = G_[g]
                        (grp, lanes, qb, mode, kbl, rbufs, epilogue) = calls[ci]
                        (qc0, qpos, nq) = qb
                        pd, bpd = next_dbl()
                        for li, ln in enumerate(lanes):
                            pe(lambda e, ln=ln, li=li, pd=pd: e.matmul(pd[0:nk, li * 512 + cs:li * 512 + ce], lhsT=ln["kt"](kc0, nk), rhs=ln["qt"](qc0 + cs, qc0 + ce),
                                                                      start=True, stop=True, **({"tile_position": ln["tp"]} if ln.get("tp") else {})),
                               rbufs, [bpd], inc=(li == 1))
                        qk[g] = (pd, bpd)

                    emit_qk(0)
                    if len(G_) > 1:
                        emit_qk(1)
                    for g, (ci, si, ns, (kbi, kc0, kpos, nk, cs, masks, ce)) in enumerate(G_):
                        (grp, lanes, qb, mode, kbl, rbufs, epilogue) = calls[ci]
                        (qc0, qpos, nq) = qb
                        scale = scale_of[grp]
                        nsub = (nq + 127) // 128
                        if si == 0:
                            accz = acc[:, :].rearrange("p (a b) -> p a b", a=2)
                            dve(lambda e, accz=accz, nsub=nsub: e.memset(accz[:, :, 0:nsub * 65], 0.0), (), [bacc])
                        pd, bpd = qk.pop(g)
                        pi = prot[0]; prot[0] = (pi + 1) % NPT
                        PT, bPT = ptiles[pi], bpt_[pi]
                        pdv = pd[:, :].rearrange("p (a b) -> p a b", a=2)
                        act(lambda e, PT=PT, pdv=pdv: e.activation(out=PT[0:nk, :, cs:ce], in_=pdv[0:nk, :, cs:ce], func=AF.Exp, bias=zc[0:nk, 0:1], scale=scale), [bpd], [bPT])
                        if g + 2 < len(G_):
                            emit_qk(g + 2)
                        relmax = (kpos + nk - 1) - (qpos + cs)
                        for li, ln in enumerate(lanes):
                            if ln.get("E") is not None and relmax > ln["Ethr"]:
                                off = ln["c0"] - (kpos - qpos)
                                Et = ln["E"]
                                dve(lambda e, li=li, Et=Et, off=off, PT=PT: e.tensor_tensor(out=PT[0:nk, li, cs:ce], in0=PT[0:nk, li, cs:ce], in1=Et[0:nk, off + cs:off + ce], op=ALU.mult), [bPT, ln["Eb"]], [bPT])
                        for (p0, p1, c0, c1) in masks:
                            dve(lambda e, PT=PT, p0=p0, p1=p1, c0=c0, c1=c1: e.memset(PT[p0:p1, :, c0:c1], 0.0), (), [bPT])
                        for li, ln in enumerate(lanes):
                            for t in range(nsub):
                                a0, a1 = t * 128, min(t * 128 + 128, nq)
                                if a1 <= cs or a0 >= ce:
                                    continue
                                last = (li == 1 and a1 >= ce)
                                pe(lambda e, li=li, t=t, a0=a0, a1=a1, ln=ln, PT=PT: e.matmul(acc[0:a1 - a0, li * 512 + t * 65:li * 512 + t * 65 + 65], lhsT=PT[0:nk, li, a0:a1], rhs=ln["v"](kbi, nk),
                                                                                            start=False, stop=False, skip_group_check=True),
                                   [bPT] + rbufs, [bacc], inc=last)
                        flush_pending(si)
                        if si == ns - 1:
                            flush_pending()
                            k = arot[0]; arot[0] = 1 - k
                            A_, bA_ = accs[k], baccs[k]
                            nn = min(128, nq)
                            accv = acc[:, :].rearrange("p (a b) -> p a b", a=2)
                            dve(lambda e, A_=A_, nn=nn, nsub=nsub, accv=accv: e.tensor_copy(out=A_[0:nn, :, 0:nsub * 65], in_=accv[0:nn, :, 0:nsub * 65]), [bacc], [bA_])
                            epilogue(A_, bA_, qb, nsub)

                def epi_plain(col_of_lane, dst=None, bdst=None):
                    def f(A_, bA_, qb, nsub):
                        pending.append((1, lambda: g(A_, bA_, qb, nsub)))

                    def g(A_, bA_, qb, nsub):
                        (qc0, qpos, nq) = qb
                        nn = min(128, nq)
                        Av = A_[0:nn, :, 0:nsub * 65].rearrange("p a (t c) -> p a t c", c=65)
                        dve(lambda e: e.reciprocal(out=rec[0:nn, :, 0:nsub], in_=Av[:, :, :, 64]), [bA_], [b_rec])
                        ti0 = qc0 // 128
                        for li in range(2):
                            col = col_of_lane[li]
                            if dst is None:
                                o_ap, o_b = a_tok[0:nn, ti0:ti0 + nsub, col:col + 64], b_atok
                            else:
                                o_ap, o_b = dst[0:nn, 0:nsub, col:col + 64], bdst
                            dve(lambda e, li=li, o_ap=o_ap: e.tensor_tensor(out=o_ap, in0=Av[:, li, :, 0:64], in1=bc_last(rec[0:nn, li, 0:nsub], 64), op=ALU.mult), [bA_, b_rec], [o_b])
                    return f

                def epi_diff(h):
                    def f(A_, bA_, qb, nsub):
                        (qc0, qpos, nq) = qb
                        nn = min(128, nq)
                        Av = A_[0:nn, :, 0:nsub * 65].rearrange("p a (t c) -> p a t c", c=65)
                        ti0 = qc0 // 128

                        def s1():
                            dve(lambda e: e.reciprocal(out=rec[0:nn, :, 0:nsub], in_=Av[:, :, :, 64]), [bA_], [b_rec])
                            dve(lambda e: e.tensor_scalar(out=rec[0:nn, 1, 0:nsub], in0=rec[0:nn, 1, 0:nsub], scalar1=neglam[0:nn, l:l + 1], scalar2=None, op0=ALU.mult), [b_rec, b_lam], [b_rec])
                            dve(lambda e: e.tensor_tensor(out=o0t[0:nn, 0:nsub, :], in0=Av[:, 0, :, 0:64], in1=bc_last(rec[0:nn, 0, 0:nsub], 64), op=ALU.mult), [bA_, b_rec], [b_o0t])
                            dve(lambda e: e.tensor_tensor(out=otmp[0:nn, 0:nsub, :], in0=Av[:, 1, :, 0:64], in1=bc_last(rec[0:nn, 1, 0:nsub], 64), op=ALU.mult), [bA_, b_rec], [b_otmp])
                            dve(lambda e: e.tensor_tensor(out=otmp[0:nn, 0:nsub, :], in0=otmp[0:nn, 0:nsub, :], in1=o0t[0:nn, 0:nsub, :], op=ALU.add), [b_otmp, b_o0t], [b_otmp])
                            dve(lambda e: e.tensor_tensor(out=sq[0:nn, 0:nsub, :], in0=otmp[0:nn, 0:nsub, :], in1=otmp[0:nn, 0:nsub, :], op=ALU.mult), [b_otmp], [b_sq])
                            dve(lambda e: e.tensor_reduce(out=ssA[0:nn, 0:nsub], in_=sq[0:nn, 0:nsub, :], axis=AX.X, op=ALU.add), [b_sq], [b_ssA])

                        def s2():
                            rstd_of(ssA[0:nn, 0:nsub], nn, 64, b_ssA)

                        def s3():
                            dve(lambda e: e.tensor_tensor(out=sq[0:nn, 0:nsub, :], in0=otmp[0:nn, 0:nsub, :], in1=bc_last(ssA[0:nn, 0:nsub], 64), op=ALU.mult), [b_otmp, b_ssA], [b_sq])
                            dve(lambda e: e.tensor_tensor(out=a_tok[0:nn, ti0:ti0 + nsub, h * 64:h * 64 + 64], in0=sq[0:nn, 0:nsub, :], in1=bc_mid(subln[0:nn, l, :], nsub), op=ALU.mult), [b_sq, b_subln], [b_atok])
                        pending.append((1, s1))
                        pending.append((5, s2))
                        pending.append((6, s3))
                    return f

                with contextlib.ExitStack() as S1:
                    a_tok = sb(S1, [128, NT, D], BF16, "atok")
                    hT = sb(S1, [128, 8, T], BF16, "hT"); b_hT = Buf()
                    gbc = sb(S1, [128, D], F32, "gbc"); b_gbc = Buf()
                    bcast_row(gbc[:], I["norms"], (l * 7 + 0) * D, D, [b_gbc])
                    xt = [sb(S1, [128, D], F32, "xt") for _ in range(4)]; bxt = [Buf() for _ in range(4)]
                    hb = [sb(S1, [128, D], BF16, "hb") for _ in range(4)]; bhb = [Buf() for _ in range(4)]
                    junkb = sb(S1, [128, D], BF16, "junkb"); b_junkb = Buf()
                    ss1 = [sb(S1, [128, 1], F32, "ss1") for _ in range(4)]; bss1 = [Buf(strict=True) for _ in range(4)]
                    def h_stages(ti, t0, n):
                        k = ti % 4

                        def g0():
                            P.dma("sp", xt[k][0:n, :], xsrc[t0:t0 + n, :], [ybufs[ti]], [bxt[k]])
                            dve(lambda e: e.scalar_tensor_tensor(out=junkb[0:n, :], in0=xt[k][0:n, :], scalar=1.0, in1=xt[k][0:n, :], op0=ALU.mult, op1=ALU.mult, accum_out=ss1[k][0:n, :]), [bxt[k]], [b_junkb, bss1[k]])

                        def g1():
                            rstd_of(ss1[k][0:n, :], n, D, bss1[k])

                        def g2():
                            dve(lambda e: e.scalar_tensor_tensor(out=hb[k][0:n, :], in0=xt[k][0:n, :], scalar=ss1[k][0:n, 0:1], in1=gbc[0:n, :], op0=ALU.mult, op1=ALU.mult), [bxt[k], bss1[k], b_gbc], [bhb[k]])

                        def g3():
                            pt, bpt = next_ptr()
                            for c in range(8):
                                pe(lambda e, c=c: e.transpose(out=pt[:, c * 128:c * 128 + n], in_=hb[k][0:n, c * 128:(c + 1) * 128], identity=ident[0:n, 0:n]), [bhb[k], b_ident], [bpt], inc=(c == 7))
                            ptv = pt[:, :].rearrange("p (c t) -> p c t", c=8)
                            act(lambda e: e.copy(out=hT[:, :, t0:t0 + n], in_=ptv[:, :, 0:n]), [bpt], [b_hT])
                        return [g0, g1, g2, g3]

                    for ti0 in range(0, NT, 4):
                        grp_ = [h_stages(ti, *tiles[ti]) for ti in range(ti0, min(NT, ti0 + 4))]
                        for si in range(4):
                            for stg in grp_:
                                stg[si]()

                    ck("hT")
                    SE = contextlib.ExitStack()
                    EA = sb(SE, [128, 8, 1024], BF16, "EAt"); b_EA = Buf()
                    EB = sb(SE, [128, 4, 1024], BF16, "EBt"); b_EB = Buf()
                    P.dma("sp", EA[:, :, :], EscrA[0:8].rearrange("h p u -> p h u"), (), [b_EA])
                    P.dma("sp", EB[:, :, :], EscrA[8 + 4 * l:12 + 4 * l].rearrange("h p u -> p h u"), (), [b_EB])

                    def proj_fm(dst_fn, wt, bw, wcols, nchunks, rows=128):
                        for q0 in range(0, T, 512):
                            nq = min(512, T - q0)
                            for j0 in range(0, nchunks, 2):
                                pd, bpd = next_dbl()
                                nj = min(2, nchunks - j0)
                                for jj in range(nj):
                                    for c in range(8):
                                        pe(lambda e, jj=jj, c=c, pd=pd, j0=j0: e.matmul(pd[0:rows, jj * 512:jj * 512 + nq], lhsT=wt[:, c, wcols[j0 + jj]:wcols[j0 + jj] + rows], rhs=hT[:, c, q0:q0 + nq], start=(c == 0), stop=(c == 7)),
                                           [bw, b_hT], [bpd], inc=(c == 7 and jj == nj - 1))
                                for jj in range(nj):
                                    dst, bd = dst_fn(j0 + jj, q0, q0 + nq)
                                    act(lambda e, jj=jj, pd=pd, dst=dst: e.copy(out=dst, in_=pd[0:rows, jj * 512:jj * 512 + nq]), [bpd], [bd])

                    for ah in range(2):
                        with contextlib.ExitStack() as SA:
                            wA_ = sb(SA, [128, 8, 768], BF16, "wA"); b_wA_ = Buf()
                            wsrc = wl["w_in"].rearrange("(c p) n -> p c n", p=128)
                            for i3 in range(3):
                                P.dma("sp", wA_[:, :, i3 * 256:(i3 + 1) * 256], wsrc[:, :, i3 * 512 + ah * 256:i3 * 512 + ah * 256 + 256], [bW], [b_wA_])
                            ck("Aw")
                            QT = sb(SA, [128, 2, T], BF16, "QTA"); b_QT = Buf()
                            KT = sb(SA, [128, 2, NK], BF16, "KTA"); b_KT = Buf()
                            VA = sb(SA, [128, NKB, 4, 65], BF16, "VA"); b_VA = Buf()
                            dve(lambda e: e.memset(VA[:, :, :, :].rearrange("p a b c -> p (a b c)"), 1.0), (), [b_VA])
                            kv32 = [sb(SA, [128, 2, 256], F32, "kv32") for _ in range(2)]; bkv32 = [Buf(), Buf()]
                            if not prm:
                                ckb = [sb(SA, [128, 256], BF16, "ckb") for _ in range(2)]; bckb = [Buf(), Buf()]
                                for j in range(NKB_past):
                                    k = j % 2
                                    P.dma("pool", ckb[k][:, :], IA["cache_a_k"][l, j * 128:(j + 1) * 128, ah * 256:(ah + 1) * 256], (), [bckb[k]])
                                    P.dma("pool", VA[:, j, :, 0:64], IA["cache_a_v"][l, j * 128:(j + 1) * 128, ah * 256:(ah + 1) * 256].rearrange("p (h d) -> p h d", h=4), (), [b_VA])
                                    pt, bpt = next_ptr()
                                    for c in range(2):
                                        pe(lambda e, c=c, k=k, pt=pt: e.transpose(out=pt[:, c * 128:(c + 1) * 128], in_=ckb[k][:, c * 128:(c + 1) * 128], identity=ident[:, :]), [bckb[k], b_ident], [bpt], inc=(c == 1))
                                    ptv = pt[:, 0:256].rearrange("p (c t) -> p c t", c=2)
                                    act(lambda e, j=j, ptv=ptv: e.copy(out=KT[:, :, j * 128:(j + 1) * 128], in_=ptv), [bpt], [b_KT])
                            proj_fm(lambda j, c0, c1: (QT[:, j, c0:c1], b_QT), wA_, b_wA_, [0, 128], 2)
                            ck("Aq")
                            proj_fm(lambda j, c0, c1: (KT[:, j, past + c0:past + c1], b_KT), wA_, b_wA_, [256, 384], 2)
                            ck("Ak")
                            for ti, (t0, n) in enumerate(tiles):
                                pd, bpd = next_dbl()
                                for jj in range(2):
                                    for c in range(8):
                                        pe(lambda e, jj=jj, c=c, pd=pd, t0=t0, n=n: e.matmul(pd[0:n, jj * 512:jj * 512 + 256], lhsT=hT[:, c, t0:t0 + n], rhs=wA_[:, c, 256 + jj * 256:512 + jj * 256], start=(c == 0), stop=(c == 7)),
                                           [b_wA_, b_hT], [bpd], inc=(c == 7 and jj == 1))
                                k = ti % 2
                                pdv = pd[:, :].rearrange("p (a b) -> p a b", a=2)
                                if '1' not in DBG:
                                    act(lambda e, k=k, n=n, pdv=pdv: e.copy(out=kv32[k][0:n, :, :], in_=pdv[0:n, :, 0:256]), [bpd], [bkv32[k]])
                                if '2' not in DBG:
                                    dve(lambda e, ti=ti, n=n, pd=pd: e.tensor_copy(out=VA[0:n, NKB_past + ti, :, 0:64], in_=pd[0:n, 512:768].rearrange("p (h d) -> p h d", h=4)), [bpd], [b_VA])
                                if 'D' in DBG and ti == 0 and l == 0 and ah == 0:
                                    P.dma("pool", ydst[384:512, 0:256], kv32[k][:, 0, :], [bkv32[k]], [])
                                    P.dma("pool", ydst[512:640, 0:768], wA_[:, 0, :], [b_wA_], [])
                                if 'O' not in DBG:
                                    P.dma("pool", OA[pfx + "a_k"][l, so, t0:t0 + n, ah * 256:(ah + 1) * 256], kv32[k][0:n, 0, :], [bkv32[k]], [])
                                    P.dma("pool", OA[pfx + "a_v"][l, so, t0:t0 + n, ah * 256:(ah + 1) * 256], kv32[k][0:n, 1, :], [bkv32[k]], [])
                            ck("Aproj")
                            callsA = []
                            for hh in range(4):
                                h = ah * 4 + hh
                                c, r0 = hh // 2, (hh % 2) * 64
                                lanes = []
                                for half in range(2):
                                    rr = r0 + 32 * half
                                    lanes.append(dict(kt=lambda c0, nk, rr=rr, c=c: KT[rr:rr + 32, c, c0:c0 + nk], qt=lambda c0, c1, rr=rr, c=c: QT[rr:rr + 32, c, c0:c1],
                                                      v=lambda kbi, nk, hh=hh: VA[0:nk, kbi, hh, :], E=EA[:, h, :], Eb=b_EA, Ethr=-91, c0=384, tp=(rr, 0)))
                                for qb in qblocks:
                                    callsA.append(("A", lanes, qb, "causal", list(enumerate(kblocks)), [b_QT, b_KT, b_VA], epi_diff(h)))
                            attend_many(callsA)
                            flush_pending()
                            P.barrier()
                            ck("Ahalf")

                    with contextlib.ExitStack() as SB:
                        wB_ = sb(SB, [128, 8, 768], BF16, "wB"); b_wB_ = Buf()
                        wsrc = wl["w_in"].rearrange("(c p) n -> p c n", p=128)
                        P.dma("sp", wB_[:, :, :], wsrc[:, :, 1536:2304], [bW], [b_wB_])
                        pastB = 0 if prm else NBS
                        NKb = pastB + T
                        kbB = [(j * 128, past - pastB + j * 128, 128) for j in range(pastB // 128)] + [(pastB + t0, past + t0, n) for (t0, n) in tiles]
                        QT = sb(SB, [128, 2, T], BF16, "QTB"); b_QT = Buf()
                        KT = sb(SB, [128, 2, NKb], BF16, "KTB"); b_KT = Buf()
                        VB = sb(SB, [128, len(kbB), 4, 65], BF16, "VB"); b_VB = Buf()
                        dve(lambda e: e.memset(VB[:, :, :, :].rearrange("p a b c -> p (a b c)"), 1.0), (), [b_VB])
                        kv32 = [sb(SB, [128, 512], F32, "kv32b") for _ in range(2)]; bkv32 = [Buf(), Buf()]
                        if not prm:
                            ckb = [sb(SB, [128, 256], BF16, "ckbb") for _ in range(2)]; bckb = [Buf(), Buf()]
                            for j in range(pastB // 128):
                                k = j % 2
                                P.dma("pool", ckb[k][:, :], IA["cache_b_k"][l, j * 128:(j + 1) * 128, :], (), [bckb[k]])
                                P.dma("pool", VB[:, j, :, 0:64], IA["cache_b_v"][l, j * 128:(j + 1) * 128, :].rearrange("p (h d) -> p h d", h=4), (), [b_VB])
                                pt, bpt = next_ptr()
                                for c in range(2):
                                    pe(lambda e, c=c, k=k, pt=pt: e.transpose(out=pt[:, c * 128:(c + 1) * 128], in_=ckb[k][:, c * 128:(c + 1) * 128], identity=ident[:, :]), [bckb[k], b_ident], [bpt], inc=(c == 1))
                                ptv = pt[:, 0:256].rearrange("p (c t) -> p c t", c=2)
                                act(lambda e, j=j, ptv=ptv: e.copy(out=KT[:, :, j * 128:(j + 1) * 128], in_=ptv), [bpt], [b_KT])
                        proj_fm(lambda j, c0, c1: (QT[:, j, c0:c1], b_QT), wB_, b_wB_, [0, 128], 2)
                        proj_fm(lambda j, c0, c1: (KT[:, j, pastB + c0:pastB + c1], b_KT), wB_, b_wB_, [256, 384], 2)
                        for ti, (t0, n) in enumerate(tiles):
                            pd, bpd = next_dbl()
                            for c in range(8):
                                pe(lambda e, c=c, pd=pd, t0=t0, n=n: e.matmul(pd[0:n, 0:512], lhsT=hT[:, c, t0:t0 + n], rhs=wB_[:, c, 256:768], start=(c == 0), stop=(c == 7)), [b_wB_, b_hT], [bpd], inc=(c == 7))
                            k = ti % 2
                            act(lambda e, k=k, n=n, pd=pd: e.copy(out=kv32[k][0:n, :], in_=pd[0:n, 0:512]), [bpd], [bkv32[k]])
                            dve(lambda e, ti=ti, n=n, pd=pd: e.tensor_copy(out=VB[0:n, pastB // 128 + ti, :, 0:64], in_=pd[0:n, 256:512].rearrange("p (h d) -> p h d", h=4)), [bpd], [b_VB])
                            if prm:
                                if t0 >= SEQ - NBP:
                                    r0_ = t0 - (SEQ - NBP)
                                    P.dma("pool", OA["p_b_k"][l, so, r0_:r0_ + n, :], kv32[k][0:n, 0:256], [bkv32[k]], [])
                                    P.dma("pool", OA["p_b_v"][l, so, r0_:r0_ + n, :], kv32[k][0:n, 256:512], [bkv32[k]], [])
                            else:
                                P.dma("pool", OA["s_b_k"][l, 0, t0:t0 + n, :], kv32[k][0:n, 0:256], [bkv32[k]], [])
                                P.dma("pool", OA["s_b_v"][l, 0, t0:t0 + n, :], kv32[k][0:n, 256:512], [bkv32[k]], [])
                        callsB = []
                        for hp in range(2):
                            lanes = []
                            for li in range(2):
                                h = hp * 2 + li
                                r0 = li * 64
                                lanes.append(dict(kt=lambda c0, nk, r0=r0, hp=hp: KT[r0:r0 + 64, hp, c0:c0 + nk], qt=lambda c0, c1, r0=r0, hp=hp: QT[r0:r0 + 64, hp, c0:c1],
                                                  v=lambda kbi, nk, h=h: VB[0:nk, kbi, h, :], E=EB[:, h, :], Eb=b_EB, Ethr=-128, c0=384, tp=None))
                            for qb in qblocks:
                                callsB.append(("B", lanes, qb, "band", list(enumerate(kbB)), [b_QT, b_KT, b_VB], epi_plain([512 + (hp * 2) * 64, 512 + (hp * 2 + 1) * 64])))
                        attend_many(callsB)
                        flush_pending()
                        P.barrier()

                    ck("B")
                    SE.close()
                    with contextlib.ExitStack() as SC:
                        wC_ = sb(SC, [128, 8, 416], BF16, "wC"); b_wC_ = Buf()
                        wsrc = wl["w_in"].rearrange("(c p) n -> p c n", p=128)
                        P.dma("sp", wC_[:, :, :], wsrc[:, :, 2304:2720], [bW], [b_wC_])
                        wuq = sb(SC, [128, 2, 384], BF16, "wuq"); b_wuq = Buf()
                        P.dma("sp", wuq[:, :, :], wl["mla_w_uq"].rearrange("(c p) n -> p c n", p=128), [bW], [b_wuq])
                        wukv = sb(SC, [128, 512], BF16, "wukv"); b_wukv = Buf()
                        P.dma("sp", wukv[:, :], wl["mla_w_ukv"], [bW], [b_wukv])
                        gq = sb(SC, [128, 256], F32, "gq"); b_gq = Buf()
                        gkv = sb(SC, [128, 128], F32, "gkv"); b_gkv = Buf()
                        bcast_row(gq[:], I["mla_q_norm"], l * 256, 256, [b_gq])
                        bcast_row(gkv[:], I["mla_kv_norm"], l * 128, 128, [b_gkv])
                        cqnT = sb(SC, [128, 2, T], BF16, "cqnT"); b_cqnT = Buf()
                        latT = sb(SC, [128, NK], BF16, "latT"); b_latT = Buf()
                        kT = sb(SC, [96, 4, NK], BF16, "kTC"); b_kT = Buf()
                        qT = sb(SC, [96, 4, T], BF16, "qTC"); b_qT = Buf()
                        VC = sb(SC, [128, NKB, 4, 65], BF16, "VC"); b_VC = Buf()
                        dve(lambda e: e.memset(VC[:, :, :, :].rearrange("p a b c -> p (a b c)"), 1.0), (), [b_VC])
                        rp = sb(SC, [128, NT, 128], F32, "rope"); b_rp = Buf()
                        for ti, (t0, n) in enumerate(tiles):
                            P.dma("sp", rp[0:n, ti, :], rope_h.ap()[t0:t0 + n, :], (), [b_rp])
                        c32 = [sb(SC, [128, 416], F32, "c32") for _ in range(4)]; bc32 = [Buf() for _ in range(4)]
                        ssc = [sb(SC, [128, 2], F32, "ssc") for _ in range(4)]; bssc = [Buf(strict=True) for _ in range(4)]
                        junkc = sb(SC, [128, 384], BF16, "junkc"); b_junkc = Buf()
                        cqn = [sb(SC, [128, 256], BF16, "cqn") for _ in range(4)]; bcqn = [Buf() for _ in range(4)]
                        lat32 = [sb(SC, [128, 128], F32, "lat32") for _ in range(4)]; blat32 = [Buf() for _ in range(4)]
                        latb = [sb(SC, [128, 160], BF16, "latb") for _ in range(4)]; blatb = [Buf() for _ in range(4)]
                        kpe32 = [sb(SC, [128, 32], F32, "kpe32") for _ in range(4)]; bkpe32 = [Buf() for _ in range(4)]
                        rt = sb(SC, [128, 4, 4, 16], F32, "rt"); b_rt = Buf()
                        q32 = [sb(SC, [128, 4, 96], F32, "q32") for _ in range(4)]; bq32 = [Buf() for _ in range(4)]
                        qb16 = [sb(SC, [128, 4, 96], BF16, "qb16") for _ in range(4)]; bqb16 = [Buf() for _ in range(4)]

                        def rope_apply(dst1, dst2, x1, x2, cs_, sn_, shape_n, rbufs_, wbufs_):
                            nh = shape_n
                            dve(lambda e: e.tensor_tensor(out=rt[0:nh[0], 0, 0:nh[1], :], in0=x1, in1=cs_, op=ALU.mult), rbufs_, [b_rt])
                            dve(lambda e: e.tensor_tensor(out=rt[0:nh[0], 1, 0:nh[1], :], in0=x2, in1=sn_, op=ALU.mult), rbufs_, [b_rt])
                            dve(lambda e: e.tensor_tensor(out=rt[0:nh[0], 2, 0:nh[1], :], in0=x2, in1=cs_, op=ALU.mult), rbufs_, [b_rt])
                            dve(lambda e: e.tensor_tensor(out=rt[0:nh[0], 3, 0:nh[1], :], in0=x1, in1=sn_, op=ALU.mult), rbufs_, [b_rt])
                            dve(lambda e: e.tensor_tensor(out=dst1, in0=rt[0:nh[0], 0, 0:nh[1], :], in1=rt[0:nh[0], 1, 0:nh[1], :], op=ALU.subtract), [b_rt], wbufs_)
                            dve(lambda e: e.tensor_tensor(out=dst2, in0=rt[0:nh[0], 2, 0:nh[1], :], in1=rt[0:nh[0], 3, 0:nh[1], :], op=ALU.add), [b_rt], wbufs_)

                        if not prm:
                            for j in range(NKB_past):
                                k = j % 2
                                P.dma("pool", latb[k][:, 0:128], IA["cache_c_latent"][l, j * 128:(j + 1) * 128, :], (), [blatb[k]])
                                P.dma("pool", latb[k][:, 128:160], IA["cache_c_rope_k"][l, j * 128:(j + 1) * 128, :], (), [blatb[k]])
                                pt, bpt = next_ptr()
                                pe(lambda e, k=k, pt=pt: e.transpose(out=pt[:, 0:128], in_=latb[k][:, 0:128], identity=ident[:, :]), [blatb[k], b_ident], [bpt], inc=False)
                                pe(lambda e, k=k, pt=pt: e.transpose(out=pt[0:32, 128:256], in_=latb[k][:, 128:160], identity=ident[:, :]), [blatb[k], b_ident], [bpt])
                                act(lambda e, j=j, pt=pt: e.copy(out=latT[:, j * 128:(j + 1) * 128], in_=pt[:, 0:128]), [bpt], [b_latT])
                                for h in range(4):
                                    dve(lambda e, j=j, h=h, pt=pt: e.tensor_copy(out=kT[64:96, h, j * 128:(j + 1) * 128], in_=pt[0:32, 128:256]), [bpt], [b_kT])
                        rt2 = [rt] + [sb(SC, [128, 4, 4, 16], F32, "rtb") for _ in range(3)]; b_rt2 = [b_rt, Buf(), Buf(), Buf()]
                        junkc2 = [junkc] * 4; b_junkc2 = [b_junkc] * 4

                        def rope2(k, dst1, dst2, x1, x2, cs_, sn_, nh, rbufs_, wbufs_):
                            R_, bR = rt2[k], b_rt2[k]
                            dve(lambda e: e.tensor_tensor(out=R_[0:nh[0], 0, 0:nh[1], :], in0=x1, in1=cs_, op=ALU.mult), rbufs_, [bR])
                            dve(lambda e: e.tensor_tensor(out=R_[0:nh[0], 1, 0:nh[1], :], in0=x2, in1=sn_, op=ALU.mult), rbufs_, [bR])
                            dve(lambda e: e.tensor_tensor(out=R_[0:nh[0], 2, 0:nh[1], :], in0=x2, in1=cs_, op=ALU.mult), rbufs_, [bR])
                            dve(lambda e: e.tensor_tensor(out=R_[0:nh[0], 3, 0:nh[1], :], in0=x1, in1=sn_, op=ALU.mult), rbufs_, [bR])
                            dve(lambda e: e.tensor_tensor(out=dst1, in0=R_[0:nh[0], 0, 0:nh[1], :], in1=R_[0:nh[0], 1, 0:nh[1], :], op=ALU.subtract), [bR], wbufs_)
                            dve(lambda e: e.tensor_tensor(out=dst2, in0=R_[0:nh[0], 2, 0:nh[1], :], in1=R_[0:nh[0], 3, 0:nh[1], :], op=ALU.add), [bR], wbufs_)

                        def c_stages(ti, t0, n):
                            k = ti % 4
                            st = {}

                            def g0():
                                pd, bpd = next_dbl()
                                for c in range(8):
                                    pe(lambda e, c=c: e.matmul(pd[0:n, 0:416], lhsT=hT[:, c, t0:t0 + n], rhs=wC_[:, c, :], start=(c == 0), stop=(c == 7)), [b_wC_, b_hT], [bpd], inc=(c == 7))
                                act(lambda e: e.copy(out=c32[k][0:n, :], in_=pd[0:n, 0:416]), [bpd], [bc32[k]])

                            def g1():
                                dve(lambda e: e.scalar_tensor_tensor(out=junkc2[k][0:n, 0:256], in0=c32[k][0:n, 0:256], scalar=1.0 / 256, in1=c32[k][0:n, 0:256], op0=ALU.mult, op1=ALU.mult, accum_out=ssc[k][0:n, 0:1]), [bc32[k]], [b_junkc2[k], bssc[k]])
                                dve(lambda e: e.scalar_tensor_tensor(out=junkc2[k][0:n, 0:128], in0=c32[k][0:n, 256:384], scalar=1.0 / 128, in1=c32[k][0:n, 256:384], op0=ALU.mult, op1=ALU.mult, accum_out=ssc[k][0:n, 1:2]), [bc32[k]], [b_junkc2[k], bssc[k]])
                                rstd_of(ssc[k][0:n, 0:2], n, 1, bssc[k])
                                rope2(k, kpe32[k][0:n, 0:16].rearrange("p (a d) -> p a d", a=1), kpe32[k][0:n, 16:32].rearrange("p (a d) -> p a d", a=1),
                                      c32[k][0:n, 384:400].rearrange("p (a d) -> p a d", a=1), c32[k][0:n, 400:416].rearrange("p (a d) -> p a d", a=1),
                                      rp[0:n, ti, 0:16].rearrange("p (a d) -> p a d", a=1), rp[0:n, ti, 64:80].rearrange("p (a d) -> p a d", a=1), (n, 1), [bc32[k], b_rp], [bkpe32[k]])
                                P.dma("pool", OA[pfx + "kpe"][l, so, t0:t0 + n, :], kpe32[k][0:n, :], [bkpe32[k]], [])
                                dve(lambda e: e.tensor_copy(out=latb[k][0:n, 128:160], in_=kpe32[k][0:n, :]), [bkpe32[k]], [blatb[k]])

                            def g2():
                                dve(lambda e: e.scalar_tensor_tensor(out=cqn[k][0:n, :], in0=c32[k][0:n, 0:256], scalar=ssc[k][0:n, 0:1], in1=gq[0:n, :], op0=ALU.mult, op1=ALU.mult), [bc32[k], bssc[k], b_gq], [bcqn[k]])
                                dve(lambda e: e.scalar_tensor_tensor(out=lat32[k][0:n, :], in0=c32[k][0:n, 256:384], scalar=ssc[k][0:n, 1:2], in1=gkv[0:n, :], op0=ALU.mult, op1=ALU.mult), [bc32[k], bssc[k], b_gkv], [blat32[k]])
                                P.dma("pool", OA[pfx + "lat"][l, so, t0:t0 + n, :], lat32[k][0:n, :], [blat32[k]], [])
                                dve(lambda e: e.tensor_copy(out=latb[k][0:n, 0:128], in_=lat32[k][0:n, :]), [blat32[k]], [blatb[k]])

                            def g3():
                                pt, bpt = next_ptr()
                                for c in range(2):
                                    pe(lambda e, c=c: e.transpose(out=pt[:, c * 128:c * 128 + n], in_=cqn[k][0:n, c * 128:(c + 1) * 128], identity=ident[0:n, 0:n]), [bcqn[k], b_ident], [bpt], inc=False)
                                pe(lambda e: e.transpose(out=pt[:, 256:256 + n], in_=latb[k][0:n, 0:128], identity=ident[0:n, 0:n]), [blatb[k], b_ident], [bpt], inc=False)
                                pe(lambda e: e.transpose(out=pt[0:32, 384:384 + n], in_=latb[k][0:n, 128:160], identity=ident[0:n, 0:n]), [blatb[k], b_ident], [bpt])
                                ptv = pt[:, 0:256].rearrange("p (c t) -> p c t", c=2)
                                act(lambda e: e.copy(out=cqnT[:, :, t0:t0 + n], in_=ptv[:, :, 0:n]), [bpt], [b_cqnT])
                                act(lambda e: e.copy(out=latT[:, past + t0:past + t0 + n], in_=pt[:, 256:256 + n]), [bpt], [b_latT])
                                for h in range(4):
                                    dve(lambda e, h=h: e.tensor_copy(out=kT[64:96, h, past + t0:past + t0 + n], in_=pt[0:32, 384:384 + n]), [bpt], [b_kT])

                            def g4():
                                pd, bpd = next_dbl()
                                for c in range(2):
                                    pe(lambda e, c=c: e.matmul(pd[0:n, 0:384], lhsT=cqnT[:, c, t0:t0 + n], rhs=wuq[:, c, :], start=(c == 0), stop=(c == 1)), [b_cqnT, b_wuq], [bpd], inc=(c == 1))
                                act(lambda e: e.copy(out=q32[k][0:n, :, :], in_=pd[0:n, 0:384].rearrange("p (h d) -> p h d", h=4)), [bpd], [bq32[k]])

                            def g5():
                                dve(lambda e: e.tensor_copy(out=qb16[k][0:n, :, 0:64], in_=q32[k][0:n, :, 0:64]), [bq32[k]], [bqb16[k]])
                                rope2(k, qb16[k][0:n, :, 64:80], qb16[k][0:n, :, 80:96], q32[k][0:n, :, 64:80], q32[k][0:n, :, 80:96],
                                      rp[0:n, ti, 0:64].rearrange("p (h d) -> p h d", h=4), rp[0:n, ti, 64:128].rearrange("p (h d) -> p h d", h=4), (n, 4), [bq32[k], b_rp], [bqb16[k]])

                            def g6():
                                pt, bpt = next_ptr()
                                for h in range(4):
                                    pe(lambda e, h=h: e.transpose(out=pt[0:96, h * 128:h * 128 + n], in_=qb16[k][0:n, h, :], identity=ident[0:n, 0:n]), [bqb16[k], b_ident], [bpt], inc=(h == 3))
                                ptv = pt[:, 0:512].rearrange("p (c t) -> p c t", c=4)
                                act(lambda e: e.copy(out=qT[:, :, t0:t0 + n], in_=ptv[0:96, :, 0:n]), [bpt], [b_qT])
                            return [g0, g1, g2, g3, g4, g5, g6]

                        for ti0 in range(0, NT, 4):
                            grp_ = [c_stages(ti, *tiles[ti]) for ti in range(ti0, min(NT, ti0 + 4))]
                            for si in range(7):
                                for stg in grp_:
                                    stg[si]()
                        for c0 in range(0, NK, 512):
                            nn_ = min(512, NK - c0)
                            for hp in range(2):
                                pd, bpd = next_dbl()
                                for jj in range(2):
                                    h = hp * 2 + jj
                                    pe(lambda e, jj=jj, h=h, pd=pd, c0=c0, nn_=nn_: e.matmul(pd[0:64, jj * 512:jj * 512 + nn_], lhsT=wukv[:, h * 128:h * 128 + 64], rhs=latT[:, c0:c0 + nn_], start=True, stop=True), [b_wukv, b_latT], [bpd], inc=(jj == 1))
                                pdv = pd[:, :].rearrange("p (a b) -> p a b", a=2)
                                act(lambda e, hp=hp, pdv=pdv, c0=c0, nn_=nn_: e.copy(out=kT[0:64, hp * 2:hp * 2 + 2, c0:c0 + nn_], in_=pdv[0:64, :, 0:nn_]), [bpd], [b_kT])
                        wv_ = wukv[:, :].rearrange("p (h d) -> p h d", h=4)
                        for kbi, (kc0, kpos, nk) in enumerate(kblocks):
                            pd, bpd = next_dbl()
                            pe(lambda e, pd=pd, kc0=kc0, nk=nk: e.matmul(pd[0:nk, 0:256].rearrange("p (h d) -> p h d", h=4), lhsT=latT[:, kc0:kc0 + nk], rhs=wv_[:, :, 64:128], start=True, stop=True), [b_wukv, b_latT], [bpd])
                            act(lambda e, kbi=kbi, nk=nk, pd=pd: e.copy(out=VC[0:nk, kbi, :, 0:64], in_=pd[0:nk, 0:256].rearrange("p (h d) -> p h d", h=4)), [bpd], [b_VC])
                        if 'Q' in DBG and l == 0:
                            for hq in range(4):
                                P.dma("pool", ydst[0:96, hq * 128:(hq + 1) * 128], qT[:, hq, 0:128], [b_qT], [])
                                P.dma("pool", ydst[128:224, hq * 128:(hq + 1) * 128], kT[:, hq, 0:128], [b_kT], [])
                            P.dma("pool", ydst[256:384, 0:260], VC[:, 0, :, :].rearrange("p a b -> p (a b)"), [b_VC], [])
                            P.dma("pool", ydst[384:512, 0:128], latT[:, 0:128], [b_latT], [])
                        callsC = []
                        for hp in range(2):
                            lanes = []
                            for li in range(2):
                                h = hp * 2 + li
                                lanes.append(dict(kt=lambda c0, nk, h=h: kT[0:96, h, c0:c0 + nk], qt=lambda c0, c1, h=h: qT[0:96, h, c0:c1], v=lambda kbi, nk, h=h: VC[0:nk, kbi, h, :], E=None, tp=None))
                            for qb in qblocks:
                                callsC.append(("C", lanes, qb, "causal", list(enumerate(kblocks)), [b_qT, b_kT, b_VC], epi_plain([768 + (hp * 2) * 64, 768 + (hp * 2 + 1) * 64])))
                        attend_many(callsC)
                        flush_pending()
                        P.barrier()
                    with contextlib.ExitStack() as SO:
                        wout = sb(SO, [128, 8, D], BF16, "wout"); b_wout = Buf()
                        P.dma("sp", wout[:, :, :], wl["w_out"].rearrange("(c p) n -> p c n", p=128), [bW], [b_wout])
                        gmp = sb(SO, [128, D], F32, "gmp"); b_gmp = Buf()
                        bcast_row(gmp[:], I["norms"], (l * 7 + 1) * D, D, [b_gmp])
                        xo_ = [sb(SO, [128, D], F32, "xo") for _ in range(2)]; bxo = [Buf(), Buf()]
                        yo_ = [sb(SO, [128, D], F32, "yo") for _ in range(2)]; byo = [Buf(), Buf()]
                        aT = [sb(SO, [128, 8, 128], BF16, "aT") for _ in range(2)]; baT = [Buf(), Buf()]
                        junko = sb(SO, [128, D], BF16, "junko"); b_junko = Buf()
                        sso = sb(SO, [128, 2], F32, "sso"); b_sso = Buf(strict=True)
                        b_sso2 = [Buf(strict=True), Buf(strict=True)]

                        def wo_xload(ti):
                            (t0, n) = tiles[ti]
                            k = ti % 2
                            P.dma("sp", xo_[k][0:n, :], xsrc[t0:t0 + n, :], [ybufs[ti]], [bxo[k]])

                        def wo_prep(ti):
                            (t0, n) = tiles[ti]
                            k = ti % 2
                            pt, bpt = next_ptr()
                            for c in range(8):
                                pe(lambda e, c=c: e.transpose(out=pt[:, c * 128:c * 128 + n], in_=a_tok[0:n, ti, c * 128:(c + 1) * 128], identity=ident[0:n, 0:n]), [b_atok, b_ident], [bpt], inc=(c == 7))
                            ptv = pt[:, :].rearrange("p (c t) -> p c t", c=8)
                            act(lambda e: e.copy(out=aT[k][:, :, 0:n], in_=ptv[:, :, 0:n]), [bpt], [baT[k]])

                        for ti in range(min(2, NT)):
                            wo_xload(ti)
                            wo_prep(ti)
                        for ti, (t0, n) in enumerate(tiles):
                            k = ti % 2
                            pd, bpd = next_dbl()
                            for hf in range(2):
                                for c in range(8):
                                    pe(lambda e, hf=hf, c=c, n=n, k=k, pd=pd: e.matmul(pd[0:n, hf * 512:(hf + 1) * 512], lhsT=aT[k][:, c, 0:n], rhs=wout[:, c, hf * 512:(hf + 1) * 512], start=(c == 0), stop=(c == 7)), [baT[k], b_wout], [bpd], inc=(c == 7 and hf == 1))
                            act(lambda e, k=k, n=n, pd=pd: e.copy(out=yo_[k][0:n, :], in_=pd[0:n, :]), [bpd], [byo[k]])
                            if ti + 2 < NT:
                                wo_prep(ti + 2)
                            dve(lambda e, k=k, n=n: e.scalar_tensor_tensor(out=junko[0:n, :], in0=yo_[k][0:n, :], scalar=1.0, in1=yo_[k][0:n, :], op0=ALU.mult, op1=ALU.mult, accum_out=sso[0:n, k:k + 1]), [byo[k]], [b_junko, b_sso2[k]])
                            rstd_of(sso[0:n, k:k + 1], n, D, b_sso2[k])
                            dve(lambda e, k=k, n=n: e.scalar_tensor_tensor(out=yo_[k][0:n, :], in0=yo_[k][0:n, :], scalar=sso[0:n, k:k + 1], in1=gmp[0:n, :], op0=ALU.mult, op1=ALU.mult), [byo[k], b_sso2[k], b_gmp], [byo[k]])
                            P.op("pool", lambda e, k=k, n=n: e.tensor_tensor(out=xo_[k][0:n, :], in0=xo_[k][0:n, :], in1=yo_[k][0:n, :], op=ALU.add), [byo[k], bxo[k]], [bxo[k]])
                            P.dma("pool", ydst[t0:t0 + n, :], xo_[k][0:n, :], [bxo[k]], [ybufs[ti]])
                            if ti + 2 < NT:
                                wo_xload(ti + 2)
                        P.barrier()
                    P.barrier()

                ck("C")
                with contextlib.ExitStack() as S2:
                    wxq = sb(S2, [128, 8, 256], BF16, "wxq"); b_wxq = Buf()
                    wxo = sb(S2, [128, 2, D], BF16, "wxo"); b_wxo = Buf()
                    gb = sb(S2, [128, 4, D], F32, "gb"); b_gb = Buf()
                    cw = sb(S2, [128, NFC, 4], F32, "cw"); b_cw = Buf()
                    mkT = sb(S2, [128, 2, MEM], BF16, "mkT"); b_mkT = Buf()
                    MV = sb(S2, [128, 2, 4, 65], BF16, "MV"); b_MV = Buf()
                    dve(lambda e: e.memset(MV[:, :, :, :].rearrange("p a b c -> p (a b c)"), 1.0), (), [b_MV])
                    wd_all = sb(S2, [128, NFC, D], BF16, "wd_all"); b_wd = Buf()
                    wds = wl["w_down"].rearrange("(f p) n -> p f n", p=128)
                    junkb = sb(S2, [128, D], BF16, "junkb2"); b_junkb = Buf()
                    ss4 = sb(S2, [128, 4], F32, "ss4"); b_ss4 = Buf(strict=True)
                    hb4 = sb(S2, [128, 4, D], BF16, "hb4"); b_hb4 = [Buf() for _ in range(4)]
                    if prm:
                        with contextlib.ExitStack() as SM:
                            wmk = sb(SM, [128, 8, 512], BF16, "wmk"); b_wmk = Buf()
                            gm = sb(SM, [128, D], F32, "gm"); b_gm = Buf()
                            mT = sb(SM, [128, 8, MEM], BF16, "mT"); b_mT = Buf()
                            m32 = sb(SM, [128, 512], F32, "m32"); b_m32 = Buf()
                            xm = [sb(SM, [128, D], F32, "xm") for _ in range(2)]; bxm = [Buf(), Buf()]
                            for mi in range(2):
                                P.dma("sp", xm[mi][:, :], IA["mem_prompt"][s, mi * 128:(mi + 1) * 128, :], (), [bxm[mi]])
                            bcast_row(gm[:], I["norms"], (l * 7 + 6) * D, D, [b_gm])
                            P.dma("sp", wmk[:, :, 0:256], wl["w_mk"].rearrange("(c p) n -> p c n", p=128), [bW], [b_wmk])
                            P.dma("sp", wmk[:, :, 256:512], wl["w_mv"].rearrange("(c p) n -> p c n", p=128), [bW], [b_wmk])
                            for mi in range(2):
                                k = mi % 2
                                dve(lambda e, k=k, mi=mi: e.scalar_tensor_tensor(out=junkb[:, :], in0=xm[k][:, :], scalar=1.0, in1=xm[k][:, :], op0=ALU.mult, op1=ALU.mult, accum_out=ss4[:, mi:mi + 1]), [bxm[k]], [b_junkb, b_ss4])
                            rstd_of(ss4[:, 0:2], 128, D, b_ss4)
                            for mi in range(2):
                                k = mi % 2
                                dve(lambda e, k=k, mi=mi: e.scalar_tensor_tensor(out=hb4[:, mi, :], in0=xm[k][:, :], scalar=ss4[:, mi:mi + 1], in1=gm[:, :], op0=ALU.mult, op1=ALU.mult), [bxm[k], b_ss4, b_gm], [b_hb4[mi]])
                                pt, bpt = next_ptr()
                                for c in range(8):
                                    pe(lambda e, c=c, pt=pt, mi=mi: e.transpose(out=pt[:, c * 128:(c + 1) * 128], in_=hb4[:, mi, c * 128:(c + 1) * 128], identity=ident[:, :]), [b_hb4[mi], b_ident], [bpt], inc=(c == 7))
                                ptv = pt[:, :].rearrange("p (c t) -> p c t", c=8)
                                act(lambda e, mi=mi, ptv=ptv: e.copy(out=mT[:, :, mi * 128:(mi + 1) * 128], in_=ptv), [bpt], [b_mT])
                            for mi in range(2):
                                pd, bpd = next_dbl()
                                for c in range(8):
                                    pe(lambda e, c=c, pd=pd, mi=mi: e.matmul(pd[:, 0:512], lhsT=mT[:, c, mi * 128:(mi + 1) * 128], rhs=wmk[:, c, :], start=(c == 0), stop=(c == 7)), [b_mT, b_wmk], [bpd], inc=(c == 7))
                                act(lambda e, pd=pd: e.copy(out=m32[:, :], in_=pd[:, 0:512]), [bpd], [b_m32])
                                dve(lambda e, mi=mi, pd=pd: e.tensor_copy(out=MV[:, mi, :, 0:64], in_=pd[:, 256:512].rearrange("p (h d) -> p h d", h=4)), [bpd], [b_MV])
                                P.dma("pool", OA["p_mk"][l, s, mi * 128:(mi + 1) * 128, :], m32[:, 0:256], [b_m32], [])
                                P.dma("pool", OA["p_mv"][l, s, mi * 128:(mi + 1) * 128, :], m32[:, 256:512], [b_m32], [])
                            pd, bpd = next_dbl()
                            for j in range(2):
                                for c in range(8):
                                    pe(lambda e, j=j, c=c, pd=pd: e.matmul(pd[:, j * 512:j * 512 + MEM], lhsT=wmk[:, c, j * 128:(j + 1) * 128], rhs=mT[:, c, :], start=(c == 0), stop=(c == 7)), [b_mT, b_wmk], [bpd], inc=(c == 7 and j == 1))
                            pdv = pd[:, :].rearrange("p (a b) -> p a b", a=2)
                            act(lambda e, pdv=pdv: e.copy(out=mkT[:, :, :], in_=pdv[:, :, 0:MEM]), [bpd], [b_mkT])
                            P.barrier()
                    else:
                        with contextlib.ExitStack() as SM:
                            ckb = sb(SM, [128, 256], BF16, "ckbm"); bckb = Buf()
                            for mi in range(2):
                                P.dma("pool", ckb[:, :], IA["cache_mem_k"][l, mi * 128:(mi + 1) * 128, :], (), [bckb])
                                P.dma("pool", MV[:, mi, :, 0:64], IA["cache_mem_v"][l, mi * 128:(mi + 1) * 128, :].rearrange("p (h d) -> p h d", h=4), (), [b_MV])
                                pt, bpt = next_ptr()
                                for c in range(2):
                                    pe(lambda e, c=c, pt=pt: e.transpose(out=pt[:, c * 128:(c + 1) * 128], in_=ckb[:, c * 128:(c + 1) * 128], identity=ident[:, :]), [bckb, b_ident], [bpt], inc=(c == 1))
                                ptv = pt[:, 0:256].rearrange("p (c t) -> p c t", c=2)
                                act(lambda e, mi=mi, ptv=ptv: e.copy(out=mkT[:, :, mi * 128:(mi + 1) * 128], in_=ptv), [bpt], [b_mkT])
                            P.barrier()
                    ck("mem")
                    P.dma("sp", wxq[:, :, :], wl["w_xq"].rearrange("(c p) n -> p c n", p=128), [bW], [b_wxq])
                    P.dma("sp", wxo[:, :, :], wl["w_xo"].rearrange("(c p) n -> p c n", p=128), [bW], [b_wxo])
                    for gi, ni in enumerate((2, 3, 4, 5)):
                        bcast_row(gb[:, gi, :], I["norms"], (l * 7 + ni) * D, D, [b_gb])
                    ysb = [sb(S2, [128, D], F32, "ysb") for _ in range(2)]; b_ysb = [Buf(), Buf()]
                    yrot = [0]
                    xa = sb(S2, [128, 4, 256], BF16, "xa"); b_xa = Buf()
                    xaT = sb(S2, [128, 2, 512], BF16, "xaT"); b_xaT = Buf()
                    xblk2 = [sb(S2, [128, 4, D], F32, "xblk") for _ in range(2)]
                    b_xb2 = [[Buf() for _ in range(4)] for _ in range(2)]
                    hT2 = sb(S2, [128, 8, 512], BF16, "hT2"); b_hT2 = Buf()
                    qxT = sb(S2, [128, 2, 512], BF16, "qxT"); b_qxT = Buf()
                    gs = [sb(S2, [128, 514], F32, "gs") for _ in range(2)]; b_gs = [Buf(), Buf()]
                    halo = sb(S2, [128, NFC, 2], F32, "halo"); b_halo = Buf()
                    cc = [sb(S2, [128, 512], F32, "cc") for _ in range(2)]; b_cc = [Buf(), Buf()]
                    sl = [sb(S2, [128, 512], F32, "sl") for _ in range(2)]; b_sl = [Buf(), Buf()]
                    aTf = sb(S2, [128, NFC, 512], BF16, "aTf"); b_aTf = Buf()
                    wg = [sb(S2, [128, 2, 8, 128], BF16, "wg") for _ in range(3)]; bwg = [Buf() for _ in range(3)]
                    ssp = sb(S2, [128, 4], F32, "ssp"); b_ssp = Buf(strict=True)
                    if 'M' in DBG:
                        print("PH2 sbuf remaining", nc.sbuf_bytes_remaining)

                    def post_residual(pd, bpd, n, t, gidx, xblk, b_xb):
                        k = yrot[0]; yrot[0] = 1 - k
                        Y, bY = ysb[k], b_ysb[k]
                        act(lambda e: e.copy(out=Y[0:n, :], in_=pd[0:n, :]), [bpd], [bY])
                        dve(lambda e: e.scalar_tensor_tensor(out=junkb[0:n, :], in0=Y[0:n, :], scalar=1.0, in1=Y[0:n, :], op0=ALU.mult, op1=ALU.mult, accum_out=ssp[0:n, t:t + 1]), [bY], [b_junkb, b_ssp])
                        rstd_of(ssp[0:n, t:t + 1], n, D, b_ssp)
                        dve(lambda e: e.scalar_tensor_tensor(out=Y[0:n, :], in0=Y[0:n, :], scalar=ssp[0:n, t:t + 1], in1=gb[0:n, gidx, :], op0=ALU.mult, op1=ALU.mult), [bY, b_ssp, b_gb], [bY])
                        P.op("pool", lambda e: e.tensor_tensor(out=xblk[0:n, t, :], in0=xblk[0:n, t, :], in1=Y[0:n, :], op=ALU.add), [bY, b_xb[t]], [b_xb[t]])

                    def pre_norm_dve(subt, gidx, xblk, b_xb):
                        nt = len(subt)
                        nn = subt[0][1]
                        for t, (a0, n) in enumerate(subt):
                            dve(lambda e, t=t, n=n: e.scalar_tensor_tensor(out=junkb[0:n, :], in0=xblk[0:n, t, :], scalar=1.0, in1=xblk[0:n, t, :], op0=ALU.mult, op1=ALU.mult, accum_out=ss4[0:n, t:t + 1]), [b_xb[t]], [b_junkb, b_ss4])
                        rstd_of(ss4[0:nn, 0:nt], nn, D, b_ss4)
                        for t, (a0, n) in enumerate(subt):
                            dve(lambda e, t=t, n=n: e.scalar_tensor_tensor(out=hb4[0:n, t, :], in0=xblk[0:n, t, :], scalar=ss4[0:n, t:t + 1], in1=gb[0:n, gidx, :], op0=ALU.mult, op1=ALU.mult), [b_xb[t], b_ss4, b_gb], [b_hb4[t]])

                    def pre_norm_pe(subt):
                        for t, (a0, n) in enumerate(subt):
                            pt, bpt = next_ptr()
                            for c in range(8):
                                pe(lambda e, c=c, pt=pt, t=t, n=n: e.transpose(out=pt[:, c * 128:c * 128 + n], in_=hb4[0:n, t, c * 128:(c + 1) * 128], identity=ident[0:n, 0:n]), [b_hb4[t], b_ident], [bpt], inc=(c == 7))
                            ptv = pt[:, :].rearrange("p (c t) -> p c t", c=8)
                            act(lambda e, ptv=ptv, a0=a0, n=n: e.copy(out=hT2[:, :, a0:a0 + n], in_=ptv[:, :, 0:n]), [bpt], [b_hT2])

                    def head_dve(bi):
                        (qc0_, qpos_, nq_) = qblocks[bi]
                        subt_ = [(a0, min(128, nq_ - a0)) for a0 in range(0, nq_, 128)]
                        X_, bX_ = xblk2[bi % 2], b_xb2[bi % 2]
                        for t, (a0, n) in enumerate(subt_):
                            ti = (qc0_ + a0) // 128
                            P.dma("sp", X_[0:n, t, :], ydst[qc0_ + a0:qc0_ + a0 + n, :], [ybufs[ti]], [bX_[t]])
                        pre_norm_dve(subt_, 0, X_, bX_)
                        return subt_

                    drot = [0]

                    def next_dbl3():
                        i = drot[0]; drot[0] = (i + 1) % 3
                        return dbl[i], bdbl[i]

                    subt0_ = head_dve(0)
                    for j in range(3):
                        P.dma("sp", cw[:, :, j], bass.AP(I["conv_w"], (l * 3 + j) * DFF, [[1, 128], [128, NFC]]), (), [b_cw], allow_slow_non_contiguous=True)
                    P.dma("sp", cw[:, :, 3], bass.AP(I["conv_b"], l * DFF, [[1, 128], [128, NFC]]), (), [b_cw], allow_slow_non_contiguous=True)
                    if prm:
                        dve(lambda e: e.memset(halo[:, :, :], 0.0), (), [b_halo])
                    else:
                        for j in range(2):
                            P.dma("sp", halo[:, :, j], bass.AP(I["state_ffn_conv"], (l * 2 + j) * DFF, [[1, 128], [128, NFC]]), (), [b_halo], allow_slow_non_contiguous=True)
                    for f0 in range(0, NFC, 6):
                        f1 = min(NFC, f0 + 6)
                        P.dma("sp", wd_all[:, f0:f1, :], wds[:, f0:f1, :], [bW], [b_wd])
                    pre_norm_pe(subt0_)
                    for bi, (qc0, qpos, nq) in enumerate(qblocks):
                        subt = [(a0, min(128, nq - a0)) for a0 in range(0, nq, 128)]
                        xblk, b_xb = xblk2[bi % 2], b_xb2[bi % 2]
                        ck("wout")
                        pd, bpd = next_dbl()
                        for j in range(2):
                            for c in range(8):
                                pe(lambda e, j=j, c=c, pd=pd: e.matmul(pd[:, j * 512:j * 512 + nq], lhsT=wxq[:, c, j * 128:(j + 1) * 128], rhs=hT2[:, c, 0:nq], start=(c == 0), stop=(c == 7)), [b_wxq, b_hT2], [bpd], inc=(c == 7 and j == 1))
                        pdv = pd[:, :].rearrange("p (a b) -> p a b", a=2)
                        act(lambda e, pdv=pdv: e.copy(out=qxT[:, :, 0:nq], in_=pdv[:, :, 0:nq]), [bpd], [b_qxT])
                        callsX = []
                        for hp in range(2):
                            lanes = []
                            for li in range(2):
                                h = hp * 2 + li
                                r0 = li * 64
                                lanes.append(dict(kt=lambda c0, nk, r0=r0, hp=hp: mkT[r0:r0 + 64, hp, c0:c0 + nk], qt=lambda c0, c1, r0=r0, hp=hp: qxT[r0:r0 + 64, hp, c0 - qc0:c1 - qc0],
                                                  v=lambda kbi, nk, h=h: MV[0:nk, kbi, h, :], E=None, tp=None))
                            callsX.append(("X", lanes, (qc0, qpos, nq), "all", [(0, (0, 0, 128)), (1, (128, 128, 128))], [b_qxT, b_mkT, b_MV], epi_plain([(hp * 2) * 64, (hp * 2 + 1) * 64], xa, b_xa)))
                        attend_many(callsX)
                        flush_pending()
                        for t, (a0, n) in enumerate(subt):
                            pt, bpt = next_ptr()
                            for c in range(2):
                                pe(lambda e, c=c, pt=pt, t=t, n=n: e.transpose(out=pt[:, c * 128:c * 128 + n], in_=xa[0:n, t, c * 128:(c + 1) * 128], identity=ident[0:n, 0:n]), [b_xa, b_ident], [bpt], inc=(c == 1))
                            ptv = pt[:, 0:256].rearrange("p (c t) -> p c t", c=2)
                            act(lambda e, ptv=ptv, a0=a0, n=n: e.copy(out=xaT[:, :, a0:a0 + n], in_=ptv[:, :, 0:n]), [bpt], [b_xaT])
                        for t, (a0, n) in enumerate(subt):
                            pd, bpd = next_dbl()
                            for hf in range(2):
                                for c in range(2):
                                    pe(lambda e, hf=hf, c=c, a0=a0, n=n, pd=pd: e.matmul(pd[0:n, hf * 512:(hf + 1) * 512], lhsT=xaT[:, c, a0:a0 + n], rhs=wxo[:, c, hf * 512:(hf + 1) * 512], start=(c == 0), stop=(c == 1)), [b_xaT, b_wxo], [bpd], inc=(c == 1 and hf == 1))
                            post_residual(pd, bpd, n, t, 1, xblk, b_xb)
                        pre_norm_dve(subt, 2, xblk, b_xb)
                        pre_norm_pe(subt)
                        ck("xatt")
                        for f in range(NFC):
                            k = f % 3
                            k2 = f % 2
                            P.dma("sp", wg[k][:, :, :, :], WGU[l, f].rearrange("p (a c j) -> p a c j", a=2, c=8), [bW], [bwg[k]])
                            pd, bpd = next_dbl()
                            for j in range(2):
                                for c in range(8):
                                    pe(lambda e, j=j, c=c, pd=pd, k=k: e.matmul(pd[:, j * 512:j * 512 + nq], lhsT=wg[k][:, j, c, :], rhs=hT2[:, c, 0:nq], start=(c == 0), stop=(c == 7)), [bwg[k], b_hT2], [bpd], inc=(c == 7 and j == 1))
                            G_, bG = gs[k2], b_gs[k2]
                            C_, bC = cc[k2], b_cc[k2]
                            S_, bS = sl[k2], b_sl[k2]
                            act(lambda e, f=f, G_=G_: e.copy(out=G_[:, 0:2], in_=halo[:, f, :]), [b_halo], [bG])
                            act(lambda e, pd=pd, G_=G_: e.copy(out=G_[:, 2:2 + nq], in_=pd[:, 0:nq]), [bpd], [bG])
                            act(lambda e, f=f, G_=G_: e.copy(out=halo[:, f, :], in_=G_[:, nq:nq + 2]), [bG], [b_halo])
                            dve(lambda e, f=f, G_=G_, C_=C_: e.tensor_scalar(out=C_[:, 0:nq], in0=G_[:, 2:2 + nq], scalar1=cw[:, f, 2:3], scalar2=cw[:, f, 3:4], op0=ALU.mult, op1=ALU.add), [bG, b_cw], [bC])
                            dve(lambda e, f=f, G_=G_, C_=C_: e.scalar_tensor_tensor(out=C_[:, 0:nq], in0=G_[:, 1:1 + nq], scalar=cw[:, f, 1:2], in1=C_[:, 0:nq], op0=ALU.mult, op1=ALU.add), [bG, b_cw, bC], [bC])
                            dve(lambda e, f=f, G_=G_, C_=C_: e.scalar_tensor_tensor(out=C_[:, 0:nq], in0=G_[:, 0:nq], scalar=cw[:, f, 0:1], in1=C_[:, 0:nq], op0=ALU.mult, op1=ALU.add), [bG, b_cw, bC], [bC])
                            act(lambda e, C_=C_, S_=S_: e.activation(out=S_[:, 0:nq], in_=C_[:, 0:nq], func=AF.Silu, bias=zc[:, 0:1]), [bC], [bS])
                            dve(lambda e, f=f, pd=pd, S_=S_: e.tensor_tensor(out=aTf[:, f, 0:nq], in0=pd[:, 512:512 + nq], in1=S_[:, 0:nq], op=ALU.mult), [bpd, bS], [b_aTf])
                        nxt = None
                        if bi + 1 < len(qblocks):
                            nxt = head_dve(bi + 1)
                        for t, (a0, n) in enumerate(subt):
                            ti = (qc0 + a0) // 128
                            if nxt is not None and t == min(2, len(subt) - 1):
                                pre_norm_pe(nxt)
                            pd, bpd = next_dbl3()
                            for f in range(NFC):
                                for hf in range(2):
                                    pe(lambda e, hf=hf, f=f, a0=a0, n=n, pd=pd: e.matmul(pd[0:n, hf * 512:(hf + 1) * 512], lhsT=aTf[:, f, a0:a0 + n], rhs=wd_all[:, f, hf * 512:(hf + 1) * 512], start=(f == 0), stop=(f == NFC - 1), skip_group_check=True), [b_aTf, b_wd], [bpd], inc=(hf == 1 and f == NFC - 1))
                            post_residual(pd, bpd, n, t, 3, xblk, b_xb)
                            P.dma("pool", ydst[qc0 + a0:qc0 + a0 + n, :], xblk[0:n, t, :], [b_xb[t]], [ybufs[ti]])
                    for j in range(2):
                        P.dma("pool", bass.AP(O[pfx + "conv"], ((l * (NP if prm else 1) + so) * 2 + j) * DFF, [[1, 128], [128, NFC]]), halo[:, :, j], [b_halo], [], allow_slow_non_contiguous=True)
                    P.barrier()

        YB = {}
        for s in range(NP):
            YB[("p", s)] = [Buf() for _ in range((SEQ + 127) // 128)]
        YB[("s", 0)] = [Buf()]
        for l in range(DEPTH):
            pass
        seqs = [("p", s) for s in range(NP)] + [("s", 0)]
        nrun = 0
        for (kind, s) in seqs:
            for l in range(DEPTH):
                if stop >= 10 and nrun >= stop - 9:
                    break
                try:
                    run_layer(kind, s, l)
                except StopBuild:
                    P.finish()
                    return nc
                nrun += 1
        P.finish()
    return nc


NCORES = 8
_cache = {}


def kernel(**inputs):
    SEQ, PAST, TS, NP = 2048, 2048, 32, 4
    key = (NP, SEQ, PAST, TS)
    if key not in _cache:
        _cache[key] = build(NP, SEQ, PAST, TS)
    nc = _cache[key]
    hc = host_consts(SEQ, PAST, TS)
    f = lambda a: np.ascontiguousarray(np.asarray(a, dtype=np.float32))
    in_maps = []
    for i in range(NCORES):
        m = {}
        m["x_prompt"] = f(inputs["x_prompt"][NP * i:NP * (i + 1)])
        m["x_sample"] = f(inputs["x_sample"][i:i + 1])
        m["mem_prompt"] = f(inputs["mem_prompt"][NP * i:NP * (i + 1)])
        for n in ("cache_a_k", "cache_a_v", "cache_b_k", "cache_b_v", "cache_c_latent", "cache_c_rope_k", "cache_mem_k", "cache_mem_v", "state_ffn_conv"):
            a = np.asarray(inputs[n])[:, i]
            m[n] = f(a.reshape(a.shape[0], a.shape[1], -1))
        for n in ("w_in", "w_out", "norms", "diff_subln", "t5_bias", "band_rel_bias", "mla_q_norm", "mla_w_uq", "mla_kv_norm", "mla_w_ukv",
                  "w_xq", "w_mk", "w_mv", "w_xo", "w_gate", "w_up", "conv_w", "conv_b", "w_down"):
            m[n] = f(inputs[n])
        m["diff_lambda"] = f(np.asarray(inputs["diff_lambda"]).reshape(2, 128))
        m.update(hc)
        in_maps.append(m)
    res = run_bass_kernel_spmd(nc, in_maps, core_ids=list(range(NCORES)))
    R = res.results
    cat0 = lambda n: np.concatenate([r[n] for r in R], axis=0)
    cat1 = lambda n: np.concatenate([r[n] for r in R], axis=1)
    y_p = cat0("y_prompt"); y_s = cat0("y_sample")
    B = y_p.shape[0]
    outs = [y_p, y_s,
            cat1("p_a_k").reshape(2, B, SEQ, 8, 64), cat1("p_a_v").reshape(2, B, SEQ, 8, 64),
            cat1("p_b_k").reshape(2, B, 512, 4, 64), cat1("p_b_v").reshape(2, B, 512, 4, 64),
            cat1("p_lat"), cat1("p_kpe"),
            cat1("p_mk").reshape(2, B, MEM, 4, 64), cat1("p_mv").reshape(2, B, MEM, 4, 64), cat1("p_conv"),
            cat1("s_a_k").reshape(2, NCORES, TS, 8, 64), cat1("s_a_v").reshape(2, NCORES, TS, 8, 64),
            cat1("s_b_k").reshape(2, NCORES, TS, 4, 64), cat1("s_b_v").reshape(2, NCORES, TS, 4, 64),
            cat1("s_lat"), cat1("s_kpe"), cat1("s_conv")]
    return tuple(np.ascontiguousarray(o, dtype=np.float32) for o in outs)
```

```python
import contextlib
import math
import os
DBG = os.environ.get('KDBG', '')
import numpy as np
import concourse.bass as bass
import concourse.mybir as mybir
from concourse.bass_utils import run_bass_kernel_spmd

F32 = mybir.dt.float32
BF16 = mybir.dt.bfloat16
ALU = mybir.AluOpType
AF = mybir.ActivationFunctionType
AX = mybir.AxisListType

D = 1024
DIN = 2720
DFF = 2816
NFC = 22
MEM = 256
EPS = 1e-6
NDS = 32
SAME_ENGINE_SYNC = False


class StopBuild(Exception):
    pass


CKSTOP = os.environ.get('KCK', '')


STOPPED = [False]


def ck(tag):
    if CKSTOP and tag == CKSTOP:
        STOPPED[0] = True


class Buf:
    __slots__ = ("w", "r", "excl", "strict")

    def __init__(self, excl=False, strict=False):
        self.w = None
        self.r = {}
        self.excl = excl
        self.strict = strict


class Prog:
    def __init__(self, nc, stack):
        self.nc = nc
        self.eh = {"pe": nc.tensor, "act": nc.scalar, "dve": nc.vector, "pool": nc.gpsimd, "sp": nc.sync}
        self.sems = {}
        for e in self.eh:
            self.sems[e] = stack.enter_context(nc.semaphore("s_" + e))
        self.cnt = {e: 0 for e in self.eh}
        self.seen = {e: {} for e in self.eh}
        for i in range(NDS):
            self.sems[("d", i)] = stack.enter_context(nc.semaphore("s_d%d" % i))
        self.dcnt = [0] * NDS
        self.dnext = {"sp": 0, "pool": 0, "act": 0}
        self.drange = {"sp": (0, NDS - 4), "pool": (NDS - 4, NDS), "act": (0, NDS - 4)}
        self.nins = 0

    def _deps(self, eng, reads, writes, extra=()):
        needs = {}
        strict_own = 0
        for b in reads:
            t = b.w
            if t is not None and needs.get(t[0], 0) < t[1]:
                needs[t[0]] = t[1]
            if t is not None and t[0] == eng and t[1] > strict_own and eng != "pe":
                strict_own = t[1]
            if b.excl:
                for k, v in b.r.items():
                    if needs.get(k, 0) < v:
                        needs[k] = v
        for b in writes:
            t = b.w
            if t is not None and needs.get(t[0], 0) < t[1]:
                needs[t[0]] = t[1]
            for k, v in b.r.items():
                if needs.get(k, 0) < v:
                    needs[k] = v
        for t in extra:
            if needs.get(t[0], 0) < t[1]:
                needs[t[0]] = t[1]
        seen = self.seen[eng]
        for k, v in needs.items():
            if k == eng and (eng == "pe" or not SAME_ENGINE_SYNC) and eng != "pool":
                if strict_own and seen.get(k, 0) < strict_own:
                    seen[k] = strict_own
                    self.eh[eng].wait_ge(self.sems[k], strict_own)
                continue
            if seen.get(k, 0) >= v:
                continue
            seen[k] = v
            self.eh[eng].wait_ge(self.sems[k], v)

    def op(self, eng, fn, reads=(), writes=(), inc=True):
        if STOPPED[0]:
            return
        self._deps(eng, reads, writes)
        self.nins += 1
        ins = fn(self.eh[eng])
        if inc:
            self.cnt[eng] += 1
            v = self.cnt[eng]
            ins.then_inc(self.sems[eng], 1)
        else:
            v = self.cnt[eng] + 1
        for b in reads:
            b.r[eng] = v
        for b in writes:
            b.w = (eng, v)
            b.r = {}

    def dma(self, eng, out, in_, reads=(), writes=(), **kw):
        if STOPPED[0]:
            return
        lo, hi = self.drange[eng]
        i = lo + self.dnext[eng]
        self.dnext[eng] = (self.dnext[eng] + 1) % (hi - lo)
        key = ("d", i)
        extra = ((key, self.dcnt[i]),) if self.dcnt[i] else ()
        self._deps(eng, reads, writes, extra)
        self.dcnt[i] += 16
        v = self.dcnt[i]
        self.nins += 1
        self.eh[eng].dma_start(out=out, in_=in_, **kw).then_inc(self.sems[key], 16)
        for b in reads:
            b.r[key] = v
        for b in writes:
            b.w = (key, v)
            b.r = {}

    def barrier(self, force=False, skip_pool_dma=False):
        if STOPPED[0] and not force:
            return
        lo, hi = self.drange["pool"]
        keys = [e for e in self.eh if self.cnt[e]] + [("d", i) for i in range(NDS) if self.dcnt[i] and not (skip_pool_dma and lo <= i < hi)]
        for e in self.eh:
            seen = self.seen[e]
            for k in keys:
                v = self.cnt[k] if not isinstance(k, tuple) else self.dcnt[k[1]]
                if k == e and e != "pool":
                    continue
                if seen.get(k, 0) >= v:
                    continue
                seen[k] = v
                self.eh[e].wait_ge(self.sems[k], v)

    def finish(self):
        self.barrier(force=True)


def t5_bucket_np(rel):
    half, exact = 16, 8
    ret = np.where(rel > 0, half, 0)
    n = np.abs(rel)
    nf = np.maximum(n, 1).astype(np.float32)
    large = exact + (np.log(nf / np.float32(exact)) / np.float32(math.log(128 / exact)) * np.float32(half - exact)).astype(np.int32)
    large = np.minimum(large, half - 1)
    return ret + np.where(n < exact, n, large)


def host_consts(SEQ, PAST, TS):
    m = np.arange(1151)
    bk = t5_bucket_np(511 - m)
    oh_t5 = (bk[None, :] == np.arange(32)[:, None]).astype(np.float32)
    m2 = np.arange(1151)
    j = np.clip(511 - m2, -128, 128) + 128
    ohb = (j[None, :] == np.arange(384)[:, None]).astype(np.float32)

    def rope_tab(pos):
        half = 16
        inv = np.float32(10000.0) ** (-np.arange(half, dtype=np.float32) / np.float32(half))
        ang = pos.astype(np.float32)[:, None] * inv[None, :]
        c = np.cos(ang).astype(np.float32)
        s = np.sin(ang).astype(np.float32)
        c4 = np.repeat(c[:, None, :], 4, axis=1)
        s4 = np.repeat(s[:, None, :], 4, axis=1)
        return np.ascontiguousarray(np.stack([c4, s4], axis=1).reshape(len(pos), 128))

    return {"oh_t5": oh_t5, "oh_b": ohb, "rope_p": rope_tab(np.arange(SEQ)), "rope_s": rope_tab(PAST + np.arange(TS))}


def vis_info(mode, q0, nq, k0, nk):
    qch = [(c, min(c + 64, nq)) for c in range(0, nq, 64)]
    kch = [(p, min(p + 64, nk)) for p in range(0, nk, 64)]
    V = {}
    for (p0, p1) in kch:
        kc = (k0 + p0) // 64
        for (c0, c1) in qch:
            qc = (q0 + c0) // 64
            if mode == "causal":
                ok = kc <= qc
            elif mode == "band":
                ok = (kc <= qc) and (kc >= qc - 8)
            else:
                ok = True
            V[(p0, c0)] = ok
    viscols = [(c0, c1) for (c0, c1) in qch if any(V[(p0, c0)] for (p0, _) in kch)]
    if not viscols:
        return None
    cs = (min(c0 for c0, _ in viscols) // 128) * 128
    ce = min(nq, ((max(c1 for _, c1 in viscols) + 127) // 128) * 128)
    masks = []
    for (p0, p1) in kch:
        run = None
        for (c0, c1) in qch:
            if c0 < cs or c0 >= ce:
                continue
            if not V[(p0, c0)]:
                if run is not None and run[1] == c0:
                    run[1] = c1
                else:
                    if run is not None:
                        masks.append((p0, p1, run[0], run[1]))
                    run = [c0, c1]
        if run is not None:
            masks.append((p0, p1, run[0], run[1]))
    return cs, ce, masks


def build(NP, SEQ, PAST, TS=32, DEPTH=2, stop=0):
    nc = bass.Bass("TRN2", target_bir_lowering=False)
    I, O = {}, {}

    def inp(n, shape):
        I[n] = nc.dram_tensor(n, list(shape), F32, kind="ExternalInput")

    def outp(n, shape):
        O[n] = nc.dram_tensor(n, list(shape), F32, kind="ExternalOutput")

    NBP = min(512, SEQ)
    NBS = min(512, PAST)
    inp("x_prompt", (NP, SEQ, D)); inp("x_sample", (1, TS, D))
    inp("cache_a_k", (DEPTH, PAST, 512)); inp("cache_a_v", (DEPTH, PAST, 512))
    inp("cache_b_k", (DEPTH, NBS, 256)); inp("cache_b_v", (DEPTH, NBS, 256))
    inp("cache_c_latent", (DEPTH, PAST, 128)); inp("cache_c_rope_k", (DEPTH, PAST, 32))
    inp("cache_mem_k", (DEPTH, MEM, 256)); inp("cache_mem_v", (DEPTH, MEM, 256))
    inp("state_ffn_conv", (DEPTH, 2, DFF)); inp("mem_prompt", (NP, MEM, D))
    inp("w_in", (DEPTH, D, DIN)); inp("w_out", (DEPTH, D, D)); inp("norms", (DEPTH, 7, D))
    inp("diff_lambda", (DEPTH, 128)); inp("diff_subln", (DEPTH, 64)); inp("t5_bias", (32, 8))
    inp("band_rel_bias", (DEPTH, 4, 257)); inp("mla_q_norm", (DEPTH, 256)); inp("mla_w_uq", (DEPTH, 256, 384))
    inp("mla_kv_norm", (DEPTH, 128)); inp("mla_w_ukv", (DEPTH, 128, 512))
    inp("w_xq", (DEPTH, D, 256)); inp("w_mk", (DEPTH, D, 256)); inp("w_mv", (DEPTH, D, 256)); inp("w_xo", (DEPTH, 256, D))
    inp("w_gate", (DEPTH, D, DFF)); inp("w_up", (DEPTH, D, DFF)); inp("conv_w", (DEPTH, 3, DFF)); inp("conv_b", (DEPTH, DFF))
    inp("w_down", (DEPTH, DFF, D))
    inp("oh_t5", (32, 1151)); inp("oh_b", (384, 1151)); inp("rope_p", (SEQ, 128)); inp("rope_s", (TS, 128))
    outp("y_prompt", (NP, SEQ, D)); outp("y_sample", (1, TS, D))
    outp("p_a_k", (DEPTH, NP, SEQ, 512)); outp("p_a_v", (DEPTH, NP, SEQ, 512))
    outp("p_b_k", (DEPTH, NP, NBP, 256)); outp("p_b_v", (DEPTH, NP, NBP, 256))
    outp("p_lat", (DEPTH, NP, SEQ, 128)); outp("p_kpe", (DEPTH, NP, SEQ, 32))
    outp("p_mk", (DEPTH, NP, MEM, 256)); outp("p_mv", (DEPTH, NP, MEM, 256)); outp("p_conv", (DEPTH, NP, 2, DFF))
    outp("s_a_k", (DEPTH, 1, TS, 512)); outp("s_a_v", (DEPTH, 1, TS, 512))
    outp("s_b_k", (DEPTH, 1, TS, 256)); outp("s_b_v", (DEPTH, 1, TS, 256))
    outp("s_lat", (DEPTH, 1, TS, 128)); outp("s_kpe", (DEPTH, 1, TS, 32)); outp("s_conv", (DEPTH, 1, 2, DFF))
    IA = {k: v.ap() for k, v in I.items()}
    OA = {k: v.ap() for k, v in O.items()}
    WB = {}
    for n in ("w_in", "w_out", "mla_w_uq", "mla_w_ukv", "w_xq", "w_mk", "w_mv", "w_xo", "w_down"):
        WB[n] = nc.dram_tensor("wb_" + n, list(I[n].shape), BF16, kind="Internal")
    WBA = {k: v.ap() for k, v in WB.items()}
    WGUh = nc.dram_tensor("wb_wgu", [DEPTH, NFC, 128, 2048], BF16, kind="Internal")
    WGU = WGUh.ap()
    Escr = nc.dram_tensor("Escr", [8 + DEPTH * 4, 128, 1024], BF16, kind="Internal")
    EscrA = Escr.ap()
    scrA = nc.dram_tensor("scrA", [8, 1151], F32, kind="Internal")
    scrB = nc.dram_tensor("scrB", [DEPTH * 4, 1151], F32, kind="Internal")

    uid = [0]
    G = contextlib.ExitStack()
    with G:
        P = Prog(nc, G)

        def sb(stack, shape, dt, name="t"):
            uid[0] += 1
            return stack.enter_context(nc.sbuf_tensor("%s_%d" % (name, uid[0]), list(shape), dt))

        dbl = [G.enter_context(nc.psum_tensor("dbl%d" % i, [128, 1024], F32)) for i in range(3)]
        bdbl = [Buf(True) for _ in range(3)]
        ptr = [G.enter_context(nc.psum_tensor("ptr%d" % i, [128, 1024], BF16)) for i in range(2)]
        bptr = [Buf(True) for _ in range(2)]
        rot = {"d": 0, "t": 0}

        def next_dbl():
            i = rot["d"]; rot["d"] = 1 - i
            return dbl[i], bdbl[i]

        def next_ptr():
            i = rot["t"]; rot["t"] = 1 - i
            return ptr[i], bptr[i]

        pe = lambda fn, r=(), w=(), inc=True: P.op("pe", fn, r, w, inc)
        act = lambda fn, r=(), w=(): P.op("act", fn, r, w)
        dve = lambda fn, r=(), w=(): P.op("dve", fn, r, w)

        zc = sb(G, [128, 1], F32, "zc"); ec = sb(G, [128, 1], F32, "ec"); b_zc = Buf()
        dve(lambda e: e.memset(zc[:], 0.0), (), [b_zc])
        dve(lambda e: e.memset(ec[:], EPS), (), [b_zc])
        P.barrier()
        nc.const_aps.register(F32, 0.0, zc[:, 0:1])
        nc.const_aps.register(F32, EPS, ec[:, 0:1])
        ident = sb(G, [128, 128], BF16, "ident"); b_ident = Buf()
        Jm = sb(G, [128, 128], F32, "J"); b_J = Buf()
        P.op("pool", lambda e: e.memset(ident[:], 0.0), (), [b_ident])
        P.op("pool", lambda e: e.affine_select(out=ident[:], in_=ident[:], pattern=[[-1, 128]], compare_op=ALU.not_equal, fill=1.0, base=0, channel_multiplier=1), [b_ident], [b_ident])
        P.op("pool", lambda e: e.memset(Jm[:], 0.0), (), [b_J])
        P.op("pool", lambda e: e.affine_select(out=Jm[:], in_=Jm[:], pattern=[[1, 128]], compare_op=ALU.not_equal, fill=1.0, base=-127, channel_multiplier=1), [b_J], [b_J])
        bW = Buf()
        for n in ([] if 'W' in DBG else WB):
            src = IA[n].rearrange("l a b -> (l a) b")
            dst = WBA[n].rearrange("l a b -> (l a) b")
            rows = src.shape[0]
            step = 256
            for r in range(0, rows, step):
                P.dma("pool", dst[r:min(r + step, rows)], src[r:min(r + step, rows)], (), [Buf()])
        for l_ in range(DEPTH):
            for j_, wn in enumerate(("w_gate", "w_up")):
                wsrc_ = IA[wn][l_].rearrange("(c p) n -> p c n", p=128)
                for f_ in range(NFC):
                    P.dma("pool", WGU[l_, f_][:, j_ * 1024:(j_ + 1) * 1024].rearrange("p (c j) -> p c j", c=8), wsrc_[:, :, f_ * 128:(f_ + 1) * 128], (), [Buf()])
        bWs = []
        lo_, hi_ = P.drange["pool"]
        for i_ in range(lo_, hi_):
            bq = Buf()
            if P.dcnt[i_]:
                bq.w = (("d", i_), P.dcnt[i_])
            bWs.append(bq)
        b_Escr = Buf()
        with contextlib.ExitStack() as S:
            EA = sb(S, [128, 8, 1024], BF16, "EA"); b_EA = Buf()
            EB = sb(S, [128, DEPTH * 4, 1024], BF16, "EB"); b_EB = Buf()
            t5 = sb(S, [32, 8], F32); b_t5 = Buf()
            oh = sb(S, [32, 1151], F32); b_oh = Buf()
            c15 = sb(S, [8, 1], F32); b_c15 = Buf(strict=True)
            wA = sb(S, [8, 1151], F32); b_wA = Buf()
            P.dma("sp", t5[:], IA["t5_bias"], (), [b_t5])
            P.dma("sp", oh[:], IA["oh_t5"], (), [b_oh])
            P.dma("sp", c15[:], bass.AP(I["t5_bias"], 15 * 8, [[1, 8], [1, 1]]), (), [b_c15])
            dve(lambda e: e.tensor_scalar(out=c15[:], in0=c15[:], scalar1=-1.0, scalar2=None, op0=ALU.mult), [b_c15], [b_c15])
            for (c0, c1) in ((0, 512), (512, 1024), (1024, 1151)):
                pd, bpd = next_dbl()
                pe(lambda e, c0=c0, c1=c1, pd=pd: e.matmul(pd[0:8, 0:c1 - c0], lhsT=t5[:, :], rhs=oh[:, c0:c1], start=True, stop=True), [b_t5, b_oh], [bpd])
                act(lambda e, c0=c0, c1=c1, pd=pd: e.activation(out=wA[:, c0:c1], in_=pd[0:8, 0:c1 - c0], func=AF.Exp, bias=c15[:, 0:1], scale=1.0), [bpd, b_c15], [b_wA])
            b_scrA = Buf()
            P.dma("sp", scrA.ap(), wA[:], [b_wA], [b_scrA])
            Gp = sb(S, [128, 1024], F32); b_Gp = Buf()
            for h in range(8):
                P.dma("sp", Gp[:], bass.AP(scrA, h * 1151, [[1, 128], [1, 1024]]), [b_scrA], [b_Gp])
                pd, bpd = next_dbl()
                for hf in range(2):
                    pe(lambda e, hf=hf, pd=pd: e.matmul(pd[:, hf * 512:(hf + 1) * 512], lhsT=Jm[:], rhs=Gp[:, hf * 512:(hf + 1) * 512], start=True, stop=True), [b_J, b_Gp], [bpd])
                act(lambda e, h=h, pd=pd: e.copy(out=EA[:, h, :], in_=pd[:, :]), [bpd], [b_EA])
            relT = sb(S, [128, 3, DEPTH * 4], F32); b_relT = Buf()
            ohb = sb(S, [128, 3, 1151], F32); b_ohb = Buf()
            c0b = sb(S, [DEPTH * 4, 1], F32); b_c0b = Buf(strict=True)
            wB = sb(S, [DEPTH * 4, 1151], F32); b_wB = Buf()
            dve(lambda e: e.memset(relT[:], 0.0), (), [b_relT])
            for c in range(3):
                nj = 128 if c < 2 else 1
                P.dma("sp", relT[0:nj, c, :], bass.AP(I["band_rel_bias"], c * 128, [[1, nj], [257, DEPTH * 4]]), (), [b_relT], allow_slow_non_contiguous=True)
            P.dma("sp", ohb[:], IA["oh_b"].rearrange("(c p) m -> p c m", p=128), (), [b_ohb])
            P.dma("sp", c0b[:], bass.AP(I["band_rel_bias"], 0, [[257, DEPTH * 4], [1, 1]]), (), [b_c0b], allow_slow_non_contiguous=True)
            dve(lambda e: e.tensor_scalar(out=c0b[:], in0=c0b[:], scalar1=-1.0, scalar2=None, op0=ALU.mult), [b_c0b], [b_c0b])
            for (c0_, c1_) in ((0, 512), (512, 1024), (1024, 1151)):
                pd, bpd = next_dbl()
                for c in range(3):
                    pe(lambda e, c=c, pd=pd, c0_=c0_, c1_=c1_: e.matmul(pd[0:DEPTH * 4, 0:c1_ - c0_], lhsT=relT[:, c, :], rhs=ohb[:, c, c0_:c1_], start=(c == 0), stop=(c == 2)), [b_relT, b_ohb], [bpd], inc=(c == 2))
                act(lambda e, pd=pd, c0_=c0_, c1_=c1_: e.activation(out=wB[:, c0_:c1_], in_=pd[0:DEPTH * 4, 0:c1_ - c0_], func=AF.Exp, bias=c0b[:, 0:1], scale=1.0), [bpd, b_c0b], [b_wB])
            b_scrB = Buf()
            P.dma("sp", scrB.ap(), wB[:], [b_wB], [b_scrB])
            for lh in range(DEPTH * 4):
                P.dma("sp", Gp[:, :], bass.AP(scrB, lh * 1151, [[1, 128], [1, 1024]]), [b_scrB], [b_Gp])
                pd, bpd = next_dbl()
                for hf in range(2):
                    pe(lambda e, hf=hf, pd=pd: e.matmul(pd[:, hf * 512:(hf + 1) * 512], lhsT=Jm[:], rhs=Gp[:, hf * 512:(hf + 1) * 512], start=True, stop=True), [b_J, b_Gp], [bpd])
                act(lambda e, lh=lh, pd=pd: e.copy(out=EB[:, lh, :], in_=pd[:, :]), [bpd], [b_EB])
            P.dma("sp", EscrA[0:8].rearrange("h p u -> p h u"), EA[:, :, :], [b_EA], [b_Escr])
            P.dma("sp", EscrA[8:8 + DEPTH * 4].rearrange("h p u -> p h u"), EB[:, :, :], [b_EB], [Buf()])
            P.barrier(skip_pool_dma=True)

        neglam = sb(G, [128, DEPTH], F32, "neglam"); b_lam = Buf(strict=True)
        subln = sb(G, [128, DEPTH, 64], F32, "subln"); b_subln = Buf()
        LAM_INIT = [0.8 - 0.6 * math.exp(-0.3 * l) for l in range(DEPTH)]
        with contextlib.ExitStack() as S:
            lt = sb(S, [128, 128], F32); b_lt = Buf()
            junk = sb(S, [128, 32], F32); b_junk = Buf()
            s12 = sb(S, [128, 2], F32); b_s12 = Buf(strict=True)
            for l in range(DEPTH):
                P.dma("sp", lt[:], bass.AP(I["diff_lambda"], l * 128, [[0, 128], [1, 128]]), (), [b_lt])
                P.dma("sp", subln[:, l, :], bass.AP(I["diff_subln"], l * 64, [[0, 128], [1, 64]]), (), [b_subln])
                for i in range(2):
                    dve(lambda e, i=i: e.scalar_tensor_tensor(out=junk[:], in0=lt[:, 64 * i:64 * i + 32], scalar=1.0, in1=lt[:, 64 * i + 32:64 * i + 64], op0=ALU.mult, op1=ALU.mult, accum_out=s12[:, i:i + 1]), [b_lt], [b_junk, b_s12])
                act(lambda e: e.activation(out=s12[:], in_=s12[:], func=AF.Exp, bias=zc[:, 0:1]), [b_s12], [b_s12])
                dve(lambda e, l=l: e.tensor_tensor(out=neglam[:, l:l + 1], in0=s12[:, 1:2], in1=s12[:, 0:1], op=ALU.subtract), [b_s12], [b_lam])
                dve(lambda e, l=l: e.tensor_scalar(out=neglam[:, l:l + 1], in0=neglam[:, l:l + 1], scalar1=-LAM_INIT[l], scalar2=None, op0=ALU.add), [b_lam], [b_lam])
                dve(lambda e, l=l: e.tensor_scalar(out=subln[:, l, :], in0=subln[:, l, :], scalar1=1.0 - LAM_INIT[l], scalar2=None, op0=ALU.mult), [b_subln], [b_subln])
            P.barrier(skip_pool_dma=True)

        if stop == 1:
            P.finish()
            return nc

        def bcast_row(dst, src_handle, off, n, bufs):
            P.dma("sp", dst, bass.AP(src_handle, off, [[0, 128], [1, n]]), (), bufs)

        def rstd_of(ss, n, Dd, bss):
            act(lambda e: e.activation(out=ss, in_=ss, func=AF.Ln, bias=ec[0:n, 0:1], scale=1.0 / Dd), [bss], [bss])
            act(lambda e: e.activation(out=ss, in_=ss, func=AF.Exp, bias=zc[0:n, 0:1], scale=-0.5), [bss], [bss])

        def run_layer(kind, s, l):
            prm = kind == "p"
            T = SEQ if prm else TS
            tiles = [(t0, min(128, T - t0)) for t0 in range(0, T, 128)]
            NT = len(tiles)
            past = 0 if prm else PAST
            NKB_past = past // 128
            NK = past + T
            kblocks = [(j * 128, j * 128, 128) for j in range(NKB_past)] + [(past + t0, past + t0, n) for (t0, n) in tiles]
            NKB = len(kblocks)
            qblocks = []
            for q0 in range(0, T, 512):
                nq = min(512, T - q0)
                qblocks.append((q0, past + q0, nq))
            xsrc = (IA["x_prompt"][s] if prm else IA["x_sample"][0]) if l == 0 else (OA["y_prompt"][s] if prm else OA["y_sample"][0])
            ydst = OA["y_prompt"][s] if prm else OA["y_sample"][0]
            ybufs = YB[(kind, s)]
            pfx = "p_" if prm else "s_"
            so = s if prm else 0
            rope_h = I["rope_p"] if prm else I["rope_s"]
            wl = {k: v[l] for k, v in WBA.items()}
            scale_of = {"A": 32 ** -0.5, "B": 64 ** -0.5, "C": 96 ** -0.5, "X": 64 ** -0.5}

            L = contextlib.ExitStack()
            with L:
                b_atok = Buf()
                NPT = 4
                ptiles = [sb(L, [128, 2, 512], BF16, "PT") for _ in range(NPT)]
                bpt_ = [Buf() for _ in range(NPT)]
                prot = [0]
                rec = sb(L, [128, 2, 4], F32, "rec"); b_rec = Buf(strict=True)
                otmp = sb(L, [128, 4, 64], F32, "otmp"); b_otmp = Buf()
                o0t = sb(L, [128, 4, 64], F32, "o0t"); b_o0t = Buf()
                sq = sb(L, [128, 4, 64], F32, "sq"); b_sq = Buf()
                ssA = sb(L, [128, 4], F32, "ssA"); b_ssA = Buf(strict=True)

                accs = [sb(L, [128, 2, 260], F32, "accs") for _ in range(2)]
                baccs = [Buf(), Buf()]
                arot = [0]
                pending = []

                def bc_last(ap2, n_last):
                    return bass.AP(ap2.tensor, ap2.offset, [list(ap2.ap[0]), list(ap2.ap[1]), [0, n_last]])

                def bc_mid(ap2, k):
                    return bass.AP(ap2.tensor, ap2.offset, [list(ap2.ap[0]), [0, k], list(ap2.ap[1])])

                def flush_pending(step=None):
                    while pending and (step is None or pending[0][0] <= step):
                        pending.pop(0)[1]()

                def attend(grp, lanes, qb, mode, kbl, rbufs, epilogue):
                    attend_many([(grp, lanes, qb, mode, kbl, rbufs, epilogue)])

                def attend_many(calls):
                    acc, bacc = dbl[2], bdbl[2]
                    G_ = []
                    for ci, (grp, lanes, qb, mode, kbl, rbufs, epilogue) in enumerate(calls):
                        (qc0, qpos, nq) = qb
                        st_ = []
                        for kbi, (kc0, kpos, nk) in kbl:
                            vi = vis_info(mode, qpos, nq, kpos, nk)
                            if vi is not None:
                                st_.append((kbi, kc0, kpos, nk, vi[0], vi[2], vi[1]))
                        def heavy(stp):
                            (kbi_, kc0_, kpos_, nk_, cs_, masks_, ce_) = stp
                            relmax_ = (kpos_ + nk_ - 1) - (qpos + cs_)
                            return bool(masks_) or any(ln.get("E") is not None and relmax_ > ln["Ethr"] for ln in lanes)
                        hv = [x for x in st_ if heavy(x)]
                        lt = [x for x in st_ if not heavy(x)]
                        mix = []
                        while hv or lt:
                            if lt:
                                mix.append(lt.pop(0))
                            if hv:
                                mix.append(hv.pop(0))
                        st_ = mix
                        for si, stp in enumerate(st_):
                            G_.append((ci, si, len(st_), stp))
                    qk = {}

                    def emit_qk(g):
                        ci, si, ns, (kbi, kc0, kpos, nk, cs, masks, ce) = G_[g]
                        (grp, lanes, qb, mode, kbl, rbufs, epilogue) = calls[ci]
                        (qc0, qpos, nq) = qb
                        pd, bpd = next_dbl()
                        for li, ln in enumerate(lanes):
                            pe(lambda e, ln=ln, li=li, pd=pd: e.matmul(pd[0:nk, li * 512 + cs:li * 512 + ce], lhsT=ln["kt"](kc0, nk), rhs=ln["qt"](qc0 + cs, qc0 + ce),
                                                                      start=True, stop=True, **({"tile_position": ln["tp"]} if ln.get("tp") else {})),
                               rbufs, [bpd], inc=(li == 1))
                        qk[g] = (pd, bpd)

                    emit_qk(0)
                    if len(G_) > 1:
                        emit_qk(1)
                    for g, (ci, si, ns, (kbi, kc0, kpos, nk, cs, masks, ce)) in enumerate(G_):
                        (grp, lanes, qb, mode, kbl, rbufs, epilogue) = calls[ci]
                        (qc0, qpos, nq) = qb
                        scale = scale_of[grp]
                        nsub = (nq + 127) // 128
                        if si == 0:
                            accz = acc[:, :].rearrange("p (a b) -> p a b", a=2)
                            dve(lambda e, accz=accz, nsub=nsub: e.memset(accz[:, :, 0:nsub * 65], 0.0), (), [bacc])
                        pd, bpd = qk.pop(g)
                        pi = prot[0]; prot[0] = (pi + 1) % NPT
                        PT, bPT = ptiles[pi], bpt_[pi]
                        pdv = pd[:, :].rearrange("p (a b) -> p a b", a=2)
                        act(lambda e, PT=PT, pdv=pdv: e.activation(out=PT[0:nk, :, cs:ce], in_=pdv[0:nk, :, cs:ce], func=AF.Exp, bias=zc[0:nk, 0:1], scale=scale), [bpd], [bPT])
                        if g + 2 < len(G_):
                            emit_qk(g + 2)
                        relmax = (kpos + nk - 1) - (qpos + cs)
                        if lanes[0].get("E") is not None and lanes[0].get("Eshared") and relmax > lanes[0]["Ethr"]:
                            off = lanes[0]["c0"] - (kpos - qpos)
                            Et = lanes[0]["E"]
                            dve(lambda e, Et=Et, off=off, PT=PT: e.tensor_tensor(out=PT[0:nk, :, cs:ce], in0=PT[0:nk, :, cs:ce], in1=bc_mid(Et[0:nk, off + cs:off + ce], 2), op=ALU.mult), [bPT, lanes[0]["Eb"]], [bPT])
                        else:
                            for li, ln in enumerate(lanes):
                                if ln.get("E") is not None and relmax > ln["Ethr"]:
                                    off = ln["c0"] - (kpos - qpos)
                                    Et = ln["E"]
                                    dve(lambda e, li=li, Et=Et, off=off, PT=PT: e.tensor_tensor(out=PT[0:nk, li, cs:ce], in0=PT[0:nk, li, cs:ce], in1=Et[0:nk, off + cs:off + ce], op=ALU.mult), [bPT, ln["Eb"]], [bPT])
                        for (p0, p1, c0, c1) in masks:
                            dve(lambda e, PT=PT, p0=p0, p1=p1, c0=c0, c1=c1: e.memset(PT[p0:p1, :, c0:c1], 0.0), (), [bPT])
                        for li, ln in enumerate(lanes):
                            for t in range(nsub):
                                a0, a1 = t * 128, min(t * 128 + 128, nq)
                                if a1 <= cs or a0 >= ce:
                                    continue
                                last = (li == 1 and a1 >= ce)
                                pe(lambda e, li=li, t=t, a0=a0, a1=a1, ln=ln, PT=PT: e.matmul(acc[0:a1 - a0, li * 512 + t * 65:li * 512 + t * 65 + 65], lhsT=PT[0:nk, li, a0:a1], rhs=ln["v"](kbi, nk),
                                                                                            start=False, stop=False, skip_group_check=True),
                                   [bPT] + rbufs, [bacc], inc=last)
                        flush_pending(si)
                        if si == ns - 1:
                            flush_pending()
                            k = arot[0]; arot[0] = 1 - k
                            A_, bA_ = accs[k], baccs[k]
                            nn = min(128, nq)
                            accv = acc[:, :].rearrange("p (a b) -> p a b", a=2)
                            dve(lambda e, A_=A_, nn=nn, nsub=nsub, accv=accv: e.tensor_copy(out=A_[0:nn, :, 0:nsub * 65], in_=accv[0:nn, :, 0:nsub * 65]), [bacc], [bA_])
                            epilogue(A_, bA_, qb, nsub)

                def epi_plain(col_of_lane, dst=None, bdst=None):
                    def f(A_, bA_, qb, nsub):
                        pending.append((1, lambda: g(A_, bA_, qb, nsub)))

                    def g(A_, bA_, qb, nsub):
                        (qc0, qpos, nq) = qb
                        nn = min(128, nq)
                        Av = A_[0:nn, :, 0:nsub * 65].rearrange("p a (t c) -> p a t c", c=65)
                        dve(lambda e: e.reciprocal(out=rec[0:nn, :, 0:nsub], in_=Av[:, :, :, 64]), [bA_], [b_rec])
                        ti0 = qc0 // 128
                        for li in range(2):
                            col = col_of_lane[li]
                            if dst is None:
                                o_ap, o_b = a_tok[0:nn, ti0:ti0 + nsub, col:col + 64], b_atok
                            else:
                                o_ap, o_b = dst[0:nn, 0:nsub, col:col + 64], bdst
                            dve(lambda e, li=li, o_ap=o_ap: e.tensor_tensor(out=o_ap, in0=Av[:, li, :, 0:64], in1=bc_last(rec[0:nn, li, 0:nsub], 64), op=ALU.mult), [bA_, b_rec], [o_b])
                    return f

                def epi_diff(h):
                    def f(A_, bA_, qb, nsub):
                        (qc0, qpos, nq) = qb
                        nn = min(128, nq)
                        Av = A_[0:nn, :, 0:nsub * 65].rearrange("p a (t c) -> p a t c", c=65)
                        ti0 = qc0 // 128

                        def s1():
                            dve(lambda e: e.reciprocal(out=rec[0:nn, :, 0:nsub], in_=Av[:, :, :, 64]), [bA_], [b_rec])
                            dve(lambda e: e.tensor_scalar(out=rec[0:nn, 1, 0:nsub], in0=rec[0:nn, 1, 0:nsub], scalar1=neglam[0:nn, l:l + 1], scalar2=None, op0=ALU.mult), [b_rec, b_lam], [b_rec])
                            dve(lambda e: e.tensor_tensor(out=o0t[0:nn, 0:nsub, :], in0=Av[:, 0, :, 0:64], in1=bc_last(rec[0:nn, 0, 0:nsub], 64), op=ALU.mult), [bA_, b_rec], [b_o0t])
                            dve(lambda e: e.tensor_tensor(out=otmp[0:nn, 0:nsub, :], in0=Av[:, 1, :, 0:64], in1=bc_last(rec[0:nn, 1, 0:nsub], 64), op=ALU.mult), [bA_, b_rec], [b_otmp])
                            dve(lambda e: e.tensor_tensor(out=otmp[0:nn, 0:nsub, :], in0=otmp[0:nn, 0:nsub, :], in1=o0t[0:nn, 0:nsub, :], op=ALU.add), [b_otmp, b_o0t], [b_otmp])
                            dve(lambda e: e.tensor_tensor(out=sq[0:nn, 0:nsub, :], in0=otmp[0:nn, 0:nsub, :], in1=otmp[0:nn, 0:nsub, :], op=ALU.mult), [b_otmp], [b_sq])
                            dve(lambda e: e.tensor_reduce(out=ssA[0:nn, 0:nsub], in_=sq[0:nn, 0:nsub, :], axis=AX.X, op=ALU.add), [b_sq], [b_ssA])

                        def s2():
                            rstd_of(ssA[0:nn, 0:nsub], nn, 64, b_ssA)

                        def s3():
                            dve(lambda e: e.tensor_tensor(out=sq[0:nn, 0:nsub, :], in0=otmp[0:nn, 0:nsub, :], in1=bc_last(ssA[0:nn, 0:nsub], 64), op=ALU.mult), [b_otmp, b_ssA], [b_sq])
                            dve(lambda e: e.tensor_tensor(out=a_tok[0:nn, ti0:ti0 + nsub, h * 64:h * 64 + 64], in0=sq[0:nn, 0:nsub, :], in1=bc_mid(subln[0:nn, l, :], nsub), op=ALU.mult), [b_sq, b_subln], [b_atok])
                        pending.append((1, s1))
                        pending.append((5, s2))
                        pending.append((6, s3))
                    return f

                with contextlib.ExitStack() as S1:
                    a_tok = sb(S1, [128, NT, D], BF16, "atok")
                    hT = sb(S1, [128, 8, T], BF16, "hT"); b_hT = Buf()
                    gbc = sb(S1, [128, D], F32, "gbc"); b_gbc = Buf()
                    bcast_row(gbc[:], I["norms"], (l * 7 + 0) * D, D, [b_gbc])
                    xt = [sb(S1, [128, D], F32, "xt") for _ in range(4)]; bxt = [Buf() for _ in range(4)]
                    hb = [sb(S1, [128, D], BF16, "hb") for _ in range(4)]; bhb = [Buf() for _ in range(4)]
                    junkb = sb(S1, [128, D], BF16, "junkb"); b_junkb = Buf()
                    ss1 = [sb(S1, [128, 1], F32, "ss1") for _ in range(4)]; bss1 = [Buf(strict=True) for _ in range(4)]
                    def h_stages(ti, t0, n):
                        k = ti % 4

                        def g0():
                            P.dma("sp", xt[k][0:n, :], xsrc[t0:t0 + n, :], [ybufs[ti]], [bxt[k]])
                            dve(lambda e: e.scalar_tensor_tensor(out=junkb[0:n, :], in0=xt[k][0:n, :], scalar=1.0, in1=xt[k][0:n, :], op0=ALU.mult, op1=ALU.mult, accum_out=ss1[k][0:n, :]), [bxt[k]], [b_junkb, bss1[k]])

                        def g1():
                            rstd_of(ss1[k][0:n, :], n, D, bss1[k])

                        def g2():
                            dve(lambda e: e.scalar_tensor_tensor(out=hb[k][0:n, :], in0=xt[k][0:n, :], scalar=ss1[k][0:n, 0:1], in1=gbc[0:n, :], op0=ALU.mult, op1=ALU.mult), [bxt[k], bss1[k], b_gbc], [bhb[k]])

                        def g3():
                            pt, bpt = next_ptr()
                            for c in range(8):
                                pe(lambda e, c=c: e.transpose(out=pt[:, c * 128:c * 128 + n], in_=hb[k][0:n, c * 128:(c + 1) * 128], identity=ident[0:n, 0:n]), [bhb[k], b_ident], [bpt], inc=(c == 7))
                            ptv = pt[:, :].rearrange("p (c t) -> p c t", c=8)
                            act(lambda e: e.copy(out=hT[:, :, t0:t0 + n], in_=ptv[:, :, 0:n]), [bpt], [b_hT])
                        return [g0, g1, g2, g3]

                    for ti0 in range(0, NT, 4):
                        grp_ = [h_stages(ti, *tiles[ti]) for ti in range(ti0, min(NT, ti0 + 4))]
                        for si in range(4):
                            for stg in grp_:
                                stg[si]()

                    ck("hT")
                    SE = contextlib.ExitStack()
                    EA = sb(SE, [128, 8, 1024], BF16, "EAt"); b_EA = Buf()
                    EB = sb(SE, [128, 4, 1024], BF16, "EBt"); b_EB = Buf()
                    P.dma("sp", EA[:, :, :], EscrA[0:8].rearrange("h p u -> p h u"), (), [b_EA])
                    P.dma("sp", EB[:, :, :], EscrA[8 + 4 * l:12 + 4 * l].rearrange("h p u -> p h u"), (), [b_EB])

                    def proj_fm(dst_fn, wt, bw, wcols, nchunks, rows=128):
                        for q0 in range(0, T, 512):
                            nq = min(512, T - q0)
                            for j0 in range(0, nchunks, 2):
                                pd, bpd = next_dbl()
                                nj = min(2, nchunks - j0)
                                for jj in range(nj):
                                    for c in range(8):
                                        pe(lambda e, jj=jj, c=c, pd=pd, j0=j0: e.matmul(pd[0:rows, jj * 512:jj * 512 + nq], lhsT=wt[:, c, wcols[j0 + jj]:wcols[j0 + jj] + rows], rhs=hT[:, c, q0:q0 + nq], start=(c == 0), stop=(c == 7)),
                                           [bw, b_hT], [bpd], inc=(c == 7 and jj == nj - 1))
                                for jj in range(nj):
                                    dst, bd = dst_fn(j0 + jj, q0, q0 + nq)
                                    act(lambda e, jj=jj, pd=pd, dst=dst: e.copy(out=dst, in_=pd[0:rows, jj * 512:jj * 512 + nq]), [bpd], [bd])

                    for ah in range(2):
                        with contextlib.ExitStack() as SA:
                            wA_ = sb(SA, [128, 8, 768], BF16, "wA"); b_wA_ = Buf()
                            wsrc = wl["w_in"].rearrange("(c p) n -> p c n", p=128)
                            for i3 in range(3):
                                P.dma("sp", wA_[:, :, i3 * 256:(i3 + 1) * 256], wsrc[:, :, i3 * 512 + ah * 256:i3 * 512 + ah * 256 + 256], bWs, [b_wA_])
                            ck("Aw")
                            QT = sb(SA, [128, 2, T], BF16, "QTA"); b_QT = Buf()
                            KT = sb(SA, [128, 2, NK], BF16, "KTA"); b_KT = Buf()
                            VA = sb(SA, [128, NKB, 4, 65], BF16, "VA"); b_VA = Buf()
                            dve(lambda e: e.memset(VA[:, :, :, :].rearrange("p a b c -> p (a b c)"), 1.0), (), [b_VA])
                            kv32 = [sb(SA, [128, 2, 256], F32, "kv32") for _ in range(2)]; bkv32 = [Buf(), Buf()]
                            if not prm:
                                ckb = [sb(SA, [128, 256], BF16, "ckb") for _ in range(2)]; bckb = [Buf(), Buf()]
                                for j in range(NKB_past):
                                    k = j % 2
                                    P.dma("pool", ckb[k][:, :], IA["cache_a_k"][l, j * 128:(j + 1) * 128, ah * 256:(ah + 1) * 256], (), [bckb[k]])
                                    P.dma("pool", VA[:, j, :, 0:64], IA["cache_a_v"][l, j * 128:(j + 1) * 128, ah * 256:(ah + 1) * 256].rearrange("p (h d) -> p h d", h=4), (), [b_VA])
                                    pt, bpt = next_ptr()
                                    for c in range(2):
                                        pe(lambda e, c=c, k=k, pt=pt: e.transpose(out=pt[:, c * 128:(c + 1) * 128], in_=ckb[k][:, c * 128:(c + 1) * 128], identity=ident[:, :]), [bckb[k], b_ident], [bpt], inc=(c == 1))
                                    ptv = pt[:, 0:256].rearrange("p (c t) -> p c t", c=2)
                                    act(lambda e, j=j, ptv=ptv: e.copy(out=KT[:, :, j * 128:(j + 1) * 128], in_=ptv), [bpt], [b_KT])
                            proj_fm(lambda j, c0, c1: (QT[:, j, c0:c1], b_QT), wA_, b_wA_, [0, 128], 2)
                            ck("Aq")
                            proj_fm(lambda j, c0, c1: (KT[:, j, past + c0:past + c1], b_KT), wA_, b_wA_, [256, 384], 2)
                            ck("Ak")
                            for ti, (t0, n) in enumerate(tiles):
                                pd, bpd = next_dbl()
                                for jj in range(2):
                                    for c in range(8):
                                        pe(lambda e, jj=jj, c=c, pd=pd, t0=t0, n=n: e.matmul(pd[0:n, jj * 512:jj * 512 + 256], lhsT=hT[:, c, t0:t0 + n], rhs=wA_[:, c, 256 + jj * 256:512 + jj * 256], start=(c == 0), stop=(c == 7)),
                                           [b_wA_, b_hT], [bpd], inc=(c == 7 and jj == 1))
                                k = ti % 2
                                pdv = pd[:, :].rearrange("p (a b) -> p a b", a=2)
                                if '1' not in DBG:
                                    act(lambda e, k=k, n=n, pdv=pdv: e.copy(out=kv32[k][0:n, :, :], in_=pdv[0:n, :, 0:256]), [bpd], [bkv32[k]])
                                if '2' not in DBG:
                                    dve(lambda e, ti=ti, n=n, pd=pd: e.tensor_copy(out=VA[0:n, NKB_past + ti, :, 0:64], in_=pd[0:n, 512:768].rearrange("p (h d) -> p h d", h=4)), [bpd], [b_VA])
                                if 'D' in DBG and ti == 0 and l == 0 and ah == 0:
                                    P.dma("pool", ydst[384:512, 0:256], kv32[k][:, 0, :], [bkv32[k]], [])
                                    P.dma("pool", ydst[512:640, 0:768], wA_[:, 0, :], [b_wA_], [])
                                if 'O' not in DBG:
                                    P.dma("pool", OA[pfx + "a_k"][l, so, t0:t0 + n, ah * 256:(ah + 1) * 256], kv32[k][0:n, 0, :], [bkv32[k]], [])
                                    P.dma("pool", OA[pfx + "a_v"][l, so, t0:t0 + n, ah * 256:(ah + 1) * 256], kv32[k][0:n, 1, :], [bkv32[k]], [])
                            ck("Aproj")
                            callsA = []
                            for hh in range(4):
                                h = ah * 4 + hh
                                c, r0 = hh // 2, (hh % 2) * 64
                                lanes = []
                                for half in range(2):
                                    rr = r0 + 32 * half
                                    lanes.append(dict(kt=lambda c0, nk, rr=rr, c=c: KT[rr:rr + 32, c, c0:c0 + nk], qt=lambda c0, c1, rr=rr, c=c: QT[rr:rr + 32, c, c0:c1],
                                                      v=lambda kbi, nk, hh=hh: VA[0:nk, kbi, hh, :], E=EA[:, h, :], Eb=b_EA, Ethr=-91, c0=384, tp=(rr, 0), Eshared=True))
                                for qb in qblocks:
                                    callsA.append(("A", lanes, qb, "causal", list(enumerate(kblocks)), [b_QT, b_KT, b_VA], epi_diff(h)))
                            attend_many(callsA)
                            flush_pending()
                            P.barrier()
                            ck("Ahalf")

                    with contextlib.ExitStack() as SB:
                        wB_ = sb(SB, [128, 8, 768], BF16, "wB"); b_wB_ = Buf()
                        wsrc = wl["w_in"].rearrange("(c p) n -> p c n", p=128)
                        P.dma("sp", wB_[:, :, :], wsrc[:, :, 1536:2304], bWs, [b_wB_])
                        pastB = 0 if prm else NBS
                        NKb = pastB + T
                        kbB = [(j * 128, past - pastB + j * 128, 128) for j in range(pastB // 128)] + [(pastB + t0, past + t0, n) for (t0, n) in tiles]
                        QT = sb(SB, [128, 2, T], BF16, "QTB"); b_QT = Buf()
                        KT = sb(SB, [128, 2, NKb], BF16, "KTB"); b_KT = Buf()
                        VB = sb(SB, [128, len(kbB), 4, 65], BF16, "VB"); b_VB = Buf()
                        dve(lambda e: e.memset(VB[:, :, :, :].rearrange("p a b c -> p (a b c)"), 1.0), (), [b_VB])
                        kv32 = [sb(SB, [128, 512], F32, "kv32b") for _ in range(2)]; bkv32 = [Buf(), Buf()]
                        if not prm:
                            ckb = [sb(SB, [128, 256], BF16, "ckbb") for _ in range(2)]; bckb = [Buf(), Buf()]
                            for j in range(pastB // 128):
                                k = j % 2
                                P.dma("pool", ckb[k][:, :], IA["cache_b_k"][l, j * 128:(j + 1) * 128, :], (), [bckb[k]])
                                P.dma("pool", VB[:, j, :, 0:64], IA["cache_b_v"][l, j * 128:(j + 1) * 128, :].rearrange("p (h d) -> p h d", h=4), (), [b_VB])
                                pt, bpt = next_ptr()
                                for c in range(2):
                                    pe(lambda e, c=c, k=k, pt=pt: e.transpose(out=pt[:, c * 128:(c + 1) * 128], in_=ckb[k][:, c * 128:(c + 1) * 128], identity=ident[:, :]), [bckb[k], b_ident], [bpt], inc=(c == 1))
                                ptv = pt[:, 0:256].rearrange("p (c t) -> p c t", c=2)
                                act(lambda e, j=j, ptv=ptv: e.copy(out=KT[:, :, j * 128:(j + 1) * 128], in_=ptv), [bpt], [b_KT])
                        proj_fm(lambda j, c0, c1: (QT[:, j, c0:c1], b_QT), wB_, b_wB_, [0, 128], 2)
                        proj_fm(lambda j, c0, c1: (KT[:, j, pastB + c0:pastB + c1], b_KT), wB_, b_wB_, [256, 384], 2)
                        for ti, (t0, n) in enumerate(tiles):
                            pd, bpd = next_dbl()
                            for c in range(8):
                                pe(lambda e, c=c, pd=pd, t0=t0, n=n: e.matmul(pd[0:n, 0:512], lhsT=hT[:, c, t0:t0 + n], rhs=wB_[:, c, 256:768], start=(c == 0), stop=(c == 7)), [b_wB_, b_hT], [bpd], inc=(c == 7))
                            k = ti % 2
                            act(lambda e, k=k, n=n, pd=pd: e.copy(out=kv32[k][0:n, :], in_=pd[0:n, 0:512]), [bpd], [bkv32[k]])
                            dve(lambda e, ti=ti, n=n, pd=pd: e.tensor_copy(out=VB[0:n, pastB // 128 + ti, :, 0:64], in_=pd[0:n, 256:512].rearrange("p (h d) -> p h d", h=4)), [bpd], [b_VB])
                            if prm:
                                if t0 >= SEQ - NBP:
                                    r0_ = t0 - (SEQ - NBP)
                                    P.dma("pool", OA["p_b_k"][l, so, r0_:r0_ + n, :], kv32[k][0:n, 0:256], [bkv32[k]], [])
                                    P.dma("pool", OA["p_b_v"][l, so, r0_:r0_ + n, :], kv32[k][0:n, 256:512], [bkv32[k]], [])
                            else:
                                P.dma("pool", OA["s_b_k"][l, 0, t0:t0 + n, :], kv32[k][0:n, 0:256], [bkv32[k]], [])
                                P.dma("pool", OA["s_b_v"][l, 0, t0:t0 + n, :], kv32[k][0:n, 256:512], [bkv32[k]], [])
                        callsB = []
                        for hp in range(2):
                            lanes = []
                            for li in range(2):
                                h = hp * 2 + li
                                r0 = li * 64
                                lanes.append(dict(kt=lambda c0, nk, r0=r0, hp=hp: KT[r0:r0 + 64, hp, c0:c0 + nk], qt=lambda c0, c1, r0=r0, hp=hp: QT[r0:r0 + 64, hp, c0:c1],
                                                  v=lambda kbi, nk, h=h: VB[0:nk, kbi, h, :], E=EB[:, h, :], Eb=b_EB, Ethr=-128, c0=384, tp=None))
                            for qb in qblocks:
                                callsB.append(("B", lanes, qb, "band", list(enumerate(kbB)), [b_QT, b_KT, b_VB], epi_plain([512 + (hp * 2) * 64, 512 + (hp * 2 + 1) * 64])))
                        attend_many(callsB)
                        flush_pending()
                        P.barrier()

                    ck("B")
                    SE.close()
                    with contextlib.ExitStack() as SC:
                        wC_ = sb(SC, [128, 8, 416], BF16, "wC"); b_wC_ = Buf()
                        wsrc = wl["w_in"].rearrange("(c p) n -> p c n", p=128)
                        P.dma("sp", wC_[:, :, :], wsrc[:, :, 2304:2720], bWs, [b_wC_])
                        wuq = sb(SC, [128, 2, 384], BF16, "wuq"); b_wuq = Buf()
                        P.dma("sp", wuq[:, :, :], wl["mla_w_uq"].rearrange("(c p) n -> p c n", p=128), bWs, [b_wuq])
                        wukv = sb(SC, [128, 512], BF16, "wukv"); b_wukv = Buf()
                        P.dma("sp", wukv[:, :], wl["mla_w_ukv"], bWs, [b_wukv])
                        gq = sb(SC, [128, 256], F32, "gq"); b_gq = Buf()
                        gkv = sb(SC, [128, 128], F32, "gkv"); b_gkv = Buf()
                        bcast_row(gq[:], I["mla_q_norm"], l * 256, 256, [b_gq])
                        bcast_row(gkv[:], I["mla_kv_norm"], l * 128, 128, [b_gkv])
                        cqnT = sb(SC, [128, 2, T], BF16, "cqnT"); b_cqnT = Buf()
                        latT = sb(SC, [128, NK], BF16, "latT"); b_latT = Buf()
                        kT = sb(SC, [96, 4, NK], BF16, "kTC"); b_kT = Buf()
                        qT = sb(SC, [96, 4, T], BF16, "qTC"); b_qT = Buf()
                        VC = sb(SC, [128, NKB, 4, 65], BF16, "VC"); b_VC = Buf()
                        dve(lambda e: e.memset(VC[:, :, :, :].rearrange("p a b c -> p (a b c)"), 1.0), (), [b_VC])
                        rp = sb(SC, [128, NT, 128], F32, "rope"); b_rp = Buf()
                        for ti, (t0, n) in enumerate(tiles):
                            P.dma("sp", rp[0:n, ti, :], rope_h.ap()[t0:t0 + n, :], (), [b_rp])
                        c32 = [sb(SC, [128, 416], F32, "c32") for _ in range(4)]; bc32 = [Buf() for _ in range(4)]
                        ssc = [sb(SC, [128, 2], F32, "ssc") for _ in range(4)]; bssc = [Buf(strict=True) for _ in range(4)]
                        junkc = sb(SC, [128, 384], BF16, "junkc"); b_junkc = Buf()
                        cqn = [sb(SC, [128, 256], BF16, "cqn") for _ in range(4)]; bcqn = [Buf() for _ in range(4)]
                        lat32 = [sb(SC, [128, 128], F32, "lat32") for _ in range(4)]; blat32 = [Buf() for _ in range(4)]
                        latb = [sb(SC, [128, 160], BF16, "latb") for _ in range(4)]; blatb = [Buf() for _ in range(4)]
                        kpe32 = [sb(SC, [128, 32], F32, "kpe32") for _ in range(4)]; bkpe32 = [Buf() for _ in range(4)]
                        rt = sb(SC, [128, 4, 4, 16], F32, "rt"); b_rt = Buf()
                        q32 = [sb(SC, [128, 4, 96], F32, "q32") for _ in range(4)]; bq32 = [Buf() for _ in range(4)]
                        qb16 = [sb(SC, [128, 4, 96], BF16, "qb16") for _ in range(4)]; bqb16 = [Buf() for _ in range(4)]

                        def rope_apply(dst1, dst2, x1, x2, cs_, sn_, shape_n, rbufs_, wbufs_):
                            nh = shape_n
                            dve(lambda e: e.tensor_tensor(out=rt[0:nh[0], 0, 0:nh[1], :], in0=x1, in1=cs_, op=ALU.mult), rbufs_, [b_rt])
                            dve(lambda e: e.tensor_tensor(out=rt[0:nh[0], 1, 0:nh[1], :], in0=x2, in1=sn_, op=ALU.mult), rbufs_, [b_rt])
                            dve(lambda e: e.tensor_tensor(out=rt[0:nh[0], 2, 0:nh[1], :], in0=x2, in1=cs_, op=ALU.mult), rbufs_, [b_rt])
                            dve(lambda e: e.tensor_tensor(out=rt[0:nh[0], 3, 0:nh[1], :], in0=x1, in1=sn_, op=ALU.mult), rbufs_, [b_rt])
                            dve(lambda e: e.tensor_tensor(out=dst1, in0=rt[0:nh[0], 0, 0:nh[1], :], in1=rt[0:nh[0], 1, 0:nh[1], :], op=ALU.subtract), [b_rt], wbufs_)
                            dve(lambda e: e.tensor_tensor(out=dst2, in0=rt[0:nh[0], 2, 0:nh[1], :], in1=rt[0:nh[0], 3, 0:nh[1], :], op=ALU.add), [b_rt], wbufs_)

                        if not prm:
                            for j in range(NKB_past):
                                k = j % 2
                                P.dma("pool", latb[k][:, 0:128], IA["cache_c_latent"][l, j * 128:(j + 1) * 128, :], (), [blatb[k]])
                                P.dma("pool", latb[k][:, 128:160], IA["cache_c_rope_k"][l, j * 128:(j + 1) * 128, :], (), [blatb[k]])
                                pt, bpt = next_ptr()
                                pe(lambda e, k=k, pt=pt: e.transpose(out=pt[:, 0:128], in_=latb[k][:, 0:128], identity=ident[:, :]), [blatb[k], b_ident], [bpt], inc=False)
                                pe(lambda e, k=k, pt=pt: e.transpose(out=pt[0:32, 128:256], in_=latb[k][:, 128:160], identity=ident[:, :]), [blatb[k], b_ident], [bpt])
                                act(lambda e, j=j, pt=pt: e.copy(out=latT[:, j * 128:(j + 1) * 128], in_=pt[:, 0:128]), [bpt], [b_latT])
                                for h in range(4):
                                    dve(lambda e, j=j, h=h, pt=pt: e.tensor_copy(out=kT[64:96, h, j * 128:(j + 1) * 128], in_=pt[0:32, 128:256]), [bpt], [b_kT])
                        rt2 = [rt] + [sb(SC, [128, 4, 4, 16], F32, "rtb") for _ in range(3)]; b_rt2 = [b_rt, Buf(), Buf(), Buf()]
                        junkc2 = [junkc] * 4; b_junkc2 = [b_junkc] * 4

                        def rope2(k, dst1, dst2, x1, x2, cs_, sn_, nh, rbufs_, wbufs_):
                            R_, bR = rt2[k], b_rt2[k]
                            dve(lambda e: e.tensor_tensor(out=R_[0:nh[0], 0, 0:nh[1], :], in0=x1, in1=cs_, op=ALU.mult), rbufs_, [bR])
                            dve(lambda e: e.tensor_tensor(out=R_[0:nh[0], 1, 0:nh[1], :], in0=x2, in1=sn_, op=ALU.mult), rbufs_, [bR])
                            dve(lambda e: e.tensor_tensor(out=R_[0:nh[0], 2, 0:nh[1], :], in0=x2, in1=cs_, op=ALU.mult), rbufs_, [bR])
                            dve(lambda e: e.tensor_tensor(out=R_[0:nh[0], 3, 0:nh[1], :], in0=x1, in1=sn_, op=ALU.mult), rbufs_, [bR])
                            dve(lambda e: e.tensor_tensor(out=dst1, in0=R_[0:nh[0], 0, 0:nh[1], :], in1=R_[0:nh[0], 1, 0:nh[1], :], op=ALU.subtract), [bR], wbufs_)
                            dve(lambda e: e.tensor_tensor(out=dst2, in0=R_[0:nh[0], 2, 0:nh[1], :], in1=R_[0:nh[0], 3, 0:nh[1], :], op=ALU.add), [bR], wbufs_)

                        def c_stages(ti, t0, n):
                            k = ti % 4
                            st = {}

                            def g0():
                                pd, bpd = next_dbl()
                                for c in range(8):
                                    pe(lambda e, c=c: e.matmul(pd[0:n, 0:416], lhsT=hT[:, c, t0:t0 + n], rhs=wC_[:, c, :], start=(c == 0), stop=(c == 7)), [b_wC_, b_hT], [bpd], inc=(c == 7))
                                act(lambda e: e.copy(out=c32[k][0:n, :], in_=pd[0:n, 0:416]), [bpd], [bc32[k]])

                            def g1():
                                dve(lambda e: e.scalar_tensor_tensor(out=junkc2[k][0:n, 0:256], in0=c32[k][0:n, 0:256], scalar=1.0 / 256, in1=c32[k][0:n, 0:256], op0=ALU.mult, op1=ALU.mult, accum_out=ssc[k][0:n, 0:1]), [bc32[k]], [b_junkc2[k], bssc[k]])
                                dve(lambda e: e.scalar_tensor_tensor(out=junkc2[k][0:n, 0:128], in0=c32[k][0:n, 256:384], scalar=1.0 / 128, in1=c32[k][0:n, 256:384], op0=ALU.mult, op1=ALU.mult, accum_out=ssc[k][0:n, 1:2]), [bc32[k]], [b_junkc2[k], bssc[k]])
                                rstd_of(ssc[k][0:n, 0:2], n, 1, bssc[k])
                                rope2(k, kpe32[k][0:n, 0:16].rearrange("p (a d) -> p a d", a=1), kpe32[k][0:n, 16:32].rearrange("p (a d) -> p a d", a=1),
                                      c32[k][0:n, 384:400].rearrange("p (a d) -> p a d", a=1), c32[k][0:n, 400:416].rearrange("p (a d) -> p a d", a=1),
                                      rp[0:n, ti, 0:16].rearrange("p (a d) -> p a d", a=1), rp[0:n, ti, 64:80].rearrange("p (a d) -> p a d", a=1), (n, 1), [bc32[k], b_rp], [bkpe32[k]])
                                P.dma("pool", OA[pfx + "kpe"][l, so, t0:t0 + n, :], kpe32[k][0:n, :], [bkpe32[k]], [])
                                dve(lambda e: e.tensor_copy(out=latb[k][0:n, 128:160], in_=kpe32[k][0:n, :]), [bkpe32[k]], [blatb[k]])

                            def g2():
                                dve(lambda e: e.scalar_tensor_tensor(out=cqn[k][0:n, :], in0=c32[k][0:n, 0:256], scalar=ssc[k][0:n, 0:1], in1=gq[0:n, :], op0=ALU.mult, op1=ALU.mult), [bc32[k], bssc[k], b_gq], [bcqn[k]])
                                dve(lambda e: e.scalar_tensor_tensor(out=lat32[k][0:n, :], in0=c32[k][0:n, 256:384], scalar=ssc[k][0:n, 1:2], in1=gkv[0:n, :], op0=ALU.mult, op1=ALU.mult), [bc32[k], bssc[k], b_gkv], [blat32[k]])
                                P.dma("pool", OA[pfx + "lat"][l, so, t0:t0 + n, :], lat32[k][0:n, :], [blat32[k]], [])
                                dve(lambda e: e.tensor_copy(out=latb[k][0:n, 0:128], in_=lat32[k][0:n, :]), [blat32[k]], [blatb[k]])

                            def g3():
                                pt, bpt = next_ptr()
                                for c in range(2):
                                    pe(lambda e, c=c: e.transpose(out=pt[:, c * 128:c * 128 + n], in_=cqn[k][0:n, c * 128:(c + 1) * 128], identity=ident[0:n, 0:n]), [bcqn[k], b_ident], [bpt], inc=False)
                                pe(lambda e: e.transpose(out=pt[:, 256:256 + n], in_=latb[k][0:n, 0:128], identity=ident[0:n, 0:n]), [blatb[k], b_ident], [bpt], inc=False)
                                pe(lambda e: e.transpose(out=pt[0:32, 384:384 + n], in_=latb[k][0:n, 128:160], identity=ident[0:n, 0:n]), [blatb[k], b_ident], [bpt])
                                ptv = pt[:, 0:256].rearrange("p (c t) -> p c t", c=2)
                                act(lambda e: e.copy(out=cqnT[:, :, t0:t0 + n], in_=ptv[:, :, 0:n]), [bpt], [b_cqnT])
                                act(lambda e: e.copy(out=latT[:, past + t0:past + t0 + n], in_=pt[:, 256:256 + n]), [bpt], [b_latT])
                                for h in range(4):
                                    dve(lambda e, h=h: e.tensor_copy(out=kT[64:96, h, past + t0:past + t0 + n], in_=pt[0:32, 384:384 + n]), [bpt], [b_kT])

                            def g4():
                                pd, bpd = next_dbl()
                                for c in range(2):
                                    pe(lambda e, c=c: e.matmul(pd[0:n, 0:384], lhsT=cqnT[:, c, t0:t0 + n], rhs=wuq[:, c, :], start=(c == 0), stop=(c == 1)), [b_cqnT, b_wuq], [bpd], inc=(c == 1))
                                act(lambda e: e.copy(out=q32[k][0:n, :, :], in_=pd[0:n, 0:384].rearrange("p (h d) -> p h d", h=4)), [bpd], [bq32[k]])

                            def g5():
                                dve(lambda e: e.tensor_copy(out=qb16[k][0:n, :, 0:64], in_=q32[k][0:n, :, 0:64]), [bq32[k]], [bqb16[k]])
                                rope2(k, qb16[k][0:n, :, 64:80], qb16[k][0:n, :, 80:96], q32[k][0:n, :, 64:80], q32[k][0:n, :, 80:96],
                                      rp[0:n, ti, 0:64].rearrange("p (h d) -> p h d", h=4), rp[0:n, ti, 64:128].rearrange("p (h d) -> p h d", h=4), (n, 4), [bq32[k], b_rp], [bqb16[k]])

                            def g6():
                                pt, bpt = next_ptr()
                                for h in range(4):
                                    pe(lambda e, h=h: e.transpose(out=pt[0:96, h * 128:h * 128 + n], in_=qb16[k][0:n, h, :], identity=ident[0:n, 0:n]), [bqb16[k], b_ident], [bpt], inc=(h == 3))
                                ptv = pt[:, 0:512].rearrange("p (c t) -> p c t", c=4)
                                act(lambda e: e.copy(out=qT[:, :, t0:t0 + n], in_=ptv[0:96, :, 0:n]), [bpt], [b_qT])
                            return [g0, g1, g2, g3, g4, g5, g6]

                        for ti0 in range(0, NT, 4):
                            grp_ = [c_stages(ti, *tiles[ti]) for ti in range(ti0, min(NT, ti0 + 4))]
                            for si in range(7):
                                for stg in grp_:
                                    stg[si]()
                        for c0 in range(0, NK, 512):
                            nn_ = min(512, NK - c0)
                            for hp in range(2):
                                pd, bpd = next_dbl()
                                for jj in range(2):
                                    h = hp * 2 + jj
                                    pe(lambda e, jj=jj, h=h, pd=pd, c0=c0, nn_=nn_: e.matmul(pd[0:64, jj * 512:jj * 512 + nn_], lhsT=wukv[:, h * 128:h * 128 + 64], rhs=latT[:, c0:c0 + nn_], start=True, stop=True), [b_wukv, b_latT], [bpd], inc=(jj == 1))
                                pdv = pd[:, :].rearrange("p (a b) -> p a b", a=2)
                                act(lambda e, hp=hp, pdv=pdv, c0=c0, nn_=nn_: e.copy(out=kT[0:64, hp * 2:hp * 2 + 2, c0:c0 + nn_], in_=pdv[0:64, :, 0:nn_]), [bpd], [b_kT])
                        wv_ = wukv[:, :].rearrange("p (h d) -> p h d", h=4)
                        for kbi, (kc0, kpos, nk) in enumerate(kblocks):
                            pd, bpd = next_dbl()
                            pe(lambda e, pd=pd, kc0=kc0, nk=nk: e.matmul(pd[0:nk, 0:256].rearrange("p (h d) -> p h d", h=4), lhsT=latT[:, kc0:kc0 + nk], rhs=wv_[:, :, 64:128], start=True, stop=True), [b_wukv, b_latT], [bpd])
                            act(lambda e, kbi=kbi, nk=nk, pd=pd: e.copy(out=VC[0:nk, kbi, :, 0:64], in_=pd[0:nk, 0:256].rearrange("p (h d) -> p h d", h=4)), [bpd], [b_VC])
                        if 'Q' in DBG and l == 0:
                            for hq in range(4):
                                P.dma("pool", ydst[0:96, hq * 128:(hq + 1) * 128], qT[:, hq, 0:128], [b_qT], [])
                                P.dma("pool", ydst[128:224, hq * 128:(hq + 1) * 128], kT[:, hq, 0:128], [b_kT], [])
                            P.dma("pool", ydst[256:384, 0:260], VC[:, 0, :, :].rearrange("p a b -> p (a b)"), [b_VC], [])
                            P.dma("pool", ydst[384:512, 0:128], latT[:, 0:128], [b_latT], [])
                        callsC = []
                        for hp in range(2):
                            lanes = []
                            for li in range(2):
                                h = hp * 2 + li
                                lanes.append(dict(kt=lambda c0, nk, h=h: kT[0:96, h, c0:c0 + nk], qt=lambda c0, c1, h=h: qT[0:96, h, c0:c1], v=lambda kbi, nk, h=h: VC[0:nk, kbi, h, :], E=None, tp=None))
                            for qb in qblocks:
                                callsC.append(("C", lanes, qb, "causal", list(enumerate(kblocks)), [b_qT, b_kT, b_VC], epi_plain([768 + (hp * 2) * 64, 768 + (hp * 2 + 1) * 64])))
                        attend_many(callsC)
                        flush_pending()
                        P.barrier()
                    with contextlib.ExitStack() as SO:
                        wout = sb(SO, [128, 8, D], BF16, "wout"); b_wout = Buf()
                        P.dma("sp", wout[:, :, :], wl["w_out"].rearrange("(c p) n -> p c n", p=128), bWs, [b_wout])
                        gmp = sb(SO, [128, D], F32, "gmp"); b_gmp = Buf()
                        bcast_row(gmp[:], I["norms"], (l * 7 + 1) * D, D, [b_gmp])
                        xo_ = [sb(SO, [128, D], F32, "xo") for _ in range(2)]; bxo = [Buf(), Buf()]
                        yo_ = [sb(SO, [128, D], F32, "yo") for _ in range(2)]; byo = [Buf(), Buf()]
                        aT = [sb(SO, [128, 8, 128], BF16, "aT") for _ in range(2)]; baT = [Buf(), Buf()]
                        junko = sb(SO, [128, D], BF16, "junko"); b_junko = Buf()
                        sso = sb(SO, [128, 2], F32, "sso"); b_sso = Buf(strict=True)
                        b_sso2 = [Buf(strict=True), Buf(strict=True)]

                        def wo_xload(ti):
                            (t0, n) = tiles[ti]
                            k = ti % 2
                            P.dma("sp", xo_[k][0:n, :], xsrc[t0:t0 + n, :], [ybufs[ti]], [bxo[k]])

                        def wo_prep(ti):
                            (t0, n) = tiles[ti]
                            k = ti % 2
                            pt, bpt = next_ptr()
                            for c in range(8):
                                pe(lambda e, c=c: e.transpose(out=pt[:, c * 128:c * 128 + n], in_=a_tok[0:n, ti, c * 128:(c + 1) * 128], identity=ident[0:n, 0:n]), [b_atok, b_ident], [bpt], inc=(c == 7))
                            ptv = pt[:, :].rearrange("p (c t) -> p c t", c=8)
                            act(lambda e: e.copy(out=aT[k][:, :, 0:n], in_=ptv[:, :, 0:n]), [bpt], [baT[k]])

                        for ti in range(min(2, NT)):
                            wo_xload(ti)
                            wo_prep(ti)
                        for ti, (t0, n) in enumerate(tiles):
                            k = ti % 2
                            pd, bpd = next_dbl()
                            for hf in range(2):
                                for c in range(8):
                                    pe(lambda e, hf=hf, c=c, n=n, k=k, pd=pd: e.matmul(pd[0:n, hf * 512:(hf + 1) * 512], lhsT=aT[k][:, c, 0:n], rhs=wout[:, c, hf * 512:(hf + 1) * 512], start=(c == 0), stop=(c == 7)), [baT[k], b_wout], [bpd], inc=(c == 7 and hf == 1))
                            act(lambda e, k=k, n=n, pd=pd: e.copy(out=yo_[k][0:n, :], in_=pd[0:n, :]), [bpd], [byo[k]])
                            if ti + 2 < NT:
                                wo_prep(ti + 2)
                            dve(lambda e, k=k, n=n: e.scalar_tensor_tensor(out=junko[0:n, :], in0=yo_[k][0:n, :], scalar=1.0, in1=yo_[k][0:n, :], op0=ALU.mult, op1=ALU.mult, accum_out=sso[0:n, k:k + 1]), [byo[k]], [b_junko, b_sso2[k]])
                            rstd_of(sso[0:n, k:k + 1], n, D, b_sso2[k])
                            dve(lambda e, k=k, n=n: e.scalar_tensor_tensor(out=yo_[k][0:n, :], in0=yo_[k][0:n, :], scalar=sso[0:n, k:k + 1], in1=gmp[0:n, :], op0=ALU.mult, op1=ALU.mult), [byo[k], b_sso2[k], b_gmp], [byo[k]])
                            P.op("pool", lambda e, k=k, n=n: e.tensor_tensor(out=xo_[k][0:n, :], in0=xo_[k][0:n, :], in1=yo_[k][0:n, :], op=ALU.add), [byo[k], bxo[k]], [bxo[k]])
                            P.dma("pool", ydst[t0:t0 + n, :], xo_[k][0:n, :], [bxo[k]], [ybufs[ti]])
                            if ti + 2 < NT:
                                wo_xload(ti + 2)
                        P.barrier()
                    P.barrier()

                ck("C")
                with contextlib.ExitStack() as S2:
                    wxq = sb(S2, [128, 8, 256], BF16, "wxq"); b_wxq = Buf()
                    wxo = sb(S2, [128, 2, D], BF16, "wxo"); b_wxo = Buf()
                    gb = sb(S2, [128, 4, D], F32, "gb"); b_gb = Buf()
                    cw = sb(S2, [128, NFC, 4], F32, "cw"); b_cw = Buf()
                    mkT = sb(S2, [128, 2, MEM], BF16, "mkT"); b_mkT = Buf()
                    MV = sb(S2, [128, 2, 4, 65], BF16, "MV"); b_MV = Buf()
                    dve(lambda e: e.memset(MV[:, :, :, :].rearrange("p a b c -> p (a b c)"), 1.0), (), [b_MV])
                    wd_all = sb(S2, [128, NFC, D], BF16, "wd_all"); b_wd = Buf()
                    wds = wl["w_down"].rearrange("(f p) n -> p f n", p=128)
                    junkb = sb(S2, [128, D], BF16, "junkb2"); b_junkb = Buf()
                    ss4 = sb(S2, [128, 4], F32, "ss4"); b_ss4 = Buf(strict=True)
                    hb4 = sb(S2, [128, 4, D], BF16, "hb4"); b_hb4 = [Buf() for _ in range(4)]
                    if prm:
                        with contextlib.ExitStack() as SM:
                            wmk = sb(SM, [128, 8, 512], BF16, "wmk"); b_wmk = Buf()
                            gm = sb(SM, [128, D], F32, "gm"); b_gm = Buf()
                            mT = sb(SM, [128, 8, MEM], BF16, "mT"); b_mT = Buf()
                            m32 = sb(SM, [128, 512], F32, "m32"); b_m32 = Buf()
                            xm = [sb(SM, [128, D], F32, "xm") for _ in range(2)]; bxm = [Buf(), Buf()]
                            for mi in range(2):
                                P.dma("sp", xm[mi][:, :], IA["mem_prompt"][s, mi * 128:(mi + 1) * 128, :], (), [bxm[mi]])
                            bcast_row(gm[:], I["norms"], (l * 7 + 6) * D, D, [b_gm])
                            P.dma("sp", wmk[:, :, 0:256], wl["w_mk"].rearrange("(c p) n -> p c n", p=128), bWs, [b_wmk])
                            P.dma("sp", wmk[:, :, 256:512], wl["w_mv"].rearrange("(c p) n -> p c n", p=128), bWs, [b_wmk])
                            for mi in range(2):
                                k = mi % 2
                                dve(lambda e, k=k, mi=mi: e.scalar_tensor_tensor(out=junkb[:, :], in0=xm[k][:, :], scalar=1.0, in1=xm[k][:, :], op0=ALU.mult, op1=ALU.mult, accum_out=ss4[:, mi:mi + 1]), [bxm[k]], [b_junkb, b_ss4])
                            rstd_of(ss4[:, 0:2], 128, D, b_ss4)
                            for mi in range(2):
                                k = mi % 2
                                dve(lambda e, k=k, mi=mi: e.scalar_tensor_tensor(out=hb4[:, mi, :], in0=xm[k][:, :], scalar=ss4[:, mi:mi + 1], in1=gm[:, :], op0=ALU.mult, op1=ALU.mult), [bxm[k], b_ss4, b_gm], [b_hb4[mi]])
                                pt, bpt = next_ptr()
                                for c in range(8):
                                    pe(lambda e, c=c, pt=pt, mi=mi: e.transpose(out=pt[:, c * 128:(c + 1) * 128], in_=hb4[:, mi, c * 128:(c + 1) * 128], identity=ident[:, :]), [b_hb4[mi], b_ident], [bpt], inc=(c == 7))
                                ptv = pt[:, :].rearrange("p (c t) -> p c t", c=8)
                                act(lambda e, mi=mi, ptv=ptv: e.copy(out=mT[:, :, mi * 128:(mi + 1) * 128], in_=ptv), [bpt], [b_mT])
                            for mi in range(2):
                                pd, bpd = next_dbl()
                                for c in range(8):
                                    pe(lambda e, c=c, pd=pd, mi=mi: e.matmul(pd[:, 0:512], lhsT=mT[:, c, mi * 128:(mi + 1) * 128], rhs=wmk[:, c, :], start=(c == 0), stop=(c == 7)), [b_mT, b_wmk], [bpd], inc=(c == 7))
                                act(lambda e, pd=pd: e.copy(out=m32[:, :], in_=pd[:, 0:512]), [bpd], [b_m32])
                                dve(lambda e, mi=mi, pd=pd: e.tensor_copy(out=MV[:, mi, :, 0:64], in_=pd[:, 256:512].rearrange("p (h d) -> p h d", h=4)), [bpd], [b_MV])
                                P.dma("pool", OA["p_mk"][l, s, mi * 128:(mi + 1) * 128, :], m32[:, 0:256], [b_m32], [])
                                P.dma("pool", OA["p_mv"][l, s, mi * 128:(mi + 1) * 128, :], m32[:, 256:512], [b_m32], [])
                            pd, bpd = next_dbl()
                            for j in range(2):
                                for c in range(8):
                                    pe(lambda e, j=j, c=c, pd=pd: e.matmul(pd[:, j * 512:j * 512 + MEM], lhsT=wmk[:, c, j * 128:(j + 1) * 128], rhs=mT[:, c, :], start=(c == 0), stop=(c == 7)), [b_mT, b_wmk], [bpd], inc=(c == 7 and j == 1))
                            pdv = pd[:, :].rearrange("p (a b) -> p a b", a=2)
                            act(lambda e, pdv=pdv: e.copy(out=mkT[:, :, :], in_=pdv[:, :, 0:MEM]), [bpd], [b_mkT])
                            P.barrier()
                    else:
                        with contextlib.ExitStack() as SM:
                            ckb = sb(SM, [128, 256], BF16, "ckbm"); bckb = Buf()
                            for mi in range(2):
                                P.dma("pool", ckb[:, :], IA["cache_mem_k"][l, mi * 128:(mi + 1) * 128, :], (), [bckb])
                                P.dma("pool", MV[:, mi, :, 0:64], IA["cache_mem_v"][l, mi * 128:(mi + 1) * 128, :].rearrange("p (h d) -> p h d", h=4), (), [b_MV])
                                pt, bpt = next_ptr()
                                for c in range(2):
                                    pe(lambda e, c=c, pt=pt: e.transpose(out=pt[:, c * 128:(c + 1) * 128], in_=ckb[:, c * 128:(c + 1) * 128], identity=ident[:, :]), [bckb, b_ident], [bpt], inc=(c == 1))
                                ptv = pt[:, 0:256].rearrange("p (c t) -> p c t", c=2)
                                act(lambda e, mi=mi, ptv=ptv: e.copy(out=mkT[:, :, mi * 128:(mi + 1) * 128], in_=ptv), [bpt], [b_mkT])
                            P.barrier()
                    ck("mem")
                    P.dma("sp", wxq[:, :, :], wl["w_xq"].rearrange("(c p) n -> p c n", p=128), bWs, [b_wxq])
                    P.dma("sp", wxo[:, :, :], wl["w_xo"].rearrange("(c p) n -> p c n", p=128), bWs, [b_wxo])
                    for gi, ni in enumerate((2, 3, 4, 5)):
                        bcast_row(gb[:, gi, :], I["norms"], (l * 7 + ni) * D, D, [b_gb])
                    ysb = [sb(S2, [128, D], F32, "ysb") for _ in range(2)]; b_ysb = [Buf(), Buf()]
                    yrot = [0]
                    xa = sb(S2, [128, 4, 256], BF16, "xa"); b_xa = Buf()
                    xaT = sb(S2, [128, 2, 512], BF16, "xaT"); b_xaT = Buf()
                    xblk2 = [sb(S2, [128, 4, D], F32, "xblk") for _ in range(2)]
                    b_xb2 = [[Buf() for _ in range(4)] for _ in range(2)]
                    hT2 = sb(S2, [128, 8, 512], BF16, "hT2"); b_hT2 = Buf()
                    qxT = sb(S2, [128, 2, 512], BF16, "qxT"); b_qxT = Buf()
                    gs = [sb(S2, [128, 514], F32, "gs") for _ in range(2)]; b_gs = [Buf(), Buf()]
                    halo = sb(S2, [128, NFC, 2], F32, "halo"); b_halo = Buf()
                    cc = [sb(S2, [128, 512], F32, "cc") for _ in range(2)]; b_cc = [Buf(), Buf()]
                    sl = [sb(S2, [128, 512], F32, "sl") for _ in range(2)]; b_sl = [Buf(), Buf()]
                    aTf = sb(S2, [128, NFC, 512], BF16, "aTf"); b_aTf = Buf()
                    wg = [sb(S2, [128, 2, 8, 128], BF16, "wg") for _ in range(3)]; bwg = [Buf() for _ in range(3)]
                    ssp = sb(S2, [128, 4], F32, "ssp"); b_ssp = Buf(strict=True)
                    if 'M' in DBG:
                        print("PH2 sbuf remaining", nc.sbuf_bytes_remaining)

                    def post_residual(pd, bpd, n, t, gidx, xblk, b_xb):
                        k = yrot[0]; yrot[0] = 1 - k
                        Y, bY = ysb[k], b_ysb[k]
                        act(lambda e: e.copy(out=Y[0:n, :], in_=pd[0:n, :]), [bpd], [bY])
                        dve(lambda e: e.scalar_tensor_tensor(out=junkb[0:n, :], in0=Y[0:n, :], scalar=1.0, in1=Y[0:n, :], op0=ALU.mult, op1=ALU.mult, accum_out=ssp[0:n, t:t + 1]), [bY], [b_junkb, b_ssp])
                        rstd_of(ssp[0:n, t:t + 1], n, D, b_ssp)
                        dve(lambda e: e.scalar_tensor_tensor(out=Y[0:n, :], in0=Y[0:n, :], scalar=ssp[0:n, t:t + 1], in1=gb[0:n, gidx, :], op0=ALU.mult, op1=ALU.mult), [bY, b_ssp, b_gb], [bY])
                        P.op("pool", lambda e: e.tensor_tensor(out=xblk[0:n, t, :], in0=xblk[0:n, t, :], in1=Y[0:n, :], op=ALU.add), [bY, b_xb[t]], [b_xb[t]])

                    def pre_norm_dve(subt, gidx, xblk, b_xb):
                        nt = len(subt)
                        nn = subt[0][1]
                        for t, (a0, n) in enumerate(subt):
                            dve(lambda e, t=t, n=n: e.scalar_tensor_tensor(out=junkb[0:n, :], in0=xblk[0:n, t, :], scalar=1.0, in1=xblk[0:n, t, :], op0=ALU.mult, op1=ALU.mult, accum_out=ss4[0:n, t:t + 1]), [b_xb[t]], [b_junkb, b_ss4])
                        rstd_of(ss4[0:nn, 0:nt], nn, D, b_ss4)
                        for t, (a0, n) in enumerate(subt):
                            dve(lambda e, t=t, n=n: e.scalar_tensor_tensor(out=hb4[0:n, t, :], in0=xblk[0:n, t, :], scalar=ss4[0:n, t:t + 1], in1=gb[0:n, gidx, :], op0=ALU.mult, op1=ALU.mult), [b_xb[t], b_ss4, b_gb], [b_hb4[t]])

                    def pre_norm_pe(subt):
                        for t, (a0, n) in enumerate(subt):
                            pt, bpt = next_ptr()
                            for c in range(8):
                                pe(lambda e, c=c, pt=pt, t=t, n=n: e.transpose(out=pt[:, c * 128:c * 128 + n], in_=hb4[0:n, t, c * 128:(c + 1) * 128], identity=ident[0:n, 0:n]), [b_hb4[t], b_ident], [bpt], inc=(c == 7))
                            ptv = pt[:, :].rearrange("p (c t) -> p c t", c=8)
                            act(lambda e, ptv=ptv, a0=a0, n=n: e.copy(out=hT2[:, :, a0:a0 + n], in_=ptv[:, :, 0:n]), [bpt], [b_hT2])

                    def head_dve(bi):
                        (qc0_, qpos_, nq_) = qblocks[bi]
                        subt_ = [(a0, min(128, nq_ - a0)) for a0 in range(0, nq_, 128)]
                        X_, bX_ = xblk2[bi % 2], b_xb2[bi % 2]
                        for t, (a0, n) in enumerate(subt_):
                            ti = (qc0_ + a0) // 128
                            P.dma("sp", X_[0:n, t, :], ydst[qc0_ + a0:qc0_ + a0 + n, :], [ybufs[ti]], [bX_[t]])
                        pre_norm_dve(subt_, 0, X_, bX_)
                        return subt_

                    drot = [0]

                    def next_dbl3():
                        i = drot[0]; drot[0] = (i + 1) % 3
                        return dbl[i], bdbl[i]

                    subt0_ = head_dve(0)
                    for j in range(3):
                        P.dma("sp", cw[:, :, j], bass.AP(I["conv_w"], (l * 3 + j) * DFF, [[1, 128], [128, NFC]]), (), [b_cw], allow_slow_non_contiguous=True)
                    P.dma("sp", cw[:, :, 3], bass.AP(I["conv_b"], l * DFF, [[1, 128], [128, NFC]]), (), [b_cw], allow_slow_non_contiguous=True)
                    if prm:
                        dve(lambda e: e.memset(halo[:, :, :], 0.0), (), [b_halo])
                    else:
                        for j in range(2):
                            P.dma("sp", halo[:, :, j], bass.AP(I["state_ffn_conv"], (l * 2 + j) * DFF, [[1, 128], [128, NFC]]), (), [b_halo], allow_slow_non_contiguous=True)
                    for f0 in range(0, NFC, 6):
                        f1 = min(NFC, f0 + 6)
                        P.dma("sp", wd_all[:, f0:f1, :], wds[:, f0:f1, :], bWs, [b_wd])
                    pre_norm_pe(subt0_)
                    for bi, (qc0, qpos, nq) in enumerate(qblocks):
                        subt = [(a0, min(128, nq - a0)) for a0 in range(0, nq, 128)]
                        xblk, b_xb = xblk2[bi % 2], b_xb2[bi % 2]
                        ck("wout")
                        pd, bpd = next_dbl()
                        for j in range(2):
                            for c in range(8):
                                pe(lambda e, j=j, c=c, pd=pd: e.matmul(pd[:, j * 512:j * 512 + nq], lhsT=wxq[:, c, j * 128:(j + 1) * 128], rhs=hT2[:, c, 0:nq], start=(c == 0), stop=(c == 7)), [b_wxq, b_hT2], [bpd], inc=(c == 7 and j == 1))
                        pdv = pd[:, :].rearrange("p (a b) -> p a b", a=2)
                        act(lambda e, pdv=pdv: e.copy(out=qxT[:, :, 0:nq], in_=pdv[:, :, 0:nq]), [bpd], [b_qxT])
                        callsX = []
                        for hp in range(2):
                            lanes = []
                            for li in range(2):
                                h = hp * 2 + li
                                r0 = li * 64
                                lanes.append(dict(kt=lambda c0, nk, r0=r0, hp=hp: mkT[r0:r0 + 64, hp, c0:c0 + nk], qt=lambda c0, c1, r0=r0, hp=hp: qxT[r0:r0 + 64, hp, c0 - qc0:c1 - qc0],
                                                  v=lambda kbi, nk, h=h: MV[0:nk, kbi, h, :], E=None, tp=None))
                            callsX.append(("X", lanes, (qc0, qpos, nq), "all", [(0, (0, 0, 128)), (1, (128, 128, 128))], [b_qxT, b_mkT, b_MV], epi_plain([(hp * 2) * 64, (hp * 2 + 1) * 64], xa, b_xa)))
                        attend_many(callsX)
                        flush_pending()
                        for t, (a0, n) in enumerate(subt):
                            pt, bpt = next_ptr()
                            for c in range(2):
                                pe(lambda e, c=c, pt=pt, t=t, n=n: e.transpose(out=pt[:, c * 128:c * 128 + n], in_=xa[0:n, t, c * 128:(c + 1) * 128], identity=ident[0:n, 0:n]), [b_xa, b_ident], [bpt], inc=(c == 1))
                            ptv = pt[:, 0:256].rearrange("p (c t) -> p c t", c=2)
                            act(lambda e, ptv=ptv, a0=a0, n=n: e.copy(out=xaT[:, :, a0:a0 + n], in_=ptv[:, :, 0:n]), [bpt], [b_xaT])
                        for t, (a0, n) in enumerate(subt):
                            pd, bpd = next_dbl()
                            for hf in range(2):
                                for c in range(2):
                                    pe(lambda e, hf=hf, c=c, a0=a0, n=n, pd=pd: e.matmul(pd[0:n, hf * 512:(hf + 1) * 512], lhsT=xaT[:, c, a0:a0 + n], rhs=wxo[:, c, hf * 512:(hf + 1) * 512], start=(c == 0), stop=(c == 1)), [b_xaT, b_wxo], [bpd], inc=(c == 1 and hf == 1))
                            post_residual(pd, bpd, n, t, 1, xblk, b_xb)
                        pre_norm_dve(subt, 2, xblk, b_xb)
                        pre_norm_pe(subt)
                        ck("xatt")
                        for f in range(NFC):
                            k = f % 3
                            k2 = f % 2
                            P.dma("sp", wg[k][:, :, :, :], WGU[l, f].rearrange("p (a c j) -> p a c j", a=2, c=8), bWs, [bwg[k]])
                            pd, bpd = next_dbl()
                            for j in range(2):
                                for c in range(8):
                                    pe(lambda e, j=j, c=c, pd=pd, k=k: e.matmul(pd[:, j * 512:j * 512 + nq], lhsT=wg[k][:, j, c, :], rhs=hT2[:, c, 0:nq], start=(c == 0), stop=(c == 7)), [bwg[k], b_hT2], [bpd], inc=(c == 7 and j == 1))
                            G_, bG = gs[k2], b_gs[k2]
                            C_, bC = cc[k2], b_cc[k2]
                            S_, bS = sl[k2], b_sl[k2]
                            act(lambda e, f=f, G_=G_: e.copy(out=G_[:, 0:2], in_=halo[:, f, :]), [b_halo], [bG])
                            act(lambda e, pd=pd, G_=G_: e.copy(out=G_[:, 2:2 + nq], in_=pd[:, 0:nq]), [bpd], [bG])
                            act(lambda e, f=f, G_=G_: e.copy(out=halo[:, f, :], in_=G_[:, nq:nq + 2]), [bG], [b_halo])
                            dve(lambda e, f=f, G_=G_, C_=C_: e.tensor_scalar(out=C_[:, 0:nq], in0=G_[:, 2:2 + nq], scalar1=cw[:, f, 2:3], scalar2=cw[:, f, 3:4], op0=ALU.mult, op1=ALU.add), [bG, b_cw], [bC])
                            dve(lambda e, f=f, G_=G_, C_=C_: e.scalar_tensor_tensor(out=C_[:, 0:nq], in0=G_[:, 1:1 + nq], scalar=cw[:, f, 1:2], in1=C_[:, 0:nq], op0=ALU.mult, op1=ALU.add), [bG, b_cw, bC], [bC])
                            dve(lambda e, f=f, G_=G_, C_=C_: e.scalar_tensor_tensor(out=C_[:, 0:nq], in0=G_[:, 0:nq], scalar=cw[:, f, 0:1], in1=C_[:, 0:nq], op0=ALU.mult, op1=ALU.add), [bG, b_cw, bC], [bC])
                            act(lambda e, C_=C_, S_=S_: e.activation(out=S_[:, 0:nq], in_=C_[:, 0:nq], func=AF.Silu, bias=zc[:, 0:1]), [bC], [bS])
                            dve(lambda e, f=f, pd=pd, S_=S_: e.tensor_tensor(out=aTf[:, f, 0:nq], in0=pd[:, 512:512 + nq], in1=S_[:, 0:nq], op=ALU.mult), [bpd, bS], [b_aTf])
                        nxt = None
                        if bi + 1 < len(qblocks):
                            nxt = head_dve(bi + 1)
                        for t, (a0, n) in enumerate(subt):
                            ti = (qc0 + a0) // 128
                            if nxt is not None and t == min(2, len(subt) - 1):
                                pre_norm_pe(nxt)
                            pd, bpd = next_dbl3()
                            for f in range(NFC):
                                for hf in range(2):
                                    pe(lambda e, hf=hf, f=f, a0=a0, n=n, pd=pd: e.matmul(pd[0:n, hf * 512:(hf + 1) * 512], lhsT=aTf[:, f, a0:a0 + n], rhs=wd_all[:, f, hf * 512:(hf + 1) * 512], start=(f == 0), stop=(f == NFC - 1), skip_group_check=True), [b_aTf, b_wd], [bpd], inc=(hf == 1 and f == NFC - 1))
                            post_residual(pd, bpd, n, t, 3, xblk, b_xb)
                            P.dma("pool", ydst[qc0 + a0:qc0 + a0 + n, :], xblk[0:n, t, :], [b_xb[t]], [ybufs[ti]])
                    for j in range(2):
                        P.dma("pool", bass.AP(O[pfx + "conv"], ((l * (NP if prm else 1) + so) * 2 + j) * DFF, [[1, 128], [128, NFC]]), halo[:, :, j], [b_halo], [], allow_slow_non_contiguous=True)
                    P.barrier()

        YB = {}
        for s in range(NP):
            YB[("p", s)] = [Buf() for _ in range((SEQ + 127) // 128)]
        YB[("s", 0)] = [Buf()]
        for l in range(DEPTH):
            pass
        seqs = [("p", s) for s in range(NP)] + [("s", 0)]
        nrun = 0
        for (kind, s) in seqs:
            for l in range(DEPTH):
                if stop >= 10 and nrun >= stop - 9:
                    break
                try:
                    run_layer(kind, s, l)
                except StopBuild:
                    P.finish()
                    return nc
                nrun += 1
        P.finish()
    return nc


NCORES = 8
_cache = {}


def kernel(**inputs):
    SEQ, PAST, TS, NP = 2048, 2048, 32, 4
    key = (NP, SEQ, PAST, TS)
    if key not in _cache:
        _cache[key] = build(NP, SEQ, PAST, TS)
    nc = _cache[key]
    hc = host_consts(SEQ, PAST, TS)
    f = lambda a: np.ascontiguousarray(np.asarray(a, dtype=np.float32))
    in_maps = []
    for i in range(NCORES):
        m = {}
        m["x_prompt"] = f(inputs["x_prompt"][NP * i:NP * (i + 1)])
        m["x_sample"] = f(inputs["x_sample"][i:i + 1])
        m["mem_prompt"] = f(inputs["mem_prompt"][NP * i:NP * (i + 1)])
        for n in ("cache_a_k", "cache_a_v", "cache_b_k", "cache_b_v", "cache_c_latent", "cache_c_rope_k", "cache_mem_k", "cache_mem_v", "state_ffn_conv"):
            a = np.asarray(inputs[n])[:, i]
            m[n] = f(a.reshape(a.shape[0], a.shape[1], -1))
        for n in ("w_in", "w_out", "norms", "diff_subln", "t5_bias", "band_rel_bias", "mla_q_norm", "mla_w_uq", "mla_kv_norm", "mla_w_ukv",
                  "w_xq", "w_mk", "w_mv", "w_xo", "w_gate", "w_up", "conv_w", "conv_b", "w_down"):
            m[n] = f(inputs[n])
        m["diff_lambda"] = f(np.asarray(inputs["diff_lambda"]).reshape(2, 128))
        m.update(hc)
        in_maps.append(m)
    res = run_bass_kernel_spmd(nc, in_maps, core_ids=list(range(NCORES)))
    R = res.results
    cat0 = lambda n: np.concatenate([r[n] for r in R], axis=0)
    cat1 = lambda n: np.concatenate([r[n] for r in R], axis=1)
    y_p = cat0("y_prompt"); y_s = cat0("y_sample")
    B = y_p.shape[0]
    outs = [y_p, y_s,
            cat1("p_a_k").reshape(2, B, SEQ, 8, 64), cat1("p_a_v").reshape(2, B, SEQ, 8, 64),
            cat1("p_b_k").reshape(2, B, 512, 4, 64), cat1("p_b_v").reshape(2, B, 512, 4, 64),
            cat1("p_lat"), cat1("p_kpe"),
            cat1("p_mk").reshape(2, B, MEM, 4, 64), cat1("p_mv").reshape(2, B, MEM, 4, 64), cat1("p_conv"),
            cat1("s_a_k").reshape(2, NCORES, TS, 8, 64), cat1("s_a_v").reshape(2, NCORES, TS, 8, 64),
            cat1("s_b_k").reshape(2, NCORES, TS, 4, 64), cat1("s_b_v").reshape(2, NCORES, TS, 4, 64),
            cat1("s_lat"), cat1("s_kpe"), cat1("s_conv")]
    return tuple(np.ascontiguousarray(o, dtype=np.float32) for o in outs)
```
